# Optimizing a Trainium2 kernel written in Bass

```python
import math
import jax
import jax.numpy as jnp
from jax import lax
import numpy as np

D_MODEL = 1024
BATCH = 4
SEQ = 4096
DEPTH = 2

HA = 4
DKA = 128
DVA = 128
CONV_K = 5
CHUNK_A = 64
HB = 4
DHB = 64
QBLK = 128
HC = 8
KVC = 2
GC = HC // KVC
DHC = 64
WINDOW = 128
HD = 4
DKD = 64
DVD = 128
CHUNK_D = 64
N_BRANCH = 4
BRANCH_W = 512
D_FF = 4 * D_MODEL
ROPE_THETA = 10000.0
ROPE_DIM = 64
EPS = 1e-6

A_QKV_W = HA * (2 * DKA + DVA)
SPLIT_SIZES = (
    A_QKV_W, HA * DVA, 4 * HA,
    HB * 2 * DHB, HB * 2 * DHB, HB * 2 * DHB,
    HC * DHC, KVC * DHC, KVC * DHC,
    HD * DKD, HD * DKD, HD * DVD, 4 * HD, HD * DVD,
)
IN_W = sum(SPLIT_SIZES)

kernel_name = 'hybrid_gdn_diff_swa_mlstm_encoder'

F32 = jnp.float32


def rmsnorm(x, g):
    xf = x.astype(F32)
    y = xf * lax.rsqrt(jnp.mean(xf * xf, axis=-1, keepdims=True) + EPS)
    return (y * g.astype(F32)).astype(x.dtype)


def l2norm(x):
    xf = x.astype(F32)
    return xf * lax.rsqrt(jnp.sum(xf * xf, axis=-1, keepdims=True) + EPS)


def rope_tables(S, dim):
    inv = 1.0 / (ROPE_THETA ** (jnp.arange(0, dim, 2, dtype=F32) / dim))
    ang = jnp.arange(S, dtype=F32)[:, None] * inv[None, :]
    ang = jnp.concatenate([ang, ang], axis=-1)
    return jnp.cos(ang), jnp.sin(ang)


def apply_rope(x, cos, sin):
    x1, x2 = jnp.split(x, 2, axis=-1)
    rot = jnp.concatenate([-x2, x1], axis=-1)
    return (x * cos[None, :, None, :] + rot * sin[None, :, None, :]).astype(x.dtype)


def centred_dwconv(x, w):
    K, C = w.shape
    return lax.conv_general_dilated(
        x, w.reshape(K, 1, C).astype(x.dtype), window_strides=(1,),
        padding=[(K // 2, K // 2)], dimension_numbers=('NWC', 'WIO', 'NWC'),
        feature_group_count=C)


def gated_delta_chunked(q, k, v, g, beta):
    B, H, S, Dk = k.shape
    Dv = v.shape[-1]
    L = CHUNK_A
    N = S // L
    q = q.reshape(B, H, N, L, Dk)
    k = k.reshape(B, H, N, L, Dk)
    v = v.reshape(B, H, N, L, Dv)
    beta = beta.reshape(B, H, N, L)
    gc = jnp.cumsum(g.reshape(B, H, N, L), axis=-1)
    tri = jnp.tril(jnp.ones((L, L), dtype=bool))
    tri_strict = jnp.tril(jnp.ones((L, L), dtype=bool), -1)
    decay = jnp.where(tri, jnp.exp(jnp.where(tri, gc[..., :, None] - gc[..., None, :], 0.0)), 0.0)
    k_beta = k * beta[..., None]
    v_beta = v * beta[..., None]
    a_strict = jnp.where(tri_strict, jnp.einsum('bhnid,bhnjd->bhnij', k_beta, k) * decay, 0.0)
    t_mat = a_strict + jnp.eye(L, dtype=F32)
    u = lax.linalg.triangular_solve(t_mat, v_beta, left_side=True, lower=True, unit_diagonal=True)
    w = lax.linalg.triangular_solve(t_mat, k_beta * jnp.exp(gc)[..., None], left_side=True,
                                    lower=True, unit_diagonal=True)
    qk_intra = jnp.where(tri, jnp.einsum('bhnid,bhnjd->bhnij', q, k) * decay, 0.0)

    def step(state, inp):
        q_c, k_c, u_c, w_c, g_c, a_c = inp
        v_new = u_c - jnp.einsum('bhld,bhde->bhle', w_c, state)
        o = (jnp.einsum('bhld,bhde->bhle', q_c * jnp.exp(g_c)[..., None], state)
             + jnp.einsum('bhij,bhje->bhie', a_c, v_new))
        g_last = g_c[..., -1]
        k_dec = k_c * jnp.exp(g_last[..., None] - g_c)[..., None]
        state = state * jnp.exp(g_last)[..., None, None] + jnp.einsum('bhld,bhle->bhde', k_dec, v_new)
        return state, o

    xs = tuple(jnp.moveaxis(t, 2, 0) for t in (q, k, u, w, gc, qk_intra))
    _, o = lax.scan(step, jnp.zeros((B, H, Dk, Dv), F32), xs)
    return jnp.moveaxis(o, 0, 2).reshape(B, H, S, Dv)


def gdn_mixer(qkv, z, ab, conv_w, a_log, dt_bias, norm_g):
    B, S, _ = qkv.shape
    qkv = jax.nn.silu(centred_dwconv(qkv, conv_w)).astype(F32)
    q, k, v = jnp.split(qkv, [HA * DKA, 2 * HA * DKA], axis=-1)
    q = l2norm(q.reshape(B, S, HA, DKA)) * (DKA ** -0.5)
    k = l2norm(k.reshape(B, S, HA, DKA))
    v = v.reshape(B, S, HA, DVA)
    ab = ab.astype(F32).reshape(B, S, 2, 2, HA)
    g = -jnp.exp(a_log.astype(F32)) * jax.nn.softplus(ab[:, :, 0] + dt_bias.astype(F32))
    beta = jax.nn.sigmoid(ab[:, :, 1])
    qh, kh, vh = (jnp.moveaxis(t, 1, 2) for t in (q, k, v))
    gh = jnp.transpose(g, (0, 2, 3, 1))
    bh = jnp.transpose(beta, (0, 2, 3, 1))
    flip = lambda t: jnp.flip(t, axis=2)
    o_f = gated_delta_chunked(qh, kh, vh, gh[:, 0], bh[:, 0])
    o_b = flip(gated_delta_chunked(flip(qh), flip(kh), flip(vh), flip(gh[:, 1]), flip(bh[:, 1])))
    o = jnp.moveaxis(o_f + o_b, 1, 2)
    o = rmsnorm(o, norm_g) * jax.nn.silu(z.astype(F32).reshape(B, S, HA, DVA))
    return o.reshape(B, S, HA * DVA)


def diff_mixer(q, k, v, lam_params, norm_g, lam_init, cos, sin):
    B, S, _ = q.shape
    q = apply_rope(q.reshape(B, S, HB * 2, DHB), cos, sin).reshape(B, S, HB, 2, DHB)
    k = apply_rope(k.reshape(B, S, HB * 2, DHB), cos, sin).reshape(B, S, HB, 2, DHB)
    v = v.reshape(B, S, HB, 2 * DHB)
    lp = lam_params.astype(F32)
    lam = jnp.exp(jnp.sum(lp[0] * lp[1])) - jnp.exp(jnp.sum(lp[2] * lp[3])) + lam_init
    scale = DHB ** -0.5
    nq = S // QBLK
    qb = jnp.moveaxis(q.reshape(B, nq, QBLK, HB, 2, DHB), 1, 0)

    def one_block(q_blk):
        s = jnp.einsum('bqhmd,bkhmd->bhmqk', q_blk, k).astype(F32) * scale
        p = jax.nn.softmax(s, axis=-1)
        a = p[:, :, 0] - lam * p[:, :, 1]
        return jnp.einsum('bhqk,bkhe->bqhe', a.astype(v.dtype), v)

    o = lax.map(one_block, qb)
    o = jnp.moveaxis(o, 0, 1).reshape(B, S, HB, 2 * DHB)
    o = rmsnorm(o, norm_g) * (1.0 - lam_init)
    return o.reshape(B, S, HB * 2 * DHB)


def swa_mixer(q, k, v, sink, cos, sin):
    B, S, _ = q.shape
    nb = S // WINDOW
    q = apply_rope(q.reshape(B, S, HC, DHC), cos, sin)
    k = apply_rope(k.reshape(B, S, KVC, DHC), cos, sin)
    v = v.reshape(B, S, KVC, DHC)

    def band(t):
        tp = jnp.pad(t, ((0, 0), (WINDOW, WINDOW), (0, 0), (0, 0))).reshape(B, nb + 2, WINDOW, KVC, DHC)
        return jnp.concatenate([tp[:, :-2], tp[:, 1:-1], tp[:, 2:]], axis=2)

    kw, vw = band(k), band(v)
    qb = q.reshape(B, nb, WINDOW, KVC, GC, DHC)
    s = jnp.einsum('bnqcgd,bnkcd->bncgqk', qb, kw).astype(F32) * (DHC ** -0.5)
    rel = jnp.arange(3 * WINDOW)[None, :] - WINDOW - jnp.arange(WINDOW)[:, None]
    kpos = jnp.arange(nb)[:, None] * WINDOW - WINDOW + jnp.arange(3 * WINDOW)[None, :]
    valid = (jnp.abs(rel) <= WINDOW)[None] & ((kpos >= 0) & (kpos < S))[:, None, :]
    s = jnp.where(valid[None, :, None, None], s, -jnp.inf)
    sk = sink.astype(F32).reshape(1, 1, KVC, GC, 1, 1)
    m = jnp.maximum(jnp.max(s, axis=-1, keepdims=True), sk)
    p = jnp.exp(s - m)
    denom = jnp.sum(p, axis=-1, keepdims=True) + jnp.exp(sk - m)
    o = jnp.einsum('bncgqk,bnkcd->bnqcgd', (p / denom).astype(v.dtype), vw)
    return o.reshape(B, S, HC * DHC)


def mlstm_chunked(q, k, v, log_i, log_f):
    B, H, S, Dk = q.shape
    Dv = v.shape[-1]
    L = CHUNK_D
    N = S // L
    q = q.reshape(B, H, N, L, Dk)
    k = k.reshape(B, H, N, L, Dk)
    v = v.reshape(B, H, N, L, Dv)
    log_i = log_i.reshape(B, H, N, L)
    b = jnp.cumsum(log_f.reshape(B, H, N, L), axis=-1)
    tri = jnp.tril(jnp.ones((L, L), dtype=bool))
    d_log = jnp.where(tri, b[..., :, None] - b[..., None, :] + log_i[..., None, :], -jnp.inf)
    qk = jnp.einsum('bhnid,bhnjd->bhnij', q, k)
    e_log = b[..., -1:] - b + log_i

    def step(carry, inp):
        c, n, m = carry
        q_c, k_c, v_c, b_c, d_c, qk_c, e_c = inp
        inter = b_c + m[..., None]
        m_t = jnp.maximum(inter, jnp.max(d_c, axis=-1))
        w_inter = jnp.exp(inter - m_t)
        w_intra = jnp.exp(d_c - m_t[..., None]) * qk_c
        num = (w_inter[..., None] * jnp.einsum('bhld,bhde->bhle', q_c, c)
               + jnp.einsum('bhij,bhje->bhie', w_intra, v_c))
        den = w_inter * jnp.einsum('bhld,bhd->bhl', q_c, n) + jnp.sum(w_intra, axis=-1)
        h = num / jnp.maximum(jnp.abs(den), jnp.exp(-m_t))[..., None]
        inter_end = b_c[..., -1] + m
        m_new = jnp.maximum(inter_end, jnp.max(e_c, axis=-1))
        k_w = jnp.exp(e_c - m_new[..., None])[..., None] * k_c
        sc = jnp.exp(inter_end - m_new)
        c = sc[..., None, None] * c + jnp.einsum('bhld,bhle->bhde', k_w, v_c)
        n = sc[..., None] * n + jnp.sum(k_w, axis=-2)
        return (c, n, m_new), h

    init = (jnp.zeros((B, H, Dk, Dv), F32), jnp.zeros((B, H, Dk), F32), jnp.zeros((B, H), F32))
    xs = tuple(jnp.moveaxis(t, 2, 0) for t in (q, k, v, b, d_log, qk, e_log))
    _, h = lax.scan(step, init, xs)
    return jnp.moveaxis(h, 0, 2).reshape(B, H, S, Dv)


def mlstm_mixer(q, k, v, gif, o, gate_b, norm_g):
    B, S, _ = q.shape
    q = q.astype(F32).reshape(B, S, HD, DKD)
    k = k.astype(F32).reshape(B, S, HD, DKD) * (DKD ** -0.5)
    v = v.astype(F32).reshape(B, S, HD, DVD)
    pre = gif.astype(F32).reshape(B, S, 2, 2, HD) + gate_b.astype(F32)
    log_i = jnp.transpose(pre[:, :, 0], (0, 2, 3, 1))
    log_f = jnp.transpose(jax.nn.log_sigmoid(pre[:, :, 1]), (0, 2, 3, 1))
    qh, kh, vh = (jnp.moveaxis(t, 1, 2) for t in (q, k, v))
    flip = lambda t: jnp.flip(t, axis=2)
    h_f = mlstm_chunked(qh, kh, vh, log_i[:, 0], log_f[:, 0])
    h_b = flip(mlstm_chunked(flip(qh), flip(kh), flip(vh), flip(log_i[:, 1]), flip(log_f[:, 1])))
    h = jnp.moveaxis(h_f + h_b, 1, 2)
    h = rmsnorm(h, norm_g) * jax.nn.sigmoid(o.astype(F32).reshape(B, S, HD, DVD))
    return h.reshape(B, S, HD * DVD)


def setup_inputs(seed: int = 0) -> dict:
    key = jax.random.key(seed)
    ks = jax.random.split(key, 20)
    L, D = DEPTH, D_MODEL
    nrm = lambda kk, shape, sc: jax.random.normal(kk, shape, F32) * sc
    gate_base = jnp.concatenate([jnp.zeros((1, 1, 2, HD), F32), jnp.full((1, 1, 2, HD), 3.0, F32)], axis=1)
    return {
        'x': nrm(ks[0], (BATCH, SEQ, D), 1.0),
        'norm1_g': 1.0 + nrm(ks[1], (L, D), 0.02),
        'w_in': nrm(ks[2], (L, D, IN_W), D ** -0.5),
        'gdn_conv_w': nrm(ks[3], (L, CONV_K, A_QKV_W), CONV_K ** -0.5),
        'gdn_a_log': jnp.log(jax.random.uniform(ks[4], (L, 2, HA), F32, 1.0, 16.0)),
        'gdn_dt_bias': nrm(ks[5], (L, 2, HA), 0.1),
        'gdn_norm_g': 1.0 + nrm(ks[6], (L, DVA), 0.02),
        'diff_lambda': nrm(ks[7], (L, 4, DHB), 0.1),
        'diff_norm_g': 1.0 + nrm(ks[8], (L, 2 * DHB), 0.02),
        'swa_sink': nrm(ks[9], (L, HC), 0.1),
        'mlstm_gate_b': gate_base + nrm(ks[10], (L, 2, 2, HD), 0.1),
        'mlstm_norm_g': 1.0 + nrm(ks[11], (L, DVD), 0.02),
        'w_branch': nrm(ks[12], (L, N_BRANCH, BRANCH_W, D), BRANCH_W ** -0.5),
        'w_gate': nrm(ks[13], (L, D, N_BRANCH, D), D ** -0.5),
        'w_out': nrm(ks[14], (L, D, D), D ** -0.5),
        'norm2_g': 1.0 + nrm(ks[15], (L, D), 0.02),
        'w_mlp1': nrm(ks[16], (L, D, D_FF), D ** -0.5),
        'w_mlp2': nrm(ks[17], (L, D_FF, D), D_FF ** -0.5),
        'final_norm_g': 1.0 + nrm(ks[18], (D,), 0.02),
    }


def reference(x, norm1_g, w_in, gdn_conv_w, gdn_a_log, gdn_dt_bias, gdn_norm_g, diff_lambda,
              diff_norm_g, swa_sink, mlstm_gate_b, mlstm_norm_g, w_branch, w_gate, w_out,
              norm2_g, w_mlp1, w_mlp2, final_norm_g):
    S = x.shape[1]
    cos, sin = rope_tables(S, ROPE_DIM)
    offsets = np.cumsum(SPLIT_SIZES)[:-1].tolist()
    for l in range(DEPTH):
        xn = rmsnorm(x, norm1_g[l])
        h = xn @ w_in[l]
        (a_qkv, a_z, a_ab, b_q, b_k, b_v, c_q, c_k, c_v,
         d_q, d_k, d_v, d_if, d_o) = jnp.split(h, offsets, axis=-1)
        lam_init = 0.8 - 0.6 * math.exp(-0.3 * l)
        y_a = gdn_mixer(a_qkv, a_z, a_ab, gdn_conv_w[l], gdn_a_log[l], gdn_dt_bias[l], gdn_norm_g[l])
        y_b = diff_mixer(b_q, b_k, b_v, diff_lambda[l], diff_norm_g[l], lam_init, cos, sin)
        y_c = swa_mixer(c_q, c_k, c_v, swa_sink[l], cos, sin)
        y_d = mlstm_mixer(d_q, d_k, d_v, d_if, d_o, mlstm_gate_b[l], mlstm_norm_g[l])
        ys = jnp.stack([y_a.astype(x.dtype), y_b.astype(x.dtype), y_c.astype(x.dtype),
                        y_d.astype(x.dtype)], axis=2)
        branch = jnp.einsum('bsnw,nwd->bsnd', ys, w_branch[l])
        gate = jax.nn.sigmoid(jnp.einsum('bsd,dne->bsne', xn, w_gate[l]))
        x = x + jnp.sum(gate * branch, axis=2) @ w_out[l]
        xn = rmsnorm(x, norm2_g[l])
        x = x + jnp.square(jax.nn.relu(xn @ w_mlp1[l])) @ w_mlp2[l]
    return rmsnorm(x, final_norm_g)
```

```python
import time
import ml_dtypes
from contextlib import ExitStack
import numpy as np
import concourse.bass as bass
import concourse.mybir as mybir
from concourse.bass_utils import run_bass_kernel_spmd

F32 = mybir.dt.float32
BF16 = mybir.dt.bfloat16
ALU = mybir.AluOpType
AF = mybir.ActivationFunctionType
AX = mybir.AxisListType

ENGS = ("pe", "act", "dve", "pool", "sp")
N_DMA_SEMS = 40


class Prog:
    def __init__(self):
        self.nc = bass.Bass("TRN2", target_bir_lowering=False)
        nc = self.nc
        self.stack = ExitStack()
        self.pstack = ExitStack()
        self.sems = {e: self.stack.enter_context(nc.semaphore("se_" + e)) for e in ENGS}
        for i in range(N_DMA_SEMS):
            self.sems[("dma", i)] = self.stack.enter_context(nc.semaphore("sd_%d" % i))
        self.sems["bar"] = self.stack.enter_context(nc.semaphore("s_bar"))
        self.cnt = {e: 0 for e in ENGS}
        self.dcnt = {("dma", i): 0 for i in range(N_DMA_SEMS)}
        self.known = {e: {} for e in ENGS}
        self.phase_no = 0
        self.uid = 0
        self._reset_phase()

    def _reset_phase(self):
        self.q = {e: [] for e in ENGS}
        self.last_w = {}
        self.readers = {}
        self.dma_map = {}
        self.out_tokens = []

    def dram(self, name, shape, dtype, kind):
        ov = getattr(self, "override", None) or {}
        if name in ov:
            return ov[name]
        full = getattr(self, "pre", "") + name
        self.ext_names = getattr(self, "ext_names", [])
        self.ext_names.append(full)
        return self.nc.dram_tensor(full, list(shape), dtype, kind=kind).ap()

    def scratch(self, name, shape, dtype):
        return self.nc.dram_tensor(name, list(shape), dtype).ap()

    def sb(self, name, shape, dtype, glob=False):
        self.uid += 1
        st = self.stack if glob else self.pstack
        return st.enter_context(self.nc.sbuf_tensor("s%d_%s" % (self.uid, name), list(shape), dtype))

    def ps(self, name, shape, dtype=F32):
        self.uid += 1
        return self.pstack.enter_context(self.nc.psum_tensor("p%d_%s" % (self.uid, name), list(shape), dtype))

    def op(self, eng, fn, reads=(), writes=(), dma=None, is_out=False, dma_inc=16):
        toks = []
        for k in reads:
            toks += self.last_w.get(k, [])
        for k in writes:
            toks += self.last_w.get(k, [])
            toks += self.readers.get(k, [])
        need = {}
        for s, v in toks:
            if eng == "pe" and s == "pe":
                continue
            if v > need.get(s, 0):
                need[s] = v
        waits = []
        kn = self.known[eng]
        for s, v in need.items():
            if kn.get(s, 0) >= v:
                continue
            kn[s] = v
            waits.append((s, v))
        if dma is not None:
            if dma not in self.dma_map:
                assert len(self.dma_map) < N_DMA_SEMS, "too many DMA groups in one phase"
                self.dma_map[dma] = ("dma", len(self.dma_map))
            sk = self.dma_map[dma]
            self.dcnt[sk] += dma_inc
            tok = (sk, self.dcnt[sk])
            inc = dma_inc
        else:
            self.cnt[eng] += 1
            tok = (eng, self.cnt[eng])
            inc = 1
        self.q[eng].append((waits, fn, tok, inc))
        for k in reads:
            self.readers.setdefault(k, []).append(tok)
        for k in writes:
            self.last_w[k] = [tok]
            self.readers[k] = []
        if is_out:
            self.out_tokens.append(tok)
        return tok

    def mm(self, out, lhsT, rhs, start=True, stop=True, reads=(), writes=()):
        return self.op("pe", lambda e: e.matmul(out, lhsT, rhs, start=start, stop=stop), reads, writes)

    def tr(self, out, in_, ident, reads=(), writes=()):
        return self.op("pe", lambda e: e.transpose(out, in_, ident), reads, writes)

    def act(self, out, in_, func, reads=(), writes=(), eng="act", **kw):
        return self.op(eng, lambda e: e.activation(out, in_, func, **kw), reads, writes)

    def dma(self, eng, out, in_, key, reads=(), writes=(), is_out=False, **kw):
        return self.op(eng, lambda e: e.dma_start(out=out, in_=in_, **kw), reads, writes, dma=key, is_out=is_out)

    def end_phase(self):
        nc = self.nc
        self.phase_no += 1
        pno = self.phase_no
        fin = {}
        for e in ENGS:
            if e != "sp" and self.cnt[e] > self.known["sp"].get(e, 0):
                fin[e] = self.cnt[e]
        for key, sk in self.dma_map.items():
            if self.dcnt[sk] > self.known["sp"].get(sk, 0):
                fin[sk] = self.dcnt[sk]
        for s, v in fin.items():
            self.known["sp"][s] = v
        sems = self.sems
        q = self.q
        first = (pno == 1)

        def replay(name, eng):
            if not first:
                eng.wait_ge(sems["bar"], pno - 1)
            for waits, fn, tok, inc in q[name]:
                for s, v in waits:
                    eng.wait_ge(sems[s], v)
                fn(eng).then_inc(sems[tok[0]], inc)
            if name == "sp":
                for s, v in fin.items():
                    eng.wait_ge(sems[s], v)
                eng.sem_inc(sems["bar"], 1)

        with nc.Block() as block:
            @block.tensor
            def _(e):
                replay("pe", e)

            @block.scalar
            def _(e):
                replay("act", e)

            @block.vector
            def _(e):
                replay("dve", e)

            @block.gpsimd
            def _(e):
                replay("pool", e)

            @block.sync
            def _(e):
                replay("sp", e)
        st = {e: len(q[e]) for e in ENGS}
        for e in ENGS:
            for e2 in ENGS:
                self.known[e][e2] = self.cnt[e2]
            for sk, v in self.dcnt.items():
                self.known[e][sk] = v
        self._reset_phase()
        self.pstack.close()
        self.pstack = ExitStack()
        return st

    def finish(self):
        self.end_phase()
        self.stack.close()
        return self.nc

    def stats(self):
        return {e: len(self.q[e]) for e in ENGS}


TG = 512


class WLoader:
    def __init__(self, P, maxcols, nbuf=3, cast_engs=("pool", "act", "dve", "pool")):
        self.P = P
        self.nbuf = nbuf
        self.st = [P.sb("wst%d" % i, [128, maxcols], F32) for i in range(nbuf)]
        self.bf = [P.sb("wbf%d" % i, [128, maxcols], BF16) for i in range(nbuf)]
        self.i = 0
        self.engs = cast_engs

    def load(self, src, ncols):
        P = self.P
        i = self.i % self.nbuf
        eng = self.engs[self.i % len(self.engs)]
        self.i += 1
        st, bf = self.st[i], self.bf[i]
        P.dma("sp", st[:, 0:ncols], src, ("wst", i), writes=[("wst", i)])
        if eng == "act":
            P.op("act", lambda e: e.copy(bf[:, 0:ncols], st[:, 0:ncols]), reads=[("wst", i)], writes=[("wbf", i)])
        else:
            P.op(eng, lambda e: e.tensor_copy(bf[:, 0:ncols], st[:, 0:ncols]), reads=[("wst", i)],
                 writes=[("wbf", i)])
        return bf, ("wbf", i)


def rmsnorm_fm(P, xs, xkey, gs, gkey, out, outkey, ones, sq, pss, rstd, T, out_scale_keyed=True):
    P.act(sq[:], xs[:], AF.Square, reads=[xkey], writes=["sq"])
    for c in range(8):
        P.mm(pss[:, 0:T], ones[:], sq[:, c, :], start=(c == 0), stop=(c == 7), reads=["ones", "sq"], writes=["pss"])
    P.act(rstd[:, 0:T], pss[:, 0:T], AF.Sqrt, reads=["pss"], writes=["rstd"], scale=1.0 / 1024, bias=1e-6)
    P.op("dve", lambda e: e.reciprocal(rstd[:, 0:T], rstd[:, 0:T]), reads=["rstd"], writes=["rstd"])
    for c in range(8):
        P.op("dve", lambda e, c=c: e.scalar_tensor_tensor(out[:, c, :], xs[:, c, :], gs[:, c:c + 1], rstd[:, 0:T],
                                                     ALU.mult, ALU.mult),
             reads=[xkey, gkey, "rstd"], writes=[outkey])


def build_dense(NT=2048, final=False, P=None):
    own = P is None
    if own:
        P = Prog()
    xT = P.dram("xT", [128, 8, NT], F32, "ExternalInput")
    ypair = (getattr(P, "override", None) or {}).get("ypair")
    SEQ = 2 * NT
    yT = P.dram("yT", [128, 16, NT], BF16, "ExternalInput") if ypair is None else None
    if ypair is not None:
        identbd = P.dram("identb", [128, 128], BF16, "ExternalInput")
        mseld = P.dram("msel", [128, 2], F32, "ExternalInput")
    g1 = P.dram("g1", [128, 8], F32, "ExternalInput")
    g2 = P.dram("g2", [128, 8], F32, "ExternalInput")
    g3 = P.dram("g3", [128, 8], F32, "ExternalInput")
    wg = P.dram("wg", [4, 8, 128, 1024], F32, "ExternalInput")
    wb = P.dram("wb", [4, 8, 128, 512], F32, "ExternalInput")
    wo = P.dram("wo", [8, 128, 1024], F32, "ExternalInput")
    w1 = P.dram("w1", [32, 128, 1024], F32, "ExternalInput")
    w2 = P.dram("w2", [8, 128, 4096], F32, "ExternalInput")
    out = P.dram("out", [128, 8, NT], F32, "ExternalOutput")

    xs = P.sb("xs", [128, 8, TG], F32)
    ys = P.sb("ys", [128, 16, TG], BF16)
    xn = P.sb("xn", [128, 8, TG], BF16)
    mg = P.sb("mg", [128, 8, TG], BF16)
    hT = P.sb("hT", [128, 32, TG], BF16)
    sq = P.sb("sq", [128, 8, TG], BF16)
    rstd = P.sb("rstd", [128, TG], F32)
    ones = P.sb("ones", [128, 128], BF16)
    g1s = P.sb("g1s", [128, 8], F32)
    g2s = P.sb("g2s", [128, 8], F32)
    g3s = P.sb("g3s", [128, 8], F32)
    sg = [P.sb("sg%d" % i, [128, TG], F32) for i in range(2)]
    pr = [P.sb("pr%d" % i, [128, TG], F32) for i in range(2)]
    acc = P.sb("acc", [128, TG], F32)
    rl = [P.sb("rl%d" % i, [128, TG], F32) for i in range(2)]
    xo = P.sb("xo", [128, 8, TG], F32)
    pss = P.ps("pss", [128, TG], F32)
    pg = [P.ps("pg%d" % i, [128, TG], F32) for i in range(2)]
    pb = [P.ps("pb%d" % i, [128, TG], F32) for i in range(2)]
    WL = WLoader(P, 1024, nbuf=6, cast_engs=("pool",))
    if ypair is not None:
        identb = P.sb("identb", [128, 128], BF16)
        msel = P.sb("msel", [128, 2], F32)
        cand = [P.sb("cand%d" % i, [128, 1024], BF16) for i in range(2)]
        ysel = P.sb("ysel", [128, 1024], BF16)
        ptr = P.ps("ptr", [128, 8, 128], BF16)
        P.dma("sp", identb[:], identbd, "identb", writes=["identb"])
        P.dma("sp", msel[:], mseld, "msel", writes=["msel"])

    P.op("pool", lambda e: e.memset(ones[:], 1.0), writes=["ones"])
    P.dma("sp", g1s[:], g1, "g1s", writes=["g1s"])
    P.dma("sp", g2s[:], g2, "g2s", writes=["g2s"])
    P.dma("sp", g3s[:], g3, "g3s", writes=["g3s"])
    cnt = 0
    for tg in range(NT // TG):
        tsl = slice(tg * TG, (tg + 1) * TG)
        P.dma("sp", xs[:], xT[:, :, tsl], "xs", writes=["xs"])
        if ypair is None:
            P.dma("sp", ys[:], yT[:, :, tsl], "ys", writes=["ys"])
        else:
            ys4 = ys[:].rearrange("p (n c) t -> p n c t", c=4)
            for tt in range(TG // 128):
                tok0 = tg * TG + tt * 128
                for r in range(2):
                    for h in range(2):
                        row0 = r * SEQ + h * NT + tok0
                        P.dma("sp", cand[h][:], ypair[row0:row0 + 128, :], ("cand", h), writes=[("cand", h)])
                    P.op("dve", lambda e: e.tensor_scalar(ysel[:], cand[0][:], msel[:, 0:1], None, ALU.mult),
                         reads=[("cand", 0), "msel"], writes=["ysel"])
                    P.op("dve", lambda e: e.scalar_tensor_tensor(ysel[:], cand[1][:], msel[:, 1:2], ysel[:], ALU.mult,
                                                                 ALU.add), reads=[("cand", 1), "msel", "ysel"],
                         writes=["ysel"])
                    for n in range(4):
                        for jj in range(2):
                            P.tr(ptr[:, n * 2 + jj, :], ysel[:, n * 256 + jj * 128:n * 256 + (jj + 1) * 128], identb[:],
                                 reads=["ysel", "identb"], writes=["ptr"])
                    for n in range(4):
                        P.act(ys4[:, n, 2 * r:2 * r + 2, tt * 128:(tt + 1) * 128], ptr[:, 2 * n:2 * n + 2, :], AF.Copy,
                              reads=["ptr"], writes=["ys"])
        rmsnorm_fm(P, xs, "xs", g1s, "g1s", xn, "xn", ones, sq, pss, rstd, TG)
        for j in range(8):
            for n in range(4):
                k = cnt % 2
                cnt += 1
                wgt, wgk = WL.load(wg[n, j], 1024)
                for c in range(8):
                    P.mm(pg[k][:], wgt[:, c * 128:(c + 1) * 128], xn[:, c, :], start=(c == 0), stop=(c == 7),
                         reads=[wgk, "xn"], writes=[("pg", k)])
                wbt, wbk = WL.load(wb[n, j], 512)
                for c in range(4):
                    P.mm(pb[k][:], wbt[:, c * 128:(c + 1) * 128], ys[:, n * 4 + c, :], start=(c == 0), stop=(c == 3),
                         reads=[wbk, "ys"], writes=[("pb", k)])
                P.act(sg[k][:], pg[k][:], AF.Sigmoid, reads=[("pg", k)], writes=[("sg", k)])
                if n == 0:
                    P.op("dve", lambda e, k=k: e.tensor_tensor(acc[:], pb[k][:], sg[k][:], ALU.mult),
                         reads=[("pb", k), ("sg", k)], writes=["acc"])
                else:
                    P.op("dve", lambda e, k=k: e.tensor_tensor(pr[k][:], pb[k][:], sg[k][:], ALU.mult),
                         reads=[("pb", k), ("sg", k)], writes=[("pr", k)])
                    if n < 3:
                        P.op("dve", lambda e, k=k: e.tensor_tensor(acc[:], acc[:], pr[k][:], ALU.add),
                             reads=["acc", ("pr", k)], writes=["acc"])
                    else:
                        P.op("dve", lambda e, k=k, j=j: e.tensor_tensor(mg[:, j, :], acc[:], pr[k][:], ALU.add),
                             reads=["acc", ("pr", k)], writes=["mg"])
        for j in range(8):
            k = cnt % 2
            cnt += 1
            wt, wk = WL.load(wo[j], 1024)
            for c in range(8):
                P.mm(pg[k][:], wt[:, c * 128:(c + 1) * 128], mg[:, c, :], start=(c == 0), stop=(c == 7),
                     reads=[wk, "mg"], writes=[("pg", k)])
            P.op("dve", lambda e, k=k, j=j: e.tensor_tensor(xs[:, j, :], pg[k][:], xs[:, j, :], ALU.add),
                 reads=[("pg", k), "xs"], writes=["xs"])
        rmsnorm_fm(P, xs, "xs", g2s, "g2s", xn, "xn", ones, sq, pss, rstd, TG)
        for f in range(32):
            k = cnt % 2
            cnt += 1
            wt, wk = WL.load(w1[f], 1024)
            for c in range(8):
                P.mm(pg[k][:], wt[:, c * 128:(c + 1) * 128], xn[:, c, :], start=(c == 0), stop=(c == 7),
                     reads=[wk, "xn"], writes=[("pg", k)])
            P.act(rl[k][:], pg[k][:], AF.Relu, reads=[("pg", k)], writes=[("rl", k)])
            P.op("dve", lambda e, k=k, f=f: e.tensor_tensor(hT[:, f, :], rl[k][:], rl[k][:], ALU.mult),
                 reads=[("rl", k)], writes=["hT"])
        for j in range(8):
            k = cnt % 2
            cnt += 1
            for fb in range(4):
                wt, wk = WL.load(w2[j][:, fb * 1024:(fb + 1) * 1024], 1024)
                for f8 in range(8):
                    f = fb * 8 + f8
                    P.mm(pb[k][:], wt[:, f8 * 128:(f8 + 1) * 128], hT[:, f, :], start=(f == 0), stop=(f == 31),
                         reads=[wk, "hT"], writes=[("pb", k)])
            P.op("dve", lambda e, k=k, j=j: e.tensor_tensor(xs[:, j, :], pb[k][:], xs[:, j, :], ALU.add),
                 reads=[("pb", k), "xs"], writes=["xs"])
        if final:
            rmsnorm_fm(P, xs, "xs", g3s, "g3s", xo, "xo", ones, sq, pss, rstd, TG)
            P.dma("sp", out[:, :, tsl], xo[:], "out", reads=["xo"], is_out=True)
        else:
            P.dma("sp", out[:, :, tsl], xs[:], "out", reads=["xs"], is_out=True)
    print("dense stats", P.stats())
    return (P.finish() if own else None)


def dense_layout(w_gate, w_branch, w_out, w_mlp1, w_mlp2):
    wg = w_gate.reshape(8, 128, 4, 8, 128).transpose(2, 3, 1, 0, 4).reshape(4, 8, 128, 1024)
    wb = w_branch.reshape(4, 4, 128, 8, 128).transpose(0, 3, 2, 1, 4).reshape(4, 8, 128, 512)
    wo = w_out.reshape(8, 128, 8, 128).transpose(2, 1, 0, 3).reshape(8, 128, 1024)
    w1 = w_mlp1.reshape(8, 128, 32, 128).transpose(2, 1, 0, 3).reshape(32, 128, 1024)
    w2 = w_mlp2.reshape(32, 128, 8, 128).transpose(2, 1, 0, 3).reshape(8, 128, 4096)
    return dict(wg=np.ascontiguousarray(wg), wb=np.ascontiguousarray(wb), wo=np.ascontiguousarray(wo),
                w1=np.ascontiguousarray(w1), w2=np.ascontiguousarray(w2))


def fm(a):
    T, D = a.shape
    return np.ascontiguousarray(a.T.reshape(D // 128, 128, T).transpose(1, 0, 2))


def unfm(a):
    p, C, T = a.shape
    return np.ascontiguousarray(a.transpose(2, 1, 0).reshape(T, C * 128))


S = 4096
NTILE = S // 128
TG = 512
NG = S // TG


def prologue(P, xT, g1, wsrc, ncols, S=S):
    xnT = P.sb("xnT", [128, 8, S], BF16)
    wbf = P.sb("wbf", [128, 8, ncols], BF16)
    xs = [P.sb("xs0", [128, 8, TG], F32)] * 2
    sq = P.sb("sq", [128, 8, TG], BF16)
    rstd = P.sb("rstd", [128, TG], F32)
    ones = P.sb("ones", [128, 128], BF16)
    g1s = P.sb("g1s", [128, 8], F32)
    pss = P.ps("pss", [128, TG], F32)
    P.op("pool", lambda e: e.memset(ones[:], 1.0), writes=["ones"])
    P.dma("sp", g1s[:], g1, "g1s", writes=["g1s"])
    wst = [P.sb("wst%d" % i, [128, ncols], F32) for i in range(2)]
    for c in range(8):
        i = c % 2
        P.dma("sp", wst[i][:], wsrc[:, c * ncols:(c + 1) * ncols], ("wst", i), writes=[("wst", i)])
        P.op("pool", lambda e, i=i, c=c: e.tensor_copy(wbf[:, c, :], wst[i][:]), reads=[("wst", i)],
             writes=["wbf"])
    for g in range(S // TG):
        i = 0
        tsl = slice(g * TG, (g + 1) * TG)
        P.dma("sp", xs[i][:], xT[:, :, tsl], ("xs", i), writes=[("xs", i)])
        P.act(sq[:], xs[i][:], AF.Square, reads=[("xs", i)], writes=["sq"])
        for c in range(8):
            P.mm(pss[:], ones[:], sq[:, c, :], start=(c == 0), stop=(c == 7), reads=["ones", "sq"], writes=["pss"])
        P.act(rstd[:], pss[:], AF.Sqrt, reads=["pss"], writes=["rstd"], scale=1.0 / 1024, bias=1e-6)
        P.op("dve", lambda e: e.reciprocal(rstd[:], rstd[:]), reads=["rstd"], writes=["rstd"])
        for c in range(8):
            P.op("dve", lambda e, c=c, i=i, tsl=tsl: e.scalar_tensor_tensor(
                xnT[:, c, tsl], xs[i][:, c, :], g1s[:, c:c + 1], rstd[:], ALU.mult, ALU.mult),
                reads=[("xs", i), "g1s", "rstd"], writes=[("xnT", g)])
    P.xs0 = xs[0]
    return xnT, wbf, ones


def proj_fm(P, out_ps, okey, wbf, col0, ncol, xnT, g):
    for c in range(8):
        P.mm(out_ps[0:ncol, :], wbf[:, c, col0:col0 + ncol], xnT[:, c, g * TG:(g + 1) * TG], start=(c == 0),
             stop=(c == 7), reads=["wbf", ("xnT", g)], writes=[okey])


def proj_tm(P, out_ps, okey, wbf, col0, ncol, xnT, t):
    g = (t * 128) // TG
    for c in range(8):
        P.mm(out_ps, xnT[:, c, t * 128:(t + 1) * 128], wbf[:, c, col0:col0 + ncol], start=(c == 0), stop=(c == 7),
             reads=["wbf", ("xnT", g)], writes=[okey])


def rope_proj(P, dstT, dkey, wbf, col_a, col_sw, xnT, cosT, sinT, pp, tmp, S=S):
    for g in range(S // TG):
        tsl = slice(g * TG, (g + 1) * TG)
        proj_fm(P, pp[0], ("pp", 0), wbf, col_a, 128, xnT, g)
        proj_fm(P, pp[1], ("pp", 1), wbf, col_sw, 128, xnT, g)
        P.op("dve", lambda e, tsl=tsl: e.tensor_tensor(tmp[0][:], pp[0][:], cosT[:, tsl], ALU.mult),
             reads=[("pp", 0), "cos"], writes=[("rtmp", 0)])
        P.op("dve", lambda e, tsl=tsl: e.tensor_tensor(tmp[1][:], pp[1][:], sinT[:, tsl], ALU.mult),
             reads=[("pp", 1), "sin"], writes=[("rtmp", 1)])
        P.op("pool", lambda e, tsl=tsl: e.tensor_tensor(dstT[:, tsl], tmp[0][:], tmp[1][:], ALU.add),
             reads=[("rtmp", 0), ("rtmp", 1)], writes=[(dkey, g)])


def rope_tables_np(S=S):
    inv = 1.0 / (10000.0 ** (np.arange(0, 64, 2, dtype=np.float32) / 64))
    ang = np.arange(S, dtype=np.float32)[:, None] * inv[None, :]
    ang = np.concatenate([ang, ang], axis=-1)
    cos, sin = np.cos(ang).astype(np.float32), np.sin(ang).astype(np.float32)
    sgn = np.concatenate([-np.ones(32, np.float32), np.ones(32, np.float32)])
    cosT = np.ascontiguousarray(np.tile(cos.T, (2, 1)))
    sinT = np.ascontiguousarray(np.tile((sin * sgn[None, :]).T, (2, 1)))
    return cosT, sinT


def swap_halves(cols):
    cols = np.asarray(cols).reshape(-1, 64)
    return np.concatenate([cols[:, 32:], cols[:, :32]], axis=1).reshape(-1)


def wlayout(w, cols):
    sub = w[:, cols]
    n = sub.shape[1]
    return np.ascontiguousarray(sub.reshape(8, 128, n).transpose(1, 0, 2).reshape(128, 8 * n))


C_NCOLS = 256 + 256 + 128 + 128 + 64


def build_swa(S=S, P=None):
    own = P is None
    if own:
        P = Prog()
    xT = P.dram("xT", [128, 8, S], F32, "ExternalInput")
    g1 = P.dram("g1", [128, 8], F32, "ExternalInput")
    w = P.dram("w", [128, 8 * C_NCOLS], F32, "ExternalInput")
    cosd = P.dram("cos", [128, S], F32, "ExternalInput")
    sind = P.dram("sin", [128, S], F32, "ExternalInput")
    maskd = P.dram("mask", [128, 2, 512], BF16, "ExternalInput")
    sinkd = P.dram("sink", [128, 4], F32, "ExternalInput")
    y = P.dram("y", [S, 256], BF16, "ExternalOutput")

    xnT, wbf, ones = prologue(P, xT, g1, w, C_NCOLS, S)
    cosT = P.sb("cosT", [128, S], F32)
    sinT = P.sb("sinT", [128, S], F32)
    P.dma("sp", cosT[:], cosd, "cos", writes=["cos"])
    P.dma("sp", sinT[:], sind, "sin", writes=["sin"])
    mask = P.sb("mask", [128, 2, 512], BF16)
    P.dma("sp", mask[:], maskd, "mask", writes=["mask"])
    esink = P.sb("esink", [128, 4], F32)
    P.dma("sp", esink[:], sinkd, "esink", writes=["esink"])
    P.act(esink[:], esink[:], AF.Exp, reads=["esink"], writes=["esink"])

    NT = S // 128
    qT = [P.sb("qT%d" % i, [128, S], BF16) for i in range(2)]
    kT = P.sb("kT", [128, S], BF16)
    vaug = P.sb("vaug", [128, NT, 65], BF16)
    pp = [P.ps("pp%d" % i, [128, TG], F32) for i in range(2)]
    tmp = [P.sb("rtmp%d" % i, [128, TG], F32) for i in range(2)]
    rope_proj(P, qT[0], "qT0", wbf, 0, 256, xnT, cosT, sinT, pp, tmp, S)
    rope_proj(P, qT[1], "qT1", wbf, 128, 384, xnT, cosT, sinT, pp, tmp, S)
    rope_proj(P, kT, "kT", wbf, 512, 640, xnT, cosT, sinT, pp, tmp, S)
    kz = [P.sb("kz%d" % i, [128, S], BF16) for i in range(2)]
    P.op("pool", lambda e: e.memset(kz[0][64:128, :], 0.0), writes=["kz0"])
    P.op("pool", lambda e: e.memset(kz[1][0:64, :], 0.0), writes=["kz1"])
    allk = [("kT", g) for g in range(S // TG)]
    P.op("pool", lambda e: e.tensor_copy(kz[0][0:64, :], kT[0:64, :]), reads=allk, writes=["kz0"])
    P.op("act", lambda e: e.copy(kz[1][64:128, :], kT[64:128, :]), reads=allk, writes=["kz1"])
    P.op("pool", lambda e: e.memset(vaug[:], 1.0), writes=["vaug"])
    pv = P.ps("pv", [128, 64], F32)
    for t in range(NT):
        proj_tm(P, pv[:], "pv", wbf, 768, 64, xnT, t)
        P.act(vaug[:, t, 0:64], pv[:], AF.Copy, reads=["pv"], writes=["vaug"])

    st = [P.ps("st%d" % i, [128, 512], F32) for i in range(3)]
    pt = [P.sb("pt%d" % i, [128, 512], BF16) for i in range(3)]
    po = P.ps("po", [128, 4, 65], F32)
    den = P.sb("den", [128, 4], F32)
    yt = [P.sb("yt%d" % i, [128, 4, 64], BF16) for i in range(2)]
    qkeys = lambda n: [("qT0", (n * 128) // TG), ("qT1", (n * 128) // TG)]
    for n in range(NT):
        ds = [d for d in (-1, 0, 1) if 0 <= n + d < NT]
        for di, d in enumerate(ds):
            m = n + d
            for h in range(4):
                P.mm(st[di][:, h * 128:(h + 1) * 128], kz[h % 2][:, m * 128:(m + 1) * 128],
                     qT[h // 2][:, n * 128:(n + 1) * 128], reads=["kz0", "kz1"] + qkeys(n),
                     writes=[("st", di)])
            P.act(pt[di][:], st[di][:], AF.Exp, reads=[("st", di)], writes=[("pt", di)], scale=0.125)
            if d != 0:
                mi = 0 if d == -1 else 1
                P.op("pool", lambda e, di=di, mi=mi: e.tensor_tensor(pt[di][:], pt[di][:], mask[:, mi, :], ALU.mult),
                     reads=[("pt", di), "mask"], writes=[("pt", di)])
        for h in range(4):
            for di, d in enumerate(ds):
                m = n + d
                P.mm(po[:, h, :], pt[di][:, h * 128:(h + 1) * 128], vaug[:, m, :], start=(di == 0),
                     stop=(di == len(ds) - 1), reads=[("pt", di), "vaug"], writes=["po"])
        P.op("dve", lambda e: e.tensor_tensor(den[:], po[:, :, 64], esink[:], ALU.add), reads=["po", "esink"],
             writes=["den"])
        P.op("dve", lambda e: e.reciprocal(den[:], den[:]), reads=["den"], writes=["den"])
        yb = yt[n % 2]
        for h in range(4):
            P.op("dve", lambda e, h=h, yb=yb: e.tensor_scalar(yb[:, h, :], po[:, h, 0:64], den[:, h:h + 1], None,
                                                           ALU.mult), reads=["po", "den"], writes=[("yt", n % 2)])
        P.dma("sp", y[n * 128:(n + 1) * 128, :], yb[:].rearrange("p h d -> p (h d)"), ("yt", n % 2),
              reads=[("yt", n % 2)], is_out=True)
    print("swa stats", P.stats())
    return (P.finish() if own else None)


def swa_inputs(xTb, g1l, w_in_l, sink_l, hh, cosT, sinT):
    offs = np.cumsum((0,) + (1536, 512, 16, 512, 512, 512, 512, 128, 128, 256, 256, 512, 16, 512))
    cq, ck, cv = offs[6], offs[7], offs[8]
    qcols = cq + np.arange(hh * 256, hh * 256 + 256)
    kcols = ck + np.arange(hh * 64, hh * 64 + 64)
    vcols = cv + np.arange(hh * 64, hh * 64 + 64)
    k2 = np.concatenate([kcols, kcols])
    cols = np.concatenate([qcols, swap_halves(qcols), k2, swap_halves(k2), vcols])
    assert len(cols) == C_NCOLS
    j = np.arange(128)[:, None]
    i = np.arange(128)[None, :]
    prev = (j >= i).astype(np.float32)
    nxt = (j <= i).astype(np.float32)
    mask = np.stack([np.tile(prev, (1, 4)), np.tile(nxt, (1, 4))], axis=1).astype(ml_dtypes.bfloat16)
    sink = np.tile(sink_l[hh * 4:hh * 4 + 4][None, :], (128, 1)).astype(np.float32)
    return dict(xT=xTb, g1=g1l, w=wlayout(w_in_l, cols), cos=cosT, sin=sinT, mask=np.ascontiguousarray(mask),
                sink=np.ascontiguousarray(sink))


B_NCOLS = 5 * 256


def build_diff(lam_init, S=S, P=None):
    own = P is None
    if own:
        P = Prog()
    xT = P.dram("xT", [128, 8, S], F32, "ExternalInput")
    g1 = P.dram("g1", [128, 8], F32, "ExternalInput")
    w = P.dram("w", [128, 8 * B_NCOLS], F32, "ExternalInput")
    cosd = P.dram("cos", [128, S], F32, "ExternalInput")
    sind = P.dram("sin", [128, S], F32, "ExternalInput")
    lpd = P.dram("lp", [128, 4, 64], F32, "ExternalInput")
    gbd = P.dram("gb", [128, 128], F32, "ExternalInput")
    y = P.dram("y", [S, 256], BF16, "ExternalOutput")
    NT = S // 128
    NQ = S // 512

    xnT, wbf, ones = prologue(P, xT, g1, w, B_NCOLS, S)
    cosT = P.sb("cosT", [128, S], F32)
    sinT = P.sb("sinT", [128, S], F32)
    P.dma("sp", cosT[:], cosd, "cos", writes=["cos"])
    P.dma("sp", sinT[:], sind, "sin", writes=["sin"])
    lp = P.sb("lp", [128, 4, 64], F32)
    gb = P.sb("gb", [128, 128], F32)
    P.dma("sp", lp[:], lpd, "lp", writes=["lp"])
    P.dma("sp", gb[:], gbd, "gb", writes=["gb"])
    P.op("dve", lambda e: e.tensor_scalar(gb[:], gb[:], 1.0 - lam_init, None, ALU.mult), reads=["gb"], writes=["gb"])
    junk = P.sb("junk", [128, 128], F32)
    s12 = P.sb("s12", [128, 2], F32)
    nlam = P.sb("nlam", [128, 1], F32)
    for i in range(2):
        P.op("dve", lambda e, i=i: e.scalar_tensor_tensor(junk[:, 0:64], lp[:, 2 * i, :], 1.0, lp[:, 2 * i + 1, :],
                                                     ALU.mult, ALU.mult, accum_out=s12[:, i:i + 1]),
             reads=["lp"], writes=["junk", "s12"])
    P.act(s12[:], s12[:], AF.Exp, reads=["s12"], writes=["s12"])
    P.op("dve", lambda e: e.tensor_tensor(nlam[:], s12[:, 0:1], s12[:, 1:2], ALU.subtract), reads=["s12"],
         writes=["nlam"])
    P.op("dve", lambda e: e.tensor_scalar(nlam[:], nlam[:], -1.0, -lam_init, ALU.mult, ALU.add), reads=["nlam"],
         writes=["nlam"])

    zt = P.sb("zt", [128, 512], BF16)
    P.op("pool", lambda e: e.memset(zt[:], 0.0), writes=["zt"])
    qT = P.sb("qT", [128, S], BF16)
    kT = P.sb("kT", [128, S], BF16)
    kz = [P.sb("kz%d" % i, [128, S], BF16) for i in range(2)]
    vt = P.sb("vt", [128, NT, 128], BF16)
    pp = [P.ps("pp%d" % i, [128, TG], F32) for i in range(2)]
    tmp = [P.sb("rtmp%d" % i, [128, TG], F32) for i in range(2)]
    st = [P.ps("st%d" % i, [128, 512], F32) for i in range(2)]
    pt = [P.sb("pt%d" % i, [128, 512], BF16) for i in range(3)]
    po = [P.ps("po%d" % i, [128, 4, 128], F32) for i in range(2)]
    pd = P.ps("pd", [128, 2, 4], F32)
    rd = P.sb("rd", [128, 2, 4], F32)
    t0 = [P.sb("t0_%d" % i, [128, 128], F32) for i in range(2)]
    ot = [P.sb("ot%d" % i, [128, 128], F32) for i in range(2)]
    ssq = P.sb("ssq", [128, 2], F32)
    yt = [P.sb("yt%d" % i, [128, 4, 128], BF16) for i in range(2)]
    P.op("pool", lambda e: e.memset(kz[0][64:128, :], 0.0), writes=["kz0"])
    P.op("pool", lambda e: e.memset(kz[1][0:64, :], 0.0), writes=["kz1"])
    allg = lambda k: [(k, g) for g in range(S // TG)]
    cnt = 0
    ycnt = 0
    for h in range(2):
        rope_proj(P, qT, "qT", wbf, h * 128, 256 + h * 128, xnT, cosT, sinT, pp, tmp, S)
        rope_proj(P, kT, "kT", wbf, 512 + h * 128, 768 + h * 128, xnT, cosT, sinT, pp, tmp, S)
        P.op("pool", lambda e: e.tensor_copy(kz[0][0:64, :], kT[0:64, :]), reads=allg("kT"), writes=["kz0"])
        P.op("act", lambda e: e.copy(kz[1][64:128, :], kT[64:128, :]), reads=allg("kT"), writes=["kz1"])
        for t in range(NT):
            proj_tm(P, pp[0][:, 0:128], ("pp", 0), wbf, 1024 + h * 128, 128, xnT, t)
            P.act(vt[:, t, :], pp[0][:, 0:128], AF.Copy, reads=[("pp", 0)], writes=["vt"])
        for g in range(NQ):
            qsl = slice(g * 512, (g + 1) * 512)
            for m in range(2):
                P.mm(po[m][:].rearrange("p a b -> p (a b)"), zt[:, 0:128], zt[:, 0:512], start=True, stop=False,
                     reads=["zt"], writes=[("po", m)])
            P.mm(pd[:].rearrange("p a b -> p (a b)"), zt[:, 0:128], zt[:, 0:8], start=True, stop=False, reads=["zt"],
                 writes=["pd"])
            for t in range(NT):
                last = (t == NT - 1)
                for m in range(2):
                    b = cnt % 2
                    b3 = cnt % 3
                    cnt += 1
                    P.mm(st[b][:], kz[m][:, t * 128:(t + 1) * 128], qT[:, qsl], reads=["kz%d" % m, ("qT", g)],
                         writes=[("st", b)])
                    P.act(pt[b3][:], st[b][:], AF.Exp, reads=[("st", b)], writes=[("pt", b3)], scale=0.125)
                    for qs in range(4):
                        P.mm(po[m][:, qs, :], pt[b3][:, qs * 128:(qs + 1) * 128], vt[:, t, :], start=False,
                             stop=(last and qs == 3), reads=[("pt", b3), "vt"], writes=[("po", m)])
                        P.mm(pd[:, m, qs:qs + 1], pt[b3][:, qs * 128:(qs + 1) * 128], ones[:, 0:1], start=False,
                             stop=(last and m == 1 and qs == 3), reads=[("pt", b3), "ones"], writes=["pd"])
            P.op("dve", lambda e: e.reciprocal(rd[:], pd[:]), reads=["pd"], writes=["rd"])
            P.op("dve", lambda e: e.tensor_scalar(rd[:, 1, :], rd[:, 1, :], nlam[:, 0:1], None, ALU.mult),
                 reads=["rd", "nlam"], writes=["rd"])
            yb = yt[ycnt % 2]
            ykey = ("yt", ycnt % 2)
            ycnt += 1
            for qs in range(4):
                k2 = qs % 2
                P.act(t0[k2][:], po[0][:, qs, :], AF.Copy, reads=[("po", 0), "rd"], writes=[("t0", k2)],
                      scale=rd[:, 0, qs:qs + 1])
                P.op("dve", lambda e, qs=qs, k2=k2: e.scalar_tensor_tensor(ot[k2][:], po[1][:, qs, :],
                                                                       rd[:, 1, qs:qs + 1], t0[k2][:], ALU.mult,
                                                                       ALU.add),
                     reads=[("po", 1), "rd", ("t0", k2)], writes=[("ot", k2)])
                P.act(junk[:], ot[k2][:], AF.Square, reads=[("ot", k2)], writes=["junk", ("ssq", k2)],
                      accum_out=ssq[:, k2:k2 + 1])
                P.act(ssq[:, k2:k2 + 1], ssq[:, k2:k2 + 1], AF.Sqrt, reads=[("ssq", k2)], writes=[("ssq", k2)],
                      scale=1.0 / 128, bias=1e-6)
                P.op("dve", lambda e, k2=k2: e.reciprocal(ssq[:, k2:k2 + 1], ssq[:, k2:k2 + 1]), reads=[("ssq", k2)],
                     writes=[("ssq", k2)])
                P.op("dve", lambda e, qs=qs, k2=k2, yb=yb: e.scalar_tensor_tensor(yb[:, qs, :], ot[k2][:],
                                                                              ssq[:, k2:k2 + 1], gb[:], ALU.mult,
                                                                              ALU.mult),
                     reads=[("ot", k2), ("ssq", k2), "gb"], writes=[ykey])
            P.dma("sp", y[g * 512:(g + 1) * 512, h * 128:(h + 1) * 128].rearrange("(q p) e -> p q e", p=128), yb[:],
                  ykey, reads=[ykey], is_out=True)
    print("diff stats", P.stats())
    return (P.finish() if own else None)


def diff_inputs(xTb, g1l, w_in_l, lam_l, ng_l, hh, cosT, sinT):
    offs = np.cumsum((0,) + (1536, 512, 16, 512, 512, 512, 512, 128, 128, 256, 256, 512, 16, 512))
    cq, ck, cv = offs[3], offs[4], offs[5]
    r = np.arange(hh * 256, hh * 256 + 256)
    cols = np.concatenate([cq + r, swap_halves(cq + r), ck + r, swap_halves(ck + r), cv + r])
    assert len(cols) == B_NCOLS
    lp = np.ascontiguousarray(np.tile(lam_l[None], (128, 1, 1)).astype(np.float32))
    gb = np.ascontiguousarray(np.tile(ng_l[None, :], (128, 1)).astype(np.float32))
    return dict(xT=xTb, g1=g1l, w=wlayout(w_in_l, cols), cos=cosT, sin=sinT, lp=lp, gb=gb)


D_NCOLS = 128 + 128 + 256 + 8 + 256


def dve(P, fn, reads, writes, eng="dve"):
    return P.op(eng, fn, reads, writes)


def build_mlstm(S=S, P=None):
    own = P is None
    if own:
        P = Prog()
    NT = S // 128
    NQ = S // 512
    xT = P.dram("xT", [128, 8, S], F32, "ExternalInput")
    g1 = P.dram("g1", [128, 8], F32, "ExternalInput")
    w = P.dram("w", [128, 8 * D_NCOLS], F32, "ExternalInput")
    identd = P.dram("ident", [128, 128], F32, "ExternalInput")
    antid = P.dram("anti", [128, 128], F32, "ExternalInput")
    seld = P.dram("sel", [NT, NT * 128], F32, "ExternalInput")
    maskd = P.dram("mask", [128, 2 * 4 * 512], BF16, "ExternalInput")
    biasd = P.dram("gbias", [128, NT * 8], F32, "ExternalInput")
    gbd = P.dram("gb", [128, 128], F32, "ExternalInput")
    y = P.dram("y", [S, 256], BF16, "ExternalOutput")

    xnT, wbf, ones = prologue(P, xT, g1, w, D_NCOLS, S)
    ident = P.sb("ident", [128, 128], F32)
    anti = P.sb("anti", [128, 128], F32)
    sel2 = P.xs0[:].rearrange("p a b -> p (a b)")
    mask = P.sb("mask", [128, 2, 4, 512], BF16)
    gbias = P.sb("gbias", [128, NT * 8], F32)
    gb = P.sb("gb", [128, 128], F32)
    P.dma("sp", ident[:], identd, "ident", writes=["ident"])
    P.dma("sp", anti[:], antid, "anti", writes=["anti"])
    P.dma("sp", sel2[0:NT, 0:NT * 128], seld, "sel", writes=["sel", ("xs", 0)])
    P.dma("sp", mask[:].rearrange("p a b c -> p (a b c)"), maskd, "mask", writes=["mask"])
    P.dma("sp", gbias[:], biasd, "gbias", writes=["gbias"])
    P.dma("sp", gb[:], gbd, "gb", writes=["gb"])
    zt = P.sb("zt", [128, 512], BF16)
    P.op("pool", lambda e: e.memset(zt[:], 0.0), writes=["zt"])
    onesf = P.sb("onesf", [NT, 128], F32)
    P.op("pool", lambda e: e.memset(onesf[:], 1.0), writes=["onesf"])

    pp = [P.ps("pp%d" % i, [128, 512], F32) for i in range(2)]
    st = [P.ps("st%d" % i, [128, 512], F32) for i in range(2)]
    po = [P.ps("po%d" % i, [128, 4, 128], F32) for i in range(2)]
    pd = P.ps("pd", [128, 8], F32)

    qT = P.sb("qT", [128, S], BF16)
    kz = [P.sb("kz%d" % i, [128, S], BF16) for i in range(2)]
    vt = P.sb("vt", [128, NT, 256], BF16)
    P.op("pool", lambda e: e.memset(kz[0][64:128, :], 0.0), writes=["kz0"])
    P.op("pool", lambda e: e.memset(kz[1][0:64, :], 0.0), writes=["kz1"])
    for g in range(S // TG):
        tsl = slice(g * TG, (g + 1) * TG)
        proj_fm(P, pp[0], ("pp", 0), wbf, 0, 128, xnT, g)
        P.act(qT[:, tsl], pp[0][:], AF.Copy, reads=[("pp", 0)], writes=[("qT", g)])
        proj_fm(P, pp[1], ("pp", 1), wbf, 128, 128, xnT, g)
        P.act(kz[0][0:64, tsl], pp[1][0:64, :], AF.Copy, reads=[("pp", 1)], writes=["kz0"], scale=0.125)
        P.act(kz[1][64:128, tsl], pp[1][64:128, :], AF.Copy, reads=[("pp", 1)], writes=["kz1"], scale=0.125)
    for t in range(NT):
        k = t % 2
        proj_tm(P, pp[k][:, 0:256], ("pp", k), wbf, 256, 256, xnT, t)
        P.act(vt[:, t, :], pp[k][:, 0:256], AF.Copy, reads=[("pp", k)], writes=["vt"])
    for t in range(NT):
        proj_tm(P, pp[0][:, t * 8:(t + 1) * 8], ("pp", 0), wbf, 512, 8, xnT, t)
    gtok = P.sb("gtok", [128, NT, 8], F32)
    gtokR = P.sb("gtokR", [128, NT, 8], F32)
    dve(P, lambda e: e.tensor_tensor(gtok[:].rearrange("p a b -> p (a b)"), pp[0][:, 0:NT * 8], gbias[:], ALU.add),
        [("pp", 0), "gbias"], ["gtok"])
    P.mm(pp[1][:, 0:NT * 8], anti[:], gtok[:].rearrange("p a b -> p (a b)"), reads=["anti", "gtok"],
         writes=[("pp", 1)])
    P.act(gtokR[:].rearrange("p a b -> p (a b)"), pp[1][:, 0:NT * 8], AF.Copy, reads=[("pp", 1)], writes=["gtokR"])

    tok = [P.sb("tok%d" % d, [128, 4, NT], F32) for d in range(2)]
    A = [P.sb("A%d" % d, [NT, 2, 128], F32) for d in range(2)]
    def gate_dir(d):
        src = gtok if d == 0 else gtokR
        sk = "gtok" if d == 0 else "gtokR"
        LP = P.sb("LP%d" % d, [NT, 4, 128], F32)
        k = d
        for j in range(4):
            col = (0, 1, 4, 5)[j] + 2 * d
            P.mm(pp[k][0:NT, j * 128:(j + 1) * 128], src[:, :, col], ident[:], reads=[sk, "ident"], writes=[("pp", k)])
        P.act(LP[:].rearrange("p a b -> p (a b)"), pp[k][0:NT, :], AF.Copy, reads=[("pp", k)], writes=["LP%d" % d])
        L = "LP%d" % d
        T1 = P.sb("T1_%d" % d, [NT, 2, 128], F32)
        T2 = P.sb("T2_%d" % d, [NT, 2, 128], F32)
        Wt = P.sb("W_%d" % d, [NT, 2, 128], F32)
        Ct = P.sb("C_%d" % d, [NT, 2, 128], F32)
        Mt = P.sb("M_%d" % d, [NT, 2, 128], F32)
        Et = P.sb("E_%d" % d, [NT, 2, 128], F32)
        n1, n2, nW, nC, nM, nE = ["%s_%d" % (s, d) for s in ("T1", "T2", "W", "C", "M", "E")]
        pf = LP[:, 2:4, :]
        li = LP[:, 0:2, :]
        P.act(T1[:], pf, AF.Abs, reads=[L], writes=[n1])
        P.act(T1[:], T1[:], AF.Exp, reads=[n1], writes=[n1], scale=-1.0)
        P.act(T1[:], T1[:], AF.Ln, reads=[n1], writes=[n1], bias=1.0)
        dve(P, lambda e: e.tensor_single_scalar(T2[:], pf, 0.0, ALU.min), [L], [n2])
        dve(P, lambda e: e.tensor_tensor(T2[:], T2[:], T1[:], ALU.subtract), [n1, n2], [n2])
        for hl in range(2):
            dve(P, lambda e, hl=hl: e.tensor_tensor_scan(Wt[:, hl, :], onesf[:], T2[:, hl, :], 0.0, ALU.mult,
                                                         ALU.add), [n2, "onesf"], [nW])
        r = P.sb("r_%d" % d, [2, NT], F32)
        rs = P.sb("rs_%d" % d, [2, NT], F32)
        tot = P.sb("tot_%d" % d, [2, 1], F32)
        car = P.sb("car_%d" % d, [NT, 2], F32)
        P.mm(pp[k][0:2, 0:NT], Wt[:, :, 127], ident[0:NT, 0:NT], reads=[nW, "ident"], writes=[("pp", k)])
        P.act(r[:], pp[k][0:2, 0:NT], AF.Copy, reads=[("pp", k)], writes=["r%d" % d])
        onesr = onesf[0:2, 0:NT]
        dve(P, lambda e: e.tensor_tensor_scan(rs[:], onesr, r[:], 0.0, ALU.mult, ALU.add), ["r%d" % d, "onesf"],
            ["rs%d" % d])
        if d == 0:
            dve(P, lambda e: e.tensor_tensor(rs[:], rs[:], r[:], ALU.subtract), ["rs%d" % d, "r%d" % d], ["rs%d" % d])
        else:
            dve(P, lambda e: e.tensor_copy(tot[:], rs[:, NT - 1:NT]), ["rs%d" % d], ["tot%d" % d])
            dve(P, lambda e: e.tensor_scalar(rs[:], rs[:], -1.0, tot[:, 0:1], ALU.mult, ALU.add),
                ["rs%d" % d, "tot%d" % d], ["rs%d" % d])
        P.mm(pp[k][0:NT, 0:2], rs[:], ident[0:2, 0:2], reads=["rs%d" % d, "ident"], writes=[("pp", k)])
        P.act(car[:], pp[k][0:NT, 0:2], AF.Copy, reads=[("pp", k)], writes=["car%d" % d])
        for hl in range(2):
            dve(P, lambda e, hl=hl: e.tensor_scalar(Wt[:, hl, :], Wt[:, hl, :], car[:, hl:hl + 1], None, ALU.add),
                [nW, "car%d" % d], [nW])
        dve(P, lambda e: e.tensor_tensor(Ct[:], li, Wt[:], ALU.subtract), [L, nW], [nC])
        for hl in range(2):
            dve(P, lambda e, hl=hl: e.tensor_tensor_scan(Mt[:, hl, :], Ct[:, hl, :], Ct[:, hl, :], -1e30, ALU.max,
                                                         ALU.max), [nC], [nM])
        mr = P.sb("mr_%d" % d, [2, NT], F32)
        mr2 = P.sb("mr2_%d" % d, [2, NT], F32)
        mx = P.sb("mx_%d" % d, [2, NT], F32)
        cmx = P.sb("cmx_%d" % d, [NT, 2], F32)
        P.mm(pp[k][0:2, 0:NT], Mt[:, :, 127], ident[0:NT, 0:NT], reads=[nM, "ident"], writes=[("pp", k)])
        P.act(mr[:], pp[k][0:2, 0:NT], AF.Copy, reads=[("pp", k)], writes=["mr%d" % d])
        dve(P, lambda e: e.memset(mx[:], -1e30), [], ["mx%d" % d])
        if NT > 1:
            if d == 0:
                dve(P, lambda e: e.tensor_tensor_scan(mr2[:], mr[:], mr[:], -1e30, ALU.max, ALU.max), ["mr%d" % d],
                    ["mr2%d" % d])
                dve(P, lambda e: e.tensor_copy(mx[:, 1:NT], mr2[:, 0:NT - 1]), ["mr2%d" % d, "mx%d" % d], ["mx%d" % d])
            else:
                cur, ck_, oth, ok_ = mr, "mr%d" % d, mr2, "mr2%d" % d
                s = 1
                while s < NT:
                    dve(P, lambda e, cur=cur, oth=oth, s=s: e.tensor_tensor(oth[:, 0:NT - s], cur[:, 0:NT - s],
                                                                        cur[:, s:NT], ALU.max), [ck_], [ok_])
                    dve(P, lambda e, cur=cur, oth=oth, s=s: e.tensor_copy(oth[:, NT - s:NT], cur[:, NT - s:NT]),
                        [ck_, ok_], [ok_])
                    cur, ck_, oth, ok_ = oth, ok_, cur, ck_
                    s *= 2
                dve(P, lambda e, cur=cur: e.tensor_copy(mx[:, 0:NT - 1], cur[:, 1:NT]), [ck_, "mx%d" % d], ["mx%d" % d])
        P.mm(pp[k][0:NT, 0:2], mx[:], ident[0:2, 0:2], reads=["mx%d" % d, "ident"], writes=[("pp", k)])
        P.act(cmx[:], pp[k][0:NT, 0:2], AF.Copy, reads=[("pp", k)], writes=["cmx%d" % d])
        for hl in range(2):
            dve(P, lambda e, hl=hl: e.tensor_scalar(Mt[:, hl, :], Mt[:, hl, :], cmx[:, hl:hl + 1], None, ALU.max),
                [nM, "cmx%d" % d], [nM])
        dve(P, lambda e: e.tensor_scalar(Mt[:], Mt[:], -1.0, 0.0, ALU.mult, ALU.min), [nM], [nM])
        dve(P, lambda e: e.tensor_tensor(Et[:], Mt[:], Wt[:], ALU.subtract), [nM, nW], [nE])
        P.act(Et[:], Et[:], AF.Exp, reads=[nE], writes=[nE])
        if d == 0:
            for j, (src_t, sn) in enumerate(((Ct, nC), (Ct, nC), (Et, nE), (Et, nE))):
                P.mm(pp[k][:, j * NT:(j + 1) * NT], src_t[:, j % 2, :], ident[0:NT, 0:NT], reads=[sn, "ident"],
                     writes=[("pp", k)])
            P.act(tok[0][:].rearrange("p a b -> p (a b)"), pp[k][:, 0:4 * NT], AF.Copy, reads=[("pp", k)],
                  writes=["tok0"])
            dve(P, lambda e: e.tensor_copy(A[0][:], Mt[:]), [nM], ["A0"])
        else:
            Yb = P.sb("Yb", [128, 6, NT], F32)
            for j, (src_t, sn) in enumerate(((Ct, nC), (Ct, nC), (Et, nE), (Et, nE), (Mt, nM), (Mt, nM))):
                P.mm(pp[k][:, j * NT:(j + 1) * NT], src_t[:, j % 2, :], ident[0:NT, 0:NT], reads=[sn, "ident"],
                     writes=[("pp", k)])
            P.act(Yb[:].rearrange("p a b -> p (a b)"), pp[k][:, 0:6 * NT], AF.Copy, reads=[("pp", k)], writes=["Yb"])
            P.mm(pp[k][:, 0:4 * NT], anti[:], Yb[:, 0:4, :].rearrange("p a b -> p (a b)"), reads=["anti", "Yb"],
                 writes=[("pp", k)])
            P.act(tok[1][:].rearrange("p a b -> p (a b)"), pp[k][:, 0:4 * NT], AF.Copy, reads=[("pp", k)],
                  writes=["tok1"])
            for hl in range(2):
                P.mm(pp[k][0:NT, hl * 128:(hl + 1) * 128], Yb[:, 4 + hl, :], anti[:], reads=["Yb", "anti"],
                     writes=[("pp", k)])
            P.act(A[1][:].rearrange("p a b -> p (a b)"), pp[k][0:NT, 0:256], AF.Copy, reads=[("pp", k)], writes=["A1"])


    for d in range(2):
        gate_dir(d)

    pa = pp[1]
    wg = [P.sb("wg%d" % i, [128, 512], F32) for i in range(2)]
    pt = [P.sb("pt%d" % i, [128, 512], BF16) for i in range(3)]
    ad = P.sb("ad", [128, 4], F32)
    hs = P.sb("hs", [128, 4, 128], F32)
    junk = P.sb("junk", [128, 128], F32)
    ssq = P.sb("ssq", [128, 2], F32)
    sgo = [P.sb("sgo%d" % i, [128, 128], F32) for i in range(2)]
    yt = [P.sb("yt%d" % i, [128, 4, 128], BF16) for i in range(2)]
    cnt = 0
    ycnt = 0
    pcnt = 0
    for hl in range(2):
        for g in range(NQ):
            qsl = slice(g * 512, (g + 1) * 512)
            for d in range(2):
                pob = po[pcnt % 2]
                pok = ("po", pcnt % 2)
                pcnt += 1
                for tl in range(4):
                    P.mm(pa[:, tl * 128:(tl + 1) * 128], sel2[0:NT, (4 * g + tl) * 128:(4 * g + tl + 1) * 128], A[d][:, hl, :], reads=["sel", "A%d" % d],
                         writes=[("pp", 1)])
                P.mm(pob[:].rearrange("p a b -> p (a b)"), zt[:, 0:128], zt[:, 0:512], start=True, stop=False,
                     reads=["zt"], writes=[pok])
                P.mm(pd[:, 0:4], zt[:, 0:128], zt[:, 0:4], start=True, stop=False, reads=["zt"], writes=["pd"])
                tiles = list(range(0, 4 * g + 4)) if d == 0 else list(range(4 * g, NT))
                for ti, t in enumerate(tiles):
                    last = (ti == len(tiles) - 1)
                    b = cnt % 2
                    b3 = cnt % 3
                    cnt += 1
                    tp = t - 4 * g
                    diag = 0 <= tp <= 3
                    P.mm(st[b][:], kz[hl][:, t * 128:(t + 1) * 128], qT[:, qsl], reads=["kz%d" % hl, ("qT", g)],
                         writes=[("st", b)])
                    if diag:
                        dve(P, lambda e, b=b, d=d, hl=hl, t=t: e.tensor_scalar(wg[b][:], pa[:], tok[d][:, hl, t:t + 1], 0.0,
                                                                         ALU.add, ALU.min),
                            [("pp", 1), "tok%d" % d], [("wg", b)])
                        P.act(wg[b][:], wg[b][:], AF.Exp, reads=[("wg", b)], writes=[("wg", b)])
                        dve(P, lambda e, b=b, d=d, tp=tp: e.tensor_tensor(wg[b][:], wg[b][:], mask[:, d, tp, :], ALU.mult),
                            [("wg", b), "mask"], [("wg", b)])
                    else:
                        P.act(wg[b][:], pa[:], AF.Exp, reads=[("pp", 1), "tok%d" % d], writes=[("wg", b)],
                              bias=tok[d][:, hl, t:t + 1])
                    dve(P, lambda e, b=b, b3=b3: e.tensor_tensor(pt[b3][:], st[b][:], wg[b][:], ALU.mult),
                        [("st", b), ("wg", b)], [("pt", b3)])
                    for qs in range(4):
                        if diag and ((d == 0 and qs < tp) or (d == 1 and qs > tp)):
                            continue
                        P.mm(pob[:, qs, :], pt[b3][:, qs * 128:(qs + 1) * 128], vt[:, t, hl * 128:(hl + 1) * 128],
                             start=False, stop=(last and qs == 3), reads=[("pt", b3), "vt"], writes=[pok])
                        P.mm(pd[:, qs:qs + 1], pt[b3][:, qs * 128:(qs + 1) * 128], ones[:, 0:1], start=False,
                             stop=(last and qs == 3), reads=[("pt", b3), "ones"], writes=["pd"])
                P.act(ad[:], pd[:, 0:4], AF.Abs, reads=["pd"], writes=["ad"])
                dve(P, lambda e, d=d, hl=hl, g=g: e.tensor_tensor(ad[:], ad[:], tok[d][:, 2 + hl, 4 * g:4 * g + 4],
                                                              ALU.max), ["ad", "tok%d" % d], ["ad"])
                dve(P, lambda e: e.reciprocal(ad[:], ad[:]), ["ad"], ["ad"])
                for qs in range(4):
                    if d == 0:
                        P.act(hs[:, qs, :], pob[:, qs, :], AF.Copy, reads=[pok, "ad"], writes=["hs"],
                              scale=ad[:, qs:qs + 1])
                    else:
                        dve(P, lambda e, qs=qs, pob=pob: e.scalar_tensor_tensor(hs[:, qs, :], pob[:, qs, :],
                                                                            ad[:, qs:qs + 1], hs[:, qs, :], ALU.mult,
                                                                            ALU.add), [pok, "ad", "hs"], ["hs"])
            yb = yt[ycnt % 2]
            ykey = ("yt", ycnt % 2)
            ycnt += 1
            for qs in range(4):
                k2 = qs % 2
                t = 4 * g + qs
                proj_tm(P, pp[0][:, 0:128], ("pp", 0), wbf, 520 + hl * 128, 128, xnT, t)
                P.act(sgo[k2][:], pp[0][:, 0:128], AF.Sigmoid, reads=[("pp", 0)], writes=[("sgo", k2)])
                P.op("pool", lambda e, k2=k2: e.tensor_tensor(sgo[k2][:], sgo[k2][:], gb[:], ALU.mult),
                     reads=[("sgo", k2), "gb"], writes=[("sgo", k2)])
                P.act(junk[:], hs[:, qs, :], AF.Square, reads=["hs"], writes=["junk", ("ssq", k2)],
                      accum_out=ssq[:, k2:k2 + 1])
                P.act(ssq[:, k2:k2 + 1], ssq[:, k2:k2 + 1], AF.Sqrt, reads=[("ssq", k2)], writes=[("ssq", k2)],
                      scale=1.0 / 128, bias=1e-6)
                dve(P, lambda e, k2=k2: e.reciprocal(ssq[:, k2:k2 + 1], ssq[:, k2:k2 + 1]), [("ssq", k2)],
                    [("ssq", k2)])
                dve(P, lambda e, qs=qs, k2=k2, yb=yb: e.scalar_tensor_tensor(yb[:, qs, :], hs[:, qs, :],
                                                                         ssq[:, k2:k2 + 1], sgo[k2][:], ALU.mult,
                                                                         ALU.mult),
                    ["hs", ("ssq", k2), ("sgo", k2)], [ykey])
            P.dma("sp", y[g * 512:(g + 1) * 512, hl * 128:(hl + 1) * 128].rearrange("(q p) e -> p q e", p=128), yb[:],
                  ykey, reads=[ykey], is_out=True)
    print("mlstm stats", P.stats())
    return (P.finish() if own else None)


def mlstm_inputs(xTb, g1l, w_in_l, gate_b_l, ng_l, hh, S=S):
    NT = S // 128
    offs = np.cumsum((0,) + (1536, 512, 16, 512, 512, 512, 512, 128, 128, 256, 256, 512, 16, 512))
    cq, ck, cv, cif, co = offs[9], offs[10], offs[11], offs[12], offs[13]
    hs_ = [2 * hh, 2 * hh + 1]
    qc = np.concatenate([cq + h * 64 + np.arange(64) for h in hs_])
    kc = np.concatenate([ck + h * 64 + np.arange(64) for h in hs_])
    vc = np.concatenate([cv + h * 128 + np.arange(128) for h in hs_])
    oc = np.concatenate([co + h * 128 + np.arange(128) for h in hs_])
    gsel = [(i_f, dr, h) for i_f in range(2) for dr in range(2) for h in hs_]
    gc = np.array([cif + i_f * 8 + dr * 4 + h for (i_f, dr, h) in gsel])
    gbv = np.array([gate_b_l[i_f, dr, h] for (i_f, dr, h) in gsel], np.float32)
    cols = np.concatenate([qc, kc, vc, gc, oc])
    assert len(cols) == D_NCOLS
    ident = np.eye(128, dtype=np.float32)
    anti = np.ascontiguousarray(ident[::-1])
    sel = np.zeros((NT, NT, 128), np.float32)
    for k in range(NT):
        sel[k, k, :] = 1.0
    j = np.arange(128)[:, None]
    i = np.arange(512)[None, :]
    mask = np.zeros((128, 2, 4, 512), np.float32)
    for tp in range(4):
        mask[:, 0, tp, :] = (128 * tp + j <= i)
        mask[:, 1, tp, :] = (128 * tp + j >= i)
    gbias = np.ascontiguousarray(np.tile(gbv[None, None, :], (128, NT, 1)).reshape(128, NT * 8))
    gb = np.ascontiguousarray(np.tile(ng_l[None, :], (128, 1)).astype(np.float32))
    return dict(xT=xTb, g1=g1l, w=wlayout(w_in_l, cols), ident=ident, anti=anti,
                sel=np.ascontiguousarray(sel.reshape(NT, NT * 128)),
                mask=np.ascontiguousarray(mask.reshape(128, -1)).astype(ml_dtypes.bfloat16), gbias=gbias, gb=gb)


A_NCOLS = 256 * 4 + 8
PTG = 256


def prologue_small(P, xT, g1, wsrc, ncols, S, wst=None):
    xnT = P.sb("xnT", [128, 8, S], BF16)
    wbf = P.sb("wbf", [128, 8, ncols], BF16)
    xs = P.sb("xs0", [128, 8, PTG], F32)
    sq = P.sb("sq", [128, 8, PTG], BF16)
    rstd = P.sb("rstd", [128, 512], F32)
    ones = P.sb("ones", [128, 128], BF16)
    g1s = P.sb("g1s", [128, 8], F32)
    pss = P.ps("pss", [128, 512], F32)
    P.op("pool", lambda e: e.memset(ones[:], 1.0), writes=["ones"])
    P.dma("sp", g1s[:], g1, "g1s", writes=["g1s"])
    if wst is None:
        wst = [P.sb("wst%d" % i, [128, ncols], F32)[:] for i in range(2)]
    for c in range(8):
        i = c % 2
        P.dma("sp", wst[i], wsrc[:, c * ncols:(c + 1) * ncols], ("wst", i), writes=[("wst", i)])
        P.op("pool", lambda e, i=i, c=c: e.tensor_copy(wbf[:, c, :], wst[i]), reads=[("wst", i)],
             writes=["wbf"])
    for g in range(S // PTG):
        tsl = slice(g * PTG, (g + 1) * PTG)
        P.dma("sp", xs[:], xT[:, :, tsl], ("xs", 0), writes=[("xs", 0)])
        P.act(sq[:], xs[:], AF.Square, reads=[("xs", 0)], writes=["sq"])
        for c in range(8):
            P.mm(pss[:, 0:PTG], ones[:], sq[:, c, :], start=(c == 0), stop=(c == 7), reads=["ones", "sq"],
                 writes=["pss"])
        P.act(rstd[:, 0:PTG], pss[:, 0:PTG], AF.Sqrt, reads=["pss"], writes=["rstd"], scale=1.0 / 1024, bias=1e-6)
        P.op("dve", lambda e: e.reciprocal(rstd[:, 0:PTG], rstd[:, 0:PTG]), reads=["rstd"], writes=["rstd"])
        for c in range(8):
            P.op("dve", lambda e, c=c, tsl=tsl: e.scalar_tensor_tensor(
                xnT[:, c, tsl], xs[:, c, :], g1s[:, c:c + 1], rstd[:, 0:PTG], ALU.mult, ALU.mult),
                reads=[("xs", 0), "g1s", "rstd"], writes=[("xnT", (g * PTG) // TG)])
    P.xs0 = xs
    return xnT, wbf, ones, pss, rstd


def build_gdn(S=S, P=None):
    import os
    STOP = int(os.environ.get("GDN_STOP", "99"))
    PST = int(os.environ.get("GDN_PST", "99"))
    LST = int(os.environ.get("GDN_LST", "99"))
    VAR = int(os.environ.get("GDN_VAR", "0"))
    own = P is None
    if own:
        P = Prog()
    NT = S // 128
    NG = S // TG
    NC = S // 64
    xT = P.dram("xT", [128, 8, S], F32, "ExternalInput")
    g1 = P.dram("g1", [128, 8], F32, "ExternalInput")
    w = P.dram("w", [128, 8 * A_NCOLS], F32, "ExternalInput")
    convd = P.dram("convw", [128, 6, 5], F32, "ExternalInput")
    identd = P.dram("ident", [128, 128], F32, "ExternalInput")
    antid = P.dram("anti", [128, 128], F32, "ExternalInput")
    maskd = P.dram("mask", [128, 4 * 128], BF16, "ExternalInput")
    biasd = P.dram("gbias", [128, NT * 8], F32, "ExternalInput")
    alogd = P.dram("alog", [128, 4], F32, "ExternalInput")
    rmd = P.dram("rmask", [NT, 128], F32, "ExternalInput")
    gbd = P.dram("gb", [128, 128], F32, "ExternalInput")
    y = P.dram("y", [S, 256], BF16, "ExternalOutput")

    raw = P.sb("raw", [128, S + 4], F32)
    half = (S + 4) // 2
    xnT, wbf, ones, pss, rstd = prologue_small(P, xT, g1, w, A_NCOLS, S,
                                               wst=[raw[:, 0:A_NCOLS], raw[:, half:half + A_NCOLS]] if half >= A_NCOLS else None)
    ident = P.sb("ident", [128, 128], F32)
    anti = P.sb("anti", [128, 128], F32)
    identb = P.sb("identb", [128, 128], BF16)
    mask = P.sb("mask", [128, 4, 128], BF16)
    gbias = P.sb("gbias", [128, NT * 8], F32)
    nea = P.sb("nea", [128, 4], F32)
    rm = P.sb("rm", [NT, 128], F32)
    gb = P.sb("gb", [128, 128], F32)
    convw = P.sb("convw", [128, 6, 5], F32)
    P.dma("sp", ident[:], identd, "ident", writes=["ident"])
    P.dma("sp", anti[:], antid, "anti", writes=["anti"])
    P.dma("sp", mask[:].rearrange("p a b -> p (a b)"), maskd, "mask", writes=["mask"])
    P.dma("sp", gbias[:], biasd, "gbias", writes=["gbias"])
    P.dma("sp", nea[:], alogd, "nea", writes=["nea"])
    P.dma("sp", rm[:], rmd, "rm", writes=["rm"])
    P.dma("sp", gb[:], gbd, "gb", writes=["gb"])
    P.dma("sp", convw[:], convd, "convw", writes=["convw"])
    P.op("pool", lambda e: e.tensor_copy(identb[:], ident[:]), reads=["ident"], writes=["identb"])
    P.act(nea[:], nea[:], AF.Exp, reads=["nea"], writes=["nea"])
    P.op("dve", lambda e: e.tensor_scalar(nea[:], nea[:], -1.0, None, ALU.mult), reads=["nea"], writes=["nea"])
    onesf = P.sb("onesf", [NT, 128], F32)
    P.op("pool", lambda e: e.memset(onesf[:], 1.0), writes=["onesf"])

    pp = [P.ps("pp%d" % i, [128, 512], F32) for i in range(2)]
    ptr = P.ps("ptr", [128, 4, 128], BF16)

    qT = [P.sb("qT%d" % h, [128, S], BF16) for h in range(2)]
    kT = [P.sb("kT%d" % h, [128, S], BF16) for h in range(2)]
    vtok = [P.sb("vtok%d" % h, [128, NT, 128], BF16) for h in range(2)]
    sz = P.sb("sz", [128, NT, 256], BF16)
    xsf = P.xs0[:].rearrange("p a b -> p (a b)")
    acc = [xsf[:, i * TG:(i + 1) * TG] for i in range(2)]
    sil = [xsf[:, (2 + i) * TG:(3 + i) * TG] for i in range(2)]
    P.op("pool", lambda e: e.memset(xsf[:, 0:4 * TG], 0.0),
         writes=[("xs", 0), ("acc", 0), ("acc", 1), ("sil", 0), ("sil", 1)])
    sqg = P.sb("sqg", [128, TG], BF16)
    vTg = P.sb("vTg", [128, TG], BF16)
    P.op("pool", lambda e: e.memset(raw[:, 0:2], 0.0), reads=["wbf"], writes=["rawpad"])
    P.op("pool", lambda e: e.memset(raw[:, S + 2:S + 4], 0.0), reads=["wbf"], writes=["rawpad"])
    allraw = [("raw", g) for g in range(NG)]
    for ci in range(6):
        kind, hl = ci // 2, ci % 2
        for g in range(NG):
            k = g % 2
            proj_fm(P, pp[k], ("pp", k), wbf, ci * 128, 128, xnT, g)
            P.act(raw[:, 2 + g * TG:2 + (g + 1) * TG], pp[k][:], AF.Copy, reads=[("pp", k)], writes=[("raw", g)])
        for g in range(NG):
            k = g % 2
            a_, s_ = acc[k], sil[k]
            nb = [("raw", gg) for gg in (g - 1, g, g + 1) if 0 <= gg < NG] + ["rawpad", "convw"]
            base = g * TG
            P.op("dve", lambda e, a_=a_, base=base, ci=ci: e.tensor_scalar(a_, raw[:, base:base + TG],
                                                                        convw[:, ci, 0:1], None, ALU.mult),
                 reads=nb, writes=[("acc", k)])
            for tap in range(1, 5):
                P.op("dve", lambda e, a_=a_, base=base, ci=ci, tap=tap: e.scalar_tensor_tensor(
                    a_, raw[:, base + tap:base + tap + TG], convw[:, ci, tap:tap + 1], a_, ALU.mult, ALU.add),
                    reads=nb + [("acc", k)], writes=[("acc", k)])
            if kind < 2:
                P.act(s_, a_, AF.Silu, reads=[("acc", k)], writes=[("sil", k)])
                P.op("pool", lambda e, s_=s_: e.tensor_tensor(sqg[:], s_, s_, ALU.mult), reads=[("sil", k)],
                     writes=["sqg"])
                P.mm(pss[:], ones[:], sqg[:], reads=["ones", "sqg"], writes=["pss"])
                P.act(rstd[:], pss[:], AF.Sqrt, reads=["pss"], writes=["rstd"], bias=1e-6)
                P.op("dve", lambda e: e.reciprocal(rstd[:], rstd[:]), reads=["rstd"], writes=["rstd"])
                dst = (qT if kind == 0 else kT)[hl]
                dkey = ("qT%d" % hl if kind == 0 else "kT%d" % hl, g)
                sc = (128 ** -0.5) if kind == 0 else 1.0
                P.op("dve", lambda e, s_=s_, dst=dst, g=g, sc=sc: e.scalar_tensor_tensor(
                    dst[:, g * TG:(g + 1) * TG], s_, sc, rstd[:], ALU.mult, ALU.mult),
                    reads=[("sil", k), "rstd"], writes=[dkey])
            else:
                P.act(vTg[:], a_, AF.Silu, reads=[("acc", k)], writes=["vTg"])
                for j in range(4):
                    P.tr(ptr[:, j, :], vTg[:, j * 128:(j + 1) * 128], identb[:], reads=["vTg", "identb"],
                         writes=["ptr"])
                P.op("pool" if False else "act", lambda e, hl=hl, g=g: e.copy(
                    vtok[hl][:, 4 * g:4 * g + 4, :], ptr[:]), reads=["ptr"], writes=["vtok%d" % hl])
    if STOP <= 1:
        return (P.finish() if own else None)
    for t in range(NT):
        k = t % 2
        proj_tm(P, pp[k][:, 0:256], ("pp", k), wbf, 768, 256, xnT, t)
        P.act(sz[:, t, :], pp[k][:, 0:256], AF.Silu, reads=[("pp", k)], writes=["sz"])
    for t in range(NT):
        proj_tm(P, pp[0][:, t * 8:(t + 1) * 8], ("pp", 0), wbf, 1024, 8, xnT, t)
    gtok = P.sb("gtok", [128, NT, 8], F32)
    gtokR = P.sb("gtokR", [128, NT, 8], F32)
    dve(P, lambda e: e.tensor_tensor(gtok[:].rearrange("p a b -> p (a b)"), pp[0][:, 0:NT * 8], gbias[:], ALU.add),
        [("pp", 0), "gbias"], ["gtok"])
    P.mm(pp[1][:, 0:NT * 8], anti[:], gtok[:].rearrange("p a b -> p (a b)"), reads=["anti", "gtok"],
         writes=[("pp", 1)])
    P.act(gtokR[:].rearrange("p a b -> p (a b)"), pp[1][:, 0:NT * 8], AF.Copy, reads=[("pp", 1)], writes=["gtokR"])

    if STOP <= 2:
        return (P.finish() if own else None)
    tokq = [P.sb("tokq%d" % d, [128, 2, 6, NT], F32) for d in range(2)]
    Gc = [P.sb("Gc%d" % d, [NT, 2, 128], F32) for d in range(2)]
    egt = [P.sb("egt%d" % d, [128, 2, 2, NT], F32) for d in range(2)]

    LPs = P.sb("LPs", [NT, 4, 128], F32)
    gtmp = [P.sb("gtmp%d" % i, [NT, 2, 128], F32) for i in range(5)]
    TBs = P.sb("TBs", [NT, 128], F32)
    Yb = P.sb("Yb", [128, 8, NT], F32)

    def gate_dir(d):
        src = gtok if d == 0 else gtokR
        sk = "gtok" if d == 0 else "gtokR"
        k = d
        LP = LPs
        L = "LP"
        for j in range(4):
            col = (0, 1, 4, 5)[j] + 2 * d
            P.mm(pp[k][0:NT, j * 128:(j + 1) * 128], src[:, :, col], ident[:], reads=[sk, "ident"], writes=[("pp", k)])
        P.act(LP[:].rearrange("p a b -> p (a b)"), pp[k][0:NT, :], AF.Copy, reads=[("pp", k)], writes=[L])
        T1, Gt, Bt, Et, Rt = gtmp
        n1, nG, nB, nE, nR = ["%s_s" % s for s in ("T1", "G", "B", "E", "R")]
        al = LP[:, 0:2, :]
        be = LP[:, 2:4, :]
        P.act(T1[:], al, AF.Abs, reads=[L], writes=[n1])
        P.act(T1[:], T1[:], AF.Exp, reads=[n1], writes=[n1], scale=-1.0)
        P.act(T1[:], T1[:], AF.Ln, reads=[n1], writes=[n1], bias=1.0)
        dve(P, lambda e: e.tensor_single_scalar(Gt[:], al, 0.0, ALU.max), [L], [nG])
        dve(P, lambda e: e.tensor_tensor(Gt[:], Gt[:], T1[:], ALU.add), [nG, n1], [nG])
        for hl in range(2):
            dve(P, lambda e, hl=hl: e.tensor_scalar(Gt[:, hl, :], Gt[:, hl, :], nea[0:NT, 2 * d + hl:2 * d + hl + 1],
                                                    None, ALU.mult), [nG, "nea"], [nG])
        P.act(Bt[:], be, AF.Sigmoid, reads=[L], writes=[nB])
        for hl in range(2):
            dve(P, lambda e, hl=hl: e.tensor_tensor_scan(T1[:, hl, :], rm[:], Gt[:, hl, :], 0.0, ALU.mult, ALU.add),
                [nG, "rm", n1], [n1])
        for hl in range(2):
            for hf in range(2):
                dve(P, lambda e, hl=hl, hf=hf: e.tensor_scalar(
                    Rt[:, hl, hf * 64:(hf + 1) * 64], T1[:, hl, hf * 64:(hf + 1) * 64], -1.0,
                    T1[:, hl, hf * 64 + 63:hf * 64 + 64], ALU.mult, ALU.add), [n1], [nR])
        P.act(Et[:], T1[:], AF.Exp, reads=[n1], writes=[nE])
        P.act(Rt[:], Rt[:], AF.Exp, reads=[nR], writes=[nR])
        TB = TBs
        for hl in range(2):
            for hf in range(2):
                ah = hf if d == 0 else 1 - hf
                dve(P, lambda e, hl=hl, hf=hf: e.tensor_scalar(TB[:], onesf[:], T1[:, hl, hf * 64 + 63:hf * 64 + 64],
                                                           None, ALU.mult), [n1, "onesf", ("pp", k)], ["TBs"])
                P.mm(pp[k][:, (hl * 2 + ah) * NT:(hl * 2 + ah + 1) * NT], TB[:], ident[0:NT, 0:NT],
                     reads=["TBs", "ident"], writes=[("pp", k)])
        P.act(egt[d][:].rearrange("p a b c -> p (a b c)"), pp[k][:, 0:4 * NT], AF.Exp, reads=[("pp", k)],
              writes=["egt%d" % d])
        srcs = ((T1, n1), (Bt, nB), (Et, nE), (Rt, nR))
        if d == 0:
            for hl in range(2):
                for qi, (tt, nn) in enumerate(srcs):
                    P.mm(pp[k][:, (hl * 4 + qi) * NT:(hl * 4 + qi + 1) * NT], tt[:, hl, :], ident[0:NT, 0:NT],
                         reads=[nn, "ident"], writes=[("pp", k)])
            for hl in range(2):
                P.act(tokq[0][:, hl, 0:4, :].rearrange("p a b -> p (a b)"), pp[k][:, hl * 4 * NT:(hl + 1) * 4 * NT],
                      AF.Copy, reads=[("pp", k)], writes=["tokq0"])
            dve(P, lambda e: e.tensor_copy(Gc[0][:], T1[:]), [n1], ["Gc0"])
        else:
            for hl in range(2):
                for qi, (tt, nn) in enumerate(srcs):
                    P.mm(pp[k][:, (hl * 4 + qi) * NT:(hl * 4 + qi + 1) * NT], tt[:, hl, :], ident[0:NT, 0:NT],
                         reads=[nn, "ident"], writes=[("pp", k)])
            P.act(Yb[:].rearrange("p a b -> p (a b)"), pp[k][:, 0:8 * NT], AF.Copy, reads=[("pp", k)], writes=["Yb"])
            P.mm(pp[k][:, 0:8 * NT], anti[:], Yb[:].rearrange("p a b -> p (a b)"), reads=["anti", "Yb"],
                 writes=[("pp", k)])
            for hl in range(2):
                P.act(tokq[1][:, hl, 0:4, :].rearrange("p a b -> p (a b)"), pp[k][:, hl * 4 * NT:(hl + 1) * 4 * NT],
                      AF.Copy, reads=[("pp", k)], writes=["tokq1"])
            for hl in range(2):
                P.mm(pp[k][0:NT, hl * 128:(hl + 1) * 128], Yb[:, hl * 4 + 0, :], anti[:], reads=["Yb", "anti"],
                     writes=[("pp", k)])
            P.act(Gc[1][:].rearrange("p a b -> p (a b)"), pp[k][0:NT, 0:256], AF.Copy, reads=[("pp", k)],
                  writes=["Gc1"])
        for hl in range(2):
            dve(P, lambda e, hl=hl: e.tensor_scalar(tokq[d][:, hl, 4, :], tokq[d][:, hl, 0, :], -1.0, None, ALU.mult),
                ["tokq%d" % d], ["tokq%d" % d])
            dve(P, lambda e, hl=hl: e.tensor_scalar(tokq[d][:, hl, 5, :], tokq[d][:, hl, 2, :], -1.0, None, ALU.mult),
                ["tokq%d" % d], ["tokq%d" % d])

    for d in range(2):
        gate_dir(d)
        if STOP <= 3 + d:
            return (P.finish() if own else None)

    XK = [("xnT", g) for g in range(NG)]
    slot = lambda i: xnT[:, i, :].rearrange("p (t c) -> p t c", c=128)
    TI = [slot(0), slot(1)]
    QK = [slot(2), slot(3)]
    KD = [slot(4), slot(5)]
    pw = pp[0]
    pn = pp[1]
    cA = [P.ps("cA%d" % i, [128, 4, 128], F32) for i in range(2)]
    cSb = [P.ps("cS%d" % i, [128, 128], F32) for i in range(2)]
    Dm = [P.sb("Dm%d" % i, [128, 128], F32) for i in range(2)]
    DT = [P.sb("DT%d" % i, [128, 128], F32) for i in range(2)]
    A0 = [P.sb("A0_%d" % i, [128, 128], BF16) for i in range(2)]
    IA = [P.sb("IA_%d" % i, [128, 128], BF16) for i in range(2)]
    AK = [P.sb("AK_%d" % i, [128, 128], BF16) for i in range(2)]
    NK = [P.sb("NK_%d" % i, [128, 128], BF16) for i in range(2)]
    PT = [P.sb("PT_%d" % i, [128, 128], BF16) for i in range(2)]
    oacc = raw[:, 0:S].rearrange("p (t c) -> p t c", c=128)
    gsel = [P.sb("gsel%d" % i, [NT, 128], F32) for i in range(2)]
    St = [P.sb("St%d" % i, [128, 128], BF16) for i in range(2)]
    Xp = [[P.sb("Xp%d_%d" % (i, hf), [128, 128], BF16) for hf in range(2)] for i in range(2)]
    Vn = [[P.sb("Vn%d_%d" % (i, hf), [128, 128], BF16) for hf in range(2)] for i in range(2)]
    otmp = [P.sb("otmp%d" % i, [128, 128], F32) for i in range(2)]
    junk_ = rstd[:, 256:384]
    ssq = P.sb("ssq", [128, 2], F32)
    gm = [rstd[:, i * 128:(i + 1) * 128] for i in range(2)]
    P.op("pool", lambda e: e.memset(rstd[:, 0:384], 0.0), writes=["rstd", ("gm", 0), ("gm", 1), "junk"])
    yt = [P.sb("yt%d" % i, [128, 128], BF16) for i in range(2)]
    for i in range(2):
        for hf in range(2):
            P.op("pool", lambda e, i=i, hf=hf: e.memset(Xp[i][hf][:], 0.0), writes=[("Xp", i, hf)])
            P.op("pool", lambda e, i=i, hf=hf: e.memset(Vn[i][hf][:], 0.0), writes=[("Vn", i, hf)])

    def precompute(hl, d, t):
        tsl = slice(t * 128, (t + 1) * 128)
        b = t % 2
        tq = tokq[d]
        gc_p, be_p, eg_p, er_p, ngc_p, neg_p = [tq[:, hl, qi, t:t + 1] for qi in range(6)]
        P.mm(pw[:, 0:128], kT[hl][:, tsl], kT[hl][:, tsl], reads=[("kT%d" % hl, t // 4)], writes=[("pp", 0)])
        P.mm(pw[:, 128:256], kT[hl][:, tsl], qT[hl][:, tsl], reads=[("kT%d" % hl, t // 4), ("qT%d" % hl, t // 4)],
             writes=[("pp", 0)])
        dve(P, lambda e: e.tensor_scalar(gsel[b][:], Gc[d][:, hl, :], ident[0:NT, t:t + 1], None, ALU.mult),
            ["Gc%d" % d, "ident"], [("gsel", b)])
        P.mm(pw[:, 256:384], onesf[:], gsel[b][:], reads=["onesf", ("gsel", b)], writes=[("pp", 0)])
        if PST <= 1:
            return
        P.act(Dm[b][:], pw[:, 256:384], AF.Abs, reads=[("pp", 0), "tokq%d" % d], writes=[("Dm", b), "pw_rd"], scale=-1.0,
              bias=gc_p)
        P.act(DT[b][:], pw[:, 256:384], AF.Abs, reads=[("pp", 0), "tokq%d" % d], writes=[("DT", b), "pw_rd"], bias=ngc_p)
        P.act(Dm[b][:], Dm[b][:], AF.Exp, reads=[("Dm", b)], writes=[("Dm", b)], scale=-1.0)
        P.act(DT[b][:], DT[b][:], AF.Exp, reads=[("DT", b)], writes=[("DT", b)], scale=-1.0)
        dve(P, lambda e: e.tensor_tensor(Dm[b][:], Dm[b][:], mask[:, d, :], ALU.mult), [("Dm", b), "mask"], [("Dm", b)])
        dve(P, lambda e: e.tensor_tensor(DT[b][:], DT[b][:], mask[:, 2 + d, :], ALU.mult), [("DT", b), "mask"],
            [("DT", b)])
        if PST <= 2:
            return
        dve(P, lambda e: e.scalar_tensor_tensor(A0[b][:], pw[:, 0:128], be_p, Dm[b][:], ALU.mult, ALU.mult),
            [("pp", 0), ("Dm", b), "tokq%d" % d], [("A0", b), "pw_rd"])
        dve(P, lambda e: e.tensor_tensor(QK[d][:, t, :], pw[:, 128:256], DT[b][:], ALU.mult),
            [("pp", 0), ("DT", b)], XK + [("QK", d), "pw_rd"])
        if PST <= 3:
            return
        P.tr(ptr[:, 0, :], A0[b][:], identb[:], reads=[("A0", b), "identb"], writes=["ptr"])
        P.act(NK[b][:], ptr[:, 0, :], AF.Copy, reads=["ptr"], writes=[("NK", b), "ptr_rd"])
        dve(P, lambda e: e.tensor_tensor(PT[b][:], identb[:], ptr[:, 0, :], ALU.subtract), ["ptr", "identb"],
            [("PT", b), "ptr_rd"])
        if PST <= 4:
            return
        cur_a = A0[b]
        ck = ("A0", b)
        for lev in range(5):
            P.mm(pn[:, 0:128], NK[b][:], cur_a[:], reads=[("NK", b), ck], writes=[("pp", 1)])
            if LST <= 1:
                return
            if lev < 4:
                P.mm(pn[:, 128:256], cur_a[:], NK[b][:], reads=[("NK", b), ck], writes=[("pp", 1)])
            if LST <= 2:
                return
            if VAR != 1 and VAR != 3:
                dve(P, lambda e: e.tensor_tensor(IA[b][:], pn[:, 0:128], ident[:], ALU.add), [("pp", 1), "ident"],
                    [("IA", b), "pn_rd"])
            if VAR == 1 or VAR == 4:
                return
            if lev < 4:
                P.act(AK[b][:], pn[:, 0:128], AF.Copy, reads=[("pp", 1)], writes=[("AK", b), "pn_rd"])
                if VAR == 2 or VAR == 3:
                    return
                P.act(NK[b][:], pn[:, 128:256], AF.Copy, reads=[("pp", 1)], writes=[("NK", b), "pn_rd"])
                cur_a = AK[b]
                ck = ("AK", b)
            if LST <= 3:
                return
            P.mm(pn[:, 256:384], IA[b][:], PT[b][:], reads=[("IA", b), ("PT", b)], writes=[("pp", 1)])
            if LST <= 4:
                return
            if lev < 4:
                dve(P, lambda e: e.tensor_copy(PT[b][:], pn[:, 256:384]), [("pp", 1)], [("PT", b), "pn_rd"])
            else:
                P.act(TI[d][:, t, :], pn[:, 256:384], AF.Copy, reads=[("pp", 1), "tokq%d" % d],
                      writes=XK + [("TI", d), "pn_rd"], scale=be_p)
        if PST <= 5:
            return
        P.tr(ptr[:, 1, :], kT[hl][:, tsl], identb[:], reads=[("kT%d" % hl, t // 4), "identb"], writes=["ptr"])
        P.act(KD[d][:, t, :], ptr[:, 1, :], AF.Copy, reads=["ptr", "tokq%d" % d], writes=XK + [("KD", d), "ptr_rd"], scale=er_p)

    def chunk_step(hl, d, c):
        t, hf = c // 2, c % 2
        R = slice(64 * hf, 64 * hf + 64)
        tsl = slice(t * 128, (t + 1) * 128)
        S_ = St[d]
        sk = ("St", d)
        ca = cA[d]
        cak = ("cA", d)
        tq = tokq[d]
        P.mm(ca[:, 0, :], kT[hl][:, tsl], S_[:], reads=[("kT%d" % hl, t // 4), sk], writes=[cak])
        P.mm(ca[:, 1, :], qT[hl][:, tsl], S_[:], reads=[("qT%d" % hl, t // 4), sk], writes=[cak])
        X = Xp[d][hf]
        dve(P, lambda e: e.scalar_tensor_tensor(X[R, :], ca[R, 0, :], tq[R, hl, 5, t:t + 1], vtok[hl][R, t, :],
                                                ALU.mult, ALU.add),
            [cak, "tokq%d" % d, "vtok%d" % hl], [("Xp", d, hf), ("ca_rd", d)])
        P.mm(ca[:, 2, :], TI[d][:, t, :], X[:], reads=[("TI", d), ("Xp", d, hf)], writes=[cak])
        V = Vn[d][hf]
        P.act(V[R, :], ca[R, 2, :], AF.Copy, reads=[cak], writes=[("Vn", d, hf), ("ca_rd", d)])
        P.mm(cSb[d][:], KD[d][:, t, :], V[:], reads=[("KD", d), ("Vn", d, hf)], writes=[("cS", d)])
        P.mm(ca[:, 3, :], QK[d][:, t, :], V[:], reads=[("QK", d), ("Vn", d, hf)], writes=[cak])
        dve(P, lambda e: e.scalar_tensor_tensor(S_[:], S_[:], egt[d][:, hl, hf, t:t + 1], cSb[d][:], ALU.mult,
                                                ALU.add), [sk, ("cS", d), "egt%d" % d], [sk])
        ot = otmp[d]
        P.act(ot[R, :], ca[R, 1, :], AF.Copy, reads=[cak, "tokq%d" % d], writes=[("otmp", d), ("ca_rd", d)],
              scale=tq[R, hl, 2, t:t + 1])
        P.op("dve", lambda e: e.tensor_tensor(ot[R, :], ca[R, 3, :], ot[R, :], ALU.add),
             reads=[cak, ("otmp", d)], writes=[("otmp", d), ("ca_rd", d)])
        P.op("pool", lambda e: e.tensor_tensor(oacc[R, t, :], oacc[R, t, :], ot[R, :], ALU.add),
             reads=[("otmp", d), ("oacc", t)], writes=[("oacc", t)])

    ycnt = 0
    for hl in range(2):
        for d in range(2):
            for t in range(NT):
                precompute(hl, d, t)
                if PST < 99:
                    return (P.finish() if own else None)
            if STOP <= 5:
                return (P.finish() if own else None)
            P.op("pool", lambda e, d=d: e.memset(St[d][:], 0.0), writes=[("St", d)])
        P.op("pool", lambda e: e.memset(raw[:, 0:S], 0.0),
             writes=[("oacc", t) for t in range(NT)] + allraw + ["rawpad"])
        for step in range(NC):
            chunk_step(hl, 0, step)
            chunk_step(hl, 1, NC - 1 - step)
        if STOP <= 6:
            return (P.finish() if own else None)
        for t in range(NT):
            k2 = t % 2
            P.act(junk_, oacc[:, t, :], AF.Square, reads=[("oacc", t)], writes=["junk", ("ssq", k2)],
                  accum_out=ssq[:, k2:k2 + 1])
            P.act(ssq[:, k2:k2 + 1], ssq[:, k2:k2 + 1], AF.Sqrt, reads=[("ssq", k2)], writes=[("ssq", k2)],
                  scale=1.0 / 128, bias=1e-6)
            dve(P, lambda e, k2=k2: e.reciprocal(ssq[:, k2:k2 + 1], ssq[:, k2:k2 + 1]), [("ssq", k2)], [("ssq", k2)])
            P.op("pool", lambda e, k2=k2, t=t, hl=hl: e.tensor_tensor(gm[k2], sz[:, t, hl * 128:(hl + 1) * 128], gb[:],
                                                                  ALU.mult), reads=["sz", "gb"], writes=[("gm", k2)])
            dve(P, lambda e, k2=k2, t=t: e.scalar_tensor_tensor(yt[k2][:], oacc[:, t, :], ssq[:, k2:k2 + 1], gm[k2],
                                                                ALU.mult, ALU.mult),
                [("oacc", t), ("ssq", k2), ("gm", k2)], [("yt", k2)])
            P.dma("sp", y[t * 128:(t + 1) * 128, hl * 128:(hl + 1) * 128], yt[k2][:], ("yt", k2), reads=[("yt", k2)],
                  is_out=True)
    print("gdn stats", P.stats())
    return (P.finish() if own else None)


def gdn_inputs(xTb, g1l, w_in_l, conv_l, alog_l, dtb_l, ng_l, hh, S=S):
    NT = S // 128
    offs = np.cumsum((0,) + (1536, 512, 16, 512, 512, 512, 512, 128, 128, 256, 256, 512, 16, 512))
    cqkv, cz, cab = offs[0], offs[1], offs[2]
    hs_ = [2 * hh, 2 * hh + 1]
    chunks = []
    for kind in range(3):
        for h in hs_:
            chunks.append(cqkv + kind * 512 + h * 128 + np.arange(128))
    qkvc = np.concatenate(chunks)
    zc = np.concatenate([cz + h * 128 + np.arange(128) for h in hs_])
    gsel = [(ab, dr, h) for ab in range(2) for dr in range(2) for h in hs_]
    gc = np.array([cab + ab * 8 + dr * 4 + h for (ab, dr, h) in gsel])
    gbv = np.array([dtb_l[dr, h] if ab == 0 else 0.0 for (ab, dr, h) in gsel], np.float32)
    cols = np.concatenate([qkvc, zc, gc])
    assert len(cols) == A_NCOLS
    convw = np.ascontiguousarray(np.stack([conv_l[:, c - cqkv].T for c in chunks], axis=1).astype(np.float32))
    ident = np.eye(128, dtype=np.float32)
    anti = np.ascontiguousarray(ident[::-1])
    sel = np.zeros((NT, NT, 128), np.float32)
    for k in range(NT):
        sel[k, k, :] = 1.0
    p = np.arange(128)[:, None]
    f = np.arange(128)[None, :]
    same = (p // 64) == (f // 64)
    MA0 = same & (f < p)
    MA1 = same & (f > p)
    MQ0 = same & (p <= f)
    MQ1 = same & (p >= f)
    mask = np.stack([MA0, MA1, MQ0, MQ1], axis=1).astype(np.float32).reshape(128, 4 * 128)
    gbias = np.ascontiguousarray(np.tile(gbv[None, None, :], (128, NT, 1)).reshape(128, NT * 8))
    alog = np.ascontiguousarray(np.tile(np.array([alog_l[dr, h] for dr in range(2) for h in hs_], np.float32)[None],
                                        (128, 1)))
    rmask = np.ones((NT, 128), np.float32)
    rmask[:, 0] = 0.0
    rmask[:, 64] = 0.0
    gb = np.ascontiguousarray(np.tile(ng_l[None, :], (128, 1)).astype(np.float32))
    return dict(xT=xTb, g1=g1l, w=wlayout(w_in_l, cols), convw=convw, ident=ident, anti=anti,
                mask=mask.astype(ml_dtypes.bfloat16),
                gbias=gbias, alog=alog, rmask=rmask, gb=gb)


class XSrc:
    def __init__(self, fn):
        self.fn = fn

    def __getitem__(self, idx):
        tsl = idx[2]
        return self.fn(tsl.start, tsl.stop)


class RowChunks:
    def __init__(self, chunks, rows_per, col0=0, ncols=None, rowmap=None):
        self.chunks, self.rows_per, self.col0, self.ncols, self.rowmap = chunks, rows_per, col0, ncols, rowmap

    def __getitem__(self, idx):
        rs, cs = idx
        a, b_ = rs.start, rs.stop
        c0 = self.col0 + (cs.start or 0)
        c1 = self.col0 + (cs.stop if cs.stop is not None else self.ncols)
        ci, off = self.rowmap(a) if self.rowmap else (a // self.rows_per, a % self.rows_per)
        return self.chunks[ci][off:off + (b_ - a), c0:c1]


def build_fused(S_=S, depth=2):
    import math
    P = Prog()
    H = S_ // 2
    x_full = P.dram("x_full", [128, 8, S_], F32, "ExternalInput")
    x_half = P.dram("x_half", [128, 8, H], F32, "ExternalInput")
    out = P.dram("out", [128, 8, H], F32, "ExternalOutput")
    YR = 1024
    NYC = S_ // YR
    ymine = [P.scratch("ymine%d" % c, [YR, 1024], BF16) for c in range(NYC)]
    ypair = [P.scratch("ypair%d" % c, [2 * YR, 1024], BF16) for c in range(NYC)]
    NXC = H // 512
    xh = [P.scratch("xh%d" % c, [1024, 512], F32) for c in range(NXC)]
    xpair = [P.scratch("xpair%d" % c, [2048, 512], F32) for c in range(NXC)]
    RG = [[0, 1], [2, 3], [4, 5], [6, 7]]

    def xpair_view(a, b):
        r, tg, off = a // H, (a % H) // 512, a % 512
        assert off + (b - a) <= 512
        return xpair[tg][r * 1024:(r + 1) * 1024, off:off + (b - a)].rearrange("(p c) t -> p c t", c=8)

    def xh_view(a, b):
        tg, off = a // 512, a % 512
        assert off + (b - a) <= 512
        return xh[tg][:, off:off + (b - a)].rearrange("(p c) t -> p c t", c=8)

    def ypair_rowmap(row):
        r, T = row // S_, row % S_
        return T // YR, r * YR + (T % YR)

    stats = []
    for l in range(depth):
        lam_init = 0.8 - 0.6 * math.exp(-0.3 * l)
        xsrc = x_full if l == 0 else XSrc(xpair_view)
        for n, (nm, bld) in enumerate((("gdn", lambda: build_gdn(S_, P=P)), ("diff", lambda: build_diff(lam_init, S_, P=P)),
                                       ("swa", lambda: build_swa(S_, P=P)), ("mlstm", lambda: build_mlstm(S_, P=P)))):
            P.pre = "l%d_%s_" % (l, nm)
            P.override = {"xT": xsrc, "y": RowChunks(ymine, YR, col0=n * 256, ncols=256)}
            bld()
            stats.append((P.pre, P.end_phase()))
        for c in range(NYC):
            P.op("pool", lambda e, c=c: e.collective_compute("AllGather", ALU.bypass, replica_groups=RG,
                                                             ins=[ymine[c].opt()], outs=[ypair[c].opt()]),
                 dma=("cc_y", c), dma_inc=1)
        P.end_phase()
        final = (l == depth - 1)
        P.pre = "l%d_dense_" % l
        P.override = {"xT": (x_half if l == 0 else XSrc(xh_view)), "ypair": RowChunks(ypair, YR, ncols=1024, rowmap=ypair_rowmap),
                      "out": (out if final else XSrc(xh_view))}
        build_dense(H, final=final, P=P)
        stats.append((P.pre, P.end_phase()))
        if not final:
            for c in range(NXC):
                P.op("pool", lambda e, c=c: e.collective_compute("AllGather", ALU.bypass, replica_groups=RG,
                                                                 ins=[xh[c].opt()], outs=[xpair[c].opt()]),
                     dma=("cc_x", c), dma_inc=1)
            P.end_phase()
    P.pre = ""
    P.override = {}
    for s_ in stats:
        print(s_)
    return P.finish()


_NC = {}


def kernel(x, norm1_g, w_in, gdn_conv_w, gdn_a_log, gdn_dt_bias, gdn_norm_g, diff_lambda, diff_norm_g, swa_sink,
           mlstm_gate_b, mlstm_norm_g, w_branch, w_gate, w_out, norm2_g, w_mlp1, w_mlp2, final_norm_g):
    f32 = lambda a: np.asarray(a, dtype=np.float32)
    x = f32(x)
    norm1_g, w_in, gdn_conv_w, gdn_a_log, gdn_dt_bias, gdn_norm_g = map(f32, (norm1_g, w_in, gdn_conv_w, gdn_a_log,
                                                                            gdn_dt_bias, gdn_norm_g))
    diff_lambda, diff_norm_g, swa_sink, mlstm_gate_b, mlstm_norm_g = map(f32, (diff_lambda, diff_norm_g, swa_sink,
                                                                             mlstm_gate_b, mlstm_norm_g))
    w_branch, w_gate, w_out, norm2_g, w_mlp1, w_mlp2, final_norm_g = map(f32, (w_branch, w_gate, w_out, norm2_g,
                                                                             w_mlp1, w_mlp2, final_norm_g))
    B, S_, D = x.shape
    depth = norm1_g.shape[0]
    H = S_ // 2
    gl = lambda g: np.ascontiguousarray(g.reshape(8, 128).T)
    if "nc" not in _NC:
        _NC["nc"] = build_fused(S_, depth)
    nc = _NC["nc"]
    xT = [fm(x[b]) for b in range(B)]
    cosT, sinT = rope_tables_np(S_)
    cores = [(b, hh) for b in range(B) for hh in range(2)]
    dense_w = [dense_layout(w_gate[l], w_branch[l], w_out[l], w_mlp1[l], w_mlp2[l]) for l in range(depth)]
    identb = np.eye(128, dtype=np.float32).astype(ml_dtypes.bfloat16)
    in_maps = []
    for (b, hh) in cores:
        im = {"x_full": xT[b], "x_half": np.ascontiguousarray(xT[b][:, :, hh * H:(hh + 1) * H])}
        for l in range(depth):
            g1 = gl(norm1_g[l])
            parts = {
                "gdn": gdn_inputs(None, g1, w_in[l], gdn_conv_w[l], gdn_a_log[l], gdn_dt_bias[l], gdn_norm_g[l], hh, S_),
                "diff": diff_inputs(None, g1, w_in[l], diff_lambda[l], diff_norm_g[l], hh, cosT, sinT),
                "swa": swa_inputs(None, g1, w_in[l], swa_sink[l], hh, cosT, sinT),
                "mlstm": mlstm_inputs(None, g1, w_in[l], mlstm_gate_b[l], mlstm_norm_g[l], hh, S_),
                "dense": dict(g1=g1, g2=gl(norm2_g[l]), g3=gl(final_norm_g), identb=identb,
                              msel=np.ascontiguousarray(np.tile(np.array([[1.0 - hh, float(hh)]], np.float32), (128, 1))),
                              **dense_w[l]),
            }
            for nm, d in parts.items():
                for k, v in d.items():
                    if k == "xT":
                        continue
                    im["l%d_%s_%s" % (l, nm, k)] = v
        in_maps.append(im)
    r = run_bass_kernel_spmd(nc, in_maps, core_ids=list(range(len(cores)))).results
    for (b, hh), o in zip(cores, r):
        xT[b][:, :, hh * H:(hh + 1) * H] = np.asarray(o["out"])
    return np.stack([unfm(xT[b]) for b in range(B)]).astype(np.float32)
```

```python
import time
import ml_dtypes
from contextlib import ExitStack
import numpy as np
import concourse.bass as bass
import concourse.mybir as mybir
from concourse.bass_utils import run_bass_kernel_spmd

F32 = mybir.dt.float32
BF16 = mybir.dt.bfloat16
ALU = mybir.AluOpType
AF = mybir.ActivationFunctionType
AX = mybir.AxisListType

ENGS = ("pe", "act", "dve", "pool", "sp")
N_DMA_SEMS = 40


class Prog:
    def __init__(self):
        self.nc = bass.Bass("TRN2", target_bir_lowering=False)
        nc = self.nc
        self.stack = ExitStack()
        self.pstack = ExitStack()
        self.sems = {e: self.stack.enter_context(nc.semaphore("se_" + e)) for e in ENGS}
        for i in range(N_DMA_SEMS):
            self.sems[("dma", i)] = self.stack.enter_context(nc.semaphore("sd_%d" % i))
        self.sems["bar"] = self.stack.enter_context(nc.semaphore("s_bar"))
        self.cnt = {e: 0 for e in ENGS}
        self.dcnt = {("dma", i): 0 for i in range(N_DMA_SEMS)}
        self.known = {e: {} for e in ENGS}
        self.phase_no = 0
        self.uid = 0
        self._reset_phase()

    def _reset_phase(self):
        self.q = {e: [] for e in ENGS}
        self.last_w = {}
        self.readers = {}
        self.dma_map = {}
        self.out_tokens = []

    def dram(self, name, shape, dtype, kind):
        ov = getattr(self, "override", None) or {}
        if name in ov:
            return ov[name]
        full = getattr(self, "pre", "") + name
        self.ext_names = getattr(self, "ext_names", [])
        self.ext_names.append(full)
        return self.nc.dram_tensor(full, list(shape), dtype, kind=kind).ap()

    def scratch(self, name, shape, dtype):
        return self.nc.dram_tensor(name, list(shape), dtype).ap()

    def sb(self, name, shape, dtype, glob=False):
        self.uid += 1
        st = self.stack if glob else self.pstack
        return st.enter_context(self.nc.sbuf_tensor("s%d_%s" % (self.uid, name), list(shape), dtype))

    def ps(self, name, shape, dtype=F32):
        self.uid += 1
        return self.pstack.enter_context(self.nc.psum_tensor("p%d_%s" % (self.uid, name), list(shape), dtype))

    def op(self, eng, fn, reads=(), writes=(), dma=None, is_out=False, dma_inc=16):
        toks = []
        for k in reads:
            toks += self.last_w.get(k, [])
        for k in writes:
            toks += self.last_w.get(k, [])
            toks += self.readers.get(k, [])
        need = {}
        for s, v in toks:
            if eng == "pe" and s == "pe":
                continue
            if v > need.get(s, 0):
                need[s] = v
        waits = []
        kn = self.known[eng]
        for s, v in need.items():
            if kn.get(s, 0) >= v:
                continue
            kn[s] = v
            waits.append((s, v))
        if dma is not None:
            if dma not in self.dma_map:
                assert len(self.dma_map) < N_DMA_SEMS, "too many DMA groups in one phase"
                self.dma_map[dma] = ("dma", len(self.dma_map))
            sk = self.dma_map[dma]
            self.dcnt[sk] += dma_inc
            tok = (sk, self.dcnt[sk])
            inc = dma_inc
        else:
            self.cnt[eng] += 1
            tok = (eng, self.cnt[eng])
            inc = 1
        self.q[eng].append((waits, fn, tok, inc))
        for k in reads:
            self.readers.setdefault(k, []).append(tok)
        for k in writes:
            self.last_w[k] = [tok]
            self.readers[k] = []
        if is_out:
            self.out_tokens.append(tok)
        return tok

    def mm(self, out, lhsT, rhs, start=True, stop=True, reads=(), writes=()):
        return self.op("pe", lambda e: e.matmul(out, lhsT, rhs, start=start, stop=stop), reads, writes)

    def tr(self, out, in_, ident, reads=(), writes=()):
        return self.op("pe", lambda e: e.transpose(out, in_, ident), reads, writes)

    def act(self, out, in_, func, reads=(), writes=(), eng="act", **kw):
        return self.op(eng, lambda e: e.activation(out, in_, func, **kw), reads, writes)

    def dma(self, eng, out, in_, key, reads=(), writes=(), is_out=False, **kw):
        return self.op(eng, lambda e: e.dma_start(out=out, in_=in_, **kw), reads, writes, dma=key, is_out=is_out)

    def end_phase(self):
        nc = self.nc
        self.phase_no += 1
        pno = self.phase_no
        fin = {}
        for e in ENGS:
            if e != "sp" and self.cnt[e] > self.known["sp"].get(e, 0):
                fin[e] = self.cnt[e]
        for key, sk in self.dma_map.items():
            if self.dcnt[sk] > self.known["sp"].get(sk, 0):
                fin[sk] = self.dcnt[sk]
        for s, v in fin.items():
            self.known["sp"][s] = v
        sems = self.sems
        q = self.q
        first = (pno == 1)

        def replay(name, eng):
            if not first:
                eng.wait_ge(sems["bar"], pno - 1)
            for waits, fn, tok, inc in q[name]:
                for s, v in waits:
                    eng.wait_ge(sems[s], v)
                fn(eng).then_inc(sems[tok[0]], inc)
            if name == "sp":
                for s, v in fin.items():
                    eng.wait_ge(sems[s], v)
                eng.sem_inc(sems["bar"], 1)

        with nc.Block() as block:
            @block.tensor
            def _(e):
                replay("pe", e)

            @block.scalar
            def _(e):
                replay("act", e)

            @block.vector
            def _(e):
                replay("dve", e)

            @block.gpsimd
            def _(e):
                replay("pool", e)

            @block.sync
            def _(e):
                replay("sp", e)
        st = {e: len(q[e]) for e in ENGS}
        for e in ENGS:
            for e2 in ENGS:
                self.known[e][e2] = self.cnt[e2]
            for sk, v in self.dcnt.items():
                self.known[e][sk] = v
        self._reset_phase()
        self.pstack.close()
        self.pstack = ExitStack()
        return st

    def finish(self):
        self.end_phase()
        self.stack.close()
        return self.nc

    def stats(self):
        return {e: len(self.q[e]) for e in ENGS}


TG = 512


class WLoader:
    def __init__(self, P, maxcols, nbuf=3, cast_engs=("pool", "act", "dve", "pool")):
        self.P = P
        self.nbuf = nbuf
        self.st = [P.sb("wst%d" % i, [128, maxcols], F32) for i in range(nbuf)]
        self.bf = [P.sb("wbf%d" % i, [128, maxcols], BF16) for i in range(nbuf)]
        self.i = 0
        self.engs = cast_engs

    def load(self, src, ncols):
        P = self.P
        i = self.i % self.nbuf
        eng = self.engs[self.i % len(self.engs)]
        self.i += 1
        st, bf = self.st[i], self.bf[i]
        P.dma("sp", st[:, 0:ncols], src, ("wst", i), writes=[("wst", i)])
        if eng == "act":
            P.op("act", lambda e: e.copy(bf[:, 0:ncols], st[:, 0:ncols]), reads=[("wst", i)], writes=[("wbf", i)])
        else:
            P.op(eng, lambda e: e.tensor_copy(bf[:, 0:ncols], st[:, 0:ncols]), reads=[("wst", i)],
                 writes=[("wbf", i)])
        return bf, ("wbf", i)


def rmsnorm_fm(P, xs, xkey, gs, gkey, out, outkey, ones, sq, pss, rstd, T, out_scale_keyed=True):
    P.act(sq[:], xs[:], AF.Square, reads=[xkey], writes=["sq"])
    for c in range(8):
        P.mm(pss[:, 0:T], ones[:], sq[:, c, :], start=(c == 0), stop=(c == 7), reads=["ones", "sq"], writes=["pss"])
    P.act(rstd[:, 0:T], pss[:, 0:T], AF.Sqrt, reads=["pss"], writes=["rstd"], scale=1.0 / 1024, bias=1e-6)
    P.op("dve", lambda e: e.reciprocal(rstd[:, 0:T], rstd[:, 0:T]), reads=["rstd"], writes=["rstd"])
    for c in range(8):
        P.op("dve", lambda e, c=c: e.scalar_tensor_tensor(out[:, c, :], xs[:, c, :], gs[:, c:c + 1], rstd[:, 0:T],
                                                     ALU.mult, ALU.mult),
             reads=[xkey, gkey, "rstd"], writes=[outkey])


def build_dense(NT=2048, final=False, P=None):
    own = P is None
    if own:
        P = Prog()
    xT = P.dram("xT", [128, 8, NT], F32, "ExternalInput")
    ypair = (getattr(P, "override", None) or {}).get("ypair")
    SEQ = 2 * NT
    yT = P.dram("yT", [128, 16, NT], BF16, "ExternalInput") if ypair is None else None
    if ypair is not None:
        identbd = P.dram("identb", [128, 128], BF16, "ExternalInput")
        mseld = P.dram("msel", [128, 2], F32, "ExternalInput")
    g1 = P.dram("g1", [128, 8], F32, "ExternalInput")
    g2 = P.dram("g2", [128, 8], F32, "ExternalInput")
    g3 = P.dram("g3", [128, 8], F32, "ExternalInput")
    wg = P.dram("wg", [4, 8, 128, 1024], F32, "ExternalInput")
    wb = P.dram("wb", [4, 8, 128, 512], F32, "ExternalInput")
    wo = P.dram("wo", [8, 128, 1024], F32, "ExternalInput")
    w1 = P.dram("w1", [32, 128, 1024], F32, "ExternalInput")
    w2 = P.dram("w2", [8, 128, 4096], F32, "ExternalInput")
    out = P.dram("out", [128, 8, NT], F32, "ExternalOutput")

    xs = P.sb("xs", [128, 8, TG], F32)
    ys = P.sb("ys", [128, 16, TG], BF16)
    xn = P.sb("xn", [128, 8, TG], BF16)
    mg = P.sb("mg", [128, 8, TG], BF16)
    hT = P.sb("hT", [128, 32, TG], BF16)
    sq = P.sb("sq", [128, 8, TG], BF16)
    rstd = P.sb("rstd", [128, TG], F32)
    ones = P.sb("ones", [128, 128], BF16)
    g1s = P.sb("g1s", [128, 8], F32)
    g2s = P.sb("g2s", [128, 8], F32)
    g3s = P.sb("g3s", [128, 8], F32)
    sg = [P.sb("sg%d" % i, [128, TG], F32) for i in range(2)]
    pr = [P.sb("pr%d" % i, [128, TG], F32) for i in range(2)]
    acc = P.sb("acc", [128, TG], F32)
    rl = [P.sb("rl%d" % i, [128, TG], F32) for i in range(2)]
    xo = P.sb("xo", [128, 8, TG], F32)
    pss = P.ps("pss", [128, TG], F32)
    pg = [P.ps("pg%d" % i, [128, TG], F32) for i in range(2)]
    pb = [P.ps("pb%d" % i, [128, TG], F32) for i in range(2)]
    WL = WLoader(P, 1024, nbuf=6, cast_engs=("pool",))
    if ypair is not None:
        identb = P.sb("identb", [128, 128], BF16)
        msel = P.sb("msel", [128, 2], F32)
        cand = [P.sb("cand%d" % i, [128, 1024], BF16) for i in range(2)]
        ysel = P.sb("ysel", [128, 1024], BF16)
        ptr = P.ps("ptr", [128, 8, 128], BF16)
        P.dma("sp", identb[:], identbd, "identb", writes=["identb"])
        P.dma("sp", msel[:], mseld, "msel", writes=["msel"])

    P.op("pool", lambda e: e.memset(ones[:], 1.0), writes=["ones"])
    P.dma("sp", g1s[:], g1, "g1s", writes=["g1s"])
    P.dma("sp", g2s[:], g2, "g2s", writes=["g2s"])
    P.dma("sp", g3s[:], g3, "g3s", writes=["g3s"])
    cnt = 0
    for tg in range(NT // TG):
        tsl = slice(tg * TG, (tg + 1) * TG)
        P.dma("sp", xs[:], xT[:, :, tsl], "xs", writes=["xs"])
        if ypair is None:
            P.dma("sp", ys[:], yT[:, :, tsl], "ys", writes=["ys"])
        else:
            ys4 = ys[:].rearrange("p (n c) t -> p n c t", c=4)
            for tt in range(TG // 128):
                tok0 = tg * TG + tt * 128
                for r in range(2):
                    for h in range(2):
                        row0 = r * SEQ + h * NT + tok0
                        P.dma("sp", cand[h][:], ypair[row0:row0 + 128, :], ("cand", h), writes=[("cand", h)])
                    P.op("dve", lambda e: e.tensor_scalar(ysel[:], cand[0][:], msel[:, 0:1], None, ALU.mult),
                         reads=[("cand", 0), "msel"], writes=["ysel"])
                    P.op("dve", lambda e: e.scalar_tensor_tensor(ysel[:], cand[1][:], msel[:, 1:2], ysel[:], ALU.mult,
                                                                 ALU.add), reads=[("cand", 1), "msel", "ysel"],
                         writes=["ysel"])
                    for n in range(4):
                        for jj in range(2):
                            P.tr(ptr[:, n * 2 + jj, :], ysel[:, n * 256 + jj * 128:n * 256 + (jj + 1) * 128], identb[:],
                                 reads=["ysel", "identb"], writes=["ptr"])
                    for n in range(4):
                        P.act(ys4[:, n, 2 * r:2 * r + 2, tt * 128:(tt + 1) * 128], ptr[:, 2 * n:2 * n + 2, :], AF.Copy,
                              reads=["ptr"], writes=["ys"])
        rmsnorm_fm(P, xs, "xs", g1s, "g1s", xn, "xn", ones, sq, pss, rstd, TG)
        for j in range(8):
            for n in range(4):
                k = cnt % 2
                cnt += 1
                wgt, wgk = WL.load(wg[n, j], 1024)
                for c in range(8):
                    P.mm(pg[k][:], wgt[:, c * 128:(c + 1) * 128], xn[:, c, :], start=(c == 0), stop=(c == 7),
                         reads=[wgk, "xn"], writes=[("pg", k)])
                wbt, wbk = WL.load(wb[n, j], 512)
                for c in range(4):
                    P.mm(pb[k][:], wbt[:, c * 128:(c + 1) * 128], ys[:, n * 4 + c, :], start=(c == 0), stop=(c == 3),
                         reads=[wbk, "ys"], writes=[("pb", k)])
                P.act(sg[k][:], pg[k][:], AF.Sigmoid, reads=[("pg", k)], writes=[("sg", k)])
                if n == 0:
                    P.op("dve", lambda e, k=k: e.tensor_tensor(acc[:], pb[k][:], sg[k][:], ALU.mult),
                         reads=[("pb", k), ("sg", k)], writes=["acc"])
                else:
                    P.op("dve", lambda e, k=k: e.tensor_tensor(pr[k][:], pb[k][:], sg[k][:], ALU.mult),
                         reads=[("pb", k), ("sg", k)], writes=[("pr", k)])
                    if n < 3:
                        P.op("dve", lambda e, k=k: e.tensor_tensor(acc[:], acc[:], pr[k][:], ALU.add),
                             reads=["acc", ("pr", k)], writes=["acc"])
                    else:
                        P.op("dve", lambda e, k=k, j=j: e.tensor_tensor(mg[:, j, :], acc[:], pr[k][:], ALU.add),
                             reads=["acc", ("pr", k)], writes=["mg"])
        for j in range(8):
            k = cnt % 2
            cnt += 1
            wt, wk = WL.load(wo[j], 1024)
            for c in range(8):
                P.mm(pg[k][:], wt[:, c * 128:(c + 1) * 128], mg[:, c, :], start=(c == 0), stop=(c == 7),
                     reads=[wk, "mg"], writes=[("pg", k)])
            P.op("dve", lambda e, k=k, j=j: e.tensor_tensor(xs[:, j, :], pg[k][:], xs[:, j, :], ALU.add),
                 reads=[("pg", k), "xs"], writes=["xs"])
        rmsnorm_fm(P, xs, "xs", g2s, "g2s", xn, "xn", ones, sq, pss, rstd, TG)
        for f in range(32):
            k = cnt % 2
            cnt += 1
            wt, wk = WL.load(w1[f], 1024)
            for c in range(8):
                P.mm(pg[k][:], wt[:, c * 128:(c + 1) * 128], xn[:, c, :], start=(c == 0), stop=(c == 7),
                     reads=[wk, "xn"], writes=[("pg", k)])
            P.act(rl[k][:], pg[k][:], AF.Relu, reads=[("pg", k)], writes=[("rl", k)])
            P.op("dve", lambda e, k=k, f=f: e.tensor_tensor(hT[:, f, :], rl[k][:], rl[k][:], ALU.mult),
                 reads=[("rl", k)], writes=["hT"])
        for j in range(8):
            k = cnt % 2
            cnt += 1
            for fb in range(4):
                wt, wk = WL.load(w2[j][:, fb * 1024:(fb + 1) * 1024], 1024)
                for f8 in range(8):
                    f = fb * 8 + f8
                    P.mm(pb[k][:], wt[:, f8 * 128:(f8 + 1) * 128], hT[:, f, :], start=(f == 0), stop=(f == 31),
                         reads=[wk, "hT"], writes=[("pb", k)])
            P.op("dve", lambda e, k=k, j=j: e.tensor_tensor(xs[:, j, :], pb[k][:], xs[:, j, :], ALU.add),
                 reads=[("pb", k), "xs"], writes=["xs"])
        if final:
            rmsnorm_fm(P, xs, "xs", g3s, "g3s", xo, "xo", ones, sq, pss, rstd, TG)
            P.dma("sp", out[:, :, tsl], xo[:], "out", reads=["xo"], is_out=True)
        else:
            P.dma("sp", out[:, :, tsl], xs[:], "out", reads=["xs"], is_out=True)
    print("dense stats", P.stats())
    return (P.finish() if own else None)


def dense_layout(w_gate, w_branch, w_out, w_mlp1, w_mlp2):
    wg = w_gate.reshape(8, 128, 4, 8, 128).transpose(2, 3, 1, 0, 4).reshape(4, 8, 128, 1024)
    wb = w_branch.reshape(4, 4, 128, 8, 128).transpose(0, 3, 2, 1, 4).reshape(4, 8, 128, 512)
    wo = w_out.reshape(8, 128, 8, 128).transpose(2, 1, 0, 3).reshape(8, 128, 1024)
    w1 = w_mlp1.reshape(8, 128, 32, 128).transpose(2, 1, 0, 3).reshape(32, 128, 1024)
    w2 = w_mlp2.reshape(32, 128, 8, 128).transpose(2, 1, 0, 3).reshape(8, 128, 4096)
    return dict(wg=np.ascontiguousarray(wg), wb=np.ascontiguousarray(wb), wo=np.ascontiguousarray(wo),
                w1=np.ascontiguousarray(w1), w2=np.ascontiguousarray(w2))


def fm(a):
    T, D = a.shape
    return np.ascontiguousarray(a.T.reshape(D // 128, 128, T).transpose(1, 0, 2))


def unfm(a):
    p, C, T = a.shape
    return np.ascontiguousarray(a.transpose(2, 1, 0).reshape(T, C * 128))


S = 4096
NTILE = S // 128
TG = 512
NG = S // TG


def prologue(P, xT, g1, wsrc, ncols, S=S):
    xnT = P.sb("xnT", [128, 8, S], BF16)
    wbf = P.sb("wbf", [128, 8, max(ncols, 1)], BF16)
    xs = [P.sb("xs0", [128, 8, TG], F32)] * 2
    sq = P.sb("sq", [128, 8, TG], BF16)
    rstd = P.sb("rstd", [128, TG], F32)
    ones = P.sb("ones", [128, 128], BF16)
    g1s = P.sb("g1s", [128, 8], F32)
    pss = P.ps("pss", [128, TG], F32)
    P.op("pool", lambda e: e.memset(ones[:], 1.0), writes=["ones"])
    P.dma("sp", g1s[:], g1, "g1s", writes=["g1s"])
    wst = [P.sb("wst%d" % i, [128, max(ncols, 1)], F32) for i in range(2)]
    for c in range(8 if wsrc is not None else 0):
        i = c % 2
        P.dma("sp", wst[i][:], wsrc[:, c * ncols:(c + 1) * ncols], ("wst", i), writes=[("wst", i)])
        P.op("pool", lambda e, i=i, c=c: e.tensor_copy(wbf[:, c, :], wst[i][:]), reads=[("wst", i)],
             writes=["wbf"])
    xn_src = (getattr(P, "override", None) or {}).get("xnT_src")
    if xn_src is not None:
        for g in range(S // TG):
            tsl = slice(g * TG, (g + 1) * TG)
            P.dma("sp", xnT[:, :, tsl], xn_src[:, :, tsl], ("xnld", g % 4), writes=[("xnT", g)])
        P.xs0 = xs[0]
        return xnT, wbf, ones
    for g in range(S // TG):
        i = 0
        tsl = slice(g * TG, (g + 1) * TG)
        P.dma("sp", xs[i][:], xT[:, :, tsl], ("xs", i), writes=[("xs", i)])
        P.act(sq[:], xs[i][:], AF.Square, reads=[("xs", i)], writes=["sq"])
        for c in range(8):
            P.mm(pss[:], ones[:], sq[:, c, :], start=(c == 0), stop=(c == 7), reads=["ones", "sq"], writes=["pss"])
        P.act(rstd[:], pss[:], AF.Sqrt, reads=["pss"], writes=["rstd"], scale=1.0 / 1024, bias=1e-6)
        P.op("dve", lambda e: e.reciprocal(rstd[:], rstd[:]), reads=["rstd"], writes=["rstd"])
        for c in range(8):
            P.op("dve", lambda e, c=c, i=i, tsl=tsl: e.scalar_tensor_tensor(
                xnT[:, c, tsl], xs[i][:, c, :], g1s[:, c:c + 1], rstd[:], ALU.mult, ALU.mult),
                reads=[("xs", i), "g1s", "rstd"], writes=[("xnT", g)])
    P.xs0 = xs[0]
    return xnT, wbf, ones


def proj_fm(P, out_ps, okey, wbf, col0, ncol, xnT, g):
    for c in range(8):
        P.mm(out_ps[0:ncol, :], wbf[:, c, col0:col0 + ncol], xnT[:, c, g * TG:(g + 1) * TG], start=(c == 0),
             stop=(c == 7), reads=["wbf", ("xnT", g)], writes=[okey])


def proj_tm(P, out_ps, okey, wbf, col0, ncol, xnT, t):
    g = (t * 128) // TG
    for c in range(8):
        P.mm(out_ps, xnT[:, c, t * 128:(t + 1) * 128], wbf[:, c, col0:col0 + ncol], start=(c == 0), stop=(c == 7),
             reads=["wbf", ("xnT", g)], writes=[okey])


def rope_proj(P, dstT, dkey, wbf, col_a, col_sw, xnT, cosT, sinT, pp, tmp, S=S):
    for g in range(S // TG):
        tsl = slice(g * TG, (g + 1) * TG)
        proj_fm(P, pp[0], ("pp", 0), wbf, col_a, 128, xnT, g)
        proj_fm(P, pp[1], ("pp", 1), wbf, col_sw, 128, xnT, g)
        P.op("dve", lambda e, tsl=tsl: e.tensor_tensor(tmp[0][:], pp[0][:], cosT[:, tsl], ALU.mult),
             reads=[("pp", 0), "cos"], writes=[("rtmp", 0)])
        P.op("dve", lambda e, tsl=tsl: e.tensor_tensor(tmp[1][:], pp[1][:], sinT[:, tsl], ALU.mult),
             reads=[("pp", 1), "sin"], writes=[("rtmp", 1)])
        P.op("pool", lambda e, tsl=tsl: e.tensor_tensor(dstT[:, tsl], tmp[0][:], tmp[1][:], ALU.add),
             reads=[("rtmp", 0), ("rtmp", 1)], writes=[(dkey, g)])


def rope_tables_np(S=S):
    inv = 1.0 / (10000.0 ** (np.arange(0, 64, 2, dtype=np.float32) / 64))
    ang = np.arange(S, dtype=np.float32)[:, None] * inv[None, :]
    ang = np.concatenate([ang, ang], axis=-1)
    cos, sin = np.cos(ang).astype(np.float32), np.sin(ang).astype(np.float32)
    sgn = np.concatenate([-np.ones(32, np.float32), np.ones(32, np.float32)])
    cosT = np.ascontiguousarray(np.tile(cos.T, (2, 1)))
    sinT = np.ascontiguousarray(np.tile((sin * sgn[None, :]).T, (2, 1)))
    return cosT, sinT


def swap_halves(cols):
    cols = np.asarray(cols).reshape(-1, 64)
    return np.concatenate([cols[:, 32:], cols[:, :32]], axis=1).reshape(-1)


def wlayout(w, cols):
    sub = w[:, cols]
    n = sub.shape[1]
    return np.ascontiguousarray(sub.reshape(8, 128, n).transpose(1, 0, 2).reshape(128, 8 * n))


C_NCOLS = 256 + 256 + 128 + 128 + 64


def build_swa(S=S, P=None):
    own = P is None
    if own:
        P = Prog()
    xT = P.dram("xT", [128, 8, S], F32, "ExternalInput")
    g1 = P.dram("g1", [128, 8], F32, "ExternalInput")
    w = P.dram("w", [128, 8 * C_NCOLS], F32, "ExternalInput")
    cosd = P.dram("cos", [128, S], F32, "ExternalInput")
    sind = P.dram("sin", [128, S], F32, "ExternalInput")
    maskd = P.dram("mask", [128, 2, 512], BF16, "ExternalInput")
    sinkd = P.dram("sink", [128, 4], F32, "ExternalInput")
    y = P.dram("y", [S, 256], BF16, "ExternalOutput")

    xnT, wbf, ones = prologue(P, xT, g1, w, C_NCOLS, S)
    cosT = P.sb("cosT", [128, S], F32)
    sinT = P.sb("sinT", [128, S], F32)
    P.dma("sp", cosT[:], cosd, "cos", writes=["cos"])
    P.dma("sp", sinT[:], sind, "sin", writes=["sin"])
    mask = P.sb("mask", [128, 2, 512], BF16)
    P.dma("sp", mask[:], maskd, "mask", writes=["mask"])
    esink = P.sb("esink", [128, 4], F32)
    P.dma("sp", esink[:], sinkd, "esink", writes=["esink"])
    P.act(esink[:], esink[:], AF.Exp, reads=["esink"], writes=["esink"])

    NT = S // 128
    qT = [P.sb("qT%d" % i, [128, S], BF16) for i in range(2)]
    kT = P.sb("kT", [128, S], BF16)
    vaug = P.sb("vaug", [128, NT, 65], BF16)
    pp = [P.ps("pp%d" % i, [128, TG], F32) for i in range(2)]
    tmp = [P.sb("rtmp%d" % i, [128, TG], F32) for i in range(2)]
    rope_proj(P, qT[0], "qT0", wbf, 0, 256, xnT, cosT, sinT, pp, tmp, S)
    rope_proj(P, qT[1], "qT1", wbf, 128, 384, xnT, cosT, sinT, pp, tmp, S)
    rope_proj(P, kT, "kT", wbf, 512, 640, xnT, cosT, sinT, pp, tmp, S)
    kz = [P.sb("kz%d" % i, [128, S], BF16) for i in range(2)]
    P.op("pool", lambda e: e.memset(kz[0][64:128, :], 0.0), writes=["kz0"])
    P.op("pool", lambda e: e.memset(kz[1][0:64, :], 0.0), writes=["kz1"])
    allk = [("kT", g) for g in range(S // TG)]
    P.op("pool", lambda e: e.tensor_copy(kz[0][0:64, :], kT[0:64, :]), reads=allk, writes=["kz0"])
    P.op("act", lambda e: e.copy(kz[1][64:128, :], kT[64:128, :]), reads=allk, writes=["kz1"])
    P.op("pool", lambda e: e.memset(vaug[:], 1.0), writes=["vaug"])
    pv = P.ps("pv", [128, 64], F32)
    for t in range(NT):
        proj_tm(P, pv[:], "pv", wbf, 768, 64, xnT, t)
        P.act(vaug[:, t, 0:64], pv[:], AF.Copy, reads=["pv"], writes=["vaug"])

    st = [P.ps("st%d" % i, [128, 512], F32) for i in range(3)]
    pt = [P.sb("pt%d" % i, [128, 512], BF16) for i in range(3)]
    po = P.ps("po", [128, 4, 65], F32)
    den = P.sb("den", [128, 4], F32)
    yt = [P.sb("yt%d" % i, [128, 4, 64], BF16) for i in range(2)]
    qkeys = lambda n: [("qT0", (n * 128) // TG), ("qT1", (n * 128) // TG)]
    for n in range(NT):
        ds = [d for d in (-1, 0, 1) if 0 <= n + d < NT]
        for di, d in enumerate(ds):
            m = n + d
            for h in range(4):
                P.mm(st[di][:, h * 128:(h + 1) * 128], kz[h % 2][:, m * 128:(m + 1) * 128],
                     qT[h // 2][:, n * 128:(n + 1) * 128], reads=["kz0", "kz1"] + qkeys(n),
                     writes=[("st", di)])
            P.act(pt[di][:], st[di][:], AF.Exp, reads=[("st", di)], writes=[("pt", di)], scale=0.125)
            if d != 0:
                mi = 0 if d == -1 else 1
                P.op("pool", lambda e, di=di, mi=mi: e.tensor_tensor(pt[di][:], pt[di][:], mask[:, mi, :], ALU.mult),
                     reads=[("pt", di), "mask"], writes=[("pt", di)])
        for h in range(4):
            for di, d in enumerate(ds):
                m = n + d
                P.mm(po[:, h, :], pt[di][:, h * 128:(h + 1) * 128], vaug[:, m, :], start=(di == 0),
                     stop=(di == len(ds) - 1), reads=[("pt", di), "vaug"], writes=["po"])
        P.op("dve", lambda e: e.tensor_tensor(den[:], po[:, :, 64], esink[:], ALU.add), reads=["po", "esink"],
             writes=["den"])
        P.op("dve", lambda e: e.reciprocal(den[:], den[:]), reads=["den"], writes=["den"])
        yb = yt[n % 2]
        for h in range(4):
            P.op("dve", lambda e, h=h, yb=yb: e.tensor_scalar(yb[:, h, :], po[:, h, 0:64], den[:, h:h + 1], None,
                                                           ALU.mult), reads=["po", "den"], writes=[("yt", n % 2)])
        P.dma("sp", y[n * 128:(n + 1) * 128, :], yb[:].rearrange("p h d -> p (h d)"), ("yt", n % 2),
              reads=[("yt", n % 2)], is_out=True)
    print("swa stats", P.stats())
    return (P.finish() if own else None)


def swa_inputs(xTb, g1l, w_in_l, sink_l, hh, cosT, sinT):
    offs = np.cumsum((0,) + (1536, 512, 16, 512, 512, 512, 512, 128, 128, 256, 256, 512, 16, 512))
    cq, ck, cv = offs[6], offs[7], offs[8]
    qcols = cq + np.arange(hh * 256, hh * 256 + 256)
    kcols = ck + np.arange(hh * 64, hh * 64 + 64)
    vcols = cv + np.arange(hh * 64, hh * 64 + 64)
    k2 = np.concatenate([kcols, kcols])
    cols = np.concatenate([qcols, swap_halves(qcols), k2, swap_halves(k2), vcols])
    assert len(cols) == C_NCOLS
    j = np.arange(128)[:, None]
    i = np.arange(128)[None, :]
    prev = (j >= i).astype(np.float32)
    nxt = (j <= i).astype(np.float32)
    mask = np.stack([np.tile(prev, (1, 4)), np.tile(nxt, (1, 4))], axis=1).astype(ml_dtypes.bfloat16)
    sink = np.tile(sink_l[hh * 4:hh * 4 + 4][None, :], (128, 1)).astype(np.float32)
    return dict(xT=xTb, g1=g1l, w=wlayout(w_in_l, cols), cos=cosT, sin=sinT, mask=np.ascontiguousarray(mask),
                sink=np.ascontiguousarray(sink))


B_NCOLS = 5 * 256


def build_diff(lam_init, S=S, P=None):
    own = P is None
    if own:
        P = Prog()
    xT = P.dram("xT", [128, 8, S], F32, "ExternalInput")
    g1 = P.dram("g1", [128, 8], F32, "ExternalInput")
    w = P.dram("w", [128, 8 * B_NCOLS], F32, "ExternalInput")
    cosd = P.dram("cos", [128, S], F32, "ExternalInput")
    sind = P.dram("sin", [128, S], F32, "ExternalInput")
    lpd = P.dram("lp", [128, 4, 64], F32, "ExternalInput")
    gbd = P.dram("gb", [128, 128], F32, "ExternalInput")
    y = P.dram("y", [S, 256], BF16, "ExternalOutput")
    NT = S // 128
    NQ = S // 512

    xnT, wbf, ones = prologue(P, xT, g1, w, B_NCOLS, S)
    cosT = P.sb("cosT", [128, S], F32)
    sinT = P.sb("sinT", [128, S], F32)
    P.dma("sp", cosT[:], cosd, "cos", writes=["cos"])
    P.dma("sp", sinT[:], sind, "sin", writes=["sin"])
    lp = P.sb("lp", [128, 4, 64], F32)
    gb = P.sb("gb", [128, 128], F32)
    P.dma("sp", lp[:], lpd, "lp", writes=["lp"])
    P.dma("sp", gb[:], gbd, "gb", writes=["gb"])
    P.op("dve", lambda e: e.tensor_scalar(gb[:], gb[:], 1.0 - lam_init, None, ALU.mult), reads=["gb"], writes=["gb"])
    junk = P.sb("junk", [128, 128], F32)
    s12 = P.sb("s12", [128, 2], F32)
    nlam = P.sb("nlam", [128, 1], F32)
    for i in range(2):
        P.op("dve", lambda e, i=i: e.scalar_tensor_tensor(junk[:, 0:64], lp[:, 2 * i, :], 1.0, lp[:, 2 * i + 1, :],
                                                     ALU.mult, ALU.mult, accum_out=s12[:, i:i + 1]),
             reads=["lp"], writes=["junk", "s12"])
    P.act(s12[:], s12[:], AF.Exp, reads=["s12"], writes=["s12"])
    P.op("dve", lambda e: e.tensor_tensor(nlam[:], s12[:, 0:1], s12[:, 1:2], ALU.subtract), reads=["s12"],
         writes=["nlam"])
    P.op("dve", lambda e: e.tensor_scalar(nlam[:], nlam[:], -1.0, -lam_init, ALU.mult, ALU.add), reads=["nlam"],
         writes=["nlam"])

    zt = P.sb("zt", [128, 512], BF16)
    P.op("pool", lambda e: e.memset(zt[:], 0.0), writes=["zt"])
    qT = P.sb("qT", [128, S], BF16)
    kT = P.sb("kT", [128, S], BF16)
    kz = [P.sb("kz%d" % i, [128, S], BF16) for i in range(2)]
    vt = P.sb("vt", [128, NT, 128], BF16)
    pp = [P.ps("pp%d" % i, [128, TG], F32) for i in range(2)]
    tmp = [P.sb("rtmp%d" % i, [128, TG], F32) for i in range(2)]
    st = [P.ps("st%d" % i, [128, 512], F32) for i in range(2)]
    pt = [P.sb("pt%d" % i, [128, 512], BF16) for i in range(3)]
    po = [P.ps("po%d" % i, [128, 4, 128], F32) for i in range(2)]
    pd = P.ps("pd", [128, 2, 4], F32)
    rd = P.sb("rd", [128, 2, 4], F32)
    t0 = [P.sb("t0_%d" % i, [128, 128], F32) for i in range(2)]
    ot = [P.sb("ot%d" % i, [128, 128], F32) for i in range(2)]
    ssq = P.sb("ssq", [128, 2], F32)
    yt = [P.sb("yt%d" % i, [128, 4, 128], BF16) for i in range(2)]
    P.op("pool", lambda e: e.memset(kz[0][64:128, :], 0.0), writes=["kz0"])
    P.op("pool", lambda e: e.memset(kz[1][0:64, :], 0.0), writes=["kz1"])
    allg = lambda k: [(k, g) for g in range(S // TG)]
    cnt = 0
    ycnt = 0
    for h in range(2):
        rope_proj(P, qT, "qT", wbf, h * 128, 256 + h * 128, xnT, cosT, sinT, pp, tmp, S)
        rope_proj(P, kT, "kT", wbf, 512 + h * 128, 768 + h * 128, xnT, cosT, sinT, pp, tmp, S)
        P.op("pool", lambda e: e.tensor_copy(kz[0][0:64, :], kT[0:64, :]), reads=allg("kT"), writes=["kz0"])
        P.op("act", lambda e: e.copy(kz[1][64:128, :], kT[64:128, :]), reads=allg("kT"), writes=["kz1"])
        for t in range(NT):
            proj_tm(P, pp[0][:, 0:128], ("pp", 0), wbf, 1024 + h * 128, 128, xnT, t)
            P.act(vt[:, t, :], pp[0][:, 0:128], AF.Copy, reads=[("pp", 0)], writes=["vt"])
        for g in range(NQ):
            qsl = slice(g * 512, (g + 1) * 512)
            for m in range(2):
                P.mm(po[m][:].rearrange("p a b -> p (a b)"), zt[:, 0:128], zt[:, 0:512], start=True, stop=False,
                     reads=["zt"], writes=[("po", m)])
            P.mm(pd[:].rearrange("p a b -> p (a b)"), zt[:, 0:128], zt[:, 0:8], start=True, stop=False, reads=["zt"],
                 writes=["pd"])
            for t in range(NT):
                last = (t == NT - 1)
                for m in range(2):
                    b = cnt % 2
                    b3 = cnt % 3
                    cnt += 1
                    P.mm(st[b][:], kz[m][:, t * 128:(t + 1) * 128], qT[:, qsl], reads=["kz%d" % m, ("qT", g)],
                         writes=[("st", b)])
                    P.act(pt[b3][:], st[b][:], AF.Exp, reads=[("st", b)], writes=[("pt", b3)], scale=0.125)
                    for qs in range(4):
                        P.mm(po[m][:, qs, :], pt[b3][:, qs * 128:(qs + 1) * 128], vt[:, t, :], start=False,
                             stop=(last and qs == 3), reads=[("pt", b3), "vt"], writes=[("po", m)])
                        P.mm(pd[:, m, qs:qs + 1], pt[b3][:, qs * 128:(qs + 1) * 128], ones[:, 0:1], start=False,
                             stop=(last and m == 1 and qs == 3), reads=[("pt", b3), "ones"], writes=["pd"])
            P.op("dve", lambda e: e.reciprocal(rd[:], pd[:]), reads=["pd"], writes=["rd"])
            P.op("dve", lambda e: e.tensor_scalar(rd[:, 1, :], rd[:, 1, :], nlam[:, 0:1], None, ALU.mult),
                 reads=["rd", "nlam"], writes=["rd"])
            yb = yt[ycnt % 2]
            ykey = ("yt", ycnt % 2)
            ycnt += 1
            for qs in range(4):
                k2 = qs % 2
                P.act(t0[k2][:], po[0][:, qs, :], AF.Copy, reads=[("po", 0), "rd"], writes=[("t0", k2)],
                      scale=rd[:, 0, qs:qs + 1])
                P.op("dve", lambda e, qs=qs, k2=k2: e.scalar_tensor_tensor(ot[k2][:], po[1][:, qs, :],
                                                                       rd[:, 1, qs:qs + 1], t0[k2][:], ALU.mult,
                                                                       ALU.add),
                     reads=[("po", 1), "rd", ("t0", k2)], writes=[("ot", k2)])
                P.act(junk[:], ot[k2][:], AF.Square, reads=[("ot", k2)], writes=["junk", ("ssq", k2)],
                      accum_out=ssq[:, k2:k2 + 1])
                P.act(ssq[:, k2:k2 + 1], ssq[:, k2:k2 + 1], AF.Sqrt, reads=[("ssq", k2)], writes=[("ssq", k2)],
                      scale=1.0 / 128, bias=1e-6)
                P.op("dve", lambda e, k2=k2: e.reciprocal(ssq[:, k2:k2 + 1], ssq[:, k2:k2 + 1]), reads=[("ssq", k2)],
                     writes=[("ssq", k2)])
                P.op("dve", lambda e, qs=qs, k2=k2, yb=yb: e.scalar_tensor_tensor(yb[:, qs, :], ot[k2][:],
                                                                              ssq[:, k2:k2 + 1], gb[:], ALU.mult,
                                                                              ALU.mult),
                     reads=[("ot", k2), ("ssq", k2), "gb"], writes=[ykey])
            P.dma("sp", y[g * 512:(g + 1) * 512, h * 128:(h + 1) * 128].rearrange("(q p) e -> p q e", p=128), yb[:],
                  ykey, reads=[ykey], is_out=True)
    print("diff stats", P.stats())
    return (P.finish() if own else None)


def diff_inputs(xTb, g1l, w_in_l, lam_l, ng_l, hh, cosT, sinT):
    offs = np.cumsum((0,) + (1536, 512, 16, 512, 512, 512, 512, 128, 128, 256, 256, 512, 16, 512))
    cq, ck, cv = offs[3], offs[4], offs[5]
    r = np.arange(hh * 256, hh * 256 + 256)
    cols = np.concatenate([cq + r, swap_halves(cq + r), ck + r, swap_halves(ck + r), cv + r])
    assert len(cols) == B_NCOLS
    lp = np.ascontiguousarray(np.tile(lam_l[None], (128, 1, 1)).astype(np.float32))
    gb = np.ascontiguousarray(np.tile(ng_l[None, :], (128, 1)).astype(np.float32))
    return dict(xT=xTb, g1=g1l, w=wlayout(w_in_l, cols), cos=cosT, sin=sinT, lp=lp, gb=gb)


D_NCOLS = 128 + 128 + 256 + 8 + 256


def dve(P, fn, reads, writes, eng="dve"):
    return P.op(eng, fn, reads, writes)


def build_mlstm(S=S, P=None):
    own = P is None
    if own:
        P = Prog()
    NT = S // 128
    NQ = S // 512
    xT = P.dram("xT", [128, 8, S], F32, "ExternalInput")
    g1 = P.dram("g1", [128, 8], F32, "ExternalInput")
    w = P.dram("w", [128, 8 * D_NCOLS], F32, "ExternalInput")
    identd = P.dram("ident", [128, 128], F32, "ExternalInput")
    antid = P.dram("anti", [128, 128], F32, "ExternalInput")
    seld = P.dram("sel", [NT, NT * 128], F32, "ExternalInput")
    maskd = P.dram("mask", [128, 2 * 4 * 512], BF16, "ExternalInput")
    biasd = P.dram("gbias", [128, NT * 8], F32, "ExternalInput")
    gbd = P.dram("gb", [128, 128], F32, "ExternalInput")
    y = P.dram("y", [S, 256], BF16, "ExternalOutput")

    xnT, wbf, ones = prologue(P, xT, g1, w, D_NCOLS, S)
    ident = P.sb("ident", [128, 128], F32)
    anti = P.sb("anti", [128, 128], F32)
    sel2 = P.xs0[:].rearrange("p a b -> p (a b)")
    mask = P.sb("mask", [128, 2, 4, 512], BF16)
    gbias = P.sb("gbias", [128, NT * 8], F32)
    gb = P.sb("gb", [128, 128], F32)
    P.dma("sp", ident[:], identd, "ident", writes=["ident"])
    P.dma("sp", anti[:], antid, "anti", writes=["anti"])
    P.dma("sp", sel2[0:NT, 0:NT * 128], seld, "sel", writes=["sel", ("xs", 0)])
    P.dma("sp", mask[:].rearrange("p a b c -> p (a b c)"), maskd, "mask", writes=["mask"])
    P.dma("sp", gbias[:], biasd, "gbias", writes=["gbias"])
    P.dma("sp", gb[:], gbd, "gb", writes=["gb"])
    zt = P.sb("zt", [128, 512], BF16)
    P.op("pool", lambda e: e.memset(zt[:], 0.0), writes=["zt"])
    onesf = P.sb("onesf", [NT, 128], F32)
    P.op("pool", lambda e: e.memset(onesf[:], 1.0), writes=["onesf"])

    pp = [P.ps("pp%d" % i, [128, 512], F32) for i in range(2)]
    st = [P.ps("st%d" % i, [128, 512], F32) for i in range(2)]
    po = [P.ps("po%d" % i, [128, 4, 128], F32) for i in range(2)]
    pd = P.ps("pd", [128, 8], F32)

    qT = P.sb("qT", [128, S], BF16)
    kz = [P.sb("kz%d" % i, [128, S], BF16) for i in range(2)]
    vt = P.sb("vt", [128, NT, 256], BF16)
    P.op("pool", lambda e: e.memset(kz[0][64:128, :], 0.0), writes=["kz0"])
    P.op("pool", lambda e: e.memset(kz[1][0:64, :], 0.0), writes=["kz1"])
    for g in range(S // TG):
        tsl = slice(g * TG, (g + 1) * TG)
        proj_fm(P, pp[0], ("pp", 0), wbf, 0, 128, xnT, g)
        P.act(qT[:, tsl], pp[0][:], AF.Copy, reads=[("pp", 0)], writes=[("qT", g)])
        proj_fm(P, pp[1], ("pp", 1), wbf, 128, 128, xnT, g)
        P.act(kz[0][0:64, tsl], pp[1][0:64, :], AF.Copy, reads=[("pp", 1)], writes=["kz0"], scale=0.125)
        P.act(kz[1][64:128, tsl], pp[1][64:128, :], AF.Copy, reads=[("pp", 1)], writes=["kz1"], scale=0.125)
    for t in range(NT):
        k = t % 2
        proj_tm(P, pp[k][:, 0:256], ("pp", k), wbf, 256, 256, xnT, t)
        P.act(vt[:, t, :], pp[k][:, 0:256], AF.Copy, reads=[("pp", k)], writes=["vt"])
    for t in range(NT):
        proj_tm(P, pp[0][:, t * 8:(t + 1) * 8], ("pp", 0), wbf, 512, 8, xnT, t)
    gtok = P.sb("gtok", [128, NT, 8], F32)
    gtokR = P.sb("gtokR", [128, NT, 8], F32)
    dve(P, lambda e: e.tensor_tensor(gtok[:].rearrange("p a b -> p (a b)"), pp[0][:, 0:NT * 8], gbias[:], ALU.add),
        [("pp", 0), "gbias"], ["gtok"])
    P.mm(pp[1][:, 0:NT * 8], anti[:], gtok[:].rearrange("p a b -> p (a b)"), reads=["anti", "gtok"],
         writes=[("pp", 1)])
    P.act(gtokR[:].rearrange("p a b -> p (a b)"), pp[1][:, 0:NT * 8], AF.Copy, reads=[("pp", 1)], writes=["gtokR"])

    tok = [P.sb("tok%d" % d, [128, 4, NT], F32) for d in range(2)]
    A = [P.sb("A%d" % d, [NT, 2, 128], F32) for d in range(2)]
    def gate_dir(d):
        src = gtok if d == 0 else gtokR
        sk = "gtok" if d == 0 else "gtokR"
        LP = P.sb("LP%d" % d, [NT, 4, 128], F32)
        k = d
        for j in range(4):
            col = (0, 1, 4, 5)[j] + 2 * d
            P.mm(pp[k][0:NT, j * 128:(j + 1) * 128], src[:, :, col], ident[:], reads=[sk, "ident"], writes=[("pp", k)])
        P.act(LP[:].rearrange("p a b -> p (a b)"), pp[k][0:NT, :], AF.Copy, reads=[("pp", k)], writes=["LP%d" % d])
        L = "LP%d" % d
        T1 = P.sb("T1_%d" % d, [NT, 2, 128], F32)
        T2 = P.sb("T2_%d" % d, [NT, 2, 128], F32)
        Wt = P.sb("W_%d" % d, [NT, 2, 128], F32)
        Ct = P.sb("C_%d" % d, [NT, 2, 128], F32)
        Mt = P.sb("M_%d" % d, [NT, 2, 128], F32)
        Et = P.sb("E_%d" % d, [NT, 2, 128], F32)
        n1, n2, nW, nC, nM, nE = ["%s_%d" % (s, d) for s in ("T1", "T2", "W", "C", "M", "E")]
        pf = LP[:, 2:4, :]
        li = LP[:, 0:2, :]
        P.act(T1[:], pf, AF.Abs, reads=[L], writes=[n1])
        P.act(T1[:], T1[:], AF.Exp, reads=[n1], writes=[n1], scale=-1.0)
        P.act(T1[:], T1[:], AF.Ln, reads=[n1], writes=[n1], bias=1.0)
        dve(P, lambda e: e.tensor_single_scalar(T2[:], pf, 0.0, ALU.min), [L], [n2])
        dve(P, lambda e: e.tensor_tensor(T2[:], T2[:], T1[:], ALU.subtract), [n1, n2], [n2])
        for hl in range(2):
            dve(P, lambda e, hl=hl: e.tensor_tensor_scan(Wt[:, hl, :], onesf[:], T2[:, hl, :], 0.0, ALU.mult,
                                                         ALU.add), [n2, "onesf"], [nW])
        r = P.sb("r_%d" % d, [2, NT], F32)
        rs = P.sb("rs_%d" % d, [2, NT], F32)
        tot = P.sb("tot_%d" % d, [2, 1], F32)
        car = P.sb("car_%d" % d, [NT, 2], F32)
        P.mm(pp[k][0:2, 0:NT], Wt[:, :, 127], ident[0:NT, 0:NT], reads=[nW, "ident"], writes=[("pp", k)])
        P.act(r[:], pp[k][0:2, 0:NT], AF.Copy, reads=[("pp", k)], writes=["r%d" % d])
        onesr = onesf[0:2, 0:NT]
        dve(P, lambda e: e.tensor_tensor_scan(rs[:], onesr, r[:], 0.0, ALU.mult, ALU.add), ["r%d" % d, "onesf"],
            ["rs%d" % d])
        if d == 0:
            dve(P, lambda e: e.tensor_tensor(rs[:], rs[:], r[:], ALU.subtract), ["rs%d" % d, "r%d" % d], ["rs%d" % d])
        else:
            dve(P, lambda e: e.tensor_copy(tot[:], rs[:, NT - 1:NT]), ["rs%d" % d], ["tot%d" % d])
            dve(P, lambda e: e.tensor_scalar(rs[:], rs[:], -1.0, tot[:, 0:1], ALU.mult, ALU.add),
                ["rs%d" % d, "tot%d" % d], ["rs%d" % d])
        P.mm(pp[k][0:NT, 0:2], rs[:], ident[0:2, 0:2], reads=["rs%d" % d, "ident"], writes=[("pp", k)])
        P.act(car[:], pp[k][0:NT, 0:2], AF.Copy, reads=[("pp", k)], writes=["car%d" % d])
        for hl in range(2):
            dve(P, lambda e, hl=hl: e.tensor_scalar(Wt[:, hl, :], Wt[:, hl, :], car[:, hl:hl + 1], None, ALU.add),
                [nW, "car%d" % d], [nW])
        dve(P, lambda e: e.tensor_tensor(Ct[:], li, Wt[:], ALU.subtract), [L, nW], [nC])
        for hl in range(2):
            dve(P, lambda e, hl=hl: e.tensor_tensor_scan(Mt[:, hl, :], Ct[:, hl, :], Ct[:, hl, :], -1e30, ALU.max,
                                                         ALU.max), [nC], [nM])
        mr = P.sb("mr_%d" % d, [2, NT], F32)
        mr2 = P.sb("mr2_%d" % d, [2, NT], F32)
        mx = P.sb("mx_%d" % d, [2, NT], F32)
        cmx = P.sb("cmx_%d" % d, [NT, 2], F32)
        P.mm(pp[k][0:2, 0:NT], Mt[:, :, 127], ident[0:NT, 0:NT], reads=[nM, "ident"], writes=[("pp", k)])
        P.act(mr[:], pp[k][0:2, 0:NT], AF.Copy, reads=[("pp", k)], writes=["mr%d" % d])
        dve(P, lambda e: e.memset(mx[:], -1e30), [], ["mx%d" % d])
        if NT > 1:
            if d == 0:
                dve(P, lambda e: e.tensor_tensor_scan(mr2[:], mr[:], mr[:], -1e30, ALU.max, ALU.max), ["mr%d" % d],
                    ["mr2%d" % d])
                dve(P, lambda e: e.tensor_copy(mx[:, 1:NT], mr2[:, 0:NT - 1]), ["mr2%d" % d, "mx%d" % d], ["mx%d" % d])
            else:
                cur, ck_, oth, ok_ = mr, "mr%d" % d, mr2, "mr2%d" % d
                s = 1
                while s < NT:
                    dve(P, lambda e, cur=cur, oth=oth, s=s: e.tensor_tensor(oth[:, 0:NT - s], cur[:, 0:NT - s],
                                                                        cur[:, s:NT], ALU.max), [ck_], [ok_])
                    dve(P, lambda e, cur=cur, oth=oth, s=s: e.tensor_copy(oth[:, NT - s:NT], cur[:, NT - s:NT]),
                        [ck_, ok_], [ok_])
                    cur, ck_, oth, ok_ = oth, ok_, cur, ck_
                    s *= 2
                dve(P, lambda e, cur=cur: e.tensor_copy(mx[:, 0:NT - 1], cur[:, 1:NT]), [ck_, "mx%d" % d], ["mx%d" % d])
        P.mm(pp[k][0:NT, 0:2], mx[:], ident[0:2, 0:2], reads=["mx%d" % d, "ident"], writes=[("pp", k)])
        P.act(cmx[:], pp[k][0:NT, 0:2], AF.Copy, reads=[("pp", k)], writes=["cmx%d" % d])
        for hl in range(2):
            dve(P, lambda e, hl=hl: e.tensor_scalar(Mt[:, hl, :], Mt[:, hl, :], cmx[:, hl:hl + 1], None, ALU.max),
                [nM, "cmx%d" % d], [nM])
        dve(P, lambda e: e.tensor_scalar(Mt[:], Mt[:], -1.0, 0.0, ALU.mult, ALU.min), [nM], [nM])
        dve(P, lambda e: e.tensor_tensor(Et[:], Mt[:], Wt[:], ALU.subtract), [nM, nW], [nE])
        P.act(Et[:], Et[:], AF.Exp, reads=[nE], writes=[nE])
        if d == 0:
            for j, (src_t, sn) in enumerate(((Ct, nC), (Ct, nC), (Et, nE), (Et, nE))):
                P.mm(pp[k][:, j * NT:(j + 1) * NT], src_t[:, j % 2, :], ident[0:NT, 0:NT], reads=[sn, "ident"],
                     writes=[("pp", k)])
            P.act(tok[0][:].rearrange("p a b -> p (a b)"), pp[k][:, 0:4 * NT], AF.Copy, reads=[("pp", k)],
                  writes=["tok0"])
            dve(P, lambda e: e.tensor_copy(A[0][:], Mt[:]), [nM], ["A0"])
        else:
            Yb = P.sb("Yb", [128, 6, NT], F32)
            for j, (src_t, sn) in enumerate(((Ct, nC), (Ct, nC), (Et, nE), (Et, nE), (Mt, nM), (Mt, nM))):
                P.mm(pp[k][:, j * NT:(j + 1) * NT], src_t[:, j % 2, :], ident[0:NT, 0:NT], reads=[sn, "ident"],
                     writes=[("pp", k)])
            P.act(Yb[:].rearrange("p a b -> p (a b)"), pp[k][:, 0:6 * NT], AF.Copy, reads=[("pp", k)], writes=["Yb"])
            P.mm(pp[k][:, 0:4 * NT], anti[:], Yb[:, 0:4, :].rearrange("p a b -> p (a b)"), reads=["anti", "Yb"],
                 writes=[("pp", k)])
            P.act(tok[1][:].rearrange("p a b -> p (a b)"), pp[k][:, 0:4 * NT], AF.Copy, reads=[("pp", k)],
                  writes=["tok1"])
            for hl in range(2):
                P.mm(pp[k][0:NT, hl * 128:(hl + 1) * 128], Yb[:, 4 + hl, :], anti[:], reads=["Yb", "anti"],
                     writes=[("pp", k)])
            P.act(A[1][:].rearrange("p a b -> p (a b)"), pp[k][0:NT, 0:256], AF.Copy, reads=[("pp", k)], writes=["A1"])


    for d in range(2):
        gate_dir(d)

    pa = pp[1]
    wg = [P.sb("wg%d" % i, [128, 512], F32) for i in range(2)]
    pt = [P.sb("pt%d" % i, [128, 512], BF16) for i in range(3)]
    ad = P.sb("ad", [128, 4], F32)
    hs = P.sb("hs", [128, 4, 128], F32)
    junk = P.sb("junk", [128, 128], F32)
    ssq = P.sb("ssq", [128, 2], F32)
    sgo = [P.sb("sgo%d" % i, [128, 128], F32) for i in range(2)]
    yt = [P.sb("yt%d" % i, [128, 4, 128], BF16) for i in range(2)]
    cnt = 0
    ycnt = 0
    pcnt = 0
    for hl in range(2):
        for g in range(NQ):
            qsl = slice(g * 512, (g + 1) * 512)
            for d in range(2):
                pob = po[pcnt % 2]
                pok = ("po", pcnt % 2)
                pcnt += 1
                for tl in range(4):
                    P.mm(pa[:, tl * 128:(tl + 1) * 128], sel2[0:NT, (4 * g + tl) * 128:(4 * g + tl + 1) * 128], A[d][:, hl, :], reads=["sel", "A%d" % d],
                         writes=[("pp", 1)])
                P.mm(pob[:].rearrange("p a b -> p (a b)"), zt[:, 0:128], zt[:, 0:512], start=True, stop=False,
                     reads=["zt"], writes=[pok])
                P.mm(pd[:, 0:4], zt[:, 0:128], zt[:, 0:4], start=True, stop=False, reads=["zt"], writes=["pd"])
                tiles = list(range(0, 4 * g + 4)) if d == 0 else list(range(4 * g, NT))
                for ti, t in enumerate(tiles):
                    last = (ti == len(tiles) - 1)
                    b = cnt % 2
                    b3 = cnt % 3
                    cnt += 1
                    tp = t - 4 * g
                    diag = 0 <= tp <= 3
                    P.mm(st[b][:], kz[hl][:, t * 128:(t + 1) * 128], qT[:, qsl], reads=["kz%d" % hl, ("qT", g)],
                         writes=[("st", b)])
                    if diag:
                        dve(P, lambda e, b=b, d=d, hl=hl, t=t: e.tensor_scalar(wg[b][:], pa[:], tok[d][:, hl, t:t + 1], 0.0,
                                                                         ALU.add, ALU.min),
                            [("pp", 1), "tok%d" % d], [("wg", b)])
                        P.act(wg[b][:], wg[b][:], AF.Exp, reads=[("wg", b)], writes=[("wg", b)])
                        dve(P, lambda e, b=b, d=d, tp=tp: e.tensor_tensor(wg[b][:], wg[b][:], mask[:, d, tp, :], ALU.mult),
                            [("wg", b), "mask"], [("wg", b)])
                    else:
                        P.act(wg[b][:], pa[:], AF.Exp, reads=[("pp", 1), "tok%d" % d], writes=[("wg", b)],
                              bias=tok[d][:, hl, t:t + 1])
                    dve(P, lambda e, b=b, b3=b3: e.tensor_tensor(pt[b3][:], st[b][:], wg[b][:], ALU.mult),
                        [("st", b), ("wg", b)], [("pt", b3)])
                    for qs in range(4):
                        if diag and ((d == 0 and qs < tp) or (d == 1 and qs > tp)):
                            continue
                        P.mm(pob[:, qs, :], pt[b3][:, qs * 128:(qs + 1) * 128], vt[:, t, hl * 128:(hl + 1) * 128],
                             start=False, stop=(last and qs == 3), reads=[("pt", b3), "vt"], writes=[pok])
                        P.mm(pd[:, qs:qs + 1], pt[b3][:, qs * 128:(qs + 1) * 128], ones[:, 0:1], start=False,
                             stop=(last and qs == 3), reads=[("pt", b3), "ones"], writes=["pd"])
                P.act(ad[:], pd[:, 0:4], AF.Abs, reads=["pd"], writes=["ad"])
                dve(P, lambda e, d=d, hl=hl, g=g: e.tensor_tensor(ad[:], ad[:], tok[d][:, 2 + hl, 4 * g:4 * g + 4],
                                                              ALU.max), ["ad", "tok%d" % d], ["ad"])
                dve(P, lambda e: e.reciprocal(ad[:], ad[:]), ["ad"], ["ad"])
                for qs in range(4):
                    if d == 0:
                        P.act(hs[:, qs, :], pob[:, qs, :], AF.Copy, reads=[pok, "ad"], writes=["hs"],
                              scale=ad[:, qs:qs + 1])
                    else:
                        dve(P, lambda e, qs=qs, pob=pob: e.scalar_tensor_tensor(hs[:, qs, :], pob[:, qs, :],
                                                                            ad[:, qs:qs + 1], hs[:, qs, :], ALU.mult,
                                                                            ALU.add), [pok, "ad", "hs"], ["hs"])
            yb = yt[ycnt % 2]
            ykey = ("yt", ycnt % 2)
            ycnt += 1
            for qs in range(4):
                k2 = qs % 2
                t = 4 * g + qs
                proj_tm(P, pp[0][:, 0:128], ("pp", 0), wbf, 520 + hl * 128, 128, xnT, t)
                P.act(sgo[k2][:], pp[0][:, 0:128], AF.Sigmoid, reads=[("pp", 0)], writes=[("sgo", k2)])
                P.op("pool", lambda e, k2=k2: e.tensor_tensor(sgo[k2][:], sgo[k2][:], gb[:], ALU.mult),
                     reads=[("sgo", k2), "gb"], writes=[("sgo", k2)])
                P.act(junk[:], hs[:, qs, :], AF.Square, reads=["hs"], writes=["junk", ("ssq", k2)],
                      accum_out=ssq[:, k2:k2 + 1])
                P.act(ssq[:, k2:k2 + 1], ssq[:, k2:k2 + 1], AF.Sqrt, reads=[("ssq", k2)], writes=[("ssq", k2)],
                      scale=1.0 / 128, bias=1e-6)
                dve(P, lambda e, k2=k2: e.reciprocal(ssq[:, k2:k2 + 1], ssq[:, k2:k2 + 1]), [("ssq", k2)],
                    [("ssq", k2)])
                dve(P, lambda e, qs=qs, k2=k2, yb=yb: e.scalar_tensor_tensor(yb[:, qs, :], hs[:, qs, :],
                                                                         ssq[:, k2:k2 + 1], sgo[k2][:], ALU.mult,
                                                                         ALU.mult),
                    ["hs", ("ssq", k2), ("sgo", k2)], [ykey])
            P.dma("sp", y[g * 512:(g + 1) * 512, hl * 128:(hl + 1) * 128].rearrange("(q p) e -> p q e", p=128), yb[:],
                  ykey, reads=[ykey], is_out=True)
    print("mlstm stats", P.stats())
    return (P.finish() if own else None)


def mlstm_inputs(xTb, g1l, w_in_l, gate_b_l, ng_l, hh, S=S):
    NT = S // 128
    offs = np.cumsum((0,) + (1536, 512, 16, 512, 512, 512, 512, 128, 128, 256, 256, 512, 16, 512))
    cq, ck, cv, cif, co = offs[9], offs[10], offs[11], offs[12], offs[13]
    hs_ = [2 * hh, 2 * hh + 1]
    qc = np.concatenate([cq + h * 64 + np.arange(64) for h in hs_])
    kc = np.concatenate([ck + h * 64 + np.arange(64) for h in hs_])
    vc = np.concatenate([cv + h * 128 + np.arange(128) for h in hs_])
    oc = np.concatenate([co + h * 128 + np.arange(128) for h in hs_])
    gsel = [(i_f, dr, h) for i_f in range(2) for dr in range(2) for h in hs_]
    gc = np.array([cif + i_f * 8 + dr * 4 + h for (i_f, dr, h) in gsel])
    gbv = np.array([gate_b_l[i_f, dr, h] for (i_f, dr, h) in gsel], np.float32)
    cols = np.concatenate([qc, kc, vc, gc, oc])
    assert len(cols) == D_NCOLS
    ident = np.eye(128, dtype=np.float32)
    anti = np.ascontiguousarray(ident[::-1])
    sel = np.zeros((NT, NT, 128), np.float32)
    for k in range(NT):
        sel[k, k, :] = 1.0
    j = np.arange(128)[:, None]
    i = np.arange(512)[None, :]
    mask = np.zeros((128, 2, 4, 512), np.float32)
    for tp in range(4):
        mask[:, 0, tp, :] = (128 * tp + j <= i)
        mask[:, 1, tp, :] = (128 * tp + j >= i)
    gbias = np.ascontiguousarray(np.tile(gbv[None, None, :], (128, NT, 1)).reshape(128, NT * 8))
    gb = np.ascontiguousarray(np.tile(ng_l[None, :], (128, 1)).astype(np.float32))
    return dict(xT=xTb, g1=g1l, w=wlayout(w_in_l, cols), ident=ident, anti=anti,
                sel=np.ascontiguousarray(sel.reshape(NT, NT * 128)),
                mask=np.ascontiguousarray(mask.reshape(128, -1)).astype(ml_dtypes.bfloat16), gbias=gbias, gb=gb)


A_NCOLS = 256 * 4 + 8
PTG = 256


def prologue_small(P, xT, g1, wsrc, ncols, S, wst=None):
    xnT = P.sb("xnT", [128, 8, S], BF16)
    wbf = P.sb("wbf", [128, 8, ncols], BF16)
    xs = P.sb("xs0", [128, 8, PTG], F32)
    sq = P.sb("sq", [128, 8, PTG], BF16)
    rstd = P.sb("rstd", [128, 512], F32)
    ones = P.sb("ones", [128, 128], BF16)
    g1s = P.sb("g1s", [128, 8], F32)
    pss = P.ps("pss", [128, 512], F32)
    P.op("pool", lambda e: e.memset(ones[:], 1.0), writes=["ones"])
    P.dma("sp", g1s[:], g1, "g1s", writes=["g1s"])
    if wst is None:
        wst = [P.sb("wst%d" % i, [128, ncols], F32)[:] for i in range(2)]
    for c in range(8):
        i = c % 2
        P.dma("sp", wst[i], wsrc[:, c * ncols:(c + 1) * ncols], ("wst", i), writes=[("wst", i)])
        P.op("pool", lambda e, i=i, c=c: e.tensor_copy(wbf[:, c, :], wst[i]), reads=[("wst", i)],
             writes=["wbf"])
    xn_src = (getattr(P, "override", None) or {}).get("xnT_src")
    if xn_src is not None:
        for g in range(S // TG):
            tsl = slice(g * TG, (g + 1) * TG)
            P.dma("sp", xnT[:, :, tsl], xn_src[:, :, tsl], ("xnld", g % 4), writes=[("xnT", g)])
        P.xs0 = xs
        return xnT, wbf, ones, pss, rstd
    for g in range(S // PTG):
        tsl = slice(g * PTG, (g + 1) * PTG)
        P.dma("sp", xs[:], xT[:, :, tsl], ("xs", 0), writes=[("xs", 0)])
        P.act(sq[:], xs[:], AF.Square, reads=[("xs", 0)], writes=["sq"])
        for c in range(8):
            P.mm(pss[:, 0:PTG], ones[:], sq[:, c, :], start=(c == 0), stop=(c == 7), reads=["ones", "sq"],
                 writes=["pss"])
        P.act(rstd[:, 0:PTG], pss[:, 0:PTG], AF.Sqrt, reads=["pss"], writes=["rstd"], scale=1.0 / 1024, bias=1e-6)
        P.op("dve", lambda e: e.reciprocal(rstd[:, 0:PTG], rstd[:, 0:PTG]), reads=["rstd"], writes=["rstd"])
        for c in range(8):
            P.op("dve", lambda e, c=c, tsl=tsl: e.scalar_tensor_tensor(
                xnT[:, c, tsl], xs[:, c, :], g1s[:, c:c + 1], rstd[:, 0:PTG], ALU.mult, ALU.mult),
                reads=[("xs", 0), "g1s", "rstd"], writes=[("xnT", (g * PTG) // TG)])
    P.xs0 = xs
    return xnT, wbf, ones, pss, rstd


def build_gdn(S=S, P=None):
    import os
    STOP = int(os.environ.get("GDN_STOP", "99"))
    PST = int(os.environ.get("GDN_PST", "99"))
    LST = int(os.environ.get("GDN_LST", "99"))
    VAR = int(os.environ.get("GDN_VAR", "0"))
    own = P is None
    if own:
        P = Prog()
    NT = S // 128
    NG = S // TG
    NC = S // 64
    xT = P.dram("xT", [128, 8, S], F32, "ExternalInput")
    g1 = P.dram("g1", [128, 8], F32, "ExternalInput")
    w = P.dram("w", [128, 8 * A_NCOLS], F32, "ExternalInput")
    convd = P.dram("convw", [128, 6, 5], F32, "ExternalInput")
    identd = P.dram("ident", [128, 128], F32, "ExternalInput")
    antid = P.dram("anti", [128, 128], F32, "ExternalInput")
    maskd = P.dram("mask", [128, 4 * 128], BF16, "ExternalInput")
    biasd = P.dram("gbias", [128, NT * 8], F32, "ExternalInput")
    alogd = P.dram("alog", [128, 4], F32, "ExternalInput")
    rmd = P.dram("rmask", [NT, 128], F32, "ExternalInput")
    gbd = P.dram("gb", [128, 128], F32, "ExternalInput")
    y = P.dram("y", [S, 256], BF16, "ExternalOutput")

    raw = P.sb("raw", [128, S + 4], F32)
    half = (S + 4) // 2
    xnT, wbf, ones, pss, rstd = prologue_small(P, xT, g1, w, A_NCOLS, S,
                                               wst=[raw[:, 0:A_NCOLS], raw[:, half:half + A_NCOLS]] if half >= A_NCOLS else None)
    ident = P.sb("ident", [128, 128], F32)
    anti = P.sb("anti", [128, 128], F32)
    identb = P.sb("identb", [128, 128], BF16)
    mask = P.sb("mask", [128, 4, 128], BF16)
    gbias = P.sb("gbias", [128, NT * 8], F32)
    nea = P.sb("nea", [128, 4], F32)
    rm = P.sb("rm", [NT, 128], F32)
    gb = P.sb("gb", [128, 128], F32)
    convw = P.sb("convw", [128, 6, 5], F32)
    P.dma("sp", ident[:], identd, "ident", writes=["ident"])
    P.dma("sp", anti[:], antid, "anti", writes=["anti"])
    P.dma("sp", mask[:].rearrange("p a b -> p (a b)"), maskd, "mask", writes=["mask"])
    P.dma("sp", gbias[:], biasd, "gbias", writes=["gbias"])
    P.dma("sp", nea[:], alogd, "nea", writes=["nea"])
    P.dma("sp", rm[:], rmd, "rm", writes=["rm"])
    P.dma("sp", gb[:], gbd, "gb", writes=["gb"])
    P.dma("sp", convw[:], convd, "convw", writes=["convw"])
    P.op("pool", lambda e: e.tensor_copy(identb[:], ident[:]), reads=["ident"], writes=["identb"])
    P.act(nea[:], nea[:], AF.Exp, reads=["nea"], writes=["nea"])
    P.op("dve", lambda e: e.tensor_scalar(nea[:], nea[:], -1.0, None, ALU.mult), reads=["nea"], writes=["nea"])
    onesf = P.sb("onesf", [NT, 128], F32)
    P.op("pool", lambda e: e.memset(onesf[:], 1.0), writes=["onesf"])

    pp = [P.ps("pp%d" % i, [128, 512], F32) for i in range(2)]
    ptr = P.ps("ptr", [128, 4, 128], BF16)

    qT = [P.sb("qT%d" % h, [128, S], BF16) for h in range(2)]
    kT = [P.sb("kT%d" % h, [128, S], BF16) for h in range(2)]
    vtok = [P.sb("vtok%d" % h, [128, NT, 128], BF16) for h in range(2)]
    sz = P.sb("sz", [128, NT, 256], BF16)
    xsf = P.xs0[:].rearrange("p a b -> p (a b)")
    acc = [xsf[:, i * TG:(i + 1) * TG] for i in range(2)]
    sil = [xsf[:, (2 + i) * TG:(3 + i) * TG] for i in range(2)]
    P.op("pool", lambda e: e.memset(xsf[:, 0:4 * TG], 0.0),
         writes=[("xs", 0), ("acc", 0), ("acc", 1), ("sil", 0), ("sil", 1)])
    sqg = P.sb("sqg", [128, TG], BF16)
    vTg = P.sb("vTg", [128, TG], BF16)
    P.op("pool", lambda e: e.memset(raw[:, 0:2], 0.0), reads=["wbf"], writes=["rawpad"])
    P.op("pool", lambda e: e.memset(raw[:, S + 2:S + 4], 0.0), reads=["wbf"], writes=["rawpad"])
    allraw = [("raw", g) for g in range(NG)]
    for ci in range(6):
        kind, hl = ci // 2, ci % 2
        for g in range(NG):
            k = g % 2
            proj_fm(P, pp[k], ("pp", k), wbf, ci * 128, 128, xnT, g)
            P.act(raw[:, 2 + g * TG:2 + (g + 1) * TG], pp[k][:], AF.Copy, reads=[("pp", k)], writes=[("raw", g)])
        for g in range(NG):
            k = g % 2
            a_, s_ = acc[k], sil[k]
            nb = [("raw", gg) for gg in (g - 1, g, g + 1) if 0 <= gg < NG] + ["rawpad", "convw"]
            base = g * TG
            P.op("dve", lambda e, a_=a_, base=base, ci=ci: e.tensor_scalar(a_, raw[:, base:base + TG],
                                                                        convw[:, ci, 0:1], None, ALU.mult),
                 reads=nb, writes=[("acc", k)])
            for tap in range(1, 5):
                P.op("dve", lambda e, a_=a_, base=base, ci=ci, tap=tap: e.scalar_tensor_tensor(
                    a_, raw[:, base + tap:base + tap + TG], convw[:, ci, tap:tap + 1], a_, ALU.mult, ALU.add),
                    reads=nb + [("acc", k)], writes=[("acc", k)])
            if kind < 2:
                P.act(s_, a_, AF.Silu, reads=[("acc", k)], writes=[("sil", k)])
                P.op("pool", lambda e, s_=s_: e.tensor_tensor(sqg[:], s_, s_, ALU.mult), reads=[("sil", k)],
                     writes=["sqg"])
                P.mm(pss[:], ones[:], sqg[:], reads=["ones", "sqg"], writes=["pss"])
                P.act(rstd[:], pss[:], AF.Sqrt, reads=["pss"], writes=["rstd"], bias=1e-6)
                P.op("dve", lambda e: e.reciprocal(rstd[:], rstd[:]), reads=["rstd"], writes=["rstd"])
                dst = (qT if kind == 0 else kT)[hl]
                dkey = ("qT%d" % hl if kind == 0 else "kT%d" % hl, g)
                sc = (128 ** -0.5) if kind == 0 else 1.0
                P.op("dve", lambda e, s_=s_, dst=dst, g=g, sc=sc: e.scalar_tensor_tensor(
                    dst[:, g * TG:(g + 1) * TG], s_, sc, rstd[:], ALU.mult, ALU.mult),
                    reads=[("sil", k), "rstd"], writes=[dkey])
            else:
                P.act(vTg[:], a_, AF.Silu, reads=[("acc", k)], writes=["vTg"])
                for j in range(4):
                    P.tr(ptr[:, j, :], vTg[:, j * 128:(j + 1) * 128], identb[:], reads=["vTg", "identb"],
                         writes=["ptr"])
                P.op("pool" if False else "act", lambda e, hl=hl, g=g: e.copy(
                    vtok[hl][:, 4 * g:4 * g + 4, :], ptr[:]), reads=["ptr"], writes=["vtok%d" % hl])
    if STOP <= 1:
        return (P.finish() if own else None)
    for t in range(NT):
        k = t % 2
        proj_tm(P, pp[k][:, 0:256], ("pp", k), wbf, 768, 256, xnT, t)
        P.act(sz[:, t, :], pp[k][:, 0:256], AF.Silu, reads=[("pp", k)], writes=["sz"])
    for t in range(NT):
        proj_tm(P, pp[0][:, t * 8:(t + 1) * 8], ("pp", 0), wbf, 1024, 8, xnT, t)
    gtok = P.sb("gtok", [128, NT, 8], F32)
    gtokR = P.sb("gtokR", [128, NT, 8], F32)
    dve(P, lambda e: e.tensor_tensor(gtok[:].rearrange("p a b -> p (a b)"), pp[0][:, 0:NT * 8], gbias[:], ALU.add),
        [("pp", 0), "gbias"], ["gtok"])
    P.mm(pp[1][:, 0:NT * 8], anti[:], gtok[:].rearrange("p a b -> p (a b)"), reads=["anti", "gtok"],
         writes=[("pp", 1)])
    P.act(gtokR[:].rearrange("p a b -> p (a b)"), pp[1][:, 0:NT * 8], AF.Copy, reads=[("pp", 1)], writes=["gtokR"])

    if STOP <= 2:
        return (P.finish() if own else None)
    tokq = [P.sb("tokq%d" % d, [128, 2, 6, NT], F32) for d in range(2)]
    Gc = [P.sb("Gc%d" % d, [NT, 2, 128], F32) for d in range(2)]
    egt = [P.sb("egt%d" % d, [128, 2, 2, NT], F32) for d in range(2)]

    LPs = P.sb("LPs", [NT, 4, 128], F32)
    gtmp = [P.sb("gtmp%d" % i, [NT, 2, 128], F32) for i in range(5)]
    TBs = P.sb("TBs", [NT, 128], F32)
    Yb = P.sb("Yb", [128, 8, NT], F32)

    def gate_dir(d):
        src = gtok if d == 0 else gtokR
        sk = "gtok" if d == 0 else "gtokR"
        k = d
        LP = LPs
        L = "LP"
        for j in range(4):
            col = (0, 1, 4, 5)[j] + 2 * d
            P.mm(pp[k][0:NT, j * 128:(j + 1) * 128], src[:, :, col], ident[:], reads=[sk, "ident"], writes=[("pp", k)])
        P.act(LP[:].rearrange("p a b -> p (a b)"), pp[k][0:NT, :], AF.Copy, reads=[("pp", k)], writes=[L])
        T1, Gt, Bt, Et, Rt = gtmp
        n1, nG, nB, nE, nR = ["%s_s" % s for s in ("T1", "G", "B", "E", "R")]
        al = LP[:, 0:2, :]
        be = LP[:, 2:4, :]
        P.act(T1[:], al, AF.Abs, reads=[L], writes=[n1])
        P.act(T1[:], T1[:], AF.Exp, reads=[n1], writes=[n1], scale=-1.0)
        P.act(T1[:], T1[:], AF.Ln, reads=[n1], writes=[n1], bias=1.0)
        dve(P, lambda e: e.tensor_single_scalar(Gt[:], al, 0.0, ALU.max), [L], [nG])
        dve(P, lambda e: e.tensor_tensor(Gt[:], Gt[:], T1[:], ALU.add), [nG, n1], [nG])
        for hl in range(2):
            dve(P, lambda e, hl=hl: e.tensor_scalar(Gt[:, hl, :], Gt[:, hl, :], nea[0:NT, 2 * d + hl:2 * d + hl + 1],
                                                    None, ALU.mult), [nG, "nea"], [nG])
        P.act(Bt[:], be, AF.Sigmoid, reads=[L], writes=[nB])
        for hl in range(2):
            dve(P, lambda e, hl=hl: e.tensor_tensor_scan(T1[:, hl, :], rm[:], Gt[:, hl, :], 0.0, ALU.mult, ALU.add),
                [nG, "rm", n1], [n1])
        for hl in range(2):
            for hf in range(2):
                dve(P, lambda e, hl=hl, hf=hf: e.tensor_scalar(
                    Rt[:, hl, hf * 64:(hf + 1) * 64], T1[:, hl, hf * 64:(hf + 1) * 64], -1.0,
                    T1[:, hl, hf * 64 + 63:hf * 64 + 64], ALU.mult, ALU.add), [n1], [nR])
        P.act(Et[:], T1[:], AF.Exp, reads=[n1], writes=[nE])
        P.act(Rt[:], Rt[:], AF.Exp, reads=[nR], writes=[nR])
        TB = TBs
        for hl in range(2):
            for hf in range(2):
                ah = hf if d == 0 else 1 - hf
                dve(P, lambda e, hl=hl, hf=hf: e.tensor_scalar(TB[:], onesf[:], T1[:, hl, hf * 64 + 63:hf * 64 + 64],
                                                           None, ALU.mult), [n1, "onesf", ("pp", k)], ["TBs"])
                P.mm(pp[k][:, (hl * 2 + ah) * NT:(hl * 2 + ah + 1) * NT], TB[:], ident[0:NT, 0:NT],
                     reads=["TBs", "ident"], writes=[("pp", k)])
        P.act(egt[d][:].rearrange("p a b c -> p (a b c)"), pp[k][:, 0:4 * NT], AF.Exp, reads=[("pp", k)],
              writes=["egt%d" % d])
        srcs = ((T1, n1), (Bt, nB), (Et, nE), (Rt, nR))
        if d == 0:
            for hl in range(2):
                for qi, (tt, nn) in enumerate(srcs):
                    P.mm(pp[k][:, (hl * 4 + qi) * NT:(hl * 4 + qi + 1) * NT], tt[:, hl, :], ident[0:NT, 0:NT],
                         reads=[nn, "ident"], writes=[("pp", k)])
            for hl in range(2):
                P.act(tokq[0][:, hl, 0:4, :].rearrange("p a b -> p (a b)"), pp[k][:, hl * 4 * NT:(hl + 1) * 4 * NT],
                      AF.Copy, reads=[("pp", k)], writes=["tokq0"])
            dve(P, lambda e: e.tensor_copy(Gc[0][:], T1[:]), [n1], ["Gc0"])
        else:
            for hl in range(2):
                for qi, (tt, nn) in enumerate(srcs):
                    P.mm(pp[k][:, (hl * 4 + qi) * NT:(hl * 4 + qi + 1) * NT], tt[:, hl, :], ident[0:NT, 0:NT],
                         reads=[nn, "ident"], writes=[("pp", k)])
            P.act(Yb[:].rearrange("p a b -> p (a b)"), pp[k][:, 0:8 * NT], AF.Copy, reads=[("pp", k)], writes=["Yb"])
            P.mm(pp[k][:, 0:8 * NT], anti[:], Yb[:].rearrange("p a b -> p (a b)"), reads=["anti", "Yb"],
                 writes=[("pp", k)])
            for hl in range(2):
                P.act(tokq[1][:, hl, 0:4, :].rearrange("p a b -> p (a b)"), pp[k][:, hl * 4 * NT:(hl + 1) * 4 * NT],
                      AF.Copy, reads=[("pp", k)], writes=["tokq1"])
            for hl in range(2):
                P.mm(pp[k][0:NT, hl * 128:(hl + 1) * 128], Yb[:, hl * 4 + 0, :], anti[:], reads=["Yb", "anti"],
                     writes=[("pp", k)])
            P.act(Gc[1][:].rearrange("p a b -> p (a b)"), pp[k][0:NT, 0:256], AF.Copy, reads=[("pp", k)],
                  writes=["Gc1"])
        for hl in range(2):
            dve(P, lambda e, hl=hl: e.tensor_scalar(tokq[d][:, hl, 4, :], tokq[d][:, hl, 0, :], -1.0, None, ALU.mult),
                ["tokq%d" % d], ["tokq%d" % d])
            dve(P, lambda e, hl=hl: e.tensor_scalar(tokq[d][:, hl, 5, :], tokq[d][:, hl, 2, :], -1.0, None, ALU.mult),
                ["tokq%d" % d], ["tokq%d" % d])

    for d in range(2):
        gate_dir(d)
        if STOP <= 3 + d:
            return (P.finish() if own else None)

    XK = [("xnT", g) for g in range(NG)]
    slot = lambda i: xnT[:, i, :].rearrange("p (t c) -> p t c", c=128)
    TI = [slot(0), slot(1)]
    QK = [slot(2), slot(3)]
    KD = [slot(4), slot(5)]
    pw = pp[0]
    pn = pp[1]
    cA = [P.ps("cA%d" % i, [128, 4, 128], F32) for i in range(2)]
    cSb = [P.ps("cS%d" % i, [128, 128], F32) for i in range(2)]
    Dm = [P.sb("Dm%d" % i, [128, 128], F32) for i in range(2)]
    DT = [P.sb("DT%d" % i, [128, 128], F32) for i in range(2)]
    A0 = [P.sb("A0_%d" % i, [128, 128], BF16) for i in range(2)]
    IA = [P.sb("IA_%d" % i, [128, 128], BF16) for i in range(2)]
    AK = [P.sb("AK_%d" % i, [128, 128], BF16) for i in range(2)]
    NK = [P.sb("NK_%d" % i, [128, 128], BF16) for i in range(2)]
    PT = [P.sb("PT_%d" % i, [128, 128], BF16) for i in range(2)]
    oacc = raw[:, 0:S].rearrange("p (t c) -> p t c", c=128)
    gsel = [P.sb("gsel%d" % i, [NT, 128], F32) for i in range(2)]
    St = [P.sb("St%d" % i, [128, 128], BF16) for i in range(2)]
    Xp = [[P.sb("Xp%d_%d" % (i, hf), [128, 128], BF16) for hf in range(2)] for i in range(2)]
    Vn = [[P.sb("Vn%d_%d" % (i, hf), [128, 128], BF16) for hf in range(2)] for i in range(2)]
    otmp = [P.sb("otmp%d" % i, [128, 128], F32) for i in range(2)]
    junk_ = rstd[:, 256:384]
    ssq = P.sb("ssq", [128, 2], F32)
    gm = [rstd[:, i * 128:(i + 1) * 128] for i in range(2)]
    P.op("pool", lambda e: e.memset(rstd[:, 0:384], 0.0), writes=["rstd", ("gm", 0), ("gm", 1), "junk"])
    yt = [P.sb("yt%d" % i, [128, 128], BF16) for i in range(2)]
    for i in range(2):
        for hf in range(2):
            P.op("pool", lambda e, i=i, hf=hf: e.memset(Xp[i][hf][:], 0.0), writes=[("Xp", i, hf)])
            P.op("pool", lambda e, i=i, hf=hf: e.memset(Vn[i][hf][:], 0.0), writes=[("Vn", i, hf)])

    cAf = [cA[i][:].rearrange("p a b -> p (a b)") for i in range(2)]
    PW = [pp[0][:], cAf[0]]
    PN = [pp[1][:], cAf[1]]

    def precompute(hl, d, t, b):
        tsl = slice(t * 128, (t + 1) * 128)
        pw, pn = PW[b], PN[b]
        pwk, pnk = (("pp", 0), ("pp", 1)) if b == 0 else (("cA", 0), ("cA", 1))
        pwr, pnr, ptrr = ("pw_rd", b), ("pn_rd", b), "ptr_rd"
        ptk = "ptr"
        tq = tokq[d]
        gc_p, be_p, eg_p, er_p, ngc_p, neg_p = [tq[:, hl, qi, t:t + 1] for qi in range(6)]
        dve(P, lambda e: e.tensor_scalar(gsel[b][:], Gc[d][:, hl, :], ident[0:NT, t:t + 1], None, ALU.mult),
            ["Gc%d" % d, "ident"], [("gsel", b)])
        P.mm(pw[:, 0:128], kT[hl][:, tsl], kT[hl][:, tsl], reads=[("kT%d" % hl, t // 4)], writes=[pwk])
        P.mm(pw[:, 128:256], kT[hl][:, tsl], qT[hl][:, tsl], reads=[("kT%d" % hl, t // 4), ("qT%d" % hl, t // 4)],
             writes=[pwk])
        P.mm(pw[:, 256:384], onesf[:], gsel[b][:], reads=["onesf", ("gsel", b)], writes=[pwk])
        yield
        P.act(Dm[b][:], pw[:, 256:384], AF.Abs, reads=[pwk, "tokq%d" % d], writes=[("Dm", b), pwr], scale=-1.0,
              bias=gc_p)
        P.act(DT[b][:], pw[:, 256:384], AF.Abs, reads=[pwk, "tokq%d" % d], writes=[("DT", b), pwr], bias=ngc_p)
        P.act(Dm[b][:], Dm[b][:], AF.Exp, reads=[("Dm", b)], writes=[("Dm", b)], scale=-1.0)
        P.act(DT[b][:], DT[b][:], AF.Exp, reads=[("DT", b)], writes=[("DT", b)], scale=-1.0)
        yield
        dve(P, lambda e: e.tensor_tensor(Dm[b][:], Dm[b][:], mask[:, d, :], ALU.mult), [("Dm", b), "mask"], [("Dm", b)])
        dve(P, lambda e: e.tensor_tensor(DT[b][:], DT[b][:], mask[:, 2 + d, :], ALU.mult), [("DT", b), "mask"],
            [("DT", b)])
        dve(P, lambda e: e.scalar_tensor_tensor(A0[b][:], pw[:, 0:128], be_p, Dm[b][:], ALU.mult, ALU.mult),
            [pwk, ("Dm", b), "tokq%d" % d], [("A0", b), pwr])
        dve(P, lambda e: e.tensor_tensor(QK[d][:, t, :], pw[:, 128:256], DT[b][:], ALU.mult),
            [pwk, ("DT", b)], XK + [("QK", d), pwr])
        yield
        P.tr(ptr[:, 2 * b, :], A0[b][:], identb[:], reads=[("A0", b), "identb"], writes=[ptk])
        yield
        P.act(NK[b][:], ptr[:, 2 * b, :], AF.Copy, reads=[ptk], writes=[("NK", b), ptrr])
        dve(P, lambda e: e.tensor_tensor(PT[b][:], identb[:], ptr[:, 2 * b, :], ALU.subtract), [ptk, "identb"],
            [("PT", b), ptrr])
        yield
        cur_a = A0[b]
        ck = ("A0", b)
        for lev in range(5):
            P.mm(pn[:, 0:128], NK[b][:], cur_a[:], reads=[("NK", b), ck], writes=[pnk])
            if lev < 4:
                P.mm(pn[:, 128:256], cur_a[:], NK[b][:], reads=[("NK", b), ck], writes=[pnk])
            yield
            dve(P, lambda e: e.tensor_tensor(IA[b][:], pn[:, 0:128], ident[:], ALU.add), [pnk, "ident"],
                [("IA", b), pnr])
            if lev < 4:
                P.act(AK[b][:], pn[:, 0:128], AF.Copy, reads=[pnk], writes=[("AK", b), pnr])
                P.act(NK[b][:], pn[:, 128:256], AF.Copy, reads=[pnk], writes=[("NK", b), pnr])
                cur_a = AK[b]
                ck = ("AK", b)
            yield
            P.mm(pn[:, 256:384], IA[b][:], PT[b][:], reads=[("IA", b), ("PT", b)], writes=[pnk])
            yield
            if lev < 4:
                dve(P, lambda e: e.tensor_copy(PT[b][:], pn[:, 256:384]), [pnk], [("PT", b), pnr])
            else:
                P.act(TI[d][:, t, :], pn[:, 256:384], AF.Copy, reads=[pnk, "tokq%d" % d],
                      writes=XK + [("TI", d), pnr], scale=be_p)
            yield
        P.tr(ptr[:, 2 * b + 1, :], kT[hl][:, tsl], identb[:], reads=[("kT%d" % hl, t // 4), "identb"], writes=[ptk])
        yield
        P.act(KD[d][:, t, :], ptr[:, 2 * b + 1, :], AF.Copy, reads=[ptk, "tokq%d" % d], writes=XK + [("KD", d), ptrr],
              scale=er_p)

    def run_interleaved(gens):
        active = list(gens)
        while active:
            for g_ in list(active):
                try:
                    next(g_)
                except StopIteration:
                    active.remove(g_)

    def chunk_step(hl, d, c):
        t, hf = c // 2, c % 2
        R = slice(64 * hf, 64 * hf + 64)
        tsl = slice(t * 128, (t + 1) * 128)
        S_ = St[d]
        sk = ("St", d)
        ca = cA[d]
        cak = ("cA", d)
        tq = tokq[d]
        P.mm(ca[:, 0, :], kT[hl][:, tsl], S_[:], reads=[("kT%d" % hl, t // 4), sk], writes=[cak])
        P.mm(ca[:, 1, :], qT[hl][:, tsl], S_[:], reads=[("qT%d" % hl, t // 4), sk], writes=[cak])
        X = Xp[d][hf]
        dve(P, lambda e: e.scalar_tensor_tensor(X[R, :], ca[R, 0, :], tq[R, hl, 5, t:t + 1], vtok[hl][R, t, :],
                                                ALU.mult, ALU.add),
            [cak, "tokq%d" % d, "vtok%d" % hl], [("Xp", d, hf), ("ca_rd", d)])
        P.mm(ca[:, 2, :], TI[d][:, t, :], X[:], reads=[("TI", d), ("Xp", d, hf)], writes=[cak])
        V = Vn[d][hf]
        P.act(V[R, :], ca[R, 2, :], AF.Copy, reads=[cak], writes=[("Vn", d, hf), ("ca_rd", d)])
        P.mm(cSb[d][:], KD[d][:, t, :], V[:], reads=[("KD", d), ("Vn", d, hf)], writes=[("cS", d)])
        P.mm(ca[:, 3, :], QK[d][:, t, :], V[:], reads=[("QK", d), ("Vn", d, hf)], writes=[cak])
        dve(P, lambda e: e.scalar_tensor_tensor(S_[:], S_[:], egt[d][:, hl, hf, t:t + 1], cSb[d][:], ALU.mult,
                                                ALU.add), [sk, ("cS", d), "egt%d" % d], [sk])
        ot = otmp[d]
        P.act(ot[R, :], ca[R, 1, :], AF.Copy, reads=[cak, "tokq%d" % d], writes=[("otmp", d), ("ca_rd", d)],
              scale=tq[R, hl, 2, t:t + 1])
        P.op("dve", lambda e: e.tensor_tensor(ot[R, :], ca[R, 3, :], ot[R, :], ALU.add),
             reads=[cak, ("otmp", d)], writes=[("otmp", d), ("ca_rd", d)])
        P.op("pool", lambda e: e.tensor_tensor(oacc[R, t, :], oacc[R, t, :], ot[R, :], ALU.add),
             reads=[("otmp", d), ("oacc", t)], writes=[("oacc", t)])

    ycnt = 0
    for hl in range(2):
        for d in range(2):
            for t in range(0, NT, 2):
                run_interleaved([precompute(hl, d, t + i, i) for i in range(2) if t + i < NT])
            P.op("pool", lambda e, d=d: e.memset(St[d][:], 0.0), writes=[("St", d)])
        P.op("pool", lambda e: e.memset(raw[:, 0:S], 0.0),
             writes=[("oacc", t) for t in range(NT)] + allraw + ["rawpad"])
        for step in range(NC):
            chunk_step(hl, 0, step)
            chunk_step(hl, 1, NC - 1 - step)
        if STOP <= 6:
            return (P.finish() if own else None)
        for t in range(NT):
            k2 = t % 2
            P.act(junk_, oacc[:, t, :], AF.Square, reads=[("oacc", t)], writes=["junk", ("ssq", k2)],
                  accum_out=ssq[:, k2:k2 + 1])
            P.act(ssq[:, k2:k2 + 1], ssq[:, k2:k2 + 1], AF.Sqrt, reads=[("ssq", k2)], writes=[("ssq", k2)],
                  scale=1.0 / 128, bias=1e-6)
            dve(P, lambda e, k2=k2: e.reciprocal(ssq[:, k2:k2 + 1], ssq[:, k2:k2 + 1]), [("ssq", k2)], [("ssq", k2)])
            P.op("pool", lambda e, k2=k2, t=t, hl=hl: e.tensor_tensor(gm[k2], sz[:, t, hl * 128:(hl + 1) * 128], gb[:],
                                                                  ALU.mult), reads=["sz", "gb"], writes=[("gm", k2)])
            dve(P, lambda e, k2=k2, t=t: e.scalar_tensor_tensor(yt[k2][:], oacc[:, t, :], ssq[:, k2:k2 + 1], gm[k2],
                                                                ALU.mult, ALU.mult),
                [("oacc", t), ("ssq", k2), ("gm", k2)], [("yt", k2)])
            P.dma("sp", y[t * 128:(t + 1) * 128, hl * 128:(hl + 1) * 128], yt[k2][:], ("yt", k2), reads=[("yt", k2)],
                  is_out=True)
    print("gdn stats", P.stats())
    return (P.finish() if own else None)


def gdn_inputs(xTb, g1l, w_in_l, conv_l, alog_l, dtb_l, ng_l, hh, S=S):
    NT = S // 128
    offs = np.cumsum((0,) + (1536, 512, 16, 512, 512, 512, 512, 128, 128, 256, 256, 512, 16, 512))
    cqkv, cz, cab = offs[0], offs[1], offs[2]
    hs_ = [2 * hh, 2 * hh + 1]
    chunks = []
    for kind in range(3):
        for h in hs_:
            chunks.append(cqkv + kind * 512 + h * 128 + np.arange(128))
    qkvc = np.concatenate(chunks)
    zc = np.concatenate([cz + h * 128 + np.arange(128) for h in hs_])
    gsel = [(ab, dr, h) for ab in range(2) for dr in range(2) for h in hs_]
    gc = np.array([cab + ab * 8 + dr * 4 + h for (ab, dr, h) in gsel])
    gbv = np.array([dtb_l[dr, h] if ab == 0 else 0.0 for (ab, dr, h) in gsel], np.float32)
    cols = np.concatenate([qkvc, zc, gc])
    assert len(cols) == A_NCOLS
    convw = np.ascontiguousarray(np.stack([conv_l[:, c - cqkv].T for c in chunks], axis=1).astype(np.float32))
    ident = np.eye(128, dtype=np.float32)
    anti = np.ascontiguousarray(ident[::-1])
    sel = np.zeros((NT, NT, 128), np.float32)
    for k in range(NT):
        sel[k, k, :] = 1.0
    p = np.arange(128)[:, None]
    f = np.arange(128)[None, :]
    same = (p // 64) == (f // 64)
    MA0 = same & (f < p)
    MA1 = same & (f > p)
    MQ0 = same & (p <= f)
    MQ1 = same & (p >= f)
    mask = np.stack([MA0, MA1, MQ0, MQ1], axis=1).astype(np.float32).reshape(128, 4 * 128)
    gbias = np.ascontiguousarray(np.tile(gbv[None, None, :], (128, NT, 1)).reshape(128, NT * 8))
    alog = np.ascontiguousarray(np.tile(np.array([alog_l[dr, h] for dr in range(2) for h in hs_], np.float32)[None],
                                        (128, 1)))
    rmask = np.ones((NT, 128), np.float32)
    rmask[:, 0] = 0.0
    rmask[:, 64] = 0.0
    gb = np.ascontiguousarray(np.tile(ng_l[None, :], (128, 1)).astype(np.float32))
    return dict(xT=xTb, g1=g1l, w=wlayout(w_in_l, cols), convw=convw, ident=ident, anti=anti,
                mask=mask.astype(ml_dtypes.bfloat16),
                gbias=gbias, alog=alog, rmask=rmask, gb=gb)


class XSrc:
    def __init__(self, fn):
        self.fn = fn

    def __getitem__(self, idx):
        tsl = idx[2]
        return self.fn(tsl.start, tsl.stop)


class RowChunks:
    def __init__(self, chunks, rows_per, col0=0, ncols=None, rowmap=None):
        self.chunks, self.rows_per, self.col0, self.ncols, self.rowmap = chunks, rows_per, col0, ncols, rowmap

    def __getitem__(self, idx):
        rs, cs = idx
        a, b_ = rs.start, rs.stop
        c0 = self.col0 + (cs.start or 0)
        c1 = self.col0 + (cs.stop if cs.stop is not None else self.ncols)
        ci, off = self.rowmap(a) if self.rowmap else (a // self.rows_per, a % self.rows_per)
        return self.chunks[ci][off:off + (b_ - a), c0:c1]


def build_fused(S_=S, depth=2):
    import math
    P = Prog()
    H = S_ // 2
    x_full = P.dram("x_full", [128, 8, S_], F32, "ExternalInput")
    x_half = P.dram("x_half", [128, 8, H], F32, "ExternalInput")
    out = P.dram("out", [128, 8, H], F32, "ExternalOutput")
    YR = 1024
    NYC = S_ // YR
    ymine = [P.scratch("ymine%d" % c, [YR, 1024], BF16) for c in range(NYC)]
    ypair = [P.scratch("ypair%d" % c, [2 * YR, 1024], BF16) for c in range(NYC)]
    NXC = H // 512
    xh = [P.scratch("xh%d" % c, [1024, 512], F32) for c in range(NXC)]
    xpair = [P.scratch("xpair%d" % c, [2048, 512], F32) for c in range(NXC)]
    RG = [[0, 1], [2, 3], [4, 5], [6, 7]]
    xn_scr = P.scratch("xn_scr", [128, 8, S_], BF16)

    def xpair_view(a, b):
        r, tg, off = a // H, (a % H) // 512, a % 512
        assert off + (b - a) <= 512
        return xpair[tg][r * 1024:(r + 1) * 1024, off:off + (b - a)].rearrange("(p c) t -> p c t", c=8)

    def xh_view(a, b):
        tg, off = a // 512, a % 512
        assert off + (b - a) <= 512
        return xh[tg][:, off:off + (b - a)].rearrange("(p c) t -> p c t", c=8)

    def ypair_rowmap(row):
        r, T = row // S_, row % S_
        return T // YR, r * YR + (T % YR)

    stats = []
    for l in range(depth):
        lam_init = 0.8 - 0.6 * math.exp(-0.3 * l)
        xsrc = x_full if l == 0 else XSrc(xpair_view)
        P.pre = "l%d_xn_" % l
        P.override = {}
        g1d = P.dram("g1", [128, 8], F32, "ExternalInput")
        xnT_, _, _ = prologue(P, xsrc, g1d, None, 0, S_)
        for c in range(8):
            P.dma("sp", xn_scr[:, c, :], xnT_[:, c, :], ("xnst", c % 4), reads=[("xnT", g) for g in range(S_ // TG)])
        stats.append((P.pre, P.end_phase()))
        for n, (nm, bld) in enumerate((("gdn", lambda: build_gdn(S_, P=P)), ("diff", lambda: build_diff(lam_init, S_, P=P)),
                                       ("swa", lambda: build_swa(S_, P=P)), ("mlstm", lambda: build_mlstm(S_, P=P)))):
            P.pre = "l%d_%s_" % (l, nm)
            P.override = {"xT": xsrc, "y": RowChunks(ymine, YR, col0=n * 256, ncols=256), "xnT_src": xn_scr}
            bld()
            stats.append((P.pre, P.end_phase()))
        for c in range(NYC):
            P.op("pool", lambda e, c=c: e.collective_compute("AllGather", ALU.bypass, replica_groups=RG,
                                                             ins=[ymine[c].opt()], outs=[ypair[c].opt()]),
                 dma=("cc_y", c), dma_inc=1)
        P.end_phase()
        final = (l == depth - 1)
        P.pre = "l%d_dense_" % l
        P.override = {"xT": (x_half if l == 0 else XSrc(xh_view)), "ypair": RowChunks(ypair, YR, ncols=1024, rowmap=ypair_rowmap),
                      "out": (out if final else XSrc(xh_view))}
        build_dense(H, final=final, P=P)
        stats.append((P.pre, P.end_phase()))
        if not final:
            for c in range(NXC):
                P.op("pool", lambda e, c=c: e.collective_compute("AllGather", ALU.bypass, replica_groups=RG,
                                                                 ins=[xh[c].opt()], outs=[xpair[c].opt()]),
                     dma=("cc_x", c), dma_inc=1)
            P.end_phase()
    P.pre = ""
    P.override = {}
    for s_ in stats:
        print(s_)
    return P.finish()


_NC = {}


def kernel(x, norm1_g, w_in, gdn_conv_w, gdn_a_log, gdn_dt_bias, gdn_norm_g, diff_lambda, diff_norm_g, swa_sink,
           mlstm_gate_b, mlstm_norm_g, w_branch, w_gate, w_out, norm2_g, w_mlp1, w_mlp2, final_norm_g):
    f32 = lambda a: np.asarray(a, dtype=np.float32)
    x = f32(x)
    norm1_g, w_in, gdn_conv_w, gdn_a_log, gdn_dt_bias, gdn_norm_g = map(f32, (norm1_g, w_in, gdn_conv_w, gdn_a_log,
                                                                            gdn_dt_bias, gdn_norm_g))
    diff_lambda, diff_norm_g, swa_sink, mlstm_gate_b, mlstm_norm_g = map(f32, (diff_lambda, diff_norm_g, swa_sink,
                                                                             mlstm_gate_b, mlstm_norm_g))
    w_branch, w_gate, w_out, norm2_g, w_mlp1, w_mlp2, final_norm_g = map(f32, (w_branch, w_gate, w_out, norm2_g,
                                                                             w_mlp1, w_mlp2, final_norm_g))
    B, S_, D = x.shape
    depth = norm1_g.shape[0]
    H = S_ // 2
    gl = lambda g: np.ascontiguousarray(g.reshape(8, 128).T)
    if "nc" not in _NC:
        _NC["nc"] = build_fused(S_, depth)
    nc = _NC["nc"]
    xT = [fm(x[b]) for b in range(B)]
    cosT, sinT = rope_tables_np(S_)
    cores = [(b, hh) for b in range(B) for hh in range(2)]
    dense_w = [dense_layout(w_gate[l], w_branch[l], w_out[l], w_mlp1[l], w_mlp2[l]) for l in range(depth)]
    identb = np.eye(128, dtype=np.float32).astype(ml_dtypes.bfloat16)
    in_maps = []
    for (b, hh) in cores:
        im = {"x_full": xT[b], "x_half": np.ascontiguousarray(xT[b][:, :, hh * H:(hh + 1) * H])}
        for l in range(depth):
            g1 = gl(norm1_g[l])
            parts = {
                "gdn": gdn_inputs(None, g1, w_in[l], gdn_conv_w[l], gdn_a_log[l], gdn_dt_bias[l], gdn_norm_g[l], hh, S_),
                "diff": diff_inputs(None, g1, w_in[l], diff_lambda[l], diff_norm_g[l], hh, cosT, sinT),
                "swa": swa_inputs(None, g1, w_in[l], swa_sink[l], hh, cosT, sinT),
                "mlstm": mlstm_inputs(None, g1, w_in[l], mlstm_gate_b[l], mlstm_norm_g[l], hh, S_),
                "dense": dict(g1=g1, g2=gl(norm2_g[l]), g3=gl(final_norm_g), identb=identb,
                              msel=np.ascontiguousarray(np.tile(np.array([[1.0 - hh, float(hh)]], np.float32), (128, 1))),
                              **dense_w[l]),
            }
            im["l%d_xn_g1" % l] = g1
            for nm, d in parts.items():
                for k, v in d.items():
                    if k == "xT":
                        continue
                    im["l%d_%s_%s" % (l, nm, k)] = v
        in_maps.append(im)
    r = run_bass_kernel_spmd(nc, in_maps, core_ids=list(range(len(cores)))).results
    for (b, hh), o in zip(cores, r):
        xT[b][:, :, hh * H:(hh + 1) * H] = np.asarray(o["out"])
    return np.stack([unfm(xT[b]) for b in range(B)]).astype(np.float32)
```

```python
import time
import ml_dtypes
from contextlib import ExitStack
import numpy as np
import concourse.bass as bass
import concourse.mybir as mybir
from concourse.bass_utils import run_bass_kernel_spmd

F32 = mybir.dt.float32
BF16 = mybir.dt.bfloat16
ALU = mybir.AluOpType
AF = mybir.ActivationFunctionType
AX = mybir.AxisListType

ENGS = ("pe", "act", "dve", "pool", "sp")
N_DMA_SEMS = 40


class Prog:
    def __init__(self):
        self.nc = bass.Bass("TRN2", target_bir_lowering=False)
        nc = self.nc
        self.stack = ExitStack()
        self.pstack = ExitStack()
        self.sems = {e: self.stack.enter_context(nc.semaphore("se_" + e)) for e in ENGS}
        for i in range(N_DMA_SEMS):
            self.sems[("dma", i)] = self.stack.enter_context(nc.semaphore("sd_%d" % i))
        self.sems["bar"] = self.stack.enter_context(nc.semaphore("s_bar"))
        self.cnt = {e: 0 for e in ENGS}
        self.dcnt = {("dma", i): 0 for i in range(N_DMA_SEMS)}
        self.known = {e: {} for e in ENGS}
        self.phase_no = 0
        self.uid = 0
        self._reset_phase()

    def _reset_phase(self):
        self.q = {e: [] for e in ENGS}
        self.last_w = {}
        self.readers = {}
        self.dma_map = {}
        self.out_tokens = []

    def dram(self, name, shape, dtype, kind):
        ov = getattr(self, "override", None) or {}
        if name in ov:
            return ov[name]
        full = getattr(self, "pre", "") + name
        self.ext_names = getattr(self, "ext_names", [])
        self.ext_names.append(full)
        return self.nc.dram_tensor(full, list(shape), dtype, kind=kind).ap()

    def scratch(self, name, shape, dtype):
        return self.nc.dram_tensor(name, list(shape), dtype).ap()

    def sb(self, name, shape, dtype, glob=False):
        self.uid += 1
        st = self.stack if glob else self.pstack
        return st.enter_context(self.nc.sbuf_tensor("s%d_%s" % (self.uid, name), list(shape), dtype))

    def ps(self, name, shape, dtype=F32):
        self.uid += 1
        return self.pstack.enter_context(self.nc.psum_tensor("p%d_%s" % (self.uid, name), list(shape), dtype))

    def op(self, eng, fn, reads=(), writes=(), dma=None, is_out=False, dma_inc=16):
        toks = []
        for k in reads:
            toks += self.last_w.get(k, [])
        for k in writes:
            toks += self.last_w.get(k, [])
            toks += self.readers.get(k, [])
        need = {}
        for s, v in toks:
            if eng == "pe" and s == "pe":
                continue
            if v > need.get(s, 0):
                need[s] = v
        waits = []
        kn = self.known[eng]
        for s, v in need.items():
            if kn.get(s, 0) >= v:
                continue
            kn[s] = v
            waits.append((s, v))
        if dma is not None:
            if dma not in self.dma_map:
                assert len(self.dma_map) < N_DMA_SEMS, "too many DMA groups in one phase"
                self.dma_map[dma] = ("dma", len(self.dma_map))
            sk = self.dma_map[dma]
            self.dcnt[sk] += dma_inc
            tok = (sk, self.dcnt[sk])
            inc = dma_inc
        else:
            self.cnt[eng] += 1
            tok = (eng, self.cnt[eng])
            inc = 1
        self.q[eng].append((waits, fn, tok, inc))
        for k in reads:
            self.readers.setdefault(k, []).append(tok)
        for k in writes:
            self.last_w[k] = [tok]
            self.readers[k] = []
        if is_out:
            self.out_tokens.append(tok)
        return tok

    def mm(self, out, lhsT, rhs, start=True, stop=True, reads=(), writes=()):
        return self.op("pe", lambda e: e.matmul(out, lhsT, rhs, start=start, stop=stop), reads, writes)

    def tr(self, out, in_, ident, reads=(), writes=()):
        return self.op("pe", lambda e: e.transpose(out, in_, ident), reads, writes)

    def act(self, out, in_, func, reads=(), writes=(), eng="act", **kw):
        return self.op(eng, lambda e: e.activation(out, in_, func, **kw), reads, writes)

    def dma(self, eng, out, in_, key, reads=(), writes=(), is_out=False, **kw):
        return self.op(eng, lambda e: e.dma_start(out=out, in_=in_, **kw), reads, writes, dma=key, is_out=is_out)

    def end_phase(self):
        nc = self.nc
        self.phase_no += 1
        pno = self.phase_no
        fin = {}
        for e in ENGS:
            if e != "sp" and self.cnt[e] > self.known["sp"].get(e, 0):
                fin[e] = self.cnt[e]
        for key, sk in self.dma_map.items():
            if self.dcnt[sk] > self.known["sp"].get(sk, 0):
                fin[sk] = self.dcnt[sk]
        for s, v in fin.items():
            self.known["sp"][s] = v
        sems = self.sems
        q = self.q
        first = (pno == 1)

        def replay(name, eng):
            if not first:
                eng.wait_ge(sems["bar"], pno - 1)
            for waits, fn, tok, inc in q[name]:
                for s, v in waits:
                    eng.wait_ge(sems[s], v)
                fn(eng).then_inc(sems[tok[0]], inc)
            if name == "sp":
                for s, v in fin.items():
                    eng.wait_ge(sems[s], v)
                eng.sem_inc(sems["bar"], 1)

        with nc.Block() as block:
            @block.tensor
            def _(e):
                replay("pe", e)

            @block.scalar
            def _(e):
                replay("act", e)

            @block.vector
            def _(e):
                replay("dve", e)

            @block.gpsimd
            def _(e):
                replay("pool", e)

            @block.sync
            def _(e):
                replay("sp", e)
        st = {e: len(q[e]) for e in ENGS}
        for e in ENGS:
            for e2 in ENGS:
                self.known[e][e2] = self.cnt[e2]
            for sk, v in self.dcnt.items():
                self.known[e][sk] = v
        self._reset_phase()
        self.pstack.close()
        self.pstack = ExitStack()
        return st

    def finish(self):
        self.end_phase()
        self.stack.close()
        return self.nc

    def stats(self):
        return {e: len(self.q[e]) for e in ENGS}


TG = 512


class WLoader:
    def __init__(self, P, maxcols, nbuf=3, cast_engs=("pool", "act", "dve", "pool")):
        self.P = P
        self.nbuf = nbuf
        self.st = [P.sb("wst%d" % i, [128, maxcols], F32) for i in range(nbuf)]
        self.bf = [P.sb("wbf%d" % i, [128, maxcols], BF16) for i in range(nbuf)]
        self.i = 0
        self.engs = cast_engs

    def load(self, src, ncols):
        P = self.P
        i = self.i % self.nbuf
        eng = self.engs[self.i % len(self.engs)]
        self.i += 1
        st, bf = self.st[i], self.bf[i]
        P.dma("sp", st[:, 0:ncols], src, ("wst", i), writes=[("wst", i)])
        if eng == "act":
            P.op("act", lambda e: e.copy(bf[:, 0:ncols], st[:, 0:ncols]), reads=[("wst", i)], writes=[("wbf", i)])
        else:
            P.op(eng, lambda e: e.tensor_copy(bf[:, 0:ncols], st[:, 0:ncols]), reads=[("wst", i)],
                 writes=[("wbf", i)])
        return bf, ("wbf", i)


def rmsnorm_fm(P, xs, xkey, gs, gkey, out, outkey, ones, sq, pss, rstd, T, out_scale_keyed=True):
    P.act(sq[:], xs[:], AF.Square, reads=[xkey], writes=["sq"])
    for c in range(8):
        P.mm(pss[:, 0:T], ones[:], sq[:, c, :], start=(c == 0), stop=(c == 7), reads=["ones", "sq"], writes=["pss"])
    P.act(rstd[:, 0:T], pss[:, 0:T], AF.Sqrt, reads=["pss"], writes=["rstd"], scale=1.0 / 1024, bias=1e-6)
    P.op("dve", lambda e: e.reciprocal(rstd[:, 0:T], rstd[:, 0:T]), reads=["rstd"], writes=["rstd"])
    for c in range(8):
        P.op("dve", lambda e, c=c: e.scalar_tensor_tensor(out[:, c, :], xs[:, c, :], gs[:, c:c + 1], rstd[:, 0:T],
                                                     ALU.mult, ALU.mult),
             reads=[xkey, gkey, "rstd"], writes=[outkey])


def build_dense(NT=2048, final=False, P=None):
    own = P is None
    if own:
        P = Prog()
    xT = P.dram("xT", [128, 8, NT], F32, "ExternalInput")
    ypair = (getattr(P, "override", None) or {}).get("ypair")
    SEQ = 2 * NT
    yT = P.dram("yT", [128, 16, NT], BF16, "ExternalInput") if ypair is None else None
    if ypair is not None:
        identbd = P.dram("identb", [128, 128], BF16, "ExternalInput")
        mseld = P.dram("msel", [128, 2], F32, "ExternalInput")
    g1 = P.dram("g1", [128, 8], F32, "ExternalInput")
    g2 = P.dram("g2", [128, 8], F32, "ExternalInput")
    g3 = P.dram("g3", [128, 8], F32, "ExternalInput")
    wg = P.dram("wg", [4, 8, 128, 1024], F32, "ExternalInput")
    wb = P.dram("wb", [4, 8, 128, 512], F32, "ExternalInput")
    wo = P.dram("wo", [8, 128, 1024], F32, "ExternalInput")
    w1 = P.dram("w1", [32, 128, 1024], F32, "ExternalInput")
    w2 = P.dram("w2", [8, 128, 4096], F32, "ExternalInput")
    out = P.dram("out", [128, 8, NT], F32, "ExternalOutput")

    xs = P.sb("xs", [128, 8, TG], F32)
    ys = P.sb("ys", [128, 16, TG], BF16)
    xn = P.sb("xn", [128, 8, TG], BF16)
    mg = P.sb("mg", [128, 8, TG], BF16)
    hT = P.sb("hT", [128, 32, TG], BF16)
    sq = P.sb("sq", [128, 8, TG], BF16)
    rstd = P.sb("rstd", [128, TG], F32)
    ones = P.sb("ones", [128, 128], BF16)
    g1s = P.sb("g1s", [128, 8], F32)
    g2s = P.sb("g2s", [128, 8], F32)
    g3s = P.sb("g3s", [128, 8], F32)
    sg = [P.sb("sg%d" % i, [128, TG], F32) for i in range(2)]
    pr = [P.sb("pr%d" % i, [128, TG], F32) for i in range(2)]
    acc = P.sb("acc", [128, TG], F32)
    rl = [P.sb("rl%d" % i, [128, TG], F32) for i in range(2)]
    xo = P.sb("xo", [128, 8, TG], F32)
    pss = P.ps("pss", [128, TG], F32)
    pg = [P.ps("pg%d" % i, [128, TG], F32) for i in range(2)]
    pb = [P.ps("pb%d" % i, [128, TG], F32) for i in range(2)]
    WL = WLoader(P, 1024, nbuf=6, cast_engs=("pool",))
    if ypair is not None:
        identb = P.sb("identb", [128, 128], BF16)
        msel = P.sb("msel", [128, 2], F32)
        cand = [P.sb("cand%d" % i, [128, 1024], BF16) for i in range(2)]
        ysel = P.sb("ysel", [128, 1024], BF16)
        ptr = P.ps("ptr", [128, 8, 128], BF16)
        P.dma("sp", identb[:], identbd, "identb", writes=["identb"])
        P.dma("sp", msel[:], mseld, "msel", writes=["msel"])

    P.op("pool", lambda e: e.memset(ones[:], 1.0), writes=["ones"])
    P.dma("sp", g1s[:], g1, "g1s", writes=["g1s"])
    P.dma("sp", g2s[:], g2, "g2s", writes=["g2s"])
    P.dma("sp", g3s[:], g3, "g3s", writes=["g3s"])
    cnt = 0
    for tg in range(NT // TG):
        tsl = slice(tg * TG, (tg + 1) * TG)
        P.dma("sp", xs[:], xT[:, :, tsl], "xs", writes=["xs"])
        if ypair is None:
            P.dma("sp", ys[:], yT[:, :, tsl], "ys", writes=["ys"])
        else:
            ys4 = ys[:].rearrange("p (n c) t -> p n c t", c=4)
            for tt in range(TG // 128):
                tok0 = tg * TG + tt * 128
                for r in range(2):
                    for h in range(2):
                        row0 = r * SEQ + h * NT + tok0
                        P.dma("sp", cand[h][:], ypair[row0:row0 + 128, :], ("cand", h), writes=[("cand", h)])
                    P.op("dve", lambda e: e.tensor_scalar(ysel[:], cand[0][:], msel[:, 0:1], None, ALU.mult),
                         reads=[("cand", 0), "msel"], writes=["ysel"])
                    P.op("dve", lambda e: e.scalar_tensor_tensor(ysel[:], cand[1][:], msel[:, 1:2], ysel[:], ALU.mult,
                                                                 ALU.add), reads=[("cand", 1), "msel", "ysel"],
                         writes=["ysel"])
                    for n in range(4):
                        for jj in range(2):
                            P.tr(ptr[:, n * 2 + jj, :], ysel[:, n * 256 + jj * 128:n * 256 + (jj + 1) * 128], identb[:],
                                 reads=["ysel", "identb"], writes=["ptr"])
                    for n in range(4):
                        P.act(ys4[:, n, 2 * r:2 * r + 2, tt * 128:(tt + 1) * 128], ptr[:, 2 * n:2 * n + 2, :], AF.Copy,
                              reads=["ptr"], writes=["ys"])
        rmsnorm_fm(P, xs, "xs", g1s, "g1s", xn, "xn", ones, sq, pss, rstd, TG)
        for j in range(8):
            for n in range(4):
                k = cnt % 2
                cnt += 1
                wgt, wgk = WL.load(wg[n, j], 1024)
                for c in range(8):
                    P.mm(pg[k][:], wgt[:, c * 128:(c + 1) * 128], xn[:, c, :], start=(c == 0), stop=(c == 7),
                         reads=[wgk, "xn"], writes=[("pg", k)])
                wbt, wbk = WL.load(wb[n, j], 512)
                for c in range(4):
                    P.mm(pb[k][:], wbt[:, c * 128:(c + 1) * 128], ys[:, n * 4 + c, :], start=(c == 0), stop=(c == 3),
                         reads=[wbk, "ys"], writes=[("pb", k)])
                P.act(sg[k][:], pg[k][:], AF.Sigmoid, reads=[("pg", k)], writes=[("sg", k)])
                if n == 0:
                    P.op("dve", lambda e, k=k: e.tensor_tensor(acc[:], pb[k][:], sg[k][:], ALU.mult),
                         reads=[("pb", k), ("sg", k)], writes=["acc"])
                else:
                    P.op("dve", lambda e, k=k: e.tensor_tensor(pr[k][:], pb[k][:], sg[k][:], ALU.mult),
                         reads=[("pb", k), ("sg", k)], writes=[("pr", k)])
                    if n < 3:
                        P.op("dve", lambda e, k=k: e.tensor_tensor(acc[:], acc[:], pr[k][:], ALU.add),
                             reads=["acc", ("pr", k)], writes=["acc"])
                    else:
                        P.op("dve", lambda e, k=k, j=j: e.tensor_tensor(mg[:, j, :], acc[:], pr[k][:], ALU.add),
                             reads=["acc", ("pr", k)], writes=["mg"])
        for j in range(8):
            k = cnt % 2
            cnt += 1
            wt, wk = WL.load(wo[j], 1024)
            for c in range(8):
                P.mm(pg[k][:], wt[:, c * 128:(c + 1) * 128], mg[:, c, :], start=(c == 0), stop=(c == 7),
                     reads=[wk, "mg"], writes=[("pg", k)])
            P.op("dve", lambda e, k=k, j=j: e.tensor_tensor(xs[:, j, :], pg[k][:], xs[:, j, :], ALU.add),
                 reads=[("pg", k), "xs"], writes=["xs"])
        rmsnorm_fm(P, xs, "xs", g2s, "g2s", xn, "xn", ones, sq, pss, rstd, TG)
        for f in range(32):
            k = cnt % 2
            cnt += 1
            wt, wk = WL.load(w1[f], 1024)
            for c in range(8):
                P.mm(pg[k][:], wt[:, c * 128:(c + 1) * 128], xn[:, c, :], start=(c == 0), stop=(c == 7),
                     reads=[wk, "xn"], writes=[("pg", k)])
            P.act(rl[k][:], pg[k][:], AF.Relu, reads=[("pg", k)], writes=[("rl", k)])
            P.op("dve", lambda e, k=k, f=f: e.tensor_tensor(hT[:, f, :], rl[k][:], rl[k][:], ALU.mult),
                 reads=[("rl", k)], writes=["hT"])
        for j in range(8):
            k = cnt % 2
            cnt += 1
            for fb in range(4):
                wt, wk = WL.load(w2[j][:, fb * 1024:(fb + 1) * 1024], 1024)
                for f8 in range(8):
                    f = fb * 8 + f8
                    P.mm(pb[k][:], wt[:, f8 * 128:(f8 + 1) * 128], hT[:, f, :], start=(f == 0), stop=(f == 31),
                         reads=[wk, "hT"], writes=[("pb", k)])
            P.op("dve", lambda e, k=k, j=j: e.tensor_tensor(xs[:, j, :], pb[k][:], xs[:, j, :], ALU.add),
                 reads=[("pb", k), "xs"], writes=["xs"])
        if final:
            rmsnorm_fm(P, xs, "xs", g3s, "g3s", xo, "xo", ones, sq, pss, rstd, TG)
            P.dma("sp", out[:, :, tsl], xo[:], "out", reads=["xo"], is_out=True)
        else:
            P.dma("sp", out[:, :, tsl], xs[:], "out", reads=["xs"], is_out=True)
    print("dense stats", P.stats())
    return (P.finish() if own else None)


def dense_layout(w_gate, w_branch, w_out, w_mlp1, w_mlp2):
    wg = w_gate.reshape(8, 128, 4, 8, 128).transpose(2, 3, 1, 0, 4).reshape(4, 8, 128, 1024)
    wb = w_branch.reshape(4, 4, 128, 8, 128).transpose(0, 3, 2, 1, 4).reshape(4, 8, 128, 512)
    wo = w_out.reshape(8, 128, 8, 128).transpose(2, 1, 0, 3).reshape(8, 128, 1024)
    w1 = w_mlp1.reshape(8, 128, 32, 128).transpose(2, 1, 0, 3).reshape(32, 128, 1024)
    w2 = w_mlp2.reshape(32, 128, 8, 128).transpose(2, 1, 0, 3).reshape(8, 128, 4096)
    return dict(wg=np.ascontiguousarray(wg), wb=np.ascontiguousarray(wb), wo=np.ascontiguousarray(wo),
                w1=np.ascontiguousarray(w1), w2=np.ascontiguousarray(w2))


def fm(a):
    T, D = a.shape
    return np.ascontiguousarray(a.T.reshape(D // 128, 128, T).transpose(1, 0, 2))


def unfm(a):
    p, C, T = a.shape
    return np.ascontiguousarray(a.transpose(2, 1, 0).reshape(T, C * 128))


S = 4096
NTILE = S // 128
TG = 512
NG = S // TG


def prologue(P, xT, g1, wsrc, ncols, S=S):
    xnT = P.sb("xnT", [128, 8, S], BF16)
    wbf = P.sb("wbf", [128, 8, max(ncols, 1)], BF16)
    xs = [P.sb("xs0", [128, 8, TG], F32)] * 2
    sq = P.sb("sq", [128, 8, TG], BF16)
    rstd = P.sb("rstd", [128, TG], F32)
    ones = P.sb("ones", [128, 128], BF16)
    g1s = P.sb("g1s", [128, 8], F32)
    pss = P.ps("pss", [128, TG], F32)
    P.op("pool", lambda e: e.memset(ones[:], 1.0), writes=["ones"])
    P.dma("sp", g1s[:], g1, "g1s", writes=["g1s"])
    wst = [P.sb("wst%d" % i, [128, max(ncols, 1)], F32) for i in range(2)]
    for c in range(8 if wsrc is not None else 0):
        i = c % 2
        P.dma("sp", wst[i][:], wsrc[:, c * ncols:(c + 1) * ncols], ("wst", i), writes=[("wst", i)])
        P.op("pool", lambda e, i=i, c=c: e.tensor_copy(wbf[:, c, :], wst[i][:]), reads=[("wst", i)],
             writes=["wbf"])
    xn_src = (getattr(P, "override", None) or {}).get("xnT_src")
    if xn_src is not None:
        for g in range(S // TG):
            tsl = slice(g * TG, (g + 1) * TG)
            P.dma("sp", xnT[:, :, tsl], xn_src[:, :, tsl], ("xnld", g), writes=[("xnT", g)])
        P.xs0 = xs[0]
        return xnT, wbf, ones
    for g in range(S // TG):
        i = 0
        tsl = slice(g * TG, (g + 1) * TG)
        P.dma("sp", xs[i][:], xT[:, :, tsl], ("xs", i), writes=[("xs", i)])
        P.act(sq[:], xs[i][:], AF.Square, reads=[("xs", i)], writes=["sq"])
        for c in range(8):
            P.mm(pss[:], ones[:], sq[:, c, :], start=(c == 0), stop=(c == 7), reads=["ones", "sq"], writes=["pss"])
        P.act(rstd[:], pss[:], AF.Sqrt, reads=["pss"], writes=["rstd"], scale=1.0 / 1024, bias=1e-6)
        P.op("dve", lambda e: e.reciprocal(rstd[:], rstd[:]), reads=["rstd"], writes=["rstd"])
        for c in range(8):
            P.op("dve", lambda e, c=c, i=i, tsl=tsl: e.scalar_tensor_tensor(
                xnT[:, c, tsl], xs[i][:, c, :], g1s[:, c:c + 1], rstd[:], ALU.mult, ALU.mult),
                reads=[("xs", i), "g1s", "rstd"], writes=[("xnT", g)])
    P.xs0 = xs[0]
    return xnT, wbf, ones


def proj_fm(P, out_ps, okey, wbf, col0, ncol, xnT, g):
    for c in range(8):
        P.mm(out_ps[0:ncol, :], wbf[:, c, col0:col0 + ncol], xnT[:, c, g * TG:(g + 1) * TG], start=(c == 0),
             stop=(c == 7), reads=["wbf", ("xnT", g)], writes=[okey])


def proj_tm(P, out_ps, okey, wbf, col0, ncol, xnT, t):
    g = (t * 128) // TG
    for c in range(8):
        P.mm(out_ps, xnT[:, c, t * 128:(t + 1) * 128], wbf[:, c, col0:col0 + ncol], start=(c == 0), stop=(c == 7),
             reads=["wbf", ("xnT", g)], writes=[okey])


def rope_proj(P, dstT, dkey, wbf, col_a, col_sw, xnT, cosT, sinT, pp, tmp, S=S):
    for g in range(S // TG):
        tsl = slice(g * TG, (g + 1) * TG)
        proj_fm(P, pp[0], ("pp", 0), wbf, col_a, 128, xnT, g)
        proj_fm(P, pp[1], ("pp", 1), wbf, col_sw, 128, xnT, g)
        P.op("dve", lambda e, tsl=tsl: e.tensor_tensor(tmp[0][:], pp[0][:], cosT[:, tsl], ALU.mult),
             reads=[("pp", 0), "cos"], writes=[("rtmp", 0)])
        P.op("dve", lambda e, tsl=tsl: e.tensor_tensor(tmp[1][:], pp[1][:], sinT[:, tsl], ALU.mult),
             reads=[("pp", 1), "sin"], writes=[("rtmp", 1)])
        P.op("pool", lambda e, tsl=tsl: e.tensor_tensor(dstT[:, tsl], tmp[0][:], tmp[1][:], ALU.add),
             reads=[("rtmp", 0), ("rtmp", 1)], writes=[(dkey, g)])


def rope_tables_np(S=S):
    inv = 1.0 / (10000.0 ** (np.arange(0, 64, 2, dtype=np.float32) / 64))
    ang = np.arange(S, dtype=np.float32)[:, None] * inv[None, :]
    ang = np.concatenate([ang, ang], axis=-1)
    cos, sin = np.cos(ang).astype(np.float32), np.sin(ang).astype(np.float32)
    sgn = np.concatenate([-np.ones(32, np.float32), np.ones(32, np.float32)])
    cosT = np.ascontiguousarray(np.tile(cos.T, (2, 1)))
    sinT = np.ascontiguousarray(np.tile((sin * sgn[None, :]).T, (2, 1)))
    return cosT, sinT


def swap_halves(cols):
    cols = np.asarray(cols).reshape(-1, 64)
    return np.concatenate([cols[:, 32:], cols[:, :32]], axis=1).reshape(-1)


def wlayout(w, cols):
    sub = w[:, cols]
    n = sub.shape[1]
    return np.ascontiguousarray(sub.reshape(8, 128, n).transpose(1, 0, 2).reshape(128, 8 * n))


C_NCOLS = 256 + 256 + 128 + 128 + 64


def build_swa(S=S, P=None):
    own = P is None
    if own:
        P = Prog()
    xT = P.dram("xT", [128, 8, S], F32, "ExternalInput")
    g1 = P.dram("g1", [128, 8], F32, "ExternalInput")
    w = P.dram("w", [128, 8 * C_NCOLS], F32, "ExternalInput")
    cosd = P.dram("cos", [128, S], F32, "ExternalInput")
    sind = P.dram("sin", [128, S], F32, "ExternalInput")
    maskd = P.dram("mask", [128, 2, 512], BF16, "ExternalInput")
    sinkd = P.dram("sink", [128, 4], F32, "ExternalInput")
    y = P.dram("y", [S, 256], BF16, "ExternalOutput")

    xnT, wbf, ones = prologue(P, xT, g1, w, C_NCOLS, S)
    cosT = P.sb("cosT", [128, S], F32)
    sinT = P.sb("sinT", [128, S], F32)
    P.dma("sp", cosT[:], cosd, "cos", writes=["cos"])
    P.dma("sp", sinT[:], sind, "sin", writes=["sin"])
    mask = P.sb("mask", [128, 2, 512], BF16)
    P.dma("sp", mask[:], maskd, "mask", writes=["mask"])
    esink = P.sb("esink", [128, 4], F32)
    P.dma("sp", esink[:], sinkd, "esink", writes=["esink"])
    P.act(esink[:], esink[:], AF.Exp, reads=["esink"], writes=["esink"])

    NT = S // 128
    qT = [P.sb("qT%d" % i, [128, S], BF16) for i in range(2)]
    kT = P.sb("kT", [128, S], BF16)
    vaug = P.sb("vaug", [128, NT, 65], BF16)
    pp = [P.ps("pp%d" % i, [128, TG], F32) for i in range(2)]
    tmp = [P.sb("rtmp%d" % i, [128, TG], F32) for i in range(2)]
    rope_proj(P, qT[0], "qT0", wbf, 0, 256, xnT, cosT, sinT, pp, tmp, S)
    rope_proj(P, qT[1], "qT1", wbf, 128, 384, xnT, cosT, sinT, pp, tmp, S)
    rope_proj(P, kT, "kT", wbf, 512, 640, xnT, cosT, sinT, pp, tmp, S)
    kz = [P.sb("kz%d" % i, [128, S], BF16) for i in range(2)]
    P.op("pool", lambda e: e.memset(kz[0][64:128, :], 0.0), writes=["kz0"])
    P.op("pool", lambda e: e.memset(kz[1][0:64, :], 0.0), writes=["kz1"])
    allk = [("kT", g) for g in range(S // TG)]
    P.op("pool", lambda e: e.tensor_copy(kz[0][0:64, :], kT[0:64, :]), reads=allk, writes=["kz0"])
    P.op("act", lambda e: e.copy(kz[1][64:128, :], kT[64:128, :]), reads=allk, writes=["kz1"])
    P.op("pool", lambda e: e.memset(vaug[:], 1.0), writes=["vaug"])
    pv = P.ps("pv", [128, 64], F32)
    for t in range(NT):
        proj_tm(P, pv[:], "pv", wbf, 768, 64, xnT, t)
        P.act(vaug[:, t, 0:64], pv[:], AF.Copy, reads=["pv"], writes=["vaug"])

    st = [P.ps("st%d" % i, [128, 512], F32) for i in range(3)]
    pt = [P.sb("pt%d" % i, [128, 512], BF16) for i in range(3)]
    po = P.ps("po", [128, 4, 65], F32)
    den = P.sb("den", [128, 4], F32)
    yt = [P.sb("yt%d" % i, [128, 4, 64], BF16) for i in range(2)]
    qkeys = lambda n: [("qT0", (n * 128) // TG), ("qT1", (n * 128) // TG)]
    for n in range(NT):
        ds = [d for d in (-1, 0, 1) if 0 <= n + d < NT]
        for di, d in enumerate(ds):
            m = n + d
            for h in range(4):
                P.mm(st[di][:, h * 128:(h + 1) * 128], kz[h % 2][:, m * 128:(m + 1) * 128],
                     qT[h // 2][:, n * 128:(n + 1) * 128], reads=["kz0", "kz1"] + qkeys(n),
                     writes=[("st", di)])
            P.act(pt[di][:], st[di][:], AF.Exp, reads=[("st", di)], writes=[("pt", di)], scale=0.125)
            if d != 0:
                mi = 0 if d == -1 else 1
                P.op("pool", lambda e, di=di, mi=mi: e.tensor_tensor(pt[di][:], pt[di][:], mask[:, mi, :], ALU.mult),
                     reads=[("pt", di), "mask"], writes=[("pt", di)])
        for h in range(4):
            for di, d in enumerate(ds):
                m = n + d
                P.mm(po[:, h, :], pt[di][:, h * 128:(h + 1) * 128], vaug[:, m, :], start=(di == 0),
                     stop=(di == len(ds) - 1), reads=[("pt", di), "vaug"], writes=["po"])
        P.op("dve", lambda e: e.tensor_tensor(den[:], po[:, :, 64], esink[:], ALU.add), reads=["po", "esink"],
             writes=["den"])
        P.op("dve", lambda e: e.reciprocal(den[:], den[:]), reads=["den"], writes=["den"])
        yb = yt[n % 2]
        for h in range(4):
            P.op("dve", lambda e, h=h, yb=yb: e.tensor_scalar(yb[:, h, :], po[:, h, 0:64], den[:, h:h + 1], None,
                                                           ALU.mult), reads=["po", "den"], writes=[("yt", n % 2)])
        P.dma("sp", y[n * 128:(n + 1) * 128, :], yb[:].rearrange("p h d -> p (h d)"), ("yt", n % 2),
              reads=[("yt", n % 2)], is_out=True)
    print("swa stats", P.stats())
    return (P.finish() if own else None)


def swa_inputs(xTb, g1l, w_in_l, sink_l, hh, cosT, sinT):
    offs = np.cumsum((0,) + (1536, 512, 16, 512, 512, 512, 512, 128, 128, 256, 256, 512, 16, 512))
    cq, ck, cv = offs[6], offs[7], offs[8]
    qcols = cq + np.arange(hh * 256, hh * 256 + 256)
    kcols = ck + np.arange(hh * 64, hh * 64 + 64)
    vcols = cv + np.arange(hh * 64, hh * 64 + 64)
    k2 = np.concatenate([kcols, kcols])
    cols = np.concatenate([qcols, swap_halves(qcols), k2, swap_halves(k2), vcols])
    assert len(cols) == C_NCOLS
    j = np.arange(128)[:, None]
    i = np.arange(128)[None, :]
    prev = (j >= i).astype(np.float32)
    nxt = (j <= i).astype(np.float32)
    mask = np.stack([np.tile(prev, (1, 4)), np.tile(nxt, (1, 4))], axis=1).astype(ml_dtypes.bfloat16)
    sink = np.tile(sink_l[hh * 4:hh * 4 + 4][None, :], (128, 1)).astype(np.float32)
    return dict(xT=xTb, g1=g1l, w=wlayout(w_in_l, cols), cos=cosT, sin=sinT, mask=np.ascontiguousarray(mask),
                sink=np.ascontiguousarray(sink))


B_NCOLS = 5 * 256


def build_diff(lam_init, S=S, P=None):
    own = P is None
    if own:
        P = Prog()
    xT = P.dram("xT", [128, 8, S], F32, "ExternalInput")
    g1 = P.dram("g1", [128, 8], F32, "ExternalInput")
    w = P.dram("w", [128, 8 * B_NCOLS], F32, "ExternalInput")
    cosd = P.dram("cos", [128, S], F32, "ExternalInput")
    sind = P.dram("sin", [128, S], F32, "ExternalInput")
    lpd = P.dram("lp", [128, 4, 64], F32, "ExternalInput")
    gbd = P.dram("gb", [128, 128], F32, "ExternalInput")
    y = P.dram("y", [S, 256], BF16, "ExternalOutput")
    NT = S // 128
    NQ = S // 512

    xnT, wbf, ones = prologue(P, xT, g1, w, B_NCOLS, S)
    cosT = P.sb("cosT", [128, S], F32)
    sinT = P.sb("sinT", [128, S], F32)
    P.dma("sp", cosT[:], cosd, "cos", writes=["cos"])
    P.dma("sp", sinT[:], sind, "sin", writes=["sin"])
    lp = P.sb("lp", [128, 4, 64], F32)
    gb = P.sb("gb", [128, 128], F32)
    P.dma("sp", lp[:], lpd, "lp", writes=["lp"])
    P.dma("sp", gb[:], gbd, "gb", writes=["gb"])
    P.op("dve", lambda e: e.tensor_scalar(gb[:], gb[:], 1.0 - lam_init, None, ALU.mult), reads=["gb"], writes=["gb"])
    junk = P.sb("junk", [128, 128], F32)
    s12 = P.sb("s12", [128, 2], F32)
    nlam = P.sb("nlam", [128, 1], F32)
    for i in range(2):
        P.op("dve", lambda e, i=i: e.scalar_tensor_tensor(junk[:, 0:64], lp[:, 2 * i, :], 1.0, lp[:, 2 * i + 1, :],
                                                     ALU.mult, ALU.mult, accum_out=s12[:, i:i + 1]),
             reads=["lp"], writes=["junk", "s12"])
    P.act(s12[:], s12[:], AF.Exp, reads=["s12"], writes=["s12"])
    P.op("dve", lambda e: e.tensor_tensor(nlam[:], s12[:, 0:1], s12[:, 1:2], ALU.subtract), reads=["s12"],
         writes=["nlam"])
    P.op("dve", lambda e: e.tensor_scalar(nlam[:], nlam[:], -1.0, -lam_init, ALU.mult, ALU.add), reads=["nlam"],
         writes=["nlam"])

    zt = P.sb("zt", [128, 512], BF16)
    P.op("pool", lambda e: e.memset(zt[:], 0.0), writes=["zt"])
    qT = P.sb("qT", [128, S], BF16)
    kT = P.sb("kT", [128, S], BF16)
    kz = [P.sb("kz%d" % i, [128, S], BF16) for i in range(2)]
    vt = P.sb("vt", [128, NT, 129], BF16)
    P.op("pool", lambda e: e.memset(vt[:], 1.0), writes=["vt"])
    pp = [P.ps("pp%d" % i, [128, TG], F32) for i in range(2)]
    tmp = [P.sb("rtmp%d" % i, [128, TG], F32) for i in range(2)]
    st = pp
    pt = [P.sb("pt%d" % i, [128, 512], BF16) for i in range(3)]
    po = [[P.ps("po%d_%d" % (i, hf), [128, 2, 129], F32) for hf in range(2)] for i in range(2)]
    rd = P.sb("rd", [128, 2, 4], F32)
    t0 = [P.sb("t0_%d" % i, [128, 128], F32) for i in range(2)]
    ot = [P.sb("ot%d" % i, [128, 128], F32) for i in range(2)]
    ssq = P.sb("ssq", [128, 2], F32)
    yt = [P.sb("yt%d" % i, [128, 4, 128], BF16) for i in range(2)]
    P.op("pool", lambda e: e.memset(kz[0][64:128, :], 0.0), writes=["kz0"])
    P.op("pool", lambda e: e.memset(kz[1][0:64, :], 0.0), writes=["kz1"])
    allg = lambda k: [(k, g) for g in range(S // TG)]
    cnt = 0
    ycnt = 0
    for h in range(2):
        rope_proj(P, qT, "qT", wbf, h * 128, 256 + h * 128, xnT, cosT, sinT, pp, tmp, S)
        rope_proj(P, kT, "kT", wbf, 512 + h * 128, 768 + h * 128, xnT, cosT, sinT, pp, tmp, S)
        P.op("pool", lambda e: e.tensor_copy(kz[0][0:64, :], kT[0:64, :]), reads=allg("kT"), writes=["kz0"])
        P.op("act", lambda e: e.copy(kz[1][64:128, :], kT[64:128, :]), reads=allg("kT"), writes=["kz1"])
        for t in range(NT):
            proj_tm(P, pp[0][:, 0:128], ("pp", 0), wbf, 1024 + h * 128, 128, xnT, t)
            P.act(vt[:, t, 0:128], pp[0][:, 0:128], AF.Copy, reads=[("pp", 0)], writes=["vt"])
        for g in range(NQ):
            qsl = slice(g * 512, (g + 1) * 512)
            for m in range(2):
                for hf in range(2):
                    P.mm(po[m][hf][:].rearrange("p a b -> p (a b)"), zt[:, 0:128], zt[:, 0:258], start=True, stop=False,
                         reads=["zt"], writes=[("po", m)])
            blocks = [(t, m) for t in range(NT) for m in range(2)]

            def front(i, base):
                t, m = blocks[i]
                b, b3 = (base + i) % 2, (base + i) % 3
                P.mm(st[b][:], kz[m][:, t * 128:(t + 1) * 128], qT[:, qsl], reads=["kz%d" % m, ("qT", g)],
                     writes=[("pp", b)])
                P.act(pt[b3][:], st[b][:], AF.Exp, reads=[("pp", b)], writes=[("pt", b3)], scale=0.125)

            def back(i, base):
                t, m = blocks[i]
                b3 = (base + i) % 3
                last = (t == NT - 1)
                for qs in range(4):
                    P.mm(po[m][qs // 2][:, qs % 2, :], pt[b3][:, qs * 128:(qs + 1) * 128], vt[:, t, :], start=False,
                         stop=(last and qs % 2 == 1), reads=[("pt", b3), "vt"], writes=[("po", m)])

            front(0, cnt)
            for i in range(len(blocks)):
                if i + 1 < len(blocks):
                    front(i + 1, cnt)
                back(i, cnt)
            cnt += len(blocks)
            for m in range(2):
                for hf in range(2):
                    P.op("dve", lambda e, m=m, hf=hf: e.reciprocal(rd[:, m, 2 * hf:2 * hf + 2], po[m][hf][:, :, 128]),
                         reads=[("po", m)], writes=["rd"])
            P.op("dve", lambda e: e.tensor_scalar(rd[:, 1, :], rd[:, 1, :], nlam[:, 0:1], None, ALU.mult),
                 reads=["rd", "nlam"], writes=["rd"])
            yb = yt[ycnt % 2]
            ykey = ("yt", ycnt % 2)
            ycnt += 1
            for qs in range(4):
                k2 = qs % 2
                P.act(t0[k2][:], po[0][qs // 2][:, qs % 2, 0:128], AF.Copy, reads=[("po", 0), "rd"], writes=[("t0", k2)],
                      scale=rd[:, 0, qs:qs + 1])
                P.op("dve", lambda e, qs=qs, k2=k2: e.scalar_tensor_tensor(ot[k2][:], po[1][qs // 2][:, qs % 2, 0:128],
                                                                       rd[:, 1, qs:qs + 1], t0[k2][:], ALU.mult,
                                                                       ALU.add),
                     reads=[("po", 1), "rd", ("t0", k2)], writes=[("ot", k2)])
                P.act(junk[:], ot[k2][:], AF.Square, reads=[("ot", k2)], writes=["junk", ("ssq", k2)],
                      accum_out=ssq[:, k2:k2 + 1])
                P.act(ssq[:, k2:k2 + 1], ssq[:, k2:k2 + 1], AF.Sqrt, reads=[("ssq", k2)], writes=[("ssq", k2)],
                      scale=1.0 / 128, bias=1e-6)
                P.op("dve", lambda e, k2=k2: e.reciprocal(ssq[:, k2:k2 + 1], ssq[:, k2:k2 + 1]), reads=[("ssq", k2)],
                     writes=[("ssq", k2)])
                P.op("dve", lambda e, qs=qs, k2=k2, yb=yb: e.scalar_tensor_tensor(yb[:, qs, :], ot[k2][:],
                                                                              ssq[:, k2:k2 + 1], gb[:], ALU.mult,
                                                                              ALU.mult),
                     reads=[("ot", k2), ("ssq", k2), "gb"], writes=[ykey])
            P.dma("sp", y[g * 512:(g + 1) * 512, h * 128:(h + 1) * 128].rearrange("(q p) e -> p q e", p=128), yb[:],
                  ykey, reads=[ykey], is_out=True)
    print("diff stats", P.stats())
    return (P.finish() if own else None)


def diff_inputs(xTb, g1l, w_in_l, lam_l, ng_l, hh, cosT, sinT):
    offs = np.cumsum((0,) + (1536, 512, 16, 512, 512, 512, 512, 128, 128, 256, 256, 512, 16, 512))
    cq, ck, cv = offs[3], offs[4], offs[5]
    r = np.arange(hh * 256, hh * 256 + 256)
    cols = np.concatenate([cq + r, swap_halves(cq + r), ck + r, swap_halves(ck + r), cv + r])
    assert len(cols) == B_NCOLS
    lp = np.ascontiguousarray(np.tile(lam_l[None], (128, 1, 1)).astype(np.float32))
    gb = np.ascontiguousarray(np.tile(ng_l[None, :], (128, 1)).astype(np.float32))
    return dict(xT=xTb, g1=g1l, w=wlayout(w_in_l, cols), cos=cosT, sin=sinT, lp=lp, gb=gb)


D_NCOLS = 128 + 128 + 256 + 8 + 256


def dve(P, fn, reads, writes, eng="dve"):
    return P.op(eng, fn, reads, writes)


def build_mlstm(S=S, P=None):
    own = P is None
    if own:
        P = Prog()
    NT = S // 128
    NQ = S // 512
    xT = P.dram("xT", [128, 8, S], F32, "ExternalInput")
    g1 = P.dram("g1", [128, 8], F32, "ExternalInput")
    w = P.dram("w", [128, 8 * D_NCOLS], F32, "ExternalInput")
    identd = P.dram("ident", [128, 128], F32, "ExternalInput")
    antid = P.dram("anti", [128, 128], F32, "ExternalInput")
    seld = P.dram("sel", [NT, NT * 128], F32, "ExternalInput")
    maskd = P.dram("mask", [128, 2 * 4 * 512], BF16, "ExternalInput")
    biasd = P.dram("gbias", [128, NT * 8], F32, "ExternalInput")
    gbd = P.dram("gb", [128, 128], F32, "ExternalInput")
    y = P.dram("y", [S, 256], BF16, "ExternalOutput")

    xnT, wbf, ones = prologue(P, xT, g1, w, D_NCOLS, S)
    ident = P.sb("ident", [128, 128], F32)
    anti = P.sb("anti", [128, 128], F32)
    sel2 = P.xs0[:].rearrange("p a b -> p (a b)")
    mask = P.sb("mask", [128, 2, 4, 512], BF16)
    gbias = P.sb("gbias", [128, NT * 8], F32)
    gb = P.sb("gb", [128, 128], F32)
    P.dma("sp", ident[:], identd, "ident", writes=["ident"])
    P.dma("sp", anti[:], antid, "anti", writes=["anti"])
    P.dma("sp", sel2[0:NT, 0:NT * 128], seld, "sel", writes=["sel", ("xs", 0)])
    P.dma("sp", mask[:].rearrange("p a b c -> p (a b c)"), maskd, "mask", writes=["mask"])
    P.dma("sp", gbias[:], biasd, "gbias", writes=["gbias"])
    P.dma("sp", gb[:], gbd, "gb", writes=["gb"])
    zt = P.sb("zt", [128, 512], BF16)
    P.op("pool", lambda e: e.memset(zt[:], 0.0), writes=["zt"])
    onesf = P.sb("onesf", [NT, 128], F32)
    P.op("pool", lambda e: e.memset(onesf[:], 1.0), writes=["onesf"])

    pp = [P.ps("pp%d" % i, [128, 512], F32) for i in range(2)]
    st = [P.ps("st%d" % i, [128, 512], F32) for i in range(2)]
    po = [P.ps("po%d" % i, [128, 4, 128], F32) for i in range(2)]
    pd = P.ps("pd", [128, 8], F32)

    qT = P.sb("qT", [128, S], BF16)
    kz = [P.sb("kz%d" % i, [128, S], BF16) for i in range(2)]
    vt = P.sb("vt", [128, NT, 256], BF16)
    P.op("pool", lambda e: e.memset(kz[0][64:128, :], 0.0), writes=["kz0"])
    P.op("pool", lambda e: e.memset(kz[1][0:64, :], 0.0), writes=["kz1"])
    for g in range(S // TG):
        tsl = slice(g * TG, (g + 1) * TG)
        proj_fm(P, pp[0], ("pp", 0), wbf, 0, 128, xnT, g)
        P.act(qT[:, tsl], pp[0][:], AF.Copy, reads=[("pp", 0)], writes=[("qT", g)])
        proj_fm(P, pp[1], ("pp", 1), wbf, 128, 128, xnT, g)
        P.act(kz[0][0:64, tsl], pp[1][0:64, :], AF.Copy, reads=[("pp", 1)], writes=["kz0"], scale=0.125)
        P.act(kz[1][64:128, tsl], pp[1][64:128, :], AF.Copy, reads=[("pp", 1)], writes=["kz1"], scale=0.125)
    for t in range(NT):
        k = t % 2
        proj_tm(P, pp[k][:, 0:256], ("pp", k), wbf, 256, 256, xnT, t)
        P.act(vt[:, t, :], pp[k][:, 0:256], AF.Copy, reads=[("pp", k)], writes=["vt"])
    for t in range(NT):
        proj_tm(P, pp[0][:, t * 8:(t + 1) * 8], ("pp", 0), wbf, 512, 8, xnT, t)
    gtok = P.sb("gtok", [128, NT, 8], F32)
    gtokR = P.sb("gtokR", [128, NT, 8], F32)
    dve(P, lambda e: e.tensor_tensor(gtok[:].rearrange("p a b -> p (a b)"), pp[0][:, 0:NT * 8], gbias[:], ALU.add),
        [("pp", 0), "gbias"], ["gtok"])
    P.mm(pp[1][:, 0:NT * 8], anti[:], gtok[:].rearrange("p a b -> p (a b)"), reads=["anti", "gtok"],
         writes=[("pp", 1)])
    P.act(gtokR[:].rearrange("p a b -> p (a b)"), pp[1][:, 0:NT * 8], AF.Copy, reads=[("pp", 1)], writes=["gtokR"])

    tok = [P.sb("tok%d" % d, [128, 4, NT], F32) for d in range(2)]
    A = [P.sb("A%d" % d, [NT, 2, 128], F32) for d in range(2)]
    def gate_dir(d):
        src = gtok if d == 0 else gtokR
        sk = "gtok" if d == 0 else "gtokR"
        LP = P.sb("LP%d" % d, [NT, 4, 128], F32)
        k = d
        for j in range(4):
            col = (0, 1, 4, 5)[j] + 2 * d
            P.mm(pp[k][0:NT, j * 128:(j + 1) * 128], src[:, :, col], ident[:], reads=[sk, "ident"], writes=[("pp", k)])
        P.act(LP[:].rearrange("p a b -> p (a b)"), pp[k][0:NT, :], AF.Copy, reads=[("pp", k)], writes=["LP%d" % d])
        L = "LP%d" % d
        T1 = P.sb("T1_%d" % d, [NT, 2, 128], F32)
        T2 = P.sb("T2_%d" % d, [NT, 2, 128], F32)
        Wt = P.sb("W_%d" % d, [NT, 2, 128], F32)
        Ct = P.sb("C_%d" % d, [NT, 2, 128], F32)
        Mt = P.sb("M_%d" % d, [NT, 2, 128], F32)
        Et = P.sb("E_%d" % d, [NT, 2, 128], F32)
        n1, n2, nW, nC, nM, nE = ["%s_%d" % (s, d) for s in ("T1", "T2", "W", "C", "M", "E")]
        pf = LP[:, 2:4, :]
        li = LP[:, 0:2, :]
        P.act(T1[:], pf, AF.Abs, reads=[L], writes=[n1])
        P.act(T1[:], T1[:], AF.Exp, reads=[n1], writes=[n1], scale=-1.0)
        P.act(T1[:], T1[:], AF.Ln, reads=[n1], writes=[n1], bias=1.0)
        dve(P, lambda e: e.tensor_single_scalar(T2[:], pf, 0.0, ALU.min), [L], [n2])
        dve(P, lambda e: e.tensor_tensor(T2[:], T2[:], T1[:], ALU.subtract), [n1, n2], [n2])
        for hl in range(2):
            dve(P, lambda e, hl=hl: e.tensor_tensor_scan(Wt[:, hl, :], onesf[:], T2[:, hl, :], 0.0, ALU.mult,
                                                         ALU.add), [n2, "onesf"], [nW])
        r = P.sb("r_%d" % d, [2, NT], F32)
        rs = P.sb("rs_%d" % d, [2, NT], F32)
        tot = P.sb("tot_%d" % d, [2, 1], F32)
        car = P.sb("car_%d" % d, [NT, 2], F32)
        P.mm(pp[k][0:2, 0:NT], Wt[:, :, 127], ident[0:NT, 0:NT], reads=[nW, "ident"], writes=[("pp", k)])
        P.act(r[:], pp[k][0:2, 0:NT], AF.Copy, reads=[("pp", k)], writes=["r%d" % d])
        onesr = onesf[0:2, 0:NT]
        dve(P, lambda e: e.tensor_tensor_scan(rs[:], onesr, r[:], 0.0, ALU.mult, ALU.add), ["r%d" % d, "onesf"],
            ["rs%d" % d])
        if d == 0:
            dve(P, lambda e: e.tensor_tensor(rs[:], rs[:], r[:], ALU.subtract), ["rs%d" % d, "r%d" % d], ["rs%d" % d])
        else:
            dve(P, lambda e: e.tensor_copy(tot[:], rs[:, NT - 1:NT]), ["rs%d" % d], ["tot%d" % d])
            dve(P, lambda e: e.tensor_scalar(rs[:], rs[:], -1.0, tot[:, 0:1], ALU.mult, ALU.add),
                ["rs%d" % d, "tot%d" % d], ["rs%d" % d])
        P.mm(pp[k][0:NT, 0:2], rs[:], ident[0:2, 0:2], reads=["rs%d" % d, "ident"], writes=[("pp", k)])
        P.act(car[:], pp[k][0:NT, 0:2], AF.Copy, reads=[("pp", k)], writes=["car%d" % d])
        for hl in range(2):
            dve(P, lambda e, hl=hl: e.tensor_scalar(Wt[:, hl, :], Wt[:, hl, :], car[:, hl:hl + 1], None, ALU.add),
                [nW, "car%d" % d], [nW])
        dve(P, lambda e: e.tensor_tensor(Ct[:], li, Wt[:], ALU.subtract), [L, nW], [nC])
        for hl in range(2):
            dve(P, lambda e, hl=hl: e.tensor_tensor_scan(Mt[:, hl, :], Ct[:, hl, :], Ct[:, hl, :], -1e30, ALU.max,
                                                         ALU.max), [nC], [nM])
        mr = P.sb("mr_%d" % d, [2, NT], F32)
        mr2 = P.sb("mr2_%d" % d, [2, NT], F32)
        mx = P.sb("mx_%d" % d, [2, NT], F32)
        cmx = P.sb("cmx_%d" % d, [NT, 2], F32)
        P.mm(pp[k][0:2, 0:NT], Mt[:, :, 127], ident[0:NT, 0:NT], reads=[nM, "ident"], writes=[("pp", k)])
        P.act(mr[:], pp[k][0:2, 0:NT], AF.Copy, reads=[("pp", k)], writes=["mr%d" % d])
        dve(P, lambda e: e.memset(mx[:], -1e30), [], ["mx%d" % d])
        if NT > 1:
            if d == 0:
                dve(P, lambda e: e.tensor_tensor_scan(mr2[:], mr[:], mr[:], -1e30, ALU.max, ALU.max), ["mr%d" % d],
                    ["mr2%d" % d])
                dve(P, lambda e: e.tensor_copy(mx[:, 1:NT], mr2[:, 0:NT - 1]), ["mr2%d" % d, "mx%d" % d], ["mx%d" % d])
            else:
                cur, ck_, oth, ok_ = mr, "mr%d" % d, mr2, "mr2%d" % d
                s = 1
                while s < NT:
                    dve(P, lambda e, cur=cur, oth=oth, s=s: e.tensor_tensor(oth[:, 0:NT - s], cur[:, 0:NT - s],
                                                                        cur[:, s:NT], ALU.max), [ck_], [ok_])
                    dve(P, lambda e, cur=cur, oth=oth, s=s: e.tensor_copy(oth[:, NT - s:NT], cur[:, NT - s:NT]),
                        [ck_, ok_], [ok_])
                    cur, ck_, oth, ok_ = oth, ok_, cur, ck_
                    s *= 2
                dve(P, lambda e, cur=cur: e.tensor_copy(mx[:, 0:NT - 1], cur[:, 1:NT]), [ck_, "mx%d" % d], ["mx%d" % d])
        P.mm(pp[k][0:NT, 0:2], mx[:], ident[0:2, 0:2], reads=["mx%d" % d, "ident"], writes=[("pp", k)])
        P.act(cmx[:], pp[k][0:NT, 0:2], AF.Copy, reads=[("pp", k)], writes=["cmx%d" % d])
        for hl in range(2):
            dve(P, lambda e, hl=hl: e.tensor_scalar(Mt[:, hl, :], Mt[:, hl, :], cmx[:, hl:hl + 1], None, ALU.max),
                [nM, "cmx%d" % d], [nM])
        dve(P, lambda e: e.tensor_scalar(Mt[:], Mt[:], -1.0, 0.0, ALU.mult, ALU.min), [nM], [nM])
        dve(P, lambda e: e.tensor_tensor(Et[:], Mt[:], Wt[:], ALU.subtract), [nM, nW], [nE])
        P.act(Et[:], Et[:], AF.Exp, reads=[nE], writes=[nE])
        if d == 0:
            for j, (src_t, sn) in enumerate(((Ct, nC), (Ct, nC), (Et, nE), (Et, nE))):
                P.mm(pp[k][:, j * NT:(j + 1) * NT], src_t[:, j % 2, :], ident[0:NT, 0:NT], reads=[sn, "ident"],
                     writes=[("pp", k)])
            P.act(tok[0][:].rearrange("p a b -> p (a b)"), pp[k][:, 0:4 * NT], AF.Copy, reads=[("pp", k)],
                  writes=["tok0"])
            dve(P, lambda e: e.tensor_copy(A[0][:], Mt[:]), [nM], ["A0"])
        else:
            Yb = P.sb("Yb", [128, 6, NT], F32)
            for j, (src_t, sn) in enumerate(((Ct, nC), (Ct, nC), (Et, nE), (Et, nE), (Mt, nM), (Mt, nM))):
                P.mm(pp[k][:, j * NT:(j + 1) * NT], src_t[:, j % 2, :], ident[0:NT, 0:NT], reads=[sn, "ident"],
                     writes=[("pp", k)])
            P.act(Yb[:].rearrange("p a b -> p (a b)"), pp[k][:, 0:6 * NT], AF.Copy, reads=[("pp", k)], writes=["Yb"])
            P.mm(pp[k][:, 0:4 * NT], anti[:], Yb[:, 0:4, :].rearrange("p a b -> p (a b)"), reads=["anti", "Yb"],
                 writes=[("pp", k)])
            P.act(tok[1][:].rearrange("p a b -> p (a b)"), pp[k][:, 0:4 * NT], AF.Copy, reads=[("pp", k)],
                  writes=["tok1"])
            for hl in range(2):
                P.mm(pp[k][0:NT, hl * 128:(hl + 1) * 128], Yb[:, 4 + hl, :], anti[:], reads=["Yb", "anti"],
                     writes=[("pp", k)])
            P.act(A[1][:].rearrange("p a b -> p (a b)"), pp[k][0:NT, 0:256], AF.Copy, reads=[("pp", k)], writes=["A1"])


    for d in range(2):
        gate_dir(d)

    pa = pp[1]
    wg = [P.sb("wg%d" % i, [128, 512], F32) for i in range(2)]
    pt = [P.sb("pt%d" % i, [128, 512], BF16) for i in range(3)]
    ad = P.sb("ad", [128, 4], F32)
    hs = P.sb("hs", [128, 4, 128], F32)
    junk = P.sb("junk", [128, 128], F32)
    ssq = P.sb("ssq", [128, 2], F32)
    sgo = [P.sb("sgo%d" % i, [128, 128], F32) for i in range(2)]
    yt = [P.sb("yt%d" % i, [128, 4, 128], BF16) for i in range(2)]
    cnt = 0
    ycnt = 0
    pcnt = 0
    for hl in range(2):
        for g in range(NQ):
            qsl = slice(g * 512, (g + 1) * 512)
            for d in range(2):
                pob = po[pcnt % 2]
                pok = ("po", pcnt % 2)
                pcnt += 1
                for tl in range(4):
                    P.mm(pa[:, tl * 128:(tl + 1) * 128], sel2[0:NT, (4 * g + tl) * 128:(4 * g + tl + 1) * 128], A[d][:, hl, :], reads=["sel", "A%d" % d],
                         writes=[("pp", 1)])
                P.mm(pob[:].rearrange("p a b -> p (a b)"), zt[:, 0:128], zt[:, 0:512], start=True, stop=False,
                     reads=["zt"], writes=[pok])
                P.mm(pd[:, 0:4], zt[:, 0:128], zt[:, 0:4], start=True, stop=False, reads=["zt"], writes=["pd"])
                tiles = list(range(0, 4 * g + 4)) if d == 0 else list(range(4 * g, NT))
                def mfront(ti, base, tiles=tiles, d=d, hl=hl, g=g, qsl=qsl):
                    t = tiles[ti]
                    b, b3 = (base + ti) % 2, (base + ti) % 3
                    tp = t - 4 * g
                    diag = 0 <= tp <= 3
                    P.mm(st[b][:], kz[hl][:, t * 128:(t + 1) * 128], qT[:, qsl], reads=["kz%d" % hl, ("qT", g)],
                         writes=[("st", b)])
                    if diag:
                        dve(P, lambda e: e.tensor_scalar(wg[b][:], pa[:], tok[d][:, hl, t:t + 1], 0.0, ALU.add, ALU.min),
                            [("pp", 1), "tok%d" % d], [("wg", b)])
                        P.act(wg[b][:], wg[b][:], AF.Exp, reads=[("wg", b)], writes=[("wg", b)])
                        dve(P, lambda e: e.tensor_tensor(wg[b][:], wg[b][:], mask[:, d, tp, :], ALU.mult),
                            [("wg", b), "mask"], [("wg", b)])
                    else:
                        P.act(wg[b][:], pa[:], AF.Exp, reads=[("pp", 1), "tok%d" % d], writes=[("wg", b)],
                              bias=tok[d][:, hl, t:t + 1])
                    dve(P, lambda e: e.tensor_tensor(pt[b3][:], st[b][:], wg[b][:], ALU.mult),
                        [("st", b), ("wg", b)], [("pt", b3)])

                def mback(ti, base, tiles=tiles, d=d, hl=hl, g=g, pob=pob, pok=pok):
                    t = tiles[ti]
                    b3 = (base + ti) % 3
                    last = (ti == len(tiles) - 1)
                    tp = t - 4 * g
                    diag = 0 <= tp <= 3
                    for qs in range(4):
                        if diag and ((d == 0 and qs < tp) or (d == 1 and qs > tp)):
                            continue
                        P.mm(pob[:, qs, :], pt[b3][:, qs * 128:(qs + 1) * 128], vt[:, t, hl * 128:(hl + 1) * 128],
                             start=False, stop=(last and qs == 3), reads=[("pt", b3), "vt"], writes=[pok])
                        P.mm(pd[:, qs:qs + 1], pt[b3][:, qs * 128:(qs + 1) * 128], ones[:, 0:1], start=False,
                             stop=(last and qs == 3), reads=[("pt", b3), "ones"], writes=["pd"])

                mfront(0, cnt)
                for ti in range(len(tiles)):
                    if ti + 1 < len(tiles):
                        mfront(ti + 1, cnt)
                    mback(ti, cnt)
                cnt += len(tiles)
                P.act(ad[:], pd[:, 0:4], AF.Abs, reads=["pd"], writes=["ad"])
                dve(P, lambda e, d=d, hl=hl, g=g: e.tensor_tensor(ad[:], ad[:], tok[d][:, 2 + hl, 4 * g:4 * g + 4],
                                                              ALU.max), ["ad", "tok%d" % d], ["ad"])
                dve(P, lambda e: e.reciprocal(ad[:], ad[:]), ["ad"], ["ad"])
                for qs in range(4):
                    if d == 0:
                        P.act(hs[:, qs, :], pob[:, qs, :], AF.Copy, reads=[pok, "ad"], writes=["hs"],
                              scale=ad[:, qs:qs + 1])
                    else:
                        dve(P, lambda e, qs=qs, pob=pob: e.scalar_tensor_tensor(hs[:, qs, :], pob[:, qs, :],
                                                                            ad[:, qs:qs + 1], hs[:, qs, :], ALU.mult,
                                                                            ALU.add), [pok, "ad", "hs"], ["hs"])
            yb = yt[ycnt % 2]
            ykey = ("yt", ycnt % 2)
            ycnt += 1
            for qs in range(4):
                k2 = qs % 2
                t = 4 * g + qs
                proj_tm(P, pp[0][:, 0:128], ("pp", 0), wbf, 520 + hl * 128, 128, xnT, t)
                P.act(sgo[k2][:], pp[0][:, 0:128], AF.Sigmoid, reads=[("pp", 0)], writes=[("sgo", k2)])
                P.op("pool", lambda e, k2=k2: e.tensor_tensor(sgo[k2][:], sgo[k2][:], gb[:], ALU.mult),
                     reads=[("sgo", k2), "gb"], writes=[("sgo", k2)])
                P.act(junk[:], hs[:, qs, :], AF.Square, reads=["hs"], writes=["junk", ("ssq", k2)],
                      accum_out=ssq[:, k2:k2 + 1])
                P.act(ssq[:, k2:k2 + 1], ssq[:, k2:k2 + 1], AF.Sqrt, reads=[("ssq", k2)], writes=[("ssq", k2)],
                      scale=1.0 / 128, bias=1e-6)
                dve(P, lambda e, k2=k2: e.reciprocal(ssq[:, k2:k2 + 1], ssq[:, k2:k2 + 1]), [("ssq", k2)],
                    [("ssq", k2)])
                dve(P, lambda e, qs=qs, k2=k2, yb=yb: e.scalar_tensor_tensor(yb[:, qs, :], hs[:, qs, :],
                                                                         ssq[:, k2:k2 + 1], sgo[k2][:], ALU.mult,
                                                                         ALU.mult),
                    ["hs", ("ssq", k2), ("sgo", k2)], [ykey])
            P.dma("sp", y[g * 512:(g + 1) * 512, hl * 128:(hl + 1) * 128].rearrange("(q p) e -> p q e", p=128), yb[:],
                  ykey, reads=[ykey], is_out=True)
    print("mlstm stats", P.stats())
    return (P.finish() if own else None)


def mlstm_inputs(xTb, g1l, w_in_l, gate_b_l, ng_l, hh, S=S):
    NT = S // 128
    offs = np.cumsum((0,) + (1536, 512, 16, 512, 512, 512, 512, 128, 128, 256, 256, 512, 16, 512))
    cq, ck, cv, cif, co = offs[9], offs[10], offs[11], offs[12], offs[13]
    hs_ = [2 * hh, 2 * hh + 1]
    qc = np.concatenate([cq + h * 64 + np.arange(64) for h in hs_])
    kc = np.concatenate([ck + h * 64 + np.arange(64) for h in hs_])
    vc = np.concatenate([cv + h * 128 + np.arange(128) for h in hs_])
    oc = np.concatenate([co + h * 128 + np.arange(128) for h in hs_])
    gsel = [(i_f, dr, h) for i_f in range(2) for dr in range(2) for h in hs_]
    gc = np.array([cif + i_f * 8 + dr * 4 + h for (i_f, dr, h) in gsel])
    gbv = np.array([gate_b_l[i_f, dr, h] for (i_f, dr, h) in gsel], np.float32)
    cols = np.concatenate([qc, kc, vc, gc, oc])
    assert len(cols) == D_NCOLS
    ident = np.eye(128, dtype=np.float32)
    anti = np.ascontiguousarray(ident[::-1])
    sel = np.zeros((NT, NT, 128), np.float32)
    for k in range(NT):
        sel[k, k, :] = 1.0
    j = np.arange(128)[:, None]
    i = np.arange(512)[None, :]
    mask = np.zeros((128, 2, 4, 512), np.float32)
    for tp in range(4):
        mask[:, 0, tp, :] = (128 * tp + j <= i)
        mask[:, 1, tp, :] = (128 * tp + j >= i)
    gbias = np.ascontiguousarray(np.tile(gbv[None, None, :], (128, NT, 1)).reshape(128, NT * 8))
    gb = np.ascontiguousarray(np.tile(ng_l[None, :], (128, 1)).astype(np.float32))
    return dict(xT=xTb, g1=g1l, w=wlayout(w_in_l, cols), ident=ident, anti=anti,
                sel=np.ascontiguousarray(sel.reshape(NT, NT * 128)),
                mask=np.ascontiguousarray(mask.reshape(128, -1)).astype(ml_dtypes.bfloat16), gbias=gbias, gb=gb)


A_NCOLS = 256 * 4 + 8
PTG = 256


def prologue_small(P, xT, g1, wsrc, ncols, S, wst=None):
    xnT = P.sb("xnT", [128, 8, S], BF16)
    wbf = P.sb("wbf", [128, 8, ncols], BF16)
    xs = P.sb("xs0", [128, 8, PTG], F32)
    sq = P.sb("sq", [128, 8, PTG], BF16)
    rstd = P.sb("rstd", [128, 512], F32)
    ones = P.sb("ones", [128, 128], BF16)
    g1s = P.sb("g1s", [128, 8], F32)
    pss = P.ps("pss", [128, 512], F32)
    P.op("pool", lambda e: e.memset(ones[:], 1.0), writes=["ones"])
    P.dma("sp", g1s[:], g1, "g1s", writes=["g1s"])
    if wst is None:
        wst = [P.sb("wst%d" % i, [128, ncols], F32)[:] for i in range(2)]
    for c in range(8):
        i = c % 2
        P.dma("sp", wst[i], wsrc[:, c * ncols:(c + 1) * ncols], ("wst", i), writes=[("wst", i)])
        P.op("pool", lambda e, i=i, c=c: e.tensor_copy(wbf[:, c, :], wst[i]), reads=[("wst", i)],
             writes=["wbf"])
    xn_src = (getattr(P, "override", None) or {}).get("xnT_src")
    if xn_src is not None:
        for g in range(S // TG):
            tsl = slice(g * TG, (g + 1) * TG)
            P.dma("sp", xnT[:, :, tsl], xn_src[:, :, tsl], ("xnld", g), writes=[("xnT", g)])
        P.xs0 = xs
        return xnT, wbf, ones, pss, rstd
    for g in range(S // PTG):
        tsl = slice(g * PTG, (g + 1) * PTG)
        P.dma("sp", xs[:], xT[:, :, tsl], ("xs", 0), writes=[("xs", 0)])
        P.act(sq[:], xs[:], AF.Square, reads=[("xs", 0)], writes=["sq"])
        for c in range(8):
            P.mm(pss[:, 0:PTG], ones[:], sq[:, c, :], start=(c == 0), stop=(c == 7), reads=["ones", "sq"],
                 writes=["pss"])
        P.act(rstd[:, 0:PTG], pss[:, 0:PTG], AF.Sqrt, reads=["pss"], writes=["rstd"], scale=1.0 / 1024, bias=1e-6)
        P.op("dve", lambda e: e.reciprocal(rstd[:, 0:PTG], rstd[:, 0:PTG]), reads=["rstd"], writes=["rstd"])
        for c in range(8):
            P.op("dve", lambda e, c=c, tsl=tsl: e.scalar_tensor_tensor(
                xnT[:, c, tsl], xs[:, c, :], g1s[:, c:c + 1], rstd[:, 0:PTG], ALU.mult, ALU.mult),
                reads=[("xs", 0), "g1s", "rstd"], writes=[("xnT", (g * PTG) // TG)])
    P.xs0 = xs
    return xnT, wbf, ones, pss, rstd


def build_gdn(S=S, P=None):
    import os
    STOP = int(os.environ.get("GDN_STOP", "99"))
    PST = int(os.environ.get("GDN_PST", "99"))
    LST = int(os.environ.get("GDN_LST", "99"))
    VAR = int(os.environ.get("GDN_VAR", "0"))
    own = P is None
    if own:
        P = Prog()
    NT = S // 128
    NG = S // TG
    NC = S // 64
    xT = P.dram("xT", [128, 8, S], F32, "ExternalInput")
    g1 = P.dram("g1", [128, 8], F32, "ExternalInput")
    w = P.dram("w", [128, 8 * A_NCOLS], F32, "ExternalInput")
    convd = P.dram("convw", [128, 6, 5], F32, "ExternalInput")
    identd = P.dram("ident", [128, 128], F32, "ExternalInput")
    antid = P.dram("anti", [128, 128], F32, "ExternalInput")
    maskd = P.dram("mask", [128, 4 * 128], BF16, "ExternalInput")
    biasd = P.dram("gbias", [128, NT * 8], F32, "ExternalInput")
    alogd = P.dram("alog", [128, 4], F32, "ExternalInput")
    rmd = P.dram("rmask", [NT, 128], F32, "ExternalInput")
    gbd = P.dram("gb", [128, 128], F32, "ExternalInput")
    y = P.dram("y", [S, 256], BF16, "ExternalOutput")

    raw = P.sb("raw", [128, S + 4], F32)
    half = (S + 4) // 2
    xnT, wbf, ones, pss, rstd = prologue_small(P, xT, g1, w, A_NCOLS, S,
                                               wst=[raw[:, 0:A_NCOLS], raw[:, half:half + A_NCOLS]] if half >= A_NCOLS else None)
    ident = P.sb("ident", [128, 128], F32)
    anti = P.sb("anti", [128, 128], F32)
    identb = P.sb("identb", [128, 128], BF16)
    mask = P.sb("mask", [128, 4, 128], BF16)
    gbias = P.sb("gbias", [128, NT * 8], F32)
    nea = P.sb("nea", [128, 4], F32)
    rm = P.sb("rm", [NT, 128], F32)
    gb = P.sb("gb", [128, 128], F32)
    convw = P.sb("convw", [128, 6, 5], F32)
    P.dma("sp", ident[:], identd, "ident", writes=["ident"])
    P.dma("sp", anti[:], antid, "anti", writes=["anti"])
    P.dma("sp", mask[:].rearrange("p a b -> p (a b)"), maskd, "mask", writes=["mask"])
    P.dma("sp", gbias[:], biasd, "gbias", writes=["gbias"])
    P.dma("sp", nea[:], alogd, "nea", writes=["nea"])
    P.dma("sp", rm[:], rmd, "rm", writes=["rm"])
    P.dma("sp", gb[:], gbd, "gb", writes=["gb"])
    P.dma("sp", convw[:], convd, "convw", writes=["convw"])
    P.op("pool", lambda e: e.tensor_copy(identb[:], ident[:]), reads=["ident"], writes=["identb"])
    P.act(nea[:], nea[:], AF.Exp, reads=["nea"], writes=["nea"])
    P.op("dve", lambda e: e.tensor_scalar(nea[:], nea[:], -1.0, None, ALU.mult), reads=["nea"], writes=["nea"])
    onesf = P.sb("onesf", [NT, 128], F32)
    P.op("pool", lambda e: e.memset(onesf[:], 1.0), writes=["onesf"])

    pp = [P.ps("pp%d" % i, [128, 512], F32) for i in range(2)]
    ptr = P.ps("ptr", [128, 4, 128], BF16)

    qT = [P.sb("qT%d" % h, [128, S], BF16) for h in range(2)]
    kT = [P.sb("kT%d" % h, [128, S], BF16) for h in range(2)]
    vtok = [P.sb("vtok%d" % h, [128, NT, 128], BF16) for h in range(2)]
    sz = P.sb("sz", [128, NT, 256], BF16)
    xsf = P.xs0[:].rearrange("p a b -> p (a b)")
    acc = [xsf[:, i * TG:(i + 1) * TG] for i in range(2)]
    sil = [xsf[:, (2 + i) * TG:(3 + i) * TG] for i in range(2)]
    P.op("pool", lambda e: e.memset(xsf[:, 0:4 * TG], 0.0),
         writes=[("xs", 0), ("acc", 0), ("acc", 1), ("sil", 0), ("sil", 1)])
    sqg = P.sb("sqg", [128, TG], BF16)
    vTg = P.sb("vTg", [128, TG], BF16)
    P.op("pool", lambda e: e.memset(raw[:, 0:2], 0.0), reads=["wbf"], writes=["rawpad"])
    P.op("pool", lambda e: e.memset(raw[:, S + 2:S + 4], 0.0), reads=["wbf"], writes=["rawpad"])
    allraw = [("raw", g) for g in range(NG)]
    for ci in range(6):
        kind, hl = ci // 2, ci % 2
        for g in range(NG):
            k = g % 2
            proj_fm(P, pp[k], ("pp", k), wbf, ci * 128, 128, xnT, g)
            P.act(raw[:, 2 + g * TG:2 + (g + 1) * TG], pp[k][:], AF.Copy, reads=[("pp", k)], writes=[("raw", g)])
        for g in range(NG):
            k = g % 2
            a_, s_ = acc[k], sil[k]
            nb = [("raw", gg) for gg in (g - 1, g, g + 1) if 0 <= gg < NG] + ["rawpad", "convw"]
            base = g * TG
            P.op("dve", lambda e, a_=a_, base=base, ci=ci: e.tensor_scalar(a_, raw[:, base:base + TG],
                                                                        convw[:, ci, 0:1], None, ALU.mult),
                 reads=nb, writes=[("acc", k)])
            for tap in range(1, 5):
                P.op("dve", lambda e, a_=a_, base=base, ci=ci, tap=tap: e.scalar_tensor_tensor(
                    a_, raw[:, base + tap:base + tap + TG], convw[:, ci, tap:tap + 1], a_, ALU.mult, ALU.add),
                    reads=nb + [("acc", k)], writes=[("acc", k)])
            if kind < 2:
                P.act(s_, a_, AF.Silu, reads=[("acc", k)], writes=[("sil", k)])
                P.op("pool", lambda e, s_=s_: e.tensor_tensor(sqg[:], s_, s_, ALU.mult), reads=[("sil", k)],
                     writes=["sqg"])
                P.mm(pss[:], ones[:], sqg[:], reads=["ones", "sqg"], writes=["pss"])
                P.act(rstd[:], pss[:], AF.Sqrt, reads=["pss"], writes=["rstd"], bias=1e-6)
                P.op("dve", lambda e: e.reciprocal(rstd[:], rstd[:]), reads=["rstd"], writes=["rstd"])
                dst = (qT if kind == 0 else kT)[hl]
                dkey = ("qT%d" % hl if kind == 0 else "kT%d" % hl, g)
                sc = (128 ** -0.5) if kind == 0 else 1.0
                P.op("dve", lambda e, s_=s_, dst=dst, g=g, sc=sc: e.scalar_tensor_tensor(
                    dst[:, g * TG:(g + 1) * TG], s_, sc, rstd[:], ALU.mult, ALU.mult),
                    reads=[("sil", k), "rstd"], writes=[dkey])
            else:
                P.act(vTg[:], a_, AF.Silu, reads=[("acc", k)], writes=["vTg"])
                for j in range(4):
                    P.tr(ptr[:, j, :], vTg[:, j * 128:(j + 1) * 128], identb[:], reads=["vTg", "identb"],
                         writes=["ptr"])
                P.op("pool" if False else "act", lambda e, hl=hl, g=g: e.copy(
                    vtok[hl][:, 4 * g:4 * g + 4, :], ptr[:]), reads=["ptr"], writes=["vtok%d" % hl])
    if STOP <= 1:
        return (P.finish() if own else None)
    for t in range(NT):
        k = t % 2
        proj_tm(P, pp[k][:, 0:256], ("pp", k), wbf, 768, 256, xnT, t)
        P.act(sz[:, t, :], pp[k][:, 0:256], AF.Silu, reads=[("pp", k)], writes=["sz"])
    for t in range(NT):
        proj_tm(P, pp[0][:, t * 8:(t + 1) * 8], ("pp", 0), wbf, 1024, 8, xnT, t)
    gtok = P.sb("gtok", [128, NT, 8], F32)
    gtokR = P.sb("gtokR", [128, NT, 8], F32)
    dve(P, lambda e: e.tensor_tensor(gtok[:].rearrange("p a b -> p (a b)"), pp[0][:, 0:NT * 8], gbias[:], ALU.add),
        [("pp", 0), "gbias"], ["gtok"])
    P.mm(pp[1][:, 0:NT * 8], anti[:], gtok[:].rearrange("p a b -> p (a b)"), reads=["anti", "gtok"],
         writes=[("pp", 1)])
    P.act(gtokR[:].rearrange("p a b -> p (a b)"), pp[1][:, 0:NT * 8], AF.Copy, reads=[("pp", 1)], writes=["gtokR"])

    if STOP <= 2:
        return (P.finish() if own else None)
    tokq = [P.sb("tokq%d" % d, [128, 2, 6, NT], F32) for d in range(2)]
    Gc = [P.sb("Gc%d" % d, [NT, 2, 128], F32) for d in range(2)]
    egt = [P.sb("egt%d" % d, [128, 2, 2, NT], F32) for d in range(2)]

    LPs = P.sb("LPs", [NT, 4, 128], F32)
    gtmp = [P.sb("gtmp%d" % i, [NT, 2, 128], F32) for i in range(5)]
    TBs = P.sb("TBs", [NT, 128], F32)
    Yb = P.sb("Yb", [128, 8, NT], F32)

    def gate_dir(d):
        src = gtok if d == 0 else gtokR
        sk = "gtok" if d == 0 else "gtokR"
        k = d
        LP = LPs
        L = "LP"
        for j in range(4):
            col = (0, 1, 4, 5)[j] + 2 * d
            P.mm(pp[k][0:NT, j * 128:(j + 1) * 128], src[:, :, col], ident[:], reads=[sk, "ident"], writes=[("pp", k)])
        P.act(LP[:].rearrange("p a b -> p (a b)"), pp[k][0:NT, :], AF.Copy, reads=[("pp", k)], writes=[L])
        T1, Gt, Bt, Et, Rt = gtmp
        n1, nG, nB, nE, nR = ["%s_s" % s for s in ("T1", "G", "B", "E", "R")]
        al = LP[:, 0:2, :]
        be = LP[:, 2:4, :]
        P.act(T1[:], al, AF.Abs, reads=[L], writes=[n1])
        P.act(T1[:], T1[:], AF.Exp, reads=[n1], writes=[n1], scale=-1.0)
        P.act(T1[:], T1[:], AF.Ln, reads=[n1], writes=[n1], bias=1.0)
        dve(P, lambda e: e.tensor_single_scalar(Gt[:], al, 0.0, ALU.max), [L], [nG])
        dve(P, lambda e: e.tensor_tensor(Gt[:], Gt[:], T1[:], ALU.add), [nG, n1], [nG])
        for hl in range(2):
            dve(P, lambda e, hl=hl: e.tensor_scalar(Gt[:, hl, :], Gt[:, hl, :], nea[0:NT, 2 * d + hl:2 * d + hl + 1],
                                                    None, ALU.mult), [nG, "nea"], [nG])
        P.act(Bt[:], be, AF.Sigmoid, reads=[L], writes=[nB])
        for hl in range(2):
            dve(P, lambda e, hl=hl: e.tensor_tensor_scan(T1[:, hl, :], rm[:], Gt[:, hl, :], 0.0, ALU.mult, ALU.add),
                [nG, "rm", n1], [n1])
        for hl in range(2):
            for hf in range(2):
                dve(P, lambda e, hl=hl, hf=hf: e.tensor_scalar(
                    Rt[:, hl, hf * 64:(hf + 1) * 64], T1[:, hl, hf * 64:(hf + 1) * 64], -1.0,
                    T1[:, hl, hf * 64 + 63:hf * 64 + 64], ALU.mult, ALU.add), [n1], [nR])
        P.act(Et[:], T1[:], AF.Exp, reads=[n1], writes=[nE])
        P.act(Rt[:], Rt[:], AF.Exp, reads=[nR], writes=[nR])
        TB = TBs
        for hl in range(2):
            for hf in range(2):
                ah = hf if d == 0 else 1 - hf
                dve(P, lambda e, hl=hl, hf=hf: e.tensor_scalar(TB[:], onesf[:], T1[:, hl, hf * 64 + 63:hf * 64 + 64],
                                                           None, ALU.mult), [n1, "onesf", ("pp", k)], ["TBs"])
                P.mm(pp[k][:, (hl * 2 + ah) * NT:(hl * 2 + ah + 1) * NT], TB[:], ident[0:NT, 0:NT],
                     reads=["TBs", "ident"], writes=[("pp", k)])
        P.act(egt[d][:].rearrange("p a b c -> p (a b c)"), pp[k][:, 0:4 * NT], AF.Exp, reads=[("pp", k)],
              writes=["egt%d" % d])
        srcs = ((T1, n1), (Bt, nB), (Et, nE), (Rt, nR))
        if d == 0:
            for hl in range(2):
                for qi, (tt, nn) in enumerate(srcs):
                    P.mm(pp[k][:, (hl * 4 + qi) * NT:(hl * 4 + qi + 1) * NT], tt[:, hl, :], ident[0:NT, 0:NT],
                         reads=[nn, "ident"], writes=[("pp", k)])
            for hl in range(2):
                P.act(tokq[0][:, hl, 0:4, :].rearrange("p a b -> p (a b)"), pp[k][:, hl * 4 * NT:(hl + 1) * 4 * NT],
                      AF.Copy, reads=[("pp", k)], writes=["tokq0"])
            dve(P, lambda e: e.tensor_copy(Gc[0][:], T1[:]), [n1], ["Gc0"])
        else:
            for hl in range(2):
                for qi, (tt, nn) in enumerate(srcs):
                    P.mm(pp[k][:, (hl * 4 + qi) * NT:(hl * 4 + qi + 1) * NT], tt[:, hl, :], ident[0:NT, 0:NT],
                         reads=[nn, "ident"], writes=[("pp", k)])
            P.act(Yb[:].rearrange("p a b -> p (a b)"), pp[k][:, 0:8 * NT], AF.Copy, reads=[("pp", k)], writes=["Yb"])
            P.mm(pp[k][:, 0:8 * NT], anti[:], Yb[:].rearrange("p a b -> p (a b)"), reads=["anti", "Yb"],
                 writes=[("pp", k)])
            for hl in range(2):
                P.act(tokq[1][:, hl, 0:4, :].rearrange("p a b -> p (a b)"), pp[k][:, hl * 4 * NT:(hl + 1) * 4 * NT],
                      AF.Copy, reads=[("pp", k)], writes=["tokq1"])
            for hl in range(2):
                P.mm(pp[k][0:NT, hl * 128:(hl + 1) * 128], Yb[:, hl * 4 + 0, :], anti[:], reads=["Yb", "anti"],
                     writes=[("pp", k)])
            P.act(Gc[1][:].rearrange("p a b -> p (a b)"), pp[k][0:NT, 0:256], AF.Copy, reads=[("pp", k)],
                  writes=["Gc1"])
        for hl in range(2):
            dve(P, lambda e, hl=hl: e.tensor_scalar(tokq[d][:, hl, 4, :], tokq[d][:, hl, 0, :], -1.0, None, ALU.mult),
                ["tokq%d" % d], ["tokq%d" % d])
            dve(P, lambda e, hl=hl: e.tensor_scalar(tokq[d][:, hl, 5, :], tokq[d][:, hl, 2, :], -1.0, None, ALU.mult),
                ["tokq%d" % d], ["tokq%d" % d])

    for d in range(2):
        gate_dir(d)
        if STOP <= 3 + d:
            return (P.finish() if own else None)

    XK = [("xnT", g) for g in range(NG)]
    slot = lambda i: xnT[:, i, :].rearrange("p (t c) -> p t c", c=128)
    TI = [slot(0), slot(1)]
    QK = [slot(2), slot(3)]
    KD = [slot(4), slot(5)]
    pw = pp[0]
    pn = pp[1]
    cA = [P.ps("cA%d" % i, [128, 4, 128], F32) for i in range(2)]
    cSb = [P.ps("cS%d" % i, [128, 128], F32) for i in range(2)]
    Dm = [P.sb("Dm%d" % i, [128, 128], F32) for i in range(2)]
    DT = [P.sb("DT%d" % i, [128, 128], F32) for i in range(2)]
    A0 = [P.sb("A0_%d" % i, [128, 128], BF16) for i in range(2)]
    IA = [P.sb("IA_%d" % i, [128, 128], BF16) for i in range(2)]
    AK = [P.sb("AK_%d" % i, [128, 128], BF16) for i in range(2)]
    NK = [P.sb("NK_%d" % i, [128, 128], BF16) for i in range(2)]
    PT = [P.sb("PT_%d" % i, [128, 128], BF16) for i in range(2)]
    oacc = raw[:, 0:S].rearrange("p (t c) -> p t c", c=128)
    gsel = [P.sb("gsel%d" % i, [NT, 128], F32) for i in range(2)]
    St = [P.sb("St%d" % i, [128, 128], BF16) for i in range(2)]
    Xp = [[P.sb("Xp%d_%d" % (i, hf), [128, 128], BF16) for hf in range(2)] for i in range(2)]
    Vn = [[P.sb("Vn%d_%d" % (i, hf), [128, 128], BF16) for hf in range(2)] for i in range(2)]
    otmp = [P.sb("otmp%d" % i, [128, 128], F32) for i in range(2)]
    junk_ = rstd[:, 256:384]
    ssq = P.sb("ssq", [128, 2], F32)
    gm = [rstd[:, i * 128:(i + 1) * 128] for i in range(2)]
    P.op("pool", lambda e: e.memset(rstd[:, 0:384], 0.0), writes=["rstd", ("gm", 0), ("gm", 1), "junk"])
    yt = [P.sb("yt%d" % i, [128, 128], BF16) for i in range(2)]
    fence = P.sb("fence", [128, 2], F32)
    for i in range(2):
        for hf in range(2):
            P.op("pool", lambda e, i=i, hf=hf: e.memset(Xp[i][hf][:], 0.0), writes=[("Xp", i, hf)])
            P.op("pool", lambda e, i=i, hf=hf: e.memset(Vn[i][hf][:], 0.0), writes=[("Vn", i, hf)])

    cAf = [cA[i][:].rearrange("p a b -> p (a b)") for i in range(2)]
    PW = [pp[0][:], cAf[0]]
    PN = [pp[1][:], cAf[1]]

    def precompute(hl, d, t, b):
        tsl = slice(t * 128, (t + 1) * 128)
        pw, pn = PW[b], PN[b]
        pwk, pnk = (("pp", 0), ("pp", 1)) if b == 0 else (("cA", 0), ("cA", 1))
        pwr, pnr, ptrr = ("pw_rd", b), ("pn_rd", b), "ptr_rd"
        ptk = "ptr"
        tq = tokq[d]
        gc_p, be_p, eg_p, er_p, ngc_p, neg_p = [tq[:, hl, qi, t:t + 1] for qi in range(6)]
        dve(P, lambda e: e.tensor_scalar(gsel[b][:], Gc[d][:, hl, :], ident[0:NT, t:t + 1], None, ALU.mult),
            ["Gc%d" % d, "ident"], [("gsel", b)])
        P.mm(pw[:, 0:128], kT[hl][:, tsl], kT[hl][:, tsl], reads=[("kT%d" % hl, t // 4)], writes=[pwk])
        P.mm(pw[:, 128:256], kT[hl][:, tsl], qT[hl][:, tsl], reads=[("kT%d" % hl, t // 4), ("qT%d" % hl, t // 4)],
             writes=[pwk])
        P.mm(pw[:, 256:384], onesf[:], gsel[b][:], reads=["onesf", ("gsel", b)], writes=[pwk])
        yield
        P.act(Dm[b][:], pw[:, 256:384], AF.Abs, reads=[pwk, "tokq%d" % d], writes=[("Dm", b), pwr], scale=-1.0,
              bias=gc_p)
        P.act(DT[b][:], pw[:, 256:384], AF.Abs, reads=[pwk, "tokq%d" % d], writes=[("DT", b), pwr], bias=ngc_p)
        P.act(Dm[b][:], Dm[b][:], AF.Exp, reads=[("Dm", b)], writes=[("Dm", b)], scale=-1.0)
        P.act(DT[b][:], DT[b][:], AF.Exp, reads=[("DT", b)], writes=[("DT", b)], scale=-1.0)
        yield
        dve(P, lambda e: e.tensor_tensor(Dm[b][:], Dm[b][:], mask[:, d, :], ALU.mult), [("Dm", b), "mask"], [("Dm", b)])
        dve(P, lambda e: e.tensor_tensor(DT[b][:], DT[b][:], mask[:, 2 + d, :], ALU.mult), [("DT", b), "mask"],
            [("DT", b)])
        dve(P, lambda e: e.scalar_tensor_tensor(A0[b][:], pw[:, 0:128], be_p, Dm[b][:], ALU.mult, ALU.mult),
            [pwk, ("Dm", b), "tokq%d" % d], [("A0", b), pwr])
        dve(P, lambda e: e.tensor_tensor(QK[d][:, t, :], pw[:, 128:256], DT[b][:], ALU.mult),
            [pwk, ("DT", b)], XK + [("QK", d), pwr])
        yield
        P.tr(ptr[:, 2 * b, :], A0[b][:], identb[:], reads=[("A0", b), "identb"], writes=[ptk])
        yield
        P.act(NK[b][:], ptr[:, 2 * b, :], AF.Copy, reads=[ptk], writes=[("NK", b), ptrr])
        dve(P, lambda e: e.tensor_tensor(PT[b][:], identb[:], ptr[:, 2 * b, :], ALU.subtract), [ptk, "identb"],
            [("PT", b), ptrr])
        yield
        cur_a = A0[b]
        ck = ("A0", b)
        for lev in range(5):
            P.mm(pn[:, 0:128], NK[b][:], cur_a[:], reads=[("NK", b), ck], writes=[pnk])
            if lev < 4:
                P.mm(pn[:, 128:256], cur_a[:], NK[b][:], reads=[("NK", b), ck], writes=[pnk])
            yield
            dve(P, lambda e: e.tensor_tensor(IA[b][:], pn[:, 0:128], ident[:], ALU.add), [pnk, "ident"],
                [("IA", b), pnr])
            if lev < 4:
                P.act(AK[b][:], pn[:, 0:128], AF.Copy, reads=[pnk], writes=[("AK", b), pnr])
                P.act(NK[b][:], pn[:, 128:256], AF.Copy, reads=[pnk], writes=[("NK", b), pnr])
                cur_a = AK[b]
                ck = ("AK", b)
            yield
            P.mm(pn[:, 256:384], IA[b][:], PT[b][:], reads=[("IA", b), ("PT", b)], writes=[pnk])
            yield
            if lev < 4:
                dve(P, lambda e: e.tensor_copy(PT[b][:], pn[:, 256:384]), [pnk], [("PT", b), pnr])
            else:
                P.act(TI[d][:, t, :], pn[:, 256:384], AF.Copy, reads=[pnk, "tokq%d" % d],
                      writes=XK + [("TI", d), pnr], scale=be_p)
            yield
        P.tr(ptr[:, 2 * b + 1, :], kT[hl][:, tsl], identb[:], reads=[("kT%d" % hl, t // 4), "identb"], writes=[ptk])
        yield
        P.act(KD[d][:, t, :], ptr[:, 2 * b + 1, :], AF.Copy, reads=[ptk, "tokq%d" % d], writes=XK + [("KD", d), ptrr],
              scale=er_p)

    def run_interleaved(gens):
        active = list(gens)
        while active:
            for g_ in list(active):
                try:
                    next(g_)
                except StopIteration:
                    active.remove(g_)

    def chunk_step(hl, d, c):
        t, hf = c // 2, c % 2
        R = slice(64 * hf, 64 * hf + 64)
        tsl = slice(t * 128, (t + 1) * 128)
        S_ = St[d]
        sk = ("St", d)
        ca = cA[d]
        cak = ("cA", d)
        tq = tokq[d]
        P.mm(ca[:, 0, :], kT[hl][:, tsl], S_[:], reads=[("kT%d" % hl, t // 4), sk], writes=[cak])
        P.mm(ca[:, 1, :], qT[hl][:, tsl], S_[:], reads=[("qT%d" % hl, t // 4), sk], writes=[cak])
        yield
        X = Xp[d][hf]
        dve(P, lambda e: e.scalar_tensor_tensor(X[R, :], ca[R, 0, :], tq[R, hl, 5, t:t + 1], vtok[hl][R, t, :],
                                                ALU.mult, ALU.add),
            [cak, "tokq%d" % d, "vtok%d" % hl], [("Xp", d, hf), ("ca_rd", d)])
        ot = otmp[d]
        P.act(ot[R, :], ca[R, 1, :], AF.Copy, reads=[cak, "tokq%d" % d], writes=[("otmp", d), ("ca_rd", d)],
              scale=tq[R, hl, 2, t:t + 1])
        yield
        P.mm(ca[:, 2, :], TI[d][:, t, :], X[:], reads=[("TI", d), ("Xp", d, hf)], writes=[cak])
        yield
        V = Vn[d][hf]
        P.act(V[R, :], ca[R, 2, :], AF.Copy, reads=[cak], writes=[("Vn", d, hf), ("ca_rd", d)])
        yield
        P.mm(cSb[d][:], KD[d][:, t, :], V[:], reads=[("KD", d), ("Vn", d, hf)], writes=[("cS", d)])
        P.mm(ca[:, 3, :], QK[d][:, t, :], V[:], reads=[("QK", d), ("Vn", d, hf)], writes=[cak])
        yield
        dve(P, lambda e: e.scalar_tensor_tensor(S_[:], S_[:], egt[d][:, hl, hf, t:t + 1], cSb[d][:], ALU.mult,
                                                ALU.add), [sk, ("cS", d), "egt%d" % d], [sk])
        P.op("dve", lambda e: e.tensor_tensor(ot[R, :], ca[R, 3, :], ot[R, :], ALU.add),
             reads=[cak, ("otmp", d)], writes=[("otmp", d), ("ca_rd", d)])
        P.op("pool", lambda e: e.tensor_tensor(oacc[R, t, :], oacc[R, t, :], ot[R, :], ALU.add),
             reads=[("otmp", d), ("oacc", t)], writes=[("oacc", t)])

    ycnt = 0
    for hl in range(2):
        for d in range(2):
            for t in range(0, NT, 2):
                run_interleaved([precompute(hl, d, t + i, i) for i in range(2) if t + i < NT])
            P.op("pool", lambda e, d=d: e.memset(St[d][:], 0.0), writes=[("St", d)])
        P.op("pool", lambda e: e.memset(raw[:, 0:S], 0.0),
             writes=[("oacc", t) for t in range(NT)] + allraw + ["rawpad"])
        for step in range(NC):
            run_interleaved([chunk_step(hl, 0, step), chunk_step(hl, 1, NC - 1 - step)])
        if STOP <= 6:
            return (P.finish() if own else None)
        for t in range(NT):
            k2 = t % 2
            P.act(junk_, oacc[:, t, :], AF.Square, reads=[("oacc", t)], writes=["junk", ("ssq", k2)],
                  accum_out=ssq[:, k2:k2 + 1])
            P.act(ssq[:, k2:k2 + 1], ssq[:, k2:k2 + 1], AF.Sqrt, reads=[("ssq", k2)], writes=[("ssq", k2)],
                  scale=1.0 / 128, bias=1e-6)
            dve(P, lambda e, k2=k2: e.reciprocal(ssq[:, k2:k2 + 1], ssq[:, k2:k2 + 1]), [("ssq", k2)], [("ssq", k2)])
            P.op("pool", lambda e, k2=k2, t=t, hl=hl: e.tensor_tensor(gm[k2], sz[:, t, hl * 128:(hl + 1) * 128], gb[:],
                                                                  ALU.mult), reads=["sz", "gb"], writes=[("gm", k2)])
            dve(P, lambda e, k2=k2, t=t: e.scalar_tensor_tensor(yt[k2][:], oacc[:, t, :], ssq[:, k2:k2 + 1], gm[k2],
                                                                ALU.mult, ALU.mult),
                [("oacc", t), ("ssq", k2), ("gm", k2)], [("yt", k2)])
            P.dma("sp", y[t * 128:(t + 1) * 128, hl * 128:(hl + 1) * 128], yt[k2][:], ("yt", k2), reads=[("yt", k2)],
                  is_out=True)
    print("gdn stats", P.stats())
    return (P.finish() if own else None)


def gdn_inputs(xTb, g1l, w_in_l, conv_l, alog_l, dtb_l, ng_l, hh, S=S):
    NT = S // 128
    offs = np.cumsum((0,) + (1536, 512, 16, 512, 512, 512, 512, 128, 128, 256, 256, 512, 16, 512))
    cqkv, cz, cab = offs[0], offs[1], offs[2]
    hs_ = [2 * hh, 2 * hh + 1]
    chunks = []
    for kind in range(3):
        for h in hs_:
            chunks.append(cqkv + kind * 512 + h * 128 + np.arange(128))
    qkvc = np.concatenate(chunks)
    zc = np.concatenate([cz + h * 128 + np.arange(128) for h in hs_])
    gsel = [(ab, dr, h) for ab in range(2) for dr in range(2) for h in hs_]
    gc = np.array([cab + ab * 8 + dr * 4 + h for (ab, dr, h) in gsel])
    gbv = np.array([dtb_l[dr, h] if ab == 0 else 0.0 for (ab, dr, h) in gsel], np.float32)
    cols = np.concatenate([qkvc, zc, gc])
    assert len(cols) == A_NCOLS
    convw = np.ascontiguousarray(np.stack([conv_l[:, c - cqkv].T for c in chunks], axis=1).astype(np.float32))
    ident = np.eye(128, dtype=np.float32)
    anti = np.ascontiguousarray(ident[::-1])
    sel = np.zeros((NT, NT, 128), np.float32)
    for k in range(NT):
        sel[k, k, :] = 1.0
    p = np.arange(128)[:, None]
    f = np.arange(128)[None, :]
    same = (p // 64) == (f // 64)
    MA0 = same & (f < p)
    MA1 = same & (f > p)
    MQ0 = same & (p <= f)
    MQ1 = same & (p >= f)
    mask = np.stack([MA0, MA1, MQ0, MQ1], axis=1).astype(np.float32).reshape(128, 4 * 128)
    gbias = np.ascontiguousarray(np.tile(gbv[None, None, :], (128, NT, 1)).reshape(128, NT * 8))
    alog = np.ascontiguousarray(np.tile(np.array([alog_l[dr, h] for dr in range(2) for h in hs_], np.float32)[None],
                                        (128, 1)))
    rmask = np.ones((NT, 128), np.float32)
    rmask[:, 0] = 0.0
    rmask[:, 64] = 0.0
    gb = np.ascontiguousarray(np.tile(ng_l[None, :], (128, 1)).astype(np.float32))
    return dict(xT=xTb, g1=g1l, w=wlayout(w_in_l, cols), convw=convw, ident=ident, anti=anti,
                mask=mask.astype(ml_dtypes.bfloat16),
                gbias=gbias, alog=alog, rmask=rmask, gb=gb)


class XSrc:
    def __init__(self, fn):
        self.fn = fn

    def __getitem__(self, idx):
        tsl = idx[2]
        return self.fn(tsl.start, tsl.stop)


class RowChunks:
    def __init__(self, chunks, rows_per, col0=0, ncols=None, rowmap=None):
        self.chunks, self.rows_per, self.col0, self.ncols, self.rowmap = chunks, rows_per, col0, ncols, rowmap

    def __getitem__(self, idx):
        rs, cs = idx
        a, b_ = rs.start, rs.stop
        c0 = self.col0 + (cs.start or 0)
        c1 = self.col0 + (cs.stop if cs.stop is not None else self.ncols)
        ci, off = self.rowmap(a) if self.rowmap else (a // self.rows_per, a % self.rows_per)
        return self.chunks[ci][off:off + (b_ - a), c0:c1]


def build_fused(S_=S, depth=2):
    import math
    P = Prog()
    H = S_ // 2
    x_full = P.dram("x_full", [128, 8, S_], F32, "ExternalInput")
    x_half = P.dram("x_half", [128, 8, H], F32, "ExternalInput")
    out = P.dram("out", [128, 8, H], F32, "ExternalOutput")
    YR = 1024
    NYC = S_ // YR
    ymine = [P.scratch("ymine%d" % c, [YR, 1024], BF16) for c in range(NYC)]
    ypair = [P.scratch("ypair%d" % c, [2 * YR, 1024], BF16) for c in range(NYC)]
    NXC = H // 512
    xh = [P.scratch("xh%d" % c, [1024, 512], F32) for c in range(NXC)]
    xpair = [P.scratch("xpair%d" % c, [2048, 512], F32) for c in range(NXC)]
    RG = [[0, 1], [2, 3], [4, 5], [6, 7]]
    xn_scr = P.scratch("xn_scr", [128, 8, S_], BF16)

    def xpair_view(a, b):
        r, tg, off = a // H, (a % H) // 512, a % 512
        assert off + (b - a) <= 512
        return xpair[tg][r * 1024:(r + 1) * 1024, off:off + (b - a)].rearrange("(p c) t -> p c t", c=8)

    def xh_view(a, b):
        tg, off = a // 512, a % 512
        assert off + (b - a) <= 512
        return xh[tg][:, off:off + (b - a)].rearrange("(p c) t -> p c t", c=8)

    def ypair_rowmap(row):
        r, T = row // S_, row % S_
        return T // YR, r * YR + (T % YR)

    stats = []
    for l in range(depth):
        lam_init = 0.8 - 0.6 * math.exp(-0.3 * l)
        xsrc = x_full if l == 0 else XSrc(xpair_view)
        P.pre = "l%d_xn_" % l
        P.override = {}
        g1d = P.dram("g1", [128, 8], F32, "ExternalInput")
        xnT_, _, _ = prologue(P, xsrc, g1d, None, 0, S_)
        for c in range(8):
            P.dma("sp", xn_scr[:, c, :], xnT_[:, c, :], ("xnst", c), reads=[("xnT", g) for g in range(S_ // TG)])
        stats.append((P.pre, P.end_phase()))
        for n, (nm, bld) in enumerate((("gdn", lambda: build_gdn(S_, P=P)), ("diff", lambda: build_diff(lam_init, S_, P=P)),
                                       ("swa", lambda: build_swa(S_, P=P)), ("mlstm", lambda: build_mlstm(S_, P=P)))):
            P.pre = "l%d_%s_" % (l, nm)
            P.override = {"xT": xsrc, "y": RowChunks(ymine, YR, col0=n * 256, ncols=256), "xnT_src": xn_scr}
            bld()
            stats.append((P.pre, P.end_phase()))
        for c in range(NYC):
            P.op("pool", lambda e, c=c: e.collective_compute("AllGather", ALU.bypass, replica_groups=RG,
                                                             ins=[ymine[c].opt()], outs=[ypair[c].opt()]),
                 dma=("cc_y", c), dma_inc=1)
        P.end_phase()
        final = (l == depth - 1)
        P.pre = "l%d_dense_" % l
        P.override = {"xT": (x_half if l == 0 else XSrc(xh_view)), "ypair": RowChunks(ypair, YR, ncols=1024, rowmap=ypair_rowmap),
                      "out": (out if final else XSrc(xh_view))}
        build_dense(H, final=final, P=P)
        stats.append((P.pre, P.end_phase()))
        if not final:
            for c in range(NXC):
                P.op("pool", lambda e, c=c: e.collective_compute("AllGather", ALU.bypass, replica_groups=RG,
                                                                 ins=[xh[c].opt()], outs=[xpair[c].opt()]),
                     dma=("cc_x", c), dma_inc=1)
            P.end_phase()
    P.pre = ""
    P.override = {}
    for s_ in stats:
        print(s_)
    return P.finish()


_NC = {}


def kernel(x, norm1_g, w_in, gdn_conv_w, gdn_a_log, gdn_dt_bias, gdn_norm_g, diff_lambda, diff_norm_g, swa_sink,
           mlstm_gate_b, mlstm_norm_g, w_branch, w_gate, w_out, norm2_g, w_mlp1, w_mlp2, final_norm_g):
    f32 = lambda a: np.asarray(a, dtype=np.float32)
    x = f32(x)
    norm1_g, w_in, gdn_conv_w, gdn_a_log, gdn_dt_bias, gdn_norm_g = map(f32, (norm1_g, w_in, gdn_conv_w, gdn_a_log,
                                                                            gdn_dt_bias, gdn_norm_g))
    diff_lambda, diff_norm_g, swa_sink, mlstm_gate_b, mlstm_norm_g = map(f32, (diff_lambda, diff_norm_g, swa_sink,
                                                                             mlstm_gate_b, mlstm_norm_g))
    w_branch, w_gate, w_out, norm2_g, w_mlp1, w_mlp2, final_norm_g = map(f32, (w_branch, w_gate, w_out, norm2_g,
                                                                             w_mlp1, w_mlp2, final_norm_g))
    B, S_, D = x.shape
    depth = norm1_g.shape[0]
    H = S_ // 2
    gl = lambda g: np.ascontiguousarray(g.reshape(8, 128).T)
    if "nc" not in _NC:
        _NC["nc"] = build_fused(S_, depth)
    nc = _NC["nc"]
    xT = [fm(x[b]) for b in range(B)]
    cosT, sinT = rope_tables_np(S_)
    cores = [(b, hh) for b in range(B) for hh in range(2)]
    dense_w = [dense_layout(w_gate[l], w_branch[l], w_out[l], w_mlp1[l], w_mlp2[l]) for l in range(depth)]
    identb = np.eye(128, dtype=np.float32).astype(ml_dtypes.bfloat16)
    in_maps = []
    for (b, hh) in cores:
        im = {"x_full": xT[b], "x_half": np.ascontiguousarray(xT[b][:, :, hh * H:(hh + 1) * H])}
        for l in range(depth):
            g1 = gl(norm1_g[l])
            parts = {
                "gdn": gdn_inputs(None, g1, w_in[l], gdn_conv_w[l], gdn_a_log[l], gdn_dt_bias[l], gdn_norm_g[l], hh, S_),
                "diff": diff_inputs(None, g1, w_in[l], diff_lambda[l], diff_norm_g[l], hh, cosT, sinT),
                "swa": swa_inputs(None, g1, w_in[l], swa_sink[l], hh, cosT, sinT),
                "mlstm": mlstm_inputs(None, g1, w_in[l], mlstm_gate_b[l], mlstm_norm_g[l], hh, S_),
                "dense": dict(g1=g1, g2=gl(norm2_g[l]), g3=gl(final_norm_g), identb=identb,
                              msel=np.ascontiguousarray(np.tile(np.array([[1.0 - hh, float(hh)]], np.float32), (128, 1))),
                              **dense_w[l]),
            }
            im["l%d_xn_g1" % l] = g1
            for nm, d in parts.items():
                for k, v in d.items():
                    if k == "xT":
                        continue
                    im["l%d_%s_%s" % (l, nm, k)] = v
        in_maps.append(im)
    r = run_bass_kernel_spmd(nc, in_maps, core_ids=list(range(len(cores)))).results
    for (b, hh), o in zip(cores, r):
        xT[b][:, :, hh * H:(hh + 1) * H] = np.asarray(o["out"])
    return np.stack([unfm(xT[b]) for b in range(B)]).astype(np.float32)
```

```python
import time
import ml_dtypes
from contextlib import ExitStack
import numpy as np
import concourse.bass as bass
import concourse.mybir as mybir
from concourse.bass_utils import run_bass_kernel_spmd

F32 = mybir.dt.float32
BF16 = mybir.dt.bfloat16
ALU = mybir.AluOpType
AF = mybir.ActivationFunctionType
AX = mybir.AxisListType

ENGS = ("pe", "act", "dve", "pool", "sp")
N_DMA_SEMS = 40


class Prog:
    def __init__(self):
        self.nc = bass.Bass("TRN2", target_bir_lowering=False)
        nc = self.nc
        self.stack = ExitStack()
        self.pstack = ExitStack()
        self.sems = {e: self.stack.enter_context(nc.semaphore("se_" + e)) for e in ENGS}
        for i in range(N_DMA_SEMS):
            self.sems[("dma", i)] = self.stack.enter_context(nc.semaphore("sd_%d" % i))
        self.sems["bar"] = self.stack.enter_context(nc.semaphore("s_bar"))
        self.cnt = {e: 0 for e in ENGS}
        self.dcnt = {("dma", i): 0 for i in range(N_DMA_SEMS)}
        self.known = {e: {} for e in ENGS}
        self.phase_no = 0
        self.uid = 0
        self._reset_phase()

    def _reset_phase(self):
        self.q = {e: [] for e in ENGS}
        self.last_w = {}
        self.readers = {}
        self.dma_map = {}
        self.out_tokens = []

    def dram(self, name, shape, dtype, kind):
        ov = getattr(self, "override", None) or {}
        if name in ov:
            return ov[name]
        full = getattr(self, "pre", "") + name
        self.ext_names = getattr(self, "ext_names", [])
        self.ext_names.append(full)
        return self.nc.dram_tensor(full, list(shape), dtype, kind=kind).ap()

    def scratch(self, name, shape, dtype):
        return self.nc.dram_tensor(name, list(shape), dtype).ap()

    def sb(self, name, shape, dtype, glob=False):
        self.uid += 1
        st = self.stack if glob else self.pstack
        return st.enter_context(self.nc.sbuf_tensor("s%d_%s" % (self.uid, name), list(shape), dtype))

    def ps(self, name, shape, dtype=F32):
        self.uid += 1
        return self.pstack.enter_context(self.nc.psum_tensor("p%d_%s" % (self.uid, name), list(shape), dtype))

    def op(self, eng, fn, reads=(), writes=(), dma=None, is_out=False, dma_inc=16):
        toks = []
        for k in reads:
            toks += self.last_w.get(k, [])
        for k in writes:
            toks += self.last_w.get(k, [])
            toks += self.readers.get(k, [])
        need = {}
        for s, v in toks:
            if eng == "pe" and s == "pe":
                continue
            if v > need.get(s, 0):
                need[s] = v
        waits = []
        kn = self.known[eng]
        for s, v in need.items():
            if kn.get(s, 0) >= v:
                continue
            kn[s] = v
            waits.append((s, v))
        if dma is not None:
            if dma not in self.dma_map:
                assert len(self.dma_map) < N_DMA_SEMS, "too many DMA groups in one phase"
                self.dma_map[dma] = ("dma", len(self.dma_map))
            sk = self.dma_map[dma]
            self.dcnt[sk] += dma_inc
            tok = (sk, self.dcnt[sk])
            inc = dma_inc
        else:
            self.cnt[eng] += 1
            tok = (eng, self.cnt[eng])
            inc = 1
        self.q[eng].append((waits, fn, tok, inc))
        for k in reads:
            self.readers.setdefault(k, []).append(tok)
        for k in writes:
            self.last_w[k] = [tok]
            self.readers[k] = []
        if is_out:
            self.out_tokens.append(tok)
        return tok

    def mm(self, out, lhsT, rhs, start=True, stop=True, reads=(), writes=()):
        return self.op("pe", lambda e: e.matmul(out, lhsT, rhs, start=start, stop=stop), reads, writes)

    def tr(self, out, in_, ident, reads=(), writes=()):
        return self.op("pe", lambda e: e.transpose(out, in_, ident), reads, writes)

    def act(self, out, in_, func, reads=(), writes=(), eng="act", **kw):
        return self.op(eng, lambda e: e.activation(out, in_, func, **kw), reads, writes)

    def dma(self, eng, out, in_, key, reads=(), writes=(), is_out=False, **kw):
        return self.op(eng, lambda e: e.dma_start(out=out, in_=in_, **kw), reads, writes, dma=key, is_out=is_out)

    def end_phase(self):
        nc = self.nc
        self.phase_no += 1
        pno = self.phase_no
        fin = {}
        for e in ENGS:
            if e != "sp" and self.cnt[e] > self.known["sp"].get(e, 0):
                fin[e] = self.cnt[e]
        for key, sk in self.dma_map.items():
            if self.dcnt[sk] > self.known["sp"].get(sk, 0):
                fin[sk] = self.dcnt[sk]
        for s, v in fin.items():
            self.known["sp"][s] = v
        sems = self.sems
        q = self.q
        first = (pno == 1)

        def replay(name, eng):
            if not first:
                eng.wait_ge(sems["bar"], pno - 1)
            for waits, fn, tok, inc in q[name]:
                for s, v in waits:
                    eng.wait_ge(sems[s], v)
                fn(eng).then_inc(sems[tok[0]], inc)
            if name == "sp":
                for s, v in fin.items():
                    eng.wait_ge(sems[s], v)
                eng.sem_inc(sems["bar"], 1)

        with nc.Block() as block:
            @block.tensor
            def _(e):
                replay("pe", e)

            @block.scalar
            def _(e):
                replay("act", e)

            @block.vector
            def _(e):
                replay("dve", e)

            @block.gpsimd
            def _(e):
                replay("pool", e)

            @block.sync
            def _(e):
                replay("sp", e)
        st = {e: len(q[e]) for e in ENGS}
        for e in ENGS:
            for e2 in ENGS:
                self.known[e][e2] = self.cnt[e2]
            for sk, v in self.dcnt.items():
                self.known[e][sk] = v
        self._reset_phase()
        self.pstack.close()
        self.pstack = ExitStack()
        return st

    def finish(self):
        self.end_phase()
        self.stack.close()
        return self.nc

    def stats(self):
        return {e: len(self.q[e]) for e in ENGS}


TG = 512


class WLoader:
    def __init__(self, P, maxcols, nbuf=3, cast_engs=("pool", "act", "dve", "pool")):
        self.P = P
        self.nbuf = nbuf
        self.st = [P.sb("wst%d" % i, [128, maxcols], F32) for i in range(nbuf)]
        self.bf = [P.sb("wbf%d" % i, [128, maxcols], BF16) for i in range(nbuf)]
        self.i = 0
        self.engs = cast_engs

    def load(self, src, ncols):
        P = self.P
        i = self.i % self.nbuf
        eng = self.engs[self.i % len(self.engs)]
        self.i += 1
        st, bf = self.st[i], self.bf[i]
        P.dma("sp", st[:, 0:ncols], src, ("wst", i), writes=[("wst", i)])
        if eng == "act":
            P.op("act", lambda e: e.copy(bf[:, 0:ncols], st[:, 0:ncols]), reads=[("wst", i)], writes=[("wbf", i)])
        else:
            P.op(eng, lambda e: e.tensor_copy(bf[:, 0:ncols], st[:, 0:ncols]), reads=[("wst", i)],
                 writes=[("wbf", i)])
        return bf, ("wbf", i)


def rmsnorm_fm(P, xs, xkey, gs, gkey, out, outkey, ones, sq, pss, rstd, T, out_scale_keyed=True):
    P.act(sq[:], xs[:], AF.Square, reads=[xkey], writes=["sq"])
    for c in range(8):
        P.mm(pss[:, 0:T], ones[:], sq[:, c, :], start=(c == 0), stop=(c == 7), reads=["ones", "sq"], writes=["pss"])
    P.act(rstd[:, 0:T], pss[:, 0:T], AF.Sqrt, reads=["pss"], writes=["rstd"], scale=1.0 / 1024, bias=1e-6)
    P.op("dve", lambda e: e.reciprocal(rstd[:, 0:T], rstd[:, 0:T]), reads=["rstd"], writes=["rstd"])
    for c in range(8):
        P.op("dve", lambda e, c=c: e.scalar_tensor_tensor(out[:, c, :], xs[:, c, :], gs[:, c:c + 1], rstd[:, 0:T],
                                                     ALU.mult, ALU.mult),
             reads=[xkey, gkey, "rstd"], writes=[outkey])


def build_dense(NT=2048, final=False, P=None):
    own = P is None
    if own:
        P = Prog()
    xT = P.dram("xT", [128, 8, NT], F32, "ExternalInput")
    ypair = (getattr(P, "override", None) or {}).get("ypair")
    SEQ = 2 * NT
    yT = P.dram("yT", [128, 16, NT], BF16, "ExternalInput") if ypair is None else None
    if ypair is not None:
        identbd = P.dram("identb", [128, 128], BF16, "ExternalInput")
        mseld = P.dram("msel", [128, 2], F32, "ExternalInput")
    g1 = P.dram("g1", [128, 8], F32, "ExternalInput")
    g2 = P.dram("g2", [128, 8], F32, "ExternalInput")
    g3 = P.dram("g3", [128, 8], F32, "ExternalInput")
    wg = P.dram("wg", [4, 8, 128, 1024], F32, "ExternalInput")
    wb = P.dram("wb", [4, 8, 128, 512], F32, "ExternalInput")
    wo = P.dram("wo", [8, 128, 1024], F32, "ExternalInput")
    w1 = P.dram("w1", [32, 128, 1024], F32, "ExternalInput")
    w2 = P.dram("w2", [8, 128, 4096], F32, "ExternalInput")
    out = P.dram("out", [128, 8, NT], F32, "ExternalOutput")

    xsb = [P.sb("xs%d" % i, [128, 8, TG], F32) for i in range(2)]
    ysb = [P.sb("ys%d" % i, [128, 16, TG], BF16) for i in range(2)]
    xn = P.sb("xn", [128, 8, TG], BF16)
    mg = P.sb("mg", [128, 8, TG], BF16)
    hT = P.sb("hT", [128, 32, TG], BF16)
    sq = P.sb("sq", [128, 8, TG], BF16)
    rstd = P.sb("rstd", [128, TG], F32)
    ones = P.sb("ones", [128, 128], BF16)
    g1s = P.sb("g1s", [128, 8], F32)
    g2s = P.sb("g2s", [128, 8], F32)
    g3s = P.sb("g3s", [128, 8], F32)
    sg = [P.sb("sg%d" % i, [128, TG], F32) for i in range(2)]
    pr = [P.sb("pr%d" % i, [128, TG], F32) for i in range(2)]
    acc = P.sb("acc", [128, TG], F32)
    rl = [P.sb("rl%d" % i, [128, TG], F32) for i in range(2)]
    xo = P.sb("xo", [128, 8, TG], F32)
    pss = P.ps("pss", [128, TG], F32)
    pg = [P.ps("pg%d" % i, [128, TG], F32) for i in range(2)]
    pb = [P.ps("pb%d" % i, [128, TG], F32) for i in range(2)]
    WL = WLoader(P, 1024, nbuf=6, cast_engs=("pool",))
    if ypair is not None:
        identb = P.sb("identb", [128, 128], BF16)
        msel = P.sb("msel", [128, 2], F32)
        cand = [P.sb("cand%d" % i, [128, 1024], BF16) for i in range(2)]
        ysel = P.sb("ysel", [128, 1024], BF16)
        ptr = P.ps("ptr", [128, 8, 128], BF16)
        P.dma("sp", identb[:], identbd, "identb", writes=["identb"])
        P.dma("sp", msel[:], mseld, "msel", writes=["msel"])

    P.op("pool", lambda e: e.memset(ones[:], 1.0), writes=["ones"])
    P.dma("sp", g1s[:], g1, "g1s", writes=["g1s"])
    P.dma("sp", g2s[:], g2, "g2s", writes=["g2s"])
    P.dma("sp", g3s[:], g3, "g3s", writes=["g3s"])
    cnt = 0

    def prep(tg):
        i_ = tg % 2
        xs, ys = xsb[i_], ysb[i_]
        xk, yk = ("xs", i_), ("ys", i_)
        tsl = slice(tg * TG, (tg + 1) * TG)
        P.dma("sp", xs[:], xT[:, :, tsl], xk, writes=[xk])
        if ypair is None:
            P.dma("sp", ys[:], yT[:, :, tsl], yk, writes=[yk])
        else:
            ys4 = ys[:].rearrange("p (n c) t -> p n c t", c=4)
            for tt in range(TG // 128):
                tok0 = tg * TG + tt * 128
                for r in range(2):
                    for h in range(2):
                        row0 = r * SEQ + h * NT + tok0
                        P.dma("sp", cand[h][:], ypair[row0:row0 + 128, :], ("cand", h), writes=[("cand", h)])
                    P.op("dve", lambda e: e.tensor_scalar(ysel[:], cand[0][:], msel[:, 0:1], None, ALU.mult),
                         reads=[("cand", 0), "msel"], writes=["ysel"])
                    P.op("dve", lambda e: e.scalar_tensor_tensor(ysel[:], cand[1][:], msel[:, 1:2], ysel[:], ALU.mult,
                                                                 ALU.add), reads=[("cand", 1), "msel", "ysel"],
                         writes=["ysel"])
                    for n in range(4):
                        for jj in range(2):
                            P.tr(ptr[:, n * 2 + jj, :], ysel[:, n * 256 + jj * 128:n * 256 + (jj + 1) * 128], identb[:],
                                 reads=["ysel", "identb"], writes=["ptr"])
                    for n in range(4):
                        P.act(ys4[:, n, 2 * r:2 * r + 2, tt * 128:(tt + 1) * 128], ptr[:, 2 * n:2 * n + 2, :], AF.Copy,
                              reads=["ptr"], writes=[yk])

    prep(0)
    for tg in range(NT // TG):
        tsl = slice(tg * TG, (tg + 1) * TG)
        xs, ys = xsb[tg % 2], ysb[tg % 2]
        xk, yk = ("xs", tg % 2), ("ys", tg % 2)
        rmsnorm_fm(P, xs, xk, g1s, "g1s", xn, "xn", ones, sq, pss, rstd, TG)
        for j in range(8):
            for n in range(4):
                k = cnt % 2
                cnt += 1
                wgt, wgk = WL.load(wg[n, j], 1024)
                for c in range(8):
                    P.mm(pg[k][:], wgt[:, c * 128:(c + 1) * 128], xn[:, c, :], start=(c == 0), stop=(c == 7),
                         reads=[wgk, "xn"], writes=[("pg", k)])
                wbt, wbk = WL.load(wb[n, j], 512)
                for c in range(4):
                    P.mm(pb[k][:], wbt[:, c * 128:(c + 1) * 128], ys[:, n * 4 + c, :], start=(c == 0), stop=(c == 3),
                         reads=[wbk, yk], writes=[("pb", k)])
                P.act(sg[k][:], pg[k][:], AF.Sigmoid, reads=[("pg", k)], writes=[("sg", k)])
                if n == 0:
                    P.op("dve", lambda e, k=k: e.tensor_tensor(acc[:], pb[k][:], sg[k][:], ALU.mult),
                         reads=[("pb", k), ("sg", k)], writes=["acc"])
                else:
                    P.op("dve", lambda e, k=k: e.tensor_tensor(pr[k][:], pb[k][:], sg[k][:], ALU.mult),
                         reads=[("pb", k), ("sg", k)], writes=[("pr", k)])
                    if n < 3:
                        P.op("dve", lambda e, k=k: e.tensor_tensor(acc[:], acc[:], pr[k][:], ALU.add),
                             reads=["acc", ("pr", k)], writes=["acc"])
                    else:
                        P.op("dve", lambda e, k=k, j=j: e.tensor_tensor(mg[:, j, :], acc[:], pr[k][:], ALU.add),
                             reads=["acc", ("pr", k)], writes=["mg"])
        for j in range(8):
            k = cnt % 2
            cnt += 1
            wt, wk = WL.load(wo[j], 1024)
            for c in range(8):
                P.mm(pg[k][:], wt[:, c * 128:(c + 1) * 128], mg[:, c, :], start=(c == 0), stop=(c == 7),
                     reads=[wk, "mg"], writes=[("pg", k)])
            P.op("dve", lambda e, k=k, j=j, xs=xs: e.tensor_tensor(xs[:, j, :], pg[k][:], xs[:, j, :], ALU.add),
                 reads=[("pg", k), xk], writes=[xk])
        rmsnorm_fm(P, xs, xk, g2s, "g2s", xn, "xn", ones, sq, pss, rstd, TG)
        for f in range(32):
            k = cnt % 2
            cnt += 1
            wt, wk = WL.load(w1[f], 1024)
            for c in range(8):
                P.mm(pg[k][:], wt[:, c * 128:(c + 1) * 128], xn[:, c, :], start=(c == 0), stop=(c == 7),
                     reads=[wk, "xn"], writes=[("pg", k)])
            P.act(rl[k][:], pg[k][:], AF.Relu, reads=[("pg", k)], writes=[("rl", k)])
            P.op("dve", lambda e, k=k, f=f: e.tensor_tensor(hT[:, f, :], rl[k][:], rl[k][:], ALU.mult),
                 reads=[("rl", k)], writes=["hT"])
            if f == 7 and tg + 1 < NT // TG:
                prep(tg + 1)
        for j in range(8):
            k = cnt % 2
            cnt += 1
            for fb in range(4):
                wt, wk = WL.load(w2[j][:, fb * 1024:(fb + 1) * 1024], 1024)
                for f8 in range(8):
                    f = fb * 8 + f8
                    P.mm(pb[k][:], wt[:, f8 * 128:(f8 + 1) * 128], hT[:, f, :], start=(f == 0), stop=(f == 31),
                         reads=[wk, "hT"], writes=[("pb", k)])
            P.op("dve", lambda e, k=k, j=j, xs=xs: e.tensor_tensor(xs[:, j, :], pb[k][:], xs[:, j, :], ALU.add),
                 reads=[("pb", k), xk], writes=[xk])
        if final:
            rmsnorm_fm(P, xs, xk, g3s, "g3s", xo, "xo", ones, sq, pss, rstd, TG)
            P.dma("sp", out[:, :, tsl], xo[:], "out", reads=["xo"], is_out=True)
        else:
            P.dma("sp", out[:, :, tsl], xs[:], ("out", tg % 2), reads=[xk], is_out=True)
    print("dense stats", P.stats())
    return (P.finish() if own else None)


def dense_layout(w_gate, w_branch, w_out, w_mlp1, w_mlp2):
    wg = w_gate.reshape(8, 128, 4, 8, 128).transpose(2, 3, 1, 0, 4).reshape(4, 8, 128, 1024)
    wb = w_branch.reshape(4, 4, 128, 8, 128).transpose(0, 3, 2, 1, 4).reshape(4, 8, 128, 512)
    wo = w_out.reshape(8, 128, 8, 128).transpose(2, 1, 0, 3).reshape(8, 128, 1024)
    w1 = w_mlp1.reshape(8, 128, 32, 128).transpose(2, 1, 0, 3).reshape(32, 128, 1024)
    w2 = w_mlp2.reshape(32, 128, 8, 128).transpose(2, 1, 0, 3).reshape(8, 128, 4096)
    return dict(wg=np.ascontiguousarray(wg), wb=np.ascontiguousarray(wb), wo=np.ascontiguousarray(wo),
                w1=np.ascontiguousarray(w1), w2=np.ascontiguousarray(w2))


def fm(a):
    T, D = a.shape
    return np.ascontiguousarray(a.T.reshape(D // 128, 128, T).transpose(1, 0, 2))


def unfm(a):
    p, C, T = a.shape
    return np.ascontiguousarray(a.transpose(2, 1, 0).reshape(T, C * 128))


S = 4096
NTILE = S // 128
TG = 512
NG = S // TG


def prologue(P, xT, g1, wsrc, ncols, S=S):
    xnT = P.sb("xnT", [128, 8, S], BF16)
    wbf = P.sb("wbf", [128, 8, max(ncols, 1)], BF16)
    xs = [P.sb("xs0", [128, 8, TG], F32)] * 2
    sq = P.sb("sq", [128, 8, TG], BF16)
    rstd = P.sb("rstd", [128, TG], F32)
    ones = P.sb("ones", [128, 128], BF16)
    g1s = P.sb("g1s", [128, 8], F32)
    pss = P.ps("pss", [128, TG], F32)
    P.op("pool", lambda e: e.memset(ones[:], 1.0), writes=["ones"])
    P.dma("sp", g1s[:], g1, "g1s", writes=["g1s"])
    wst = [P.sb("wst%d" % i, [128, max(ncols, 1)], F32) for i in range(2)]
    for c in range(8 if wsrc is not None else 0):
        i = c % 2
        P.dma("sp", wst[i][:], wsrc[:, c * ncols:(c + 1) * ncols], ("wst", i), writes=[("wst", i)])
        P.op("pool", lambda e, i=i, c=c: e.tensor_copy(wbf[:, c, :], wst[i][:]), reads=[("wst", i)],
             writes=["wbf"])
    xn_src = (getattr(P, "override", None) or {}).get("xnT_src")
    if xn_src is not None:
        for g in range(S // TG):
            tsl = slice(g * TG, (g + 1) * TG)
            P.dma("sp", xnT[:, :, tsl], xn_src[:, :, tsl], ("xnld", g), writes=[("xnT", g)])
        P.xs0 = xs[0]
        return xnT, wbf, ones
    for g in range(S // TG):
        i = 0
        tsl = slice(g * TG, (g + 1) * TG)
        P.dma("sp", xs[i][:], xT[:, :, tsl], ("xs", i), writes=[("xs", i)])
        P.act(sq[:], xs[i][:], AF.Square, reads=[("xs", i)], writes=["sq"])
        for c in range(8):
            P.mm(pss[:], ones[:], sq[:, c, :], start=(c == 0), stop=(c == 7), reads=["ones", "sq"], writes=["pss"])
        P.act(rstd[:], pss[:], AF.Sqrt, reads=["pss"], writes=["rstd"], scale=1.0 / 1024, bias=1e-6)
        P.op("dve", lambda e: e.reciprocal(rstd[:], rstd[:]), reads=["rstd"], writes=["rstd"])
        for c in range(8):
            P.op("dve", lambda e, c=c, i=i, tsl=tsl: e.scalar_tensor_tensor(
                xnT[:, c, tsl], xs[i][:, c, :], g1s[:, c:c + 1], rstd[:], ALU.mult, ALU.mult),
                reads=[("xs", i), "g1s", "rstd"], writes=[("xnT", g)])
    P.xs0 = xs[0]
    return xnT, wbf, ones


def proj_fm(P, out_ps, okey, wbf, col0, ncol, xnT, g):
    for c in range(8):
        P.mm(out_ps[0:ncol, :], wbf[:, c, col0:col0 + ncol], xnT[:, c, g * TG:(g + 1) * TG], start=(c == 0),
             stop=(c == 7), reads=["wbf", ("xnT", g)], writes=[okey])


def proj_tm(P, out_ps, okey, wbf, col0, ncol, xnT, t):
    g = (t * 128) // TG
    for c in range(8):
        P.mm(out_ps, xnT[:, c, t * 128:(t + 1) * 128], wbf[:, c, col0:col0 + ncol], start=(c == 0), stop=(c == 7),
             reads=["wbf", ("xnT", g)], writes=[okey])


def rope_proj(P, dstT, dkey, wbf, col_a, col_sw, xnT, cosT, sinT, pp, tmp, S=S):
    for g in range(S // TG):
        tsl = slice(g * TG, (g + 1) * TG)
        proj_fm(P, pp[0], ("pp", 0), wbf, col_a, 128, xnT, g)
        proj_fm(P, pp[1], ("pp", 1), wbf, col_sw, 128, xnT, g)
        P.op("dve", lambda e, tsl=tsl: e.tensor_tensor(tmp[0][:], pp[0][:], cosT[:, tsl], ALU.mult),
             reads=[("pp", 0), "cos"], writes=[("rtmp", 0)])
        P.op("dve", lambda e, tsl=tsl: e.tensor_tensor(tmp[1][:], pp[1][:], sinT[:, tsl], ALU.mult),
             reads=[("pp", 1), "sin"], writes=[("rtmp", 1)])
        P.op("pool", lambda e, tsl=tsl: e.tensor_tensor(dstT[:, tsl], tmp[0][:], tmp[1][:], ALU.add),
             reads=[("rtmp", 0), ("rtmp", 1)], writes=[(dkey, g)])


def rope_tables_np(S=S):
    inv = 1.0 / (10000.0 ** (np.arange(0, 64, 2, dtype=np.float32) / 64))
    ang = np.arange(S, dtype=np.float32)[:, None] * inv[None, :]
    ang = np.concatenate([ang, ang], axis=-1)
    cos, sin = np.cos(ang).astype(np.float32), np.sin(ang).astype(np.float32)
    sgn = np.concatenate([-np.ones(32, np.float32), np.ones(32, np.float32)])
    cosT = np.ascontiguousarray(np.tile(cos.T, (2, 1)))
    sinT = np.ascontiguousarray(np.tile((sin * sgn[None, :]).T, (2, 1)))
    return cosT, sinT


def swap_halves(cols):
    cols = np.asarray(cols).reshape(-1, 64)
    return np.concatenate([cols[:, 32:], cols[:, :32]], axis=1).reshape(-1)


def wlayout(w, cols):
    sub = w[:, cols]
    n = sub.shape[1]
    return np.ascontiguousarray(sub.reshape(8, 128, n).transpose(1, 0, 2).reshape(128, 8 * n))


C_NCOLS = 256 + 256 + 128 + 128 + 64


def build_swa(S=S, P=None):
    own = P is None
    if own:
        P = Prog()
    xT = P.dram("xT", [128, 8, S], F32, "ExternalInput")
    g1 = P.dram("g1", [128, 8], F32, "ExternalInput")
    w = P.dram("w", [128, 8 * C_NCOLS], F32, "ExternalInput")
    cosd = P.dram("cos", [128, S], F32, "ExternalInput")
    sind = P.dram("sin", [128, S], F32, "ExternalInput")
    maskd = P.dram("mask", [128, 2, 512], BF16, "ExternalInput")
    sinkd = P.dram("sink", [128, 4], F32, "ExternalInput")
    y = P.dram("y", [S, 256], BF16, "ExternalOutput")

    xnT, wbf, ones = prologue(P, xT, g1, w, C_NCOLS, S)
    cosT = P.sb("cosT", [128, S], F32)
    sinT = P.sb("sinT", [128, S], F32)
    P.dma("sp", cosT[:], cosd, "cos", writes=["cos"])
    P.dma("sp", sinT[:], sind, "sin", writes=["sin"])
    mask = P.sb("mask", [128, 2, 512], BF16)
    P.dma("sp", mask[:], maskd, "mask", writes=["mask"])
    esink = P.sb("esink", [128, 4], F32)
    P.dma("sp", esink[:], sinkd, "esink", writes=["esink"])
    P.act(esink[:], esink[:], AF.Exp, reads=["esink"], writes=["esink"])

    NT = S // 128
    qT = [P.sb("qT%d" % i, [128, S], BF16) for i in range(2)]
    kT = P.sb("kT", [128, S], BF16)
    vaug = P.sb("vaug", [128, NT, 65], BF16)
    pp = [P.ps("pp%d" % i, [128, TG], F32) for i in range(2)]
    tmp = [P.sb("rtmp%d" % i, [128, TG], F32) for i in range(2)]
    rope_proj(P, qT[0], "qT0", wbf, 0, 256, xnT, cosT, sinT, pp, tmp, S)
    rope_proj(P, qT[1], "qT1", wbf, 128, 384, xnT, cosT, sinT, pp, tmp, S)
    rope_proj(P, kT, "kT", wbf, 512, 640, xnT, cosT, sinT, pp, tmp, S)
    kz = [P.sb("kz%d" % i, [128, S], BF16) for i in range(2)]
    P.op("pool", lambda e: e.memset(kz[0][64:128, :], 0.0), writes=["kz0"])
    P.op("pool", lambda e: e.memset(kz[1][0:64, :], 0.0), writes=["kz1"])
    allk = [("kT", g) for g in range(S // TG)]
    P.op("pool", lambda e: e.tensor_copy(kz[0][0:64, :], kT[0:64, :]), reads=allk, writes=["kz0"])
    P.op("act", lambda e: e.copy(kz[1][64:128, :], kT[64:128, :]), reads=allk, writes=["kz1"])
    P.op("pool", lambda e: e.memset(vaug[:], 1.0), writes=["vaug"])
    pv = P.ps("pv", [128, 64], F32)
    for t in range(NT):
        proj_tm(P, pv[:], "pv", wbf, 768, 64, xnT, t)
        P.act(vaug[:, t, 0:64], pv[:], AF.Copy, reads=["pv"], writes=["vaug"])

    st = [P.ps("st%d" % i, [128, 512], F32) for i in range(3)]
    pt = [P.sb("pt%d" % i, [128, 512], BF16) for i in range(3)]
    po = P.ps("po", [128, 4, 65], F32)
    den = P.sb("den", [128, 4], F32)
    yt = [P.sb("yt%d" % i, [128, 4, 64], BF16) for i in range(2)]
    qkeys = lambda n: [("qT0", (n * 128) // TG), ("qT1", (n * 128) // TG)]
    for n in range(NT):
        ds = [d for d in (-1, 0, 1) if 0 <= n + d < NT]
        for di, d in enumerate(ds):
            m = n + d
            for h in range(4):
                P.mm(st[di][:, h * 128:(h + 1) * 128], kz[h % 2][:, m * 128:(m + 1) * 128],
                     qT[h // 2][:, n * 128:(n + 1) * 128], reads=["kz0", "kz1"] + qkeys(n),
                     writes=[("st", di)])
            P.act(pt[di][:], st[di][:], AF.Exp, reads=[("st", di)], writes=[("pt", di)], scale=0.125)
            if d != 0:
                mi = 0 if d == -1 else 1
                P.op("pool", lambda e, di=di, mi=mi: e.tensor_tensor(pt[di][:], pt[di][:], mask[:, mi, :], ALU.mult),
                     reads=[("pt", di), "mask"], writes=[("pt", di)])
        for h in range(4):
            for di, d in enumerate(ds):
                m = n + d
                P.mm(po[:, h, :], pt[di][:, h * 128:(h + 1) * 128], vaug[:, m, :], start=(di == 0),
                     stop=(di == len(ds) - 1), reads=[("pt", di), "vaug"], writes=["po"])
        P.op("dve", lambda e: e.tensor_tensor(den[:], po[:, :, 64], esink[:], ALU.add), reads=["po", "esink"],
             writes=["den"])
        P.op("dve", lambda e: e.reciprocal(den[:], den[:]), reads=["den"], writes=["den"])
        yb = yt[n % 2]
        for h in range(4):
            P.op("dve", lambda e, h=h, yb=yb: e.tensor_scalar(yb[:, h, :], po[:, h, 0:64], den[:, h:h + 1], None,
                                                           ALU.mult), reads=["po", "den"], writes=[("yt", n % 2)])
        P.dma("sp", y[n * 128:(n + 1) * 128, :], yb[:].rearrange("p h d -> p (h d)"), ("yt", n % 2),
              reads=[("yt", n % 2)], is_out=True)
    print("swa stats", P.stats())
    return (P.finish() if own else None)


def swa_inputs(xTb, g1l, w_in_l, sink_l, hh, cosT, sinT):
    offs = np.cumsum((0,) + (1536, 512, 16, 512, 512, 512, 512, 128, 128, 256, 256, 512, 16, 512))
    cq, ck, cv = offs[6], offs[7], offs[8]
    qcols = cq + np.arange(hh * 256, hh * 256 + 256)
    kcols = ck + np.arange(hh * 64, hh * 64 + 64)
    vcols = cv + np.arange(hh * 64, hh * 64 + 64)
    k2 = np.concatenate([kcols, kcols])
    cols = np.concatenate([qcols, swap_halves(qcols), k2, swap_halves(k2), vcols])
    assert len(cols) == C_NCOLS
    j = np.arange(128)[:, None]
    i = np.arange(128)[None, :]
    prev = (j >= i).astype(np.float32)
    nxt = (j <= i).astype(np.float32)
    mask = np.stack([np.tile(prev, (1, 4)), np.tile(nxt, (1, 4))], axis=1).astype(ml_dtypes.bfloat16)
    sink = np.tile(sink_l[hh * 4:hh * 4 + 4][None, :], (128, 1)).astype(np.float32)
    return dict(xT=xTb, g1=g1l, w=wlayout(w_in_l, cols), cos=cosT, sin=sinT, mask=np.ascontiguousarray(mask),
                sink=np.ascontiguousarray(sink))


B_NCOLS = 5 * 256


def build_diff(lam_init, S=S, P=None):
    own = P is None
    if own:
        P = Prog()
    xT = P.dram("xT", [128, 8, S], F32, "ExternalInput")
    g1 = P.dram("g1", [128, 8], F32, "ExternalInput")
    w = P.dram("w", [128, 8 * B_NCOLS], F32, "ExternalInput")
    cosd = P.dram("cos", [128, S], F32, "ExternalInput")
    sind = P.dram("sin", [128, S], F32, "ExternalInput")
    lpd = P.dram("lp", [128, 4, 64], F32, "ExternalInput")
    gbd = P.dram("gb", [128, 128], F32, "ExternalInput")
    y = P.dram("y", [S, 256], BF16, "ExternalOutput")
    NT = S // 128
    NQ = S // 512

    xnT, wbf, ones = prologue(P, xT, g1, w, B_NCOLS, S)
    cosT = P.sb("cosT", [128, S], F32)
    sinT = P.sb("sinT", [128, S], F32)
    P.dma("sp", cosT[:], cosd, "cos", writes=["cos"])
    P.dma("sp", sinT[:], sind, "sin", writes=["sin"])
    lp = P.sb("lp", [128, 4, 64], F32)
    gb = P.sb("gb", [128, 128], F32)
    P.dma("sp", lp[:], lpd, "lp", writes=["lp"])
    P.dma("sp", gb[:], gbd, "gb", writes=["gb"])
    P.op("dve", lambda e: e.tensor_scalar(gb[:], gb[:], 1.0 - lam_init, None, ALU.mult), reads=["gb"], writes=["gb"])
    junk = P.sb("junk", [128, 128], F32)
    s12 = P.sb("s12", [128, 2], F32)
    nlam = P.sb("nlam", [128, 1], F32)
    for i in range(2):
        P.op("dve", lambda e, i=i: e.scalar_tensor_tensor(junk[:, 0:64], lp[:, 2 * i, :], 1.0, lp[:, 2 * i + 1, :],
                                                     ALU.mult, ALU.mult, accum_out=s12[:, i:i + 1]),
             reads=["lp"], writes=["junk", "s12"])
    P.act(s12[:], s12[:], AF.Exp, reads=["s12"], writes=["s12"])
    P.op("dve", lambda e: e.tensor_tensor(nlam[:], s12[:, 0:1], s12[:, 1:2], ALU.subtract), reads=["s12"],
         writes=["nlam"])
    P.op("dve", lambda e: e.tensor_scalar(nlam[:], nlam[:], -1.0, -lam_init, ALU.mult, ALU.add), reads=["nlam"],
         writes=["nlam"])

    zt = P.sb("zt", [128, 512], BF16)
    P.op("pool", lambda e: e.memset(zt[:], 0.0), writes=["zt"])
    qT = P.sb("qT", [128, S], BF16)
    kT = P.sb("kT", [128, S], BF16)
    kz = [P.sb("kz%d" % i, [128, S], BF16) for i in range(2)]
    vt = P.sb("vt", [128, NT, 129], BF16)
    P.op("pool", lambda e: e.memset(vt[:], 1.0), writes=["vt"])
    pp = [P.ps("pp%d" % i, [128, TG], F32) for i in range(2)]
    tmp = [P.sb("rtmp%d" % i, [128, TG], F32) for i in range(2)]
    st = pp
    pt = [P.sb("pt%d" % i, [128, 512], BF16) for i in range(3)]
    po = [[P.ps("po%d_%d" % (i, hf), [128, 2, 129], F32) for hf in range(2)] for i in range(2)]
    rd = P.sb("rd", [128, 2, 4], F32)
    t0 = [P.sb("t0_%d" % i, [128, 128], F32) for i in range(2)]
    ot = [P.sb("ot%d" % i, [128, 128], F32) for i in range(2)]
    ssq = P.sb("ssq", [128, 2], F32)
    yt = [P.sb("yt%d" % i, [128, 4, 128], BF16) for i in range(2)]
    P.op("pool", lambda e: e.memset(kz[0][64:128, :], 0.0), writes=["kz0"])
    P.op("pool", lambda e: e.memset(kz[1][0:64, :], 0.0), writes=["kz1"])
    allg = lambda k: [(k, g) for g in range(S // TG)]
    cnt = 0
    ycnt = 0
    for h in range(2):
        rope_proj(P, qT, "qT", wbf, h * 128, 256 + h * 128, xnT, cosT, sinT, pp, tmp, S)
        rope_proj(P, kT, "kT", wbf, 512 + h * 128, 768 + h * 128, xnT, cosT, sinT, pp, tmp, S)
        P.op("pool", lambda e: e.tensor_copy(kz[0][0:64, :], kT[0:64, :]), reads=allg("kT"), writes=["kz0"])
        P.op("act", lambda e: e.copy(kz[1][64:128, :], kT[64:128, :]), reads=allg("kT"), writes=["kz1"])
        for t in range(NT):
            proj_tm(P, pp[0][:, 0:128], ("pp", 0), wbf, 1024 + h * 128, 128, xnT, t)
            P.act(vt[:, t, 0:128], pp[0][:, 0:128], AF.Copy, reads=[("pp", 0)], writes=["vt"])
        for g in range(NQ):
            qsl = slice(g * 512, (g + 1) * 512)
            for m in range(2):
                for hf in range(2):
                    P.mm(po[m][hf][:].rearrange("p a b -> p (a b)"), zt[:, 0:128], zt[:, 0:258], start=True, stop=False,
                         reads=["zt"], writes=[("po", m)])
            blocks = [(t, m) for t in range(NT) for m in range(2)]

            def front(i, base):
                t, m = blocks[i]
                b, b3 = (base + i) % 2, (base + i) % 3
                P.mm(st[b][:], kz[m][:, t * 128:(t + 1) * 128], qT[:, qsl], reads=["kz%d" % m, ("qT", g)],
                     writes=[("pp", b)])
                P.act(pt[b3][:], st[b][:], AF.Exp, reads=[("pp", b)], writes=[("pt", b3)], scale=0.125)

            def back(i, base):
                t, m = blocks[i]
                b3 = (base + i) % 3
                last = (t == NT - 1)
                for qs in range(4):
                    P.mm(po[m][qs // 2][:, qs % 2, :], pt[b3][:, qs * 128:(qs + 1) * 128], vt[:, t, :], start=False,
                         stop=(last and qs % 2 == 1), reads=[("pt", b3), "vt"], writes=[("po", m)])

            front(0, cnt)
            for i in range(len(blocks)):
                if i + 1 < len(blocks):
                    front(i + 1, cnt)
                back(i, cnt)
            cnt += len(blocks)
            for m in range(2):
                for hf in range(2):
                    P.op("dve", lambda e, m=m, hf=hf: e.reciprocal(rd[:, m, 2 * hf:2 * hf + 2], po[m][hf][:, :, 128]),
                         reads=[("po", m)], writes=["rd"])
            P.op("dve", lambda e: e.tensor_scalar(rd[:, 1, :], rd[:, 1, :], nlam[:, 0:1], None, ALU.mult),
                 reads=["rd", "nlam"], writes=["rd"])
            yb = yt[ycnt % 2]
            ykey = ("yt", ycnt % 2)
            ycnt += 1
            for qs in range(4):
                k2 = qs % 2
                P.act(t0[k2][:], po[0][qs // 2][:, qs % 2, 0:128], AF.Copy, reads=[("po", 0), "rd"], writes=[("t0", k2)],
                      scale=rd[:, 0, qs:qs + 1])
                P.op("dve", lambda e, qs=qs, k2=k2: e.scalar_tensor_tensor(ot[k2][:], po[1][qs // 2][:, qs % 2, 0:128],
                                                                       rd[:, 1, qs:qs + 1], t0[k2][:], ALU.mult,
                                                                       ALU.add),
                     reads=[("po", 1), "rd", ("t0", k2)], writes=[("ot", k2)])
                P.act(junk[:], ot[k2][:], AF.Square, reads=[("ot", k2)], writes=["junk", ("ssq", k2)],
                      accum_out=ssq[:, k2:k2 + 1])
                P.act(ssq[:, k2:k2 + 1], ssq[:, k2:k2 + 1], AF.Sqrt, reads=[("ssq", k2)], writes=[("ssq", k2)],
                      scale=1.0 / 128, bias=1e-6)
                P.op("dve", lambda e, k2=k2: e.reciprocal(ssq[:, k2:k2 + 1], ssq[:, k2:k2 + 1]), reads=[("ssq", k2)],
                     writes=[("ssq", k2)])
                P.op("dve", lambda e, qs=qs, k2=k2, yb=yb: e.scalar_tensor_tensor(yb[:, qs, :], ot[k2][:],
                                                                              ssq[:, k2:k2 + 1], gb[:], ALU.mult,
                                                                              ALU.mult),
                     reads=[("ot", k2), ("ssq", k2), "gb"], writes=[ykey])
            P.dma("sp", y[g * 512:(g + 1) * 512, h * 128:(h + 1) * 128].rearrange("(q p) e -> p q e", p=128), yb[:],
                  ykey, reads=[ykey], is_out=True)
    print("diff stats", P.stats())
    return (P.finish() if own else None)


def diff_inputs(xTb, g1l, w_in_l, lam_l, ng_l, hh, cosT, sinT):
    offs = np.cumsum((0,) + (1536, 512, 16, 512, 512, 512, 512, 128, 128, 256, 256, 512, 16, 512))
    cq, ck, cv = offs[3], offs[4], offs[5]
    r = np.arange(hh * 256, hh * 256 + 256)
    cols = np.concatenate([cq + r, swap_halves(cq + r), ck + r, swap_halves(ck + r), cv + r])
    assert len(cols) == B_NCOLS
    lp = np.ascontiguousarray(np.tile(lam_l[None], (128, 1, 1)).astype(np.float32))
    gb = np.ascontiguousarray(np.tile(ng_l[None, :], (128, 1)).astype(np.float32))
    return dict(xT=xTb, g1=g1l, w=wlayout(w_in_l, cols), cos=cosT, sin=sinT, lp=lp, gb=gb)


D_NCOLS = 128 + 128 + 256 + 8 + 256


def dve(P, fn, reads, writes, eng="dve"):
    return P.op(eng, fn, reads, writes)


def build_mlstm(S=S, P=None):
    own = P is None
    if own:
        P = Prog()
    NT = S // 128
    NQ = S // 512
    xT = P.dram("xT", [128, 8, S], F32, "ExternalInput")
    g1 = P.dram("g1", [128, 8], F32, "ExternalInput")
    w = P.dram("w", [128, 8 * D_NCOLS], F32, "ExternalInput")
    identd = P.dram("ident", [128, 128], F32, "ExternalInput")
    antid = P.dram("anti", [128, 128], F32, "ExternalInput")
    seld = P.dram("sel", [NT, NT * 128], F32, "ExternalInput")
    maskd = P.dram("mask", [128, 2 * 4 * 512], BF16, "ExternalInput")
    biasd = P.dram("gbias", [128, NT * 8], F32, "ExternalInput")
    gbd = P.dram("gb", [128, 128], F32, "ExternalInput")
    y = P.dram("y", [S, 256], BF16, "ExternalOutput")

    xnT, wbf, ones = prologue(P, xT, g1, w, D_NCOLS, S)
    ident = P.sb("ident", [128, 128], F32)
    anti = P.sb("anti", [128, 128], F32)
    sel2 = P.xs0[:].rearrange("p a b -> p (a b)")
    mask = P.sb("mask", [128, 2, 4, 512], BF16)
    gbias = P.sb("gbias", [128, NT * 8], F32)
    gb = P.sb("gb", [128, 128], F32)
    P.dma("sp", ident[:], identd, "ident", writes=["ident"])
    P.dma("sp", anti[:], antid, "anti", writes=["anti"])
    P.dma("sp", sel2[0:NT, 0:NT * 128], seld, "sel", writes=["sel", ("xs", 0)])
    P.dma("sp", mask[:].rearrange("p a b c -> p (a b c)"), maskd, "mask", writes=["mask"])
    P.dma("sp", gbias[:], biasd, "gbias", writes=["gbias"])
    P.dma("sp", gb[:], gbd, "gb", writes=["gb"])
    zt = P.sb("zt", [128, 512], BF16)
    P.op("pool", lambda e: e.memset(zt[:], 0.0), writes=["zt"])
    onesf = P.sb("onesf", [NT, 128], F32)
    P.op("pool", lambda e: e.memset(onesf[:], 1.0), writes=["onesf"])

    pp = [P.ps("pp%d" % i, [128, 512], F32) for i in range(2)]
    st = [P.ps("st%d" % i, [128, 512], F32) for i in range(2)]
    po = [P.ps("po%d" % i, [128, 4, 128], F32) for i in range(2)]
    pd = P.ps("pd", [128, 8], F32)

    qT = P.sb("qT", [128, S], BF16)
    kz = [P.sb("kz%d" % i, [128, S], BF16) for i in range(2)]
    vt = P.sb("vt", [128, NT, 256], BF16)
    P.op("pool", lambda e: e.memset(kz[0][64:128, :], 0.0), writes=["kz0"])
    P.op("pool", lambda e: e.memset(kz[1][0:64, :], 0.0), writes=["kz1"])
    for g in range(S // TG):
        tsl = slice(g * TG, (g + 1) * TG)
        proj_fm(P, pp[0], ("pp", 0), wbf, 0, 128, xnT, g)
        P.act(qT[:, tsl], pp[0][:], AF.Copy, reads=[("pp", 0)], writes=[("qT", g)])
        proj_fm(P, pp[1], ("pp", 1), wbf, 128, 128, xnT, g)
        P.act(kz[0][0:64, tsl], pp[1][0:64, :], AF.Copy, reads=[("pp", 1)], writes=["kz0"], scale=0.125)
        P.act(kz[1][64:128, tsl], pp[1][64:128, :], AF.Copy, reads=[("pp", 1)], writes=["kz1"], scale=0.125)
    for t in range(NT):
        k = t % 2
        proj_tm(P, pp[k][:, 0:256], ("pp", k), wbf, 256, 256, xnT, t)
        P.act(vt[:, t, :], pp[k][:, 0:256], AF.Copy, reads=[("pp", k)], writes=["vt"])
    for t in range(NT):
        proj_tm(P, pp[0][:, t * 8:(t + 1) * 8], ("pp", 0), wbf, 512, 8, xnT, t)
    gtok = P.sb("gtok", [128, NT, 8], F32)
    gtokR = P.sb("gtokR", [128, NT, 8], F32)
    dve(P, lambda e: e.tensor_tensor(gtok[:].rearrange("p a b -> p (a b)"), pp[0][:, 0:NT * 8], gbias[:], ALU.add),
        [("pp", 0), "gbias"], ["gtok"])
    P.mm(pp[1][:, 0:NT * 8], anti[:], gtok[:].rearrange("p a b -> p (a b)"), reads=["anti", "gtok"],
         writes=[("pp", 1)])
    P.act(gtokR[:].rearrange("p a b -> p (a b)"), pp[1][:, 0:NT * 8], AF.Copy, reads=[("pp", 1)], writes=["gtokR"])

    tok = [P.sb("tok%d" % d, [128, 4, NT], F32) for d in range(2)]
    A = [P.sb("A%d" % d, [NT, 2, 128], F32) for d in range(2)]
    def gate_dir(d):
        src = gtok if d == 0 else gtokR
        sk = "gtok" if d == 0 else "gtokR"
        LP = P.sb("LP%d" % d, [NT, 4, 128], F32)
        k = d
        for j in range(4):
            col = (0, 1, 4, 5)[j] + 2 * d
            P.mm(pp[k][0:NT, j * 128:(j + 1) * 128], src[:, :, col], ident[:], reads=[sk, "ident"], writes=[("pp", k)])
        P.act(LP[:].rearrange("p a b -> p (a b)"), pp[k][0:NT, :], AF.Copy, reads=[("pp", k)], writes=["LP%d" % d])
        L = "LP%d" % d
        T1 = P.sb("T1_%d" % d, [NT, 2, 128], F32)
        T2 = P.sb("T2_%d" % d, [NT, 2, 128], F32)
        Wt = P.sb("W_%d" % d, [NT, 2, 128], F32)
        Ct = P.sb("C_%d" % d, [NT, 2, 128], F32)
        Mt = P.sb("M_%d" % d, [NT, 2, 128], F32)
        Et = P.sb("E_%d" % d, [NT, 2, 128], F32)
        n1, n2, nW, nC, nM, nE = ["%s_%d" % (s, d) for s in ("T1", "T2", "W", "C", "M", "E")]
        pf = LP[:, 2:4, :]
        li = LP[:, 0:2, :]
        P.act(T1[:], pf, AF.Abs, reads=[L], writes=[n1])
        P.act(T1[:], T1[:], AF.Exp, reads=[n1], writes=[n1], scale=-1.0)
        P.act(T1[:], T1[:], AF.Ln, reads=[n1], writes=[n1], bias=1.0)
        dve(P, lambda e: e.tensor_single_scalar(T2[:], pf, 0.0, ALU.min), [L], [n2])
        dve(P, lambda e: e.tensor_tensor(T2[:], T2[:], T1[:], ALU.subtract), [n1, n2], [n2])
        for hl in range(2):
            dve(P, lambda e, hl=hl: e.tensor_tensor_scan(Wt[:, hl, :], onesf[:], T2[:, hl, :], 0.0, ALU.mult,
                                                         ALU.add), [n2, "onesf"], [nW])
        r = P.sb("r_%d" % d, [2, NT], F32)
        rs = P.sb("rs_%d" % d, [2, NT], F32)
        tot = P.sb("tot_%d" % d, [2, 1], F32)
        car = P.sb("car_%d" % d, [NT, 2], F32)
        P.mm(pp[k][0:2, 0:NT], Wt[:, :, 127], ident[0:NT, 0:NT], reads=[nW, "ident"], writes=[("pp", k)])
        P.act(r[:], pp[k][0:2, 0:NT], AF.Copy, reads=[("pp", k)], writes=["r%d" % d])
        onesr = onesf[0:2, 0:NT]
        dve(P, lambda e: e.tensor_tensor_scan(rs[:], onesr, r[:], 0.0, ALU.mult, ALU.add), ["r%d" % d, "onesf"],
            ["rs%d" % d])
        if d == 0:
            dve(P, lambda e: e.tensor_tensor(rs[:], rs[:], r[:], ALU.subtract), ["rs%d" % d, "r%d" % d], ["rs%d" % d])
        else:
            dve(P, lambda e: e.tensor_copy(tot[:], rs[:, NT - 1:NT]), ["rs%d" % d], ["tot%d" % d])
            dve(P, lambda e: e.tensor_scalar(rs[:], rs[:], -1.0, tot[:, 0:1], ALU.mult, ALU.add),
                ["rs%d" % d, "tot%d" % d], ["rs%d" % d])
        P.mm(pp[k][0:NT, 0:2], rs[:], ident[0:2, 0:2], reads=["rs%d" % d, "ident"], writes=[("pp", k)])
        P.act(car[:], pp[k][0:NT, 0:2], AF.Copy, reads=[("pp", k)], writes=["car%d" % d])
        for hl in range(2):
            dve(P, lambda e, hl=hl: e.tensor_scalar(Wt[:, hl, :], Wt[:, hl, :], car[:, hl:hl + 1], None, ALU.add),
                [nW, "car%d" % d], [nW])
        dve(P, lambda e: e.tensor_tensor(Ct[:], li, Wt[:], ALU.subtract), [L, nW], [nC])
        for hl in range(2):
            dve(P, lambda e, hl=hl: e.tensor_tensor_scan(Mt[:, hl, :], Ct[:, hl, :], Ct[:, hl, :], -1e30, ALU.max,
                                                         ALU.max), [nC], [nM])
        mr = P.sb("mr_%d" % d, [2, NT], F32)
        mr2 = P.sb("mr2_%d" % d, [2, NT], F32)
        mx = P.sb("mx_%d" % d, [2, NT], F32)
        cmx = P.sb("cmx_%d" % d, [NT, 2], F32)
        P.mm(pp[k][0:2, 0:NT], Mt[:, :, 127], ident[0:NT, 0:NT], reads=[nM, "ident"], writes=[("pp", k)])
        P.act(mr[:], pp[k][0:2, 0:NT], AF.Copy, reads=[("pp", k)], writes=["mr%d" % d])
        dve(P, lambda e: e.memset(mx[:], -1e30), [], ["mx%d" % d])
        if NT > 1:
            if d == 0:
                dve(P, lambda e: e.tensor_tensor_scan(mr2[:], mr[:], mr[:], -1e30, ALU.max, ALU.max), ["mr%d" % d],
                    ["mr2%d" % d])
                dve(P, lambda e: e.tensor_copy(mx[:, 1:NT], mr2[:, 0:NT - 1]), ["mr2%d" % d, "mx%d" % d], ["mx%d" % d])
            else:
                cur, ck_, oth, ok_ = mr, "mr%d" % d, mr2, "mr2%d" % d
                s = 1
                while s < NT:
                    dve(P, lambda e, cur=cur, oth=oth, s=s: e.tensor_tensor(oth[:, 0:NT - s], cur[:, 0:NT - s],
                                                                        cur[:, s:NT], ALU.max), [ck_], [ok_])
                    dve(P, lambda e, cur=cur, oth=oth, s=s: e.tensor_copy(oth[:, NT - s:NT], cur[:, NT - s:NT]),
                        [ck_, ok_], [ok_])
                    cur, ck_, oth, ok_ = oth, ok_, cur, ck_
                    s *= 2
                dve(P, lambda e, cur=cur: e.tensor_copy(mx[:, 0:NT - 1], cur[:, 1:NT]), [ck_, "mx%d" % d], ["mx%d" % d])
        P.mm(pp[k][0:NT, 0:2], mx[:], ident[0:2, 0:2], reads=["mx%d" % d, "ident"], writes=[("pp", k)])
        P.act(cmx[:], pp[k][0:NT, 0:2], AF.Copy, reads=[("pp", k)], writes=["cmx%d" % d])
        for hl in range(2):
            dve(P, lambda e, hl=hl: e.tensor_scalar(Mt[:, hl, :], Mt[:, hl, :], cmx[:, hl:hl + 1], None, ALU.max),
                [nM, "cmx%d" % d], [nM])
        dve(P, lambda e: e.tensor_scalar(Mt[:], Mt[:], -1.0, 0.0, ALU.mult, ALU.min), [nM], [nM])
        dve(P, lambda e: e.tensor_tensor(Et[:], Mt[:], Wt[:], ALU.subtract), [nM, nW], [nE])
        P.act(Et[:], Et[:], AF.Exp, reads=[nE], writes=[nE])
        if d == 0:
            for j, (src_t, sn) in enumerate(((Ct, nC), (Ct, nC), (Et, nE), (Et, nE))):
                P.mm(pp[k][:, j * NT:(j + 1) * NT], src_t[:, j % 2, :], ident[0:NT, 0:NT], reads=[sn, "ident"],
                     writes=[("pp", k)])
            P.act(tok[0][:].rearrange("p a b -> p (a b)"), pp[k][:, 0:4 * NT], AF.Copy, reads=[("pp", k)],
                  writes=["tok0"])
            dve(P, lambda e: e.tensor_copy(A[0][:], Mt[:]), [nM], ["A0"])
        else:
            Yb = P.sb("Yb", [128, 6, NT], F32)
            for j, (src_t, sn) in enumerate(((Ct, nC), (Ct, nC), (Et, nE), (Et, nE), (Mt, nM), (Mt, nM))):
                P.mm(pp[k][:, j * NT:(j + 1) * NT], src_t[:, j % 2, :], ident[0:NT, 0:NT], reads=[sn, "ident"],
                     writes=[("pp", k)])
            P.act(Yb[:].rearrange("p a b -> p (a b)"), pp[k][:, 0:6 * NT], AF.Copy, reads=[("pp", k)], writes=["Yb"])
            P.mm(pp[k][:, 0:4 * NT], anti[:], Yb[:, 0:4, :].rearrange("p a b -> p (a b)"), reads=["anti", "Yb"],
                 writes=[("pp", k)])
            P.act(tok[1][:].rearrange("p a b -> p (a b)"), pp[k][:, 0:4 * NT], AF.Copy, reads=[("pp", k)],
                  writes=["tok1"])
            for hl in range(2):
                P.mm(pp[k][0:NT, hl * 128:(hl + 1) * 128], Yb[:, 4 + hl, :], anti[:], reads=["Yb", "anti"],
                     writes=[("pp", k)])
            P.act(A[1][:].rearrange("p a b -> p (a b)"), pp[k][0:NT, 0:256], AF.Copy, reads=[("pp", k)], writes=["A1"])


    for d in range(2):
        gate_dir(d)

    pa = pp[1]
    wg = [P.sb("wg%d" % i, [128, 512], F32) for i in range(2)]
    pt = [P.sb("pt%d" % i, [128, 512], BF16) for i in range(3)]
    ad = P.sb("ad", [128, 4], F32)
    hs = P.sb("hs", [128, 4, 128], F32)
    junk = P.sb("junk", [128, 128], F32)
    ssq = P.sb("ssq", [128, 2], F32)
    sgo = [P.sb("sgo%d" % i, [128, 128], F32) for i in range(2)]
    yt = [P.sb("yt%d" % i, [128, 4, 128], BF16) for i in range(2)]
    cnt = 0
    ycnt = 0
    pcnt = 0
    for hl in range(2):
        for g in range(NQ):
            qsl = slice(g * 512, (g + 1) * 512)
            for d in range(2):
                pob = po[pcnt % 2]
                pok = ("po", pcnt % 2)
                pcnt += 1
                for tl in range(4):
                    P.mm(pa[:, tl * 128:(tl + 1) * 128], sel2[0:NT, (4 * g + tl) * 128:(4 * g + tl + 1) * 128], A[d][:, hl, :], reads=["sel", "A%d" % d],
                         writes=[("pp", 1)])
                P.mm(pob[:].rearrange("p a b -> p (a b)"), zt[:, 0:128], zt[:, 0:512], start=True, stop=False,
                     reads=["zt"], writes=[pok])
                P.mm(pd[:, 0:4], zt[:, 0:128], zt[:, 0:4], start=True, stop=False, reads=["zt"], writes=["pd"])
                tiles = list(range(0, 4 * g + 4)) if d == 0 else list(range(4 * g, NT))
                def mfront(ti, base, tiles=tiles, d=d, hl=hl, g=g, qsl=qsl):
                    t = tiles[ti]
                    b, b3 = (base + ti) % 2, (base + ti) % 3
                    tp = t - 4 * g
                    diag = 0 <= tp <= 3
                    P.mm(st[b][:], kz[hl][:, t * 128:(t + 1) * 128], qT[:, qsl], reads=["kz%d" % hl, ("qT", g)],
                         writes=[("st", b)])
                    if diag:
                        dve(P, lambda e: e.tensor_scalar(wg[b][:], pa[:], tok[d][:, hl, t:t + 1], 0.0, ALU.add, ALU.min),
                            [("pp", 1), "tok%d" % d], [("wg", b)])
                        P.act(wg[b][:], wg[b][:], AF.Exp, reads=[("wg", b)], writes=[("wg", b)])
                        dve(P, lambda e: e.tensor_tensor(wg[b][:], wg[b][:], mask[:, d, tp, :], ALU.mult),
                            [("wg", b), "mask"], [("wg", b)])
                    else:
                        P.act(wg[b][:], pa[:], AF.Exp, reads=[("pp", 1), "tok%d" % d], writes=[("wg", b)],
                              bias=tok[d][:, hl, t:t + 1])
                    dve(P, lambda e: e.tensor_tensor(pt[b3][:], st[b][:], wg[b][:], ALU.mult),
                        [("st", b), ("wg", b)], [("pt", b3)])

                def mback(ti, base, tiles=tiles, d=d, hl=hl, g=g, pob=pob, pok=pok):
                    t = tiles[ti]
                    b3 = (base + ti) % 3
                    last = (ti == len(tiles) - 1)
                    tp = t - 4 * g
                    diag = 0 <= tp <= 3
                    for qs in range(4):
                        if diag and ((d == 0 and qs < tp) or (d == 1 and qs > tp)):
                            continue
                        P.mm(pob[:, qs, :], pt[b3][:, qs * 128:(qs + 1) * 128], vt[:, t, hl * 128:(hl + 1) * 128],
                             start=False, stop=(last and qs == 3), reads=[("pt", b3), "vt"], writes=[pok])
                        P.mm(pd[:, qs:qs + 1], pt[b3][:, qs * 128:(qs + 1) * 128], ones[:, 0:1], start=False,
                             stop=(last and qs == 3), reads=[("pt", b3), "ones"], writes=["pd"])

                mfront(0, cnt)
                for ti in range(len(tiles)):
                    if ti + 1 < len(tiles):
                        mfront(ti + 1, cnt)
                    mback(ti, cnt)
                cnt += len(tiles)
                P.act(ad[:], pd[:, 0:4], AF.Abs, reads=["pd"], writes=["ad"])
                dve(P, lambda e, d=d, hl=hl, g=g: e.tensor_tensor(ad[:], ad[:], tok[d][:, 2 + hl, 4 * g:4 * g + 4],
                                                              ALU.max), ["ad", "tok%d" % d], ["ad"])
                dve(P, lambda e: e.reciprocal(ad[:], ad[:]), ["ad"], ["ad"])
                for qs in range(4):
                    if d == 0:
                        P.act(hs[:, qs, :], pob[:, qs, :], AF.Copy, reads=[pok, "ad"], writes=["hs"],
                              scale=ad[:, qs:qs + 1])
                    else:
                        dve(P, lambda e, qs=qs, pob=pob: e.scalar_tensor_tensor(hs[:, qs, :], pob[:, qs, :],
                                                                            ad[:, qs:qs + 1], hs[:, qs, :], ALU.mult,
                                                                            ALU.add), [pok, "ad", "hs"], ["hs"])
            yb = yt[ycnt % 2]
            ykey = ("yt", ycnt % 2)
            ycnt += 1
            for qs in range(4):
                k2 = qs % 2
                t = 4 * g + qs
                proj_tm(P, pp[0][:, 0:128], ("pp", 0), wbf, 520 + hl * 128, 128, xnT, t)
                P.act(sgo[k2][:], pp[0][:, 0:128], AF.Sigmoid, reads=[("pp", 0)], writes=[("sgo", k2)])
                P.op("pool", lambda e, k2=k2: e.tensor_tensor(sgo[k2][:], sgo[k2][:], gb[:], ALU.mult),
                     reads=[("sgo", k2), "gb"], writes=[("sgo", k2)])
                P.act(junk[:], hs[:, qs, :], AF.Square, reads=["hs"], writes=["junk", ("ssq", k2)],
                      accum_out=ssq[:, k2:k2 + 1])
                P.act(ssq[:, k2:k2 + 1], ssq[:, k2:k2 + 1], AF.Sqrt, reads=[("ssq", k2)], writes=[("ssq", k2)],
                      scale=1.0 / 128, bias=1e-6)
                dve(P, lambda e, k2=k2: e.reciprocal(ssq[:, k2:k2 + 1], ssq[:, k2:k2 + 1]), [("ssq", k2)],
                    [("ssq", k2)])
                dve(P, lambda e, qs=qs, k2=k2, yb=yb: e.scalar_tensor_tensor(yb[:, qs, :], hs[:, qs, :],
                                                                         ssq[:, k2:k2 + 1], sgo[k2][:], ALU.mult,
                                                                         ALU.mult),
                    ["hs", ("ssq", k2), ("sgo", k2)], [ykey])
            P.dma("sp", y[g * 512:(g + 1) * 512, hl * 128:(hl + 1) * 128].rearrange("(q p) e -> p q e", p=128), yb[:],
                  ykey, reads=[ykey], is_out=True)
    print("mlstm stats", P.stats())
    return (P.finish() if own else None)


def mlstm_inputs(xTb, g1l, w_in_l, gate_b_l, ng_l, hh, S=S):
    NT = S // 128
    offs = np.cumsum((0,) + (1536, 512, 16, 512, 512, 512, 512, 128, 128, 256, 256, 512, 16, 512))
    cq, ck, cv, cif, co = offs[9], offs[10], offs[11], offs[12], offs[13]
    hs_ = [2 * hh, 2 * hh + 1]
    qc = np.concatenate([cq + h * 64 + np.arange(64) for h in hs_])
    kc = np.concatenate([ck + h * 64 + np.arange(64) for h in hs_])
    vc = np.concatenate([cv + h * 128 + np.arange(128) for h in hs_])
    oc = np.concatenate([co + h * 128 + np.arange(128) for h in hs_])
    gsel = [(i_f, dr, h) for i_f in range(2) for dr in range(2) for h in hs_]
    gc = np.array([cif + i_f * 8 + dr * 4 + h for (i_f, dr, h) in gsel])
    gbv = np.array([gate_b_l[i_f, dr, h] for (i_f, dr, h) in gsel], np.float32)
    cols = np.concatenate([qc, kc, vc, gc, oc])
    assert len(cols) == D_NCOLS
    ident = np.eye(128, dtype=np.float32)
    anti = np.ascontiguousarray(ident[::-1])
    sel = np.zeros((NT, NT, 128), np.float32)
    for k in range(NT):
        sel[k, k, :] = 1.0
    j = np.arange(128)[:, None]
    i = np.arange(512)[None, :]
    mask = np.zeros((128, 2, 4, 512), np.float32)
    for tp in range(4):
        mask[:, 0, tp, :] = (128 * tp + j <= i)
        mask[:, 1, tp, :] = (128 * tp + j >= i)
    gbias = np.ascontiguousarray(np.tile(gbv[None, None, :], (128, NT, 1)).reshape(128, NT * 8))
    gb = np.ascontiguousarray(np.tile(ng_l[None, :], (128, 1)).astype(np.float32))
    return dict(xT=xTb, g1=g1l, w=wlayout(w_in_l, cols), ident=ident, anti=anti,
                sel=np.ascontiguousarray(sel.reshape(NT, NT * 128)),
                mask=np.ascontiguousarray(mask.reshape(128, -1)).astype(ml_dtypes.bfloat16), gbias=gbias, gb=gb)


A_NCOLS = 256 * 4 + 8
PTG = 256


def prologue_small(P, xT, g1, wsrc, ncols, S, wst=None):
    xnT = P.sb("xnT", [128, 8, S], BF16)
    wbf = P.sb("wbf", [128, 8, ncols], BF16)
    xs = P.sb("xs0", [128, 8, PTG], F32)
    sq = P.sb("sq", [128, 8, PTG], BF16)
    rstd = P.sb("rstd", [128, 512], F32)
    ones = P.sb("ones", [128, 128], BF16)
    g1s = P.sb("g1s", [128, 8], F32)
    pss = P.ps("pss", [128, 512], F32)
    P.op("pool", lambda e: e.memset(ones[:], 1.0), writes=["ones"])
    P.dma("sp", g1s[:], g1, "g1s", writes=["g1s"])
    if wst is None:
        wst = [P.sb("wst%d" % i, [128, ncols], F32)[:] for i in range(2)]
    for c in range(8):
        i = c % 2
        P.dma("sp", wst[i], wsrc[:, c * ncols:(c + 1) * ncols], ("wst", i), writes=[("wst", i)])
        P.op("pool", lambda e, i=i, c=c: e.tensor_copy(wbf[:, c, :], wst[i]), reads=[("wst", i)],
             writes=["wbf"])
    xn_src = (getattr(P, "override", None) or {}).get("xnT_src")
    if xn_src is not None:
        for g in range(S // TG):
            tsl = slice(g * TG, (g + 1) * TG)
            P.dma("sp", xnT[:, :, tsl], xn_src[:, :, tsl], ("xnld", g), writes=[("xnT", g)])
        P.xs0 = xs
        return xnT, wbf, ones, pss, rstd
    for g in range(S // PTG):
        tsl = slice(g * PTG, (g + 1) * PTG)
        P.dma("sp", xs[:], xT[:, :, tsl], ("xs", 0), writes=[("xs", 0)])
        P.act(sq[:], xs[:], AF.Square, reads=[("xs", 0)], writes=["sq"])
        for c in range(8):
            P.mm(pss[:, 0:PTG], ones[:], sq[:, c, :], start=(c == 0), stop=(c == 7), reads=["ones", "sq"],
                 writes=["pss"])
        P.act(rstd[:, 0:PTG], pss[:, 0:PTG], AF.Sqrt, reads=["pss"], writes=["rstd"], scale=1.0 / 1024, bias=1e-6)
        P.op("dve", lambda e: e.reciprocal(rstd[:, 0:PTG], rstd[:, 0:PTG]), reads=["rstd"], writes=["rstd"])
        for c in range(8):
            P.op("dve", lambda e, c=c, tsl=tsl: e.scalar_tensor_tensor(
                xnT[:, c, tsl], xs[:, c, :], g1s[:, c:c + 1], rstd[:, 0:PTG], ALU.mult, ALU.mult),
                reads=[("xs", 0), "g1s", "rstd"], writes=[("xnT", (g * PTG) // TG)])
    P.xs0 = xs
    return xnT, wbf, ones, pss, rstd


def build_gdn(S=S, P=None):
    import os
    STOP = int(os.environ.get("GDN_STOP", "99"))
    PST = int(os.environ.get("GDN_PST", "99"))
    LST = int(os.environ.get("GDN_LST", "99"))
    VAR = int(os.environ.get("GDN_VAR", "0"))
    own = P is None
    if own:
        P = Prog()
    NT = S // 128
    NG = S // TG
    NC = S // 64
    xT = P.dram("xT", [128, 8, S], F32, "ExternalInput")
    g1 = P.dram("g1", [128, 8], F32, "ExternalInput")
    w = P.dram("w", [128, 8 * A_NCOLS], F32, "ExternalInput")
    convd = P.dram("convw", [128, 6, 5], F32, "ExternalInput")
    identd = P.dram("ident", [128, 128], F32, "ExternalInput")
    antid = P.dram("anti", [128, 128], F32, "ExternalInput")
    maskd = P.dram("mask", [128, 4 * 128], BF16, "ExternalInput")
    biasd = P.dram("gbias", [128, NT * 8], F32, "ExternalInput")
    alogd = P.dram("alog", [128, 4], F32, "ExternalInput")
    rmd = P.dram("rmask", [NT, 128], F32, "ExternalInput")
    gbd = P.dram("gb", [128, 128], F32, "ExternalInput")
    y = P.dram("y", [S, 256], BF16, "ExternalOutput")

    raw = P.sb("raw", [128, S + 4], F32)
    half = (S + 4) // 2
    xnT, wbf, ones, pss, rstd = prologue_small(P, xT, g1, w, A_NCOLS, S,
                                               wst=[raw[:, 0:A_NCOLS], raw[:, half:half + A_NCOLS]] if half >= A_NCOLS else None)
    ident = P.sb("ident", [128, 128], F32)
    anti = P.sb("anti", [128, 128], F32)
    identb = P.sb("identb", [128, 128], BF16)
    mask = P.sb("mask", [128, 4, 128], BF16)
    gbias = P.sb("gbias", [128, NT * 8], F32)
    nea = P.sb("nea", [128, 4], F32)
    rm = P.sb("rm", [NT, 128], F32)
    gb = P.sb("gb", [128, 128], F32)
    convw = P.sb("convw", [128, 6, 5], F32)
    P.dma("sp", ident[:], identd, "ident", writes=["ident"])
    P.dma("sp", anti[:], antid, "anti", writes=["anti"])
    P.dma("sp", mask[:].rearrange("p a b -> p (a b)"), maskd, "mask", writes=["mask"])
    P.dma("sp", gbias[:], biasd, "gbias", writes=["gbias"])
    P.dma("sp", nea[:], alogd, "nea", writes=["nea"])
    P.dma("sp", rm[:], rmd, "rm", writes=["rm"])
    P.dma("sp", gb[:], gbd, "gb", writes=["gb"])
    P.dma("sp", convw[:], convd, "convw", writes=["convw"])
    P.op("pool", lambda e: e.tensor_copy(identb[:], ident[:]), reads=["ident"], writes=["identb"])
    P.act(nea[:], nea[:], AF.Exp, reads=["nea"], writes=["nea"])
    P.op("dve", lambda e: e.tensor_scalar(nea[:], nea[:], -1.0, None, ALU.mult), reads=["nea"], writes=["nea"])
    onesf = P.sb("onesf", [NT, 128], F32)
    P.op("pool", lambda e: e.memset(onesf[:], 1.0), writes=["onesf"])

    pp = [P.ps("pp%d" % i, [128, 512], F32) for i in range(2)]
    ptr = P.ps("ptr", [128, 4, 128], BF16)

    qT = [P.sb("qT%d" % h, [128, S], BF16) for h in range(2)]
    kT = [P.sb("kT%d" % h, [128, S], BF16) for h in range(2)]
    vtok = [P.sb("vtok%d" % h, [128, NT, 128], BF16) for h in range(2)]
    sz = P.sb("sz", [128, NT, 256], BF16)
    xsf = P.xs0[:].rearrange("p a b -> p (a b)")
    acc = [xsf[:, i * TG:(i + 1) * TG] for i in range(2)]
    sil = [xsf[:, (2 + i) * TG:(3 + i) * TG] for i in range(2)]
    P.op("pool", lambda e: e.memset(xsf[:, 0:4 * TG], 0.0),
         writes=[("xs", 0), ("acc", 0), ("acc", 1), ("sil", 0), ("sil", 1)])
    sqg = P.sb("sqg", [128, TG], BF16)
    vTg = P.sb("vTg", [128, TG], BF16)
    P.op("pool", lambda e: e.memset(raw[:, 0:2], 0.0), reads=["wbf"], writes=["rawpad"])
    P.op("pool", lambda e: e.memset(raw[:, S + 2:S + 4], 0.0), reads=["wbf"], writes=["rawpad"])
    allraw = [("raw", g) for g in range(NG)]
    for ci in range(6):
        kind, hl = ci // 2, ci % 2
        for g in range(NG):
            k = g % 2
            proj_fm(P, pp[k], ("pp", k), wbf, ci * 128, 128, xnT, g)
            P.act(raw[:, 2 + g * TG:2 + (g + 1) * TG], pp[k][:], AF.Copy, reads=[("pp", k)], writes=[("raw", g)])
        for g in range(NG):
            k = g % 2
            a_, s_ = acc[k], sil[k]
            nb = [("raw", gg) for gg in (g - 1, g, g + 1) if 0 <= gg < NG] + ["rawpad", "convw"]
            base = g * TG
            P.op("dve", lambda e, a_=a_, base=base, ci=ci: e.tensor_scalar(a_, raw[:, base:base + TG],
                                                                        convw[:, ci, 0:1], None, ALU.mult),
                 reads=nb, writes=[("acc", k)])
            for tap in range(1, 5):
                P.op("dve", lambda e, a_=a_, base=base, ci=ci, tap=tap: e.scalar_tensor_tensor(
                    a_, raw[:, base + tap:base + tap + TG], convw[:, ci, tap:tap + 1], a_, ALU.mult, ALU.add),
                    reads=nb + [("acc", k)], writes=[("acc", k)])
            if kind < 2:
                P.act(s_, a_, AF.Silu, reads=[("acc", k)], writes=[("sil", k)])
                P.op("pool", lambda e, s_=s_: e.tensor_tensor(sqg[:], s_, s_, ALU.mult), reads=[("sil", k)],
                     writes=["sqg"])
                P.mm(pss[:], ones[:], sqg[:], reads=["ones", "sqg"], writes=["pss"])
                P.act(rstd[:], pss[:], AF.Sqrt, reads=["pss"], writes=["rstd"], bias=1e-6)
                P.op("dve", lambda e: e.reciprocal(rstd[:], rstd[:]), reads=["rstd"], writes=["rstd"])
                dst = (qT if kind == 0 else kT)[hl]
                dkey = ("qT%d" % hl if kind == 0 else "kT%d" % hl, g)
                sc = (128 ** -0.5) if kind == 0 else 1.0
                P.op("dve", lambda e, s_=s_, dst=dst, g=g, sc=sc: e.scalar_tensor_tensor(
                    dst[:, g * TG:(g + 1) * TG], s_, sc, rstd[:], ALU.mult, ALU.mult),
                    reads=[("sil", k), "rstd"], writes=[dkey])
            else:
                P.act(vTg[:], a_, AF.Silu, reads=[("acc", k)], writes=["vTg"])
                for j in range(4):
                    P.tr(ptr[:, j, :], vTg[:, j * 128:(j + 1) * 128], identb[:], reads=["vTg", "identb"],
                         writes=["ptr"])
                P.op("pool" if False else "act", lambda e, hl=hl, g=g: e.copy(
                    vtok[hl][:, 4 * g:4 * g + 4, :], ptr[:]), reads=["ptr"], writes=["vtok%d" % hl])
    if STOP <= 1:
        return (P.finish() if own else None)
    for t in range(NT):
        k = t % 2
        proj_tm(P, pp[k][:, 0:256], ("pp", k), wbf, 768, 256, xnT, t)
        P.act(sz[:, t, :], pp[k][:, 0:256], AF.Silu, reads=[("pp", k)], writes=["sz"])
    for t in range(NT):
        proj_tm(P, pp[0][:, t * 8:(t + 1) * 8], ("pp", 0), wbf, 1024, 8, xnT, t)
    gtok = P.sb("gtok", [128, NT, 8], F32)
    gtokR = P.sb("gtokR", [128, NT, 8], F32)
    dve(P, lambda e: e.tensor_tensor(gtok[:].rearrange("p a b -> p (a b)"), pp[0][:, 0:NT * 8], gbias[:], ALU.add),
        [("pp", 0), "gbias"], ["gtok"])
    P.mm(pp[1][:, 0:NT * 8], anti[:], gtok[:].rearrange("p a b -> p (a b)"), reads=["anti", "gtok"],
         writes=[("pp", 1)])
    P.act(gtokR[:].rearrange("p a b -> p (a b)"), pp[1][:, 0:NT * 8], AF.Copy, reads=[("pp", 1)], writes=["gtokR"])

    if STOP <= 2:
        return (P.finish() if own else None)
    tokq = [P.sb("tokq%d" % d, [128, 2, 6, NT], F32) for d in range(2)]
    Gc = [P.sb("Gc%d" % d, [NT, 2, 128], F32) for d in range(2)]
    egt = [P.sb("egt%d" % d, [128, 2, 2, NT], F32) for d in range(2)]

    LPs = P.sb("LPs", [NT, 4, 128], F32)
    gtmp = [P.sb("gtmp%d" % i, [NT, 2, 128], F32) for i in range(5)]
    TBs = P.sb("TBs", [NT, 128], F32)
    Yb = P.sb("Yb", [128, 8, NT], F32)

    def gate_dir(d):
        src = gtok if d == 0 else gtokR
        sk = "gtok" if d == 0 else "gtokR"
        k = d
        LP = LPs
        L = "LP"
        for j in range(4):
            col = (0, 1, 4, 5)[j] + 2 * d
            P.mm(pp[k][0:NT, j * 128:(j + 1) * 128], src[:, :, col], ident[:], reads=[sk, "ident"], writes=[("pp", k)])
        P.act(LP[:].rearrange("p a b -> p (a b)"), pp[k][0:NT, :], AF.Copy, reads=[("pp", k)], writes=[L])
        T1, Gt, Bt, Et, Rt = gtmp
        n1, nG, nB, nE, nR = ["%s_s" % s for s in ("T1", "G", "B", "E", "R")]
        al = LP[:, 0:2, :]
        be = LP[:, 2:4, :]
        P.act(T1[:], al, AF.Abs, reads=[L], writes=[n1])
        P.act(T1[:], T1[:], AF.Exp, reads=[n1], writes=[n1], scale=-1.0)
        P.act(T1[:], T1[:], AF.Ln, reads=[n1], writes=[n1], bias=1.0)
        dve(P, lambda e: e.tensor_single_scalar(Gt[:], al, 0.0, ALU.max), [L], [nG])
        dve(P, lambda e: e.tensor_tensor(Gt[:], Gt[:], T1[:], ALU.add), [nG, n1], [nG])
        for hl in range(2):
            dve(P, lambda e, hl=hl: e.tensor_scalar(Gt[:, hl, :], Gt[:, hl, :], nea[0:NT, 2 * d + hl:2 * d + hl + 1],
                                                    None, ALU.mult), [nG, "nea"], [nG])
        P.act(Bt[:], be, AF.Sigmoid, reads=[L], writes=[nB])
        for hl in range(2):
            dve(P, lambda e, hl=hl: e.tensor_tensor_scan(T1[:, hl, :], rm[:], Gt[:, hl, :], 0.0, ALU.mult, ALU.add),
                [nG, "rm", n1], [n1])
        for hl in range(2):
            for hf in range(2):
                dve(P, lambda e, hl=hl, hf=hf: e.tensor_scalar(
                    Rt[:, hl, hf * 64:(hf + 1) * 64], T1[:, hl, hf * 64:(hf + 1) * 64], -1.0,
                    T1[:, hl, hf * 64 + 63:hf * 64 + 64], ALU.mult, ALU.add), [n1], [nR])
        P.act(Et[:], T1[:], AF.Exp, reads=[n1], writes=[nE])
        P.act(Rt[:], Rt[:], AF.Exp, reads=[nR], writes=[nR])
        TB = TBs
        for hl in range(2):
            for hf in range(2):
                ah = hf if d == 0 else 1 - hf
                dve(P, lambda e, hl=hl, hf=hf: e.tensor_scalar(TB[:], onesf[:], T1[:, hl, hf * 64 + 63:hf * 64 + 64],
                                                           None, ALU.mult), [n1, "onesf", ("pp", k)], ["TBs"])
                P.mm(pp[k][:, (hl * 2 + ah) * NT:(hl * 2 + ah + 1) * NT], TB[:], ident[0:NT, 0:NT],
                     reads=["TBs", "ident"], writes=[("pp", k)])
        P.act(egt[d][:].rearrange("p a b c -> p (a b c)"), pp[k][:, 0:4 * NT], AF.Exp, reads=[("pp", k)],
              writes=["egt%d" % d])
        srcs = ((T1, n1), (Bt, nB), (Et, nE), (Rt, nR))
        if d == 0:
            for hl in range(2):
                for qi, (tt, nn) in enumerate(srcs):
                    P.mm(pp[k][:, (hl * 4 + qi) * NT:(hl * 4 + qi + 1) * NT], tt[:, hl, :], ident[0:NT, 0:NT],
                         reads=[nn, "ident"], writes=[("pp", k)])
            for hl in range(2):
                P.act(tokq[0][:, hl, 0:4, :].rearrange("p a b -> p (a b)"), pp[k][:, hl * 4 * NT:(hl + 1) * 4 * NT],
                      AF.Copy, reads=[("pp", k)], writes=["tokq0"])
            dve(P, lambda e: e.tensor_copy(Gc[0][:], T1[:]), [n1], ["Gc0"])
        else:
            for hl in range(2):
                for qi, (tt, nn) in enumerate(srcs):
                    P.mm(pp[k][:, (hl * 4 + qi) * NT:(hl * 4 + qi + 1) * NT], tt[:, hl, :], ident[0:NT, 0:NT],
                         reads=[nn, "ident"], writes=[("pp", k)])
            P.act(Yb[:].rearrange("p a b -> p (a b)"), pp[k][:, 0:8 * NT], AF.Copy, reads=[("pp", k)], writes=["Yb"])
            P.mm(pp[k][:, 0:8 * NT], anti[:], Yb[:].rearrange("p a b -> p (a b)"), reads=["anti", "Yb"],
                 writes=[("pp", k)])
            for hl in range(2):
                P.act(tokq[1][:, hl, 0:4, :].rearrange("p a b -> p (a b)"), pp[k][:, hl * 4 * NT:(hl + 1) * 4 * NT],
                      AF.Copy, reads=[("pp", k)], writes=["tokq1"])
            for hl in range(2):
                P.mm(pp[k][0:NT, hl * 128:(hl + 1) * 128], Yb[:, hl * 4 + 0, :], anti[:], reads=["Yb", "anti"],
                     writes=[("pp", k)])
            P.act(Gc[1][:].rearrange("p a b -> p (a b)"), pp[k][0:NT, 0:256], AF.Copy, reads=[("pp", k)],
                  writes=["Gc1"])
        for hl in range(2):
            dve(P, lambda e, hl=hl: e.tensor_scalar(tokq[d][:, hl, 4, :], tokq[d][:, hl, 0, :], -1.0, None, ALU.mult),
                ["tokq%d" % d], ["tokq%d" % d])
            dve(P, lambda e, hl=hl: e.tensor_scalar(tokq[d][:, hl, 5, :], tokq[d][:, hl, 2, :], -1.0, None, ALU.mult),
                ["tokq%d" % d], ["tokq%d" % d])

    for d in range(2):
        gate_dir(d)
        if STOP <= 3 + d:
            return (P.finish() if own else None)

    XK = [("xnT", g) for g in range(NG)]
    slot = lambda i: xnT[:, i, :].rearrange("p (t c) -> p t c", c=128)
    TI = [slot(0), slot(1)]
    QK = [slot(2), slot(3)]
    KD = [slot(4), slot(5)]
    pw = pp[0]
    pn = pp[1]
    cA = [P.ps("cA%d" % i, [128, 4, 128], F32) for i in range(2)]
    cSb = [P.ps("cS%d" % i, [128, 128], F32) for i in range(2)]
    Dm = [P.sb("Dm%d" % i, [128, 128], F32) for i in range(2)]
    DT = [P.sb("DT%d" % i, [128, 128], F32) for i in range(2)]
    A0 = [P.sb("A0_%d" % i, [128, 128], BF16) for i in range(2)]
    IA = [P.sb("IA_%d" % i, [128, 128], BF16) for i in range(2)]
    AK = [P.sb("AK_%d" % i, [128, 128], BF16) for i in range(2)]
    NK = [P.sb("NK_%d" % i, [128, 128], BF16) for i in range(2)]
    PT = [P.sb("PT_%d" % i, [128, 128], BF16) for i in range(2)]
    oacc = raw[:, 0:S].rearrange("p (t c) -> p t c", c=128)
    gsel = [P.sb("gsel%d" % i, [NT, 128], F32) for i in range(2)]
    St = [P.sb("St%d" % i, [128, 128], BF16) for i in range(2)]
    Xp = [[P.sb("Xp%d_%d" % (i, hf), [128, 128], BF16) for hf in range(2)] for i in range(2)]
    Vn = [[P.sb("Vn%d_%d" % (i, hf), [128, 128], BF16) for hf in range(2)] for i in range(2)]
    otmp = [P.sb("otmp%d" % i, [128, 128], F32) for i in range(2)]
    junk_ = rstd[:, 256:384]
    ssq = P.sb("ssq", [128, 2], F32)
    gm = [rstd[:, i * 128:(i + 1) * 128] for i in range(2)]
    P.op("pool", lambda e: e.memset(rstd[:, 0:384], 0.0), writes=["rstd", ("gm", 0), ("gm", 1), "junk"])
    yt = [P.sb("yt%d" % i, [128, 128], BF16) for i in range(2)]
    fence = P.sb("fence", [128, 2], F32)
    for i in range(2):
        for hf in range(2):
            P.op("pool", lambda e, i=i, hf=hf: e.memset(Xp[i][hf][:], 0.0), writes=[("Xp", i, hf)])
            P.op("pool", lambda e, i=i, hf=hf: e.memset(Vn[i][hf][:], 0.0), writes=[("Vn", i, hf)])

    cAf = [cA[i][:].rearrange("p a b -> p (a b)") for i in range(2)]
    PW = [pp[0][:], cAf[0]]
    PN = [pp[1][:], cAf[1]]

    def precompute(hl, d, t, b):
        tsl = slice(t * 128, (t + 1) * 128)
        pw, pn = PW[b], PN[b]
        pwk, pnk = (("pp", 0), ("pp", 1)) if b == 0 else (("cA", 0), ("cA", 1))
        pwr, pnr, ptrr = ("pw_rd", b), ("pn_rd", b), "ptr_rd"
        ptk = "ptr"
        tq = tokq[d]
        gc_p, be_p, eg_p, er_p, ngc_p, neg_p = [tq[:, hl, qi, t:t + 1] for qi in range(6)]
        dve(P, lambda e: e.tensor_scalar(gsel[b][:], Gc[d][:, hl, :], ident[0:NT, t:t + 1], None, ALU.mult),
            ["Gc%d" % d, "ident"], [("gsel", b)])
        P.mm(pw[:, 0:128], kT[hl][:, tsl], kT[hl][:, tsl], reads=[("kT%d" % hl, t // 4)], writes=[pwk])
        P.mm(pw[:, 128:256], kT[hl][:, tsl], qT[hl][:, tsl], reads=[("kT%d" % hl, t // 4), ("qT%d" % hl, t // 4)],
             writes=[pwk])
        P.mm(pw[:, 256:384], onesf[:], gsel[b][:], reads=["onesf", ("gsel", b)], writes=[pwk])
        yield
        P.act(Dm[b][:], pw[:, 256:384], AF.Abs, reads=[pwk, "tokq%d" % d], writes=[("Dm", b), pwr], scale=-1.0,
              bias=gc_p)
        P.act(DT[b][:], pw[:, 256:384], AF.Abs, reads=[pwk, "tokq%d" % d], writes=[("DT", b), pwr], bias=ngc_p)
        P.act(Dm[b][:], Dm[b][:], AF.Exp, reads=[("Dm", b)], writes=[("Dm", b)], scale=-1.0)
        P.act(DT[b][:], DT[b][:], AF.Exp, reads=[("DT", b)], writes=[("DT", b)], scale=-1.0)
        yield
        dve(P, lambda e: e.tensor_tensor(Dm[b][:], Dm[b][:], mask[:, d, :], ALU.mult), [("Dm", b), "mask"], [("Dm", b)])
        dve(P, lambda e: e.tensor_tensor(DT[b][:], DT[b][:], mask[:, 2 + d, :], ALU.mult), [("DT", b), "mask"],
            [("DT", b)])
        dve(P, lambda e: e.scalar_tensor_tensor(A0[b][:], pw[:, 0:128], be_p, Dm[b][:], ALU.mult, ALU.mult),
            [pwk, ("Dm", b), "tokq%d" % d], [("A0", b), pwr])
        dve(P, lambda e: e.tensor_tensor(QK[d][:, t, :], pw[:, 128:256], DT[b][:], ALU.mult),
            [pwk, ("DT", b)], XK + [("QK", d), pwr])
        yield
        P.tr(ptr[:, 2 * b, :], A0[b][:], identb[:], reads=[("A0", b), "identb"], writes=[ptk])
        yield
        P.act(NK[b][:], ptr[:, 2 * b, :], AF.Copy, reads=[ptk], writes=[("NK", b), ptrr])
        dve(P, lambda e: e.tensor_tensor(PT[b][:], identb[:], ptr[:, 2 * b, :], ALU.subtract), [ptk, "identb"],
            [("PT", b), ptrr])
        yield
        cur_a = A0[b]
        ck = ("A0", b)
        for lev in range(5):
            P.mm(pn[:, 0:128], NK[b][:], cur_a[:], reads=[("NK", b), ck], writes=[pnk])
            if lev < 4:
                P.mm(pn[:, 128:256], cur_a[:], NK[b][:], reads=[("NK", b), ck], writes=[pnk])
            yield
            dve(P, lambda e: e.tensor_tensor(IA[b][:], pn[:, 0:128], ident[:], ALU.add), [pnk, "ident"],
                [("IA", b), pnr])
            if lev < 4:
                P.act(AK[b][:], pn[:, 0:128], AF.Copy, reads=[pnk], writes=[("AK", b), pnr])
                P.act(NK[b][:], pn[:, 128:256], AF.Copy, reads=[pnk], writes=[("NK", b), pnr])
                cur_a = AK[b]
                ck = ("AK", b)
            yield
            P.mm(pn[:, 256:384], IA[b][:], PT[b][:], reads=[("IA", b), ("PT", b)], writes=[pnk])
            yield
            if lev < 4:
                dve(P, lambda e: e.tensor_copy(PT[b][:], pn[:, 256:384]), [pnk], [("PT", b), pnr])
            else:
                P.act(TI[d][:, t, :], pn[:, 256:384], AF.Copy, reads=[pnk, "tokq%d" % d],
                      writes=XK + [("TI", d), pnr], scale=be_p)
            yield
        P.tr(ptr[:, 2 * b + 1, :], kT[hl][:, tsl], identb[:], reads=[("kT%d" % hl, t // 4), "identb"], writes=[ptk])
        yield
        P.act(KD[d][:, t, :], ptr[:, 2 * b + 1, :], AF.Copy, reads=[ptk, "tokq%d" % d], writes=XK + [("KD", d), ptrr],
              scale=er_p)

    def run_interleaved(gens):
        active = list(gens)
        while active:
            for g_ in list(active):
                try:
                    next(g_)
                except StopIteration:
                    active.remove(g_)

    def chunk_step(hl, d, c):
        t, hf = c // 2, c % 2
        R = slice(64 * hf, 64 * hf + 64)
        tsl = slice(t * 128, (t + 1) * 128)
        S_ = St[d]
        sk = ("St", d)
        ca = cA[d]
        cak = ("cA", d)
        tq = tokq[d]
        P.mm(ca[:, 0, :], kT[hl][:, tsl], S_[:], reads=[("kT%d" % hl, t // 4), sk], writes=[cak])
        P.mm(ca[:, 1, :], qT[hl][:, tsl], S_[:], reads=[("qT%d" % hl, t // 4), sk], writes=[cak])
        yield
        X = Xp[d][hf]
        dve(P, lambda e: e.scalar_tensor_tensor(X[R, :], ca[R, 0, :], tq[R, hl, 5, t:t + 1], vtok[hl][R, t, :],
                                                ALU.mult, ALU.add),
            [cak, "tokq%d" % d, "vtok%d" % hl], [("Xp", d, hf), ("ca_rd", d)])
        ot = otmp[d]
        P.act(ot[R, :], ca[R, 1, :], AF.Copy, reads=[cak, "tokq%d" % d], writes=[("otmp", d), ("ca_rd", d)],
              scale=tq[R, hl, 2, t:t + 1])
        yield
        P.mm(ca[:, 2, :], TI[d][:, t, :], X[:], reads=[("TI", d), ("Xp", d, hf)], writes=[cak])
        yield
        V = Vn[d][hf]
        P.act(V[R, :], ca[R, 2, :], AF.Copy, reads=[cak], writes=[("Vn", d, hf), ("ca_rd", d)])
        yield
        P.mm(cSb[d][:], KD[d][:, t, :], V[:], reads=[("KD", d), ("Vn", d, hf)], writes=[("cS", d)])
        P.mm(ca[:, 3, :], QK[d][:, t, :], V[:], reads=[("QK", d), ("Vn", d, hf)], writes=[cak])
        yield
        dve(P, lambda e: e.scalar_tensor_tensor(S_[:], S_[:], egt[d][:, hl, hf, t:t + 1], cSb[d][:], ALU.mult,
                                                ALU.add), [sk, ("cS", d), "egt%d" % d], [sk])
        P.op("dve", lambda e: e.tensor_tensor(ot[R, :], ca[R, 3, :], ot[R, :], ALU.add),
             reads=[cak, ("otmp", d)], writes=[("otmp", d), ("ca_rd", d)])
        P.op("pool", lambda e: e.tensor_tensor(oacc[R, t, :], oacc[R, t, :], ot[R, :], ALU.add),
             reads=[("otmp", d), ("oacc", t)], writes=[("oacc", t)])

    ycnt = 0
    for hl in range(2):
        for d in range(2):
            for t in range(0, NT, 2):
                run_interleaved([precompute(hl, d, t + i, i) for i in range(2) if t + i < NT])
            P.op("pool", lambda e, d=d: e.memset(St[d][:], 0.0), writes=[("St", d)])
        P.op("pool", lambda e: e.memset(raw[:, 0:S], 0.0),
             writes=[("oacc", t) for t in range(NT)] + allraw + ["rawpad"])
        for step in range(NC):
            run_interleaved([chunk_step(hl, 0, step), chunk_step(hl, 1, NC - 1 - step)])
        if STOP <= 6:
            return (P.finish() if own else None)
        for t in range(NT):
            k2 = t % 2
            P.act(junk_, oacc[:, t, :], AF.Square, reads=[("oacc", t)], writes=["junk", ("ssq", k2)],
                  accum_out=ssq[:, k2:k2 + 1])
            P.act(ssq[:, k2:k2 + 1], ssq[:, k2:k2 + 1], AF.Sqrt, reads=[("ssq", k2)], writes=[("ssq", k2)],
                  scale=1.0 / 128, bias=1e-6)
            dve(P, lambda e, k2=k2: e.reciprocal(ssq[:, k2:k2 + 1], ssq[:, k2:k2 + 1]), [("ssq", k2)], [("ssq", k2)])
            P.op("pool", lambda e, k2=k2, t=t, hl=hl: e.tensor_tensor(gm[k2], sz[:, t, hl * 128:(hl + 1) * 128], gb[:],
                                                                  ALU.mult), reads=["sz", "gb"], writes=[("gm", k2)])
            dve(P, lambda e, k2=k2, t=t: e.scalar_tensor_tensor(yt[k2][:], oacc[:, t, :], ssq[:, k2:k2 + 1], gm[k2],
                                                                ALU.mult, ALU.mult),
                [("oacc", t), ("ssq", k2), ("gm", k2)], [("yt", k2)])
            P.dma("sp", y[t * 128:(t + 1) * 128, hl * 128:(hl + 1) * 128], yt[k2][:], ("yt", k2), reads=[("yt", k2)],
                  is_out=True)
    print("gdn stats", P.stats())
    return (P.finish() if own else None)


def gdn_inputs(xTb, g1l, w_in_l, conv_l, alog_l, dtb_l, ng_l, hh, S=S):
    NT = S // 128
    offs = np.cumsum((0,) + (1536, 512, 16, 512, 512, 512, 512, 128, 128, 256, 256, 512, 16, 512))
    cqkv, cz, cab = offs[0], offs[1], offs[2]
    hs_ = [2 * hh, 2 * hh + 1]
    chunks = []
    for kind in range(3):
        for h in hs_:
            chunks.append(cqkv + kind * 512 + h * 128 + np.arange(128))
    qkvc = np.concatenate(chunks)
    zc = np.concatenate([cz + h * 128 + np.arange(128) for h in hs_])
    gsel = [(ab, dr, h) for ab in range(2) for dr in range(2) for h in hs_]
    gc = np.array([cab + ab * 8 + dr * 4 + h for (ab, dr, h) in gsel])
    gbv = np.array([dtb_l[dr, h] if ab == 0 else 0.0 for (ab, dr, h) in gsel], np.float32)
    cols = np.concatenate([qkvc, zc, gc])
    assert len(cols) == A_NCOLS
    convw = np.ascontiguousarray(np.stack([conv_l[:, c - cqkv].T for c in chunks], axis=1).astype(np.float32))
    ident = np.eye(128, dtype=np.float32)
    anti = np.ascontiguousarray(ident[::-1])
    sel = np.zeros((NT, NT, 128), np.float32)
    for k in range(NT):
        sel[k, k, :] = 1.0
    p = np.arange(128)[:, None]
    f = np.arange(128)[None, :]
    same = (p // 64) == (f // 64)
    MA0 = same & (f < p)
    MA1 = same & (f > p)
    MQ0 = same & (p <= f)
    MQ1 = same & (p >= f)
    mask = np.stack([MA0, MA1, MQ0, MQ1], axis=1).astype(np.float32).reshape(128, 4 * 128)
    gbias = np.ascontiguousarray(np.tile(gbv[None, None, :], (128, NT, 1)).reshape(128, NT * 8))
    alog = np.ascontiguousarray(np.tile(np.array([alog_l[dr, h] for dr in range(2) for h in hs_], np.float32)[None],
                                        (128, 1)))
    rmask = np.ones((NT, 128), np.float32)
    rmask[:, 0] = 0.0
    rmask[:, 64] = 0.0
    gb = np.ascontiguousarray(np.tile(ng_l[None, :], (128, 1)).astype(np.float32))
    return dict(xT=xTb, g1=g1l, w=wlayout(w_in_l, cols), convw=convw, ident=ident, anti=anti,
                mask=mask.astype(ml_dtypes.bfloat16),
                gbias=gbias, alog=alog, rmask=rmask, gb=gb)


class XSrc:
    def __init__(self, fn):
        self.fn = fn

    def __getitem__(self, idx):
        tsl = idx[2]
        return self.fn(tsl.start, tsl.stop)


class RowChunks:
    def __init__(self, chunks, rows_per, col0=0, ncols=None, rowmap=None):
        self.chunks, self.rows_per, self.col0, self.ncols, self.rowmap = chunks, rows_per, col0, ncols, rowmap

    def __getitem__(self, idx):
        rs, cs = idx
        a, b_ = rs.start, rs.stop
        c0 = self.col0 + (cs.start or 0)
        c1 = self.col0 + (cs.stop if cs.stop is not None else self.ncols)
        ci, off = self.rowmap(a) if self.rowmap else (a // self.rows_per, a % self.rows_per)
        return self.chunks[ci][off:off + (b_ - a), c0:c1]


def build_fused(S_=S, depth=2):
    import math
    P = Prog()
    H = S_ // 2
    x_full = P.dram("x_full", [128, 8, S_], F32, "ExternalInput")
    x_half = P.dram("x_half", [128, 8, H], F32, "ExternalInput")
    out = P.dram("out", [128, 8, H], F32, "ExternalOutput")
    YR = 1024
    NYC = S_ // YR
    ymine = [P.scratch("ymine%d" % c, [YR, 1024], BF16) for c in range(NYC)]
    ypair = [P.scratch("ypair%d" % c, [2 * YR, 1024], BF16) for c in range(NYC)]
    NXC = H // 512
    xh = [P.scratch("xh%d" % c, [1024, 512], F32) for c in range(NXC)]
    xpair = [P.scratch("xpair%d" % c, [2048, 512], F32) for c in range(NXC)]
    RG = [[0, 1], [2, 3], [4, 5], [6, 7]]
    xn_scr = P.scratch("xn_scr", [128, 8, S_], BF16)

    def xpair_view(a, b):
        r, tg, off = a // H, (a % H) // 512, a % 512
        assert off + (b - a) <= 512
        return xpair[tg][r * 1024:(r + 1) * 1024, off:off + (b - a)].rearrange("(p c) t -> p c t", c=8)

    def xh_view(a, b):
        tg, off = a // 512, a % 512
        assert off + (b - a) <= 512
        return xh[tg][:, off:off + (b - a)].rearrange("(p c) t -> p c t", c=8)

    def ypair_rowmap(row):
        r, T = row // S_, row % S_
        return T // YR, r * YR + (T % YR)

    stats = []
    for l in range(depth):
        lam_init = 0.8 - 0.6 * math.exp(-0.3 * l)
        xsrc = x_full if l == 0 else XSrc(xpair_view)
        P.pre = "l%d_xn_" % l
        P.override = {}
        g1d = P.dram("g1", [128, 8], F32, "ExternalInput")
        xnT_, _, _ = prologue(P, xsrc, g1d, None, 0, S_)
        for c in range(8):
            P.dma("sp", xn_scr[:, c, :], xnT_[:, c, :], ("xnst", c), reads=[("xnT", g) for g in range(S_ // TG)])
        stats.append((P.pre, P.end_phase()))
        for n, (nm, bld) in enumerate((("gdn", lambda: build_gdn(S_, P=P)), ("diff", lambda: build_diff(lam_init, S_, P=P)),
                                       ("swa", lambda: build_swa(S_, P=P)), ("mlstm", lambda: build_mlstm(S_, P=P)))):
            P.pre = "l%d_%s_" % (l, nm)
            P.override = {"xT": xsrc, "y": RowChunks(ymine, YR, col0=n * 256, ncols=256), "xnT_src": xn_scr}
            bld()
            stats.append((P.pre, P.end_phase()))
        for c in range(NYC):
            P.op("pool", lambda e, c=c: e.collective_compute("AllGather", ALU.bypass, replica_groups=RG,
                                                             ins=[ymine[c].opt()], outs=[ypair[c].opt()]),
                 dma=("cc_y", c), dma_inc=1)
        P.end_phase()
        final = (l == depth - 1)
        P.pre = "l%d_dense_" % l
        P.override = {"xT": (x_half if l == 0 else XSrc(xh_view)), "ypair": RowChunks(ypair, YR, ncols=1024, rowmap=ypair_rowmap),
                      "out": (out if final else XSrc(xh_view))}
        build_dense(H, final=final, P=P)
        stats.append((P.pre, P.end_phase()))
        if not final:
            for c in range(NXC):
                P.op("pool", lambda e, c=c: e.collective_compute("AllGather", ALU.bypass, replica_groups=RG,
                                                                 ins=[xh[c].opt()], outs=[xpair[c].opt()]),
                     dma=("cc_x", c), dma_inc=1)
            P.end_phase()
    P.pre = ""
    P.override = {}
    for s_ in stats:
        print(s_)
    return P.finish()


_NC = {}


def kernel(x, norm1_g, w_in, gdn_conv_w, gdn_a_log, gdn_dt_bias, gdn_norm_g, diff_lambda, diff_norm_g, swa_sink,
           mlstm_gate_b, mlstm_norm_g, w_branch, w_gate, w_out, norm2_g, w_mlp1, w_mlp2, final_norm_g):
    f32 = lambda a: np.asarray(a, dtype=np.float32)
    x = f32(x)
    norm1_g, w_in, gdn_conv_w, gdn_a_log, gdn_dt_bias, gdn_norm_g = map(f32, (norm1_g, w_in, gdn_conv_w, gdn_a_log,
                                                                            gdn_dt_bias, gdn_norm_g))
    diff_lambda, diff_norm_g, swa_sink, mlstm_gate_b, mlstm_norm_g = map(f32, (diff_lambda, diff_norm_g, swa_sink,
                                                                             mlstm_gate_b, mlstm_norm_g))
    w_branch, w_gate, w_out, norm2_g, w_mlp1, w_mlp2, final_norm_g = map(f32, (w_branch, w_gate, w_out, norm2_g,
                                                                             w_mlp1, w_mlp2, final_norm_g))
    B, S_, D = x.shape
    depth = norm1_g.shape[0]
    H = S_ // 2
    gl = lambda g: np.ascontiguousarray(g.reshape(8, 128).T)
    if "nc" not in _NC:
        _NC["nc"] = build_fused(S_, depth)
    nc = _NC["nc"]
    xT = [fm(x[b]) for b in range(B)]
    cosT, sinT = rope_tables_np(S_)
    cores = [(b, hh) for b in range(B) for hh in range(2)]
    dense_w = [dense_layout(w_gate[l], w_branch[l], w_out[l], w_mlp1[l], w_mlp2[l]) for l in range(depth)]
    identb = np.eye(128, dtype=np.float32).astype(ml_dtypes.bfloat16)
    in_maps = []
    for (b, hh) in cores:
        im = {"x_full": xT[b], "x_half": np.ascontiguousarray(xT[b][:, :, hh * H:(hh + 1) * H])}
        for l in range(depth):
            g1 = gl(norm1_g[l])
            parts = {
                "gdn": gdn_inputs(None, g1, w_in[l], gdn_conv_w[l], gdn_a_log[l], gdn_dt_bias[l], gdn_norm_g[l], hh, S_),
                "diff": diff_inputs(None, g1, w_in[l], diff_lambda[l], diff_norm_g[l], hh, cosT, sinT),
                "swa": swa_inputs(None, g1, w_in[l], swa_sink[l], hh, cosT, sinT),
                "mlstm": mlstm_inputs(None, g1, w_in[l], mlstm_gate_b[l], mlstm_norm_g[l], hh, S_),
                "dense": dict(g1=g1, g2=gl(norm2_g[l]), g3=gl(final_norm_g), identb=identb,
                              msel=np.ascontiguousarray(np.tile(np.array([[1.0 - hh, float(hh)]], np.float32), (128, 1))),
                              **dense_w[l]),
            }
            im["l%d_xn_g1" % l] = g1
            for nm, d in parts.items():
                for k, v in d.items():
                    if k == "xT":
                        continue
                    im["l%d_%s_%s" % (l, nm, k)] = v
        in_maps.append(im)
    r = run_bass_kernel_spmd(nc, in_maps, core_ids=list(range(len(cores)))).results
    for (b, hh), o in zip(cores, r):
        xT[b][:, :, hh * H:(hh + 1) * H] = np.asarray(o["out"])
    return np.stack([unfm(xT[b]) for b in range(B)]).astype(np.float32)
```

```python
import time
import ml_dtypes
from contextlib import ExitStack
import numpy as np
import concourse.bass as bass
import concourse.mybir as mybir
from concourse.bass_utils import run_bass_kernel_spmd

F32 = mybir.dt.float32
BF16 = mybir.dt.bfloat16
ALU = mybir.AluOpType
AF = mybir.ActivationFunctionType
AX = mybir.AxisListType

ENGS = ("pe", "act", "dve", "pool", "sp")
N_DMA_SEMS = 40


class Prog:
    def __init__(self):
        self.nc = bass.Bass("TRN2", target_bir_lowering=False)
        nc = self.nc
        self.stack = ExitStack()
        self.pstack = ExitStack()
        self.sems = {e: self.stack.enter_context(nc.semaphore("se_" + e)) for e in ENGS}
        for i in range(N_DMA_SEMS):
            self.sems[("dma", i)] = self.stack.enter_context(nc.semaphore("sd_%d" % i))
        self.sems["bar"] = self.stack.enter_context(nc.semaphore("s_bar"))
        self.cnt = {e: 0 for e in ENGS}
        self.dcnt = {("dma", i): 0 for i in range(N_DMA_SEMS)}
        self.known = {e: {} for e in ENGS}
        self.phase_no = 0
        self.uid = 0
        self._reset_phase()

    def _reset_phase(self):
        self.q = {e: [] for e in ENGS}
        self.last_w = {}
        self.readers = {}
        self.dma_map = {}
        self.out_tokens = []

    def dram(self, name, shape, dtype, kind):
        ov = getattr(self, "override", None) or {}
        if name in ov:
            return ov[name]
        full = getattr(self, "pre", "") + name
        self.ext_names = getattr(self, "ext_names", [])
        self.ext_names.append(full)
        return self.nc.dram_tensor(full, list(shape), dtype, kind=kind).ap()

    def scratch(self, name, shape, dtype):
        return self.nc.dram_tensor(name, list(shape), dtype).ap()

    def sb(self, name, shape, dtype, glob=False):
        self.uid += 1
        st = self.stack if glob else self.pstack
        return st.enter_context(self.nc.sbuf_tensor("s%d_%s" % (self.uid, name), list(shape), dtype))

    def ps(self, name, shape, dtype=F32):
        self.uid += 1
        return self.pstack.enter_context(self.nc.psum_tensor("p%d_%s" % (self.uid, name), list(shape), dtype))

    def op(self, eng, fn, reads=(), writes=(), dma=None, is_out=False, dma_inc=16):
        toks = []
        for k in reads:
            toks += self.last_w.get(k, [])
        for k in writes:
            toks += self.last_w.get(k, [])
            toks += self.readers.get(k, [])
        need = {}
        for s, v in toks:
            if eng == "pe" and s == "pe":
                continue
            if v > need.get(s, 0):
                need[s] = v
        waits = []
        kn = self.known[eng]
        for s, v in need.items():
            if kn.get(s, 0) >= v:
                continue
            kn[s] = v
            waits.append((s, v))
        if dma is not None:
            if dma not in self.dma_map:
                assert len(self.dma_map) < N_DMA_SEMS, "too many DMA groups in one phase"
                self.dma_map[dma] = ("dma", len(self.dma_map))
            sk = self.dma_map[dma]
            self.dcnt[sk] += dma_inc
            tok = (sk, self.dcnt[sk])
            inc = dma_inc
        else:
            self.cnt[eng] += 1
            tok = (eng, self.cnt[eng])
            inc = 1
        self.q[eng].append((waits, fn, tok, inc))
        for k in reads:
            self.readers.setdefault(k, []).append(tok)
        for k in writes:
            self.last_w[k] = [tok]
            self.readers[k] = []
        if is_out:
            self.out_tokens.append(tok)
        return tok

    def mm(self, out, lhsT, rhs, start=True, stop=True, reads=(), writes=()):
        return self.op("pe", lambda e: e.matmul(out, lhsT, rhs, start=start, stop=stop), reads, writes)

    def tr(self, out, in_, ident, reads=(), writes=()):
        return self.op("pe", lambda e: e.transpose(out, in_, ident), reads, writes)

    def act(self, out, in_, func, reads=(), writes=(), eng="act", **kw):
        return self.op(eng, lambda e: e.activation(out, in_, func, **kw), reads, writes)

    def dma(self, eng, out, in_, key, reads=(), writes=(), is_out=False, **kw):
        return self.op(eng, lambda e: e.dma_start(out=out, in_=in_, **kw), reads, writes, dma=key, is_out=is_out)

    def end_phase(self):
        nc = self.nc
        self.phase_no += 1
        pno = self.phase_no
        fin = {}
        for e in ENGS:
            if e != "sp" and self.cnt[e] > self.known["sp"].get(e, 0):
                fin[e] = self.cnt[e]
        for key, sk in self.dma_map.items():
            if self.dcnt[sk] > self.known["sp"].get(sk, 0):
                fin[sk] = self.dcnt[sk]
        for s, v in fin.items():
            self.known["sp"][s] = v
        sems = self.sems
        q = self.q
        first = (pno == 1)

        def replay(name, eng):
            if not first:
                eng.wait_ge(sems["bar"], pno - 1)
            for waits, fn, tok, inc in q[name]:
                for s, v in waits:
                    eng.wait_ge(sems[s], v)
                fn(eng).then_inc(sems[tok[0]], inc)
            if name == "sp":
                for s, v in fin.items():
                    eng.wait_ge(sems[s], v)
                eng.sem_inc(sems["bar"], 1)

        with nc.Block() as block:
            @block.tensor
            def _(e):
                replay("pe", e)

            @block.scalar
            def _(e):
                replay("act", e)

            @block.vector
            def _(e):
                replay("dve", e)

            @block.gpsimd
            def _(e):
                replay("pool", e)

            @block.sync
            def _(e):
                replay("sp", e)
        st = {e: len(q[e]) for e in ENGS}
        for e in ENGS:
            for e2 in ENGS:
                self.known[e][e2] = self.cnt[e2]
            for sk, v in self.dcnt.items():
                self.known[e][sk] = v
        self._reset_phase()
        self.pstack.close()
        self.pstack = ExitStack()
        return st

    def finish(self):
        self.end_phase()
        self.stack.close()
        return self.nc

    def stats(self):
        return {e: len(self.q[e]) for e in ENGS}


TG = 512


class WLoader:
    def __init__(self, P, maxcols, nbuf=3, cast_engs=("pool", "act", "dve", "pool")):
        self.P = P
        self.nbuf = nbuf
        self.st = [P.sb("wst%d" % i, [128, maxcols], F32) for i in range(nbuf)]
        self.bf = [P.sb("wbf%d" % i, [128, maxcols], BF16) for i in range(nbuf)]
        self.i = 0
        self.engs = cast_engs

    def load(self, src, ncols):
        P = self.P
        i = self.i % self.nbuf
        eng = self.engs[self.i % len(self.engs)]
        self.i += 1
        st, bf = self.st[i], self.bf[i]
        P.dma("sp", st[:, 0:ncols], src, ("wst", i), writes=[("wst", i)])
        if eng == "act":
            P.op("act", lambda e: e.copy(bf[:, 0:ncols], st[:, 0:ncols]), reads=[("wst", i)], writes=[("wbf", i)])
        else:
            P.op(eng, lambda e: e.tensor_copy(bf[:, 0:ncols], st[:, 0:ncols]), reads=[("wst", i)],
                 writes=[("wbf", i)])
        return bf, ("wbf", i)


def rmsnorm_fm(P, xs, xkey, gs, gkey, out, outkey, ones, sq, pss, rstd, T, out_scale_keyed=True):
    P.act(sq[:], xs[:], AF.Square, reads=[xkey], writes=["sq"])
    for c in range(8):
        P.mm(pss[:, 0:T], ones[:], sq[:, c, :], start=(c == 0), stop=(c == 7), reads=["ones", "sq"], writes=["pss"])
    P.act(rstd[:, 0:T], pss[:, 0:T], AF.Sqrt, reads=["pss"], writes=["rstd"], scale=1.0 / 1024, bias=1e-6)
    P.op("dve", lambda e: e.reciprocal(rstd[:, 0:T], rstd[:, 0:T]), reads=["rstd"], writes=["rstd"])
    for c in range(8):
        P.op("dve", lambda e, c=c: e.scalar_tensor_tensor(out[:, c, :], xs[:, c, :], gs[:, c:c + 1], rstd[:, 0:T],
                                                     ALU.mult, ALU.mult),
             reads=[xkey, gkey, "rstd"], writes=[outkey])


def build_dense(NT=2048, final=False, P=None):
    own = P is None
    if own:
        P = Prog()
    xT = P.dram("xT", [128, 8, NT], F32, "ExternalInput")
    ypair = (getattr(P, "override", None) or {}).get("ypair")
    SEQ = 2 * NT
    yT = P.dram("yT", [128, 16, NT], BF16, "ExternalInput") if ypair is None else None
    if ypair is not None:
        identbd = P.dram("identb", [128, 128], BF16, "ExternalInput")
        mseld = P.dram("msel", [128, 2], F32, "ExternalInput")
    g1 = P.dram("g1", [128, 8], F32, "ExternalInput")
    g2 = P.dram("g2", [128, 8], F32, "ExternalInput")
    g3 = P.dram("g3", [128, 8], F32, "ExternalInput")
    wg = P.dram("wg", [4, 8, 128, 1024], F32, "ExternalInput")
    wb = P.dram("wb", [4, 8, 128, 512], F32, "ExternalInput")
    wo = P.dram("wo", [8, 128, 1024], F32, "ExternalInput")
    w1 = P.dram("w1", [32, 128, 1024], F32, "ExternalInput")
    w2 = P.dram("w2", [8, 128, 4096], F32, "ExternalInput")
    out = P.dram("out", [128, 8, NT], F32, "ExternalOutput")

    xsb = [P.sb("xs%d" % i, [128, 8, TG], F32) for i in range(2)]
    ysb = [P.sb("ys%d" % i, [128, 16, TG], BF16) for i in range(2)]
    xn = P.sb("xn", [128, 8, TG], BF16)
    mg = P.sb("mg", [128, 8, TG], BF16)
    hT = P.sb("hT", [128, 32, TG], BF16)
    sq = P.sb("sq", [128, 8, TG], BF16)
    rstd = P.sb("rstd", [128, TG], F32)
    ones = P.sb("ones", [128, 128], BF16)
    g1s = P.sb("g1s", [128, 8], F32)
    g2s = P.sb("g2s", [128, 8], F32)
    g3s = P.sb("g3s", [128, 8], F32)
    sg = [P.sb("sg%d" % i, [128, TG], F32) for i in range(2)]
    pr = [P.sb("pr%d" % i, [128, TG], F32) for i in range(2)]
    acc = P.sb("acc", [128, TG], F32)
    rl = [P.sb("rl%d" % i, [128, TG], F32) for i in range(2)]
    xo = P.sb("xo", [128, 8, TG], F32)
    pss = P.ps("pss", [128, TG], F32)
    pg = [P.ps("pg%d" % i, [128, TG], F32) for i in range(2)]
    pb = [P.ps("pb%d" % i, [128, TG], F32) for i in range(2)]
    WL = WLoader(P, 1024, nbuf=6, cast_engs=("pool",))
    if ypair is not None:
        identb = P.sb("identb", [128, 128], BF16)
        msel = P.sb("msel", [128, 2], F32)
        cand = [P.sb("cand%d" % i, [128, 1024], BF16) for i in range(2)]
        ysel = P.sb("ysel", [128, 1024], BF16)
        ptr = P.ps("ptr", [128, 8, 128], BF16)
        P.dma("sp", identb[:], identbd, "identb", writes=["identb"])
        P.dma("sp", msel[:], mseld, "msel", writes=["msel"])

    P.op("pool", lambda e: e.memset(ones[:], 1.0), writes=["ones"])
    P.dma("sp", g1s[:], g1, "g1s", writes=["g1s"])
    P.dma("sp", g2s[:], g2, "g2s", writes=["g2s"])
    P.dma("sp", g3s[:], g3, "g3s", writes=["g3s"])
    cnt = 0

    def prep(tg):
        i_ = tg % 2
        xs, ys = xsb[i_], ysb[i_]
        xk, yk = ("xs", i_), ("ys", i_)
        tsl = slice(tg * TG, (tg + 1) * TG)
        P.dma("sp", xs[:], xT[:, :, tsl], xk, writes=[xk])
        if ypair is None:
            P.dma("sp", ys[:], yT[:, :, tsl], yk, writes=[yk])
        else:
            ys4 = ys[:].rearrange("p (n c) t -> p n c t", c=4)
            for tt in range(TG // 128):
                tok0 = tg * TG + tt * 128
                for r in range(2):
                    for h in range(2):
                        row0 = r * SEQ + h * NT + tok0
                        P.dma("sp", cand[h][:], ypair[row0:row0 + 128, :], ("cand", h), writes=[("cand", h)])
                    P.op("dve", lambda e: e.tensor_scalar(ysel[:], cand[0][:], msel[:, 0:1], None, ALU.mult),
                         reads=[("cand", 0), "msel"], writes=["ysel"])
                    P.op("dve", lambda e: e.scalar_tensor_tensor(ysel[:], cand[1][:], msel[:, 1:2], ysel[:], ALU.mult,
                                                                 ALU.add), reads=[("cand", 1), "msel", "ysel"],
                         writes=["ysel"])
                    for n in range(4):
                        for jj in range(2):
                            P.tr(ptr[:, n * 2 + jj, :], ysel[:, n * 256 + jj * 128:n * 256 + (jj + 1) * 128], identb[:],
                                 reads=["ysel", "identb"], writes=["ptr"])
                    for n in range(4):
                        P.act(ys4[:, n, 2 * r:2 * r + 2, tt * 128:(tt + 1) * 128], ptr[:, 2 * n:2 * n + 2, :], AF.Copy,
                              reads=["ptr"], writes=[yk])

    prep(0)
    for tg in range(NT // TG):
        tsl = slice(tg * TG, (tg + 1) * TG)
        xs, ys = xsb[tg % 2], ysb[tg % 2]
        xk, yk = ("xs", tg % 2), ("ys", tg % 2)
        rmsnorm_fm(P, xs, xk, g1s, "g1s", xn, "xn", ones, sq, pss, rstd, TG)
        for j in range(8):
            for n in range(4):
                k = cnt % 2
                cnt += 1
                wgt, wgk = WL.load(wg[n, j], 1024)
                for c in range(8):
                    P.mm(pg[k][:], wgt[:, c * 128:(c + 1) * 128], xn[:, c, :], start=(c == 0), stop=(c == 7),
                         reads=[wgk, "xn"], writes=[("pg", k)])
                wbt, wbk = WL.load(wb[n, j], 512)
                for c in range(4):
                    P.mm(pb[k][:], wbt[:, c * 128:(c + 1) * 128], ys[:, n * 4 + c, :], start=(c == 0), stop=(c == 3),
                         reads=[wbk, yk], writes=[("pb", k)])
                P.act(sg[k][:], pg[k][:], AF.Sigmoid, reads=[("pg", k)], writes=[("sg", k)])
                if n == 0:
                    P.op("dve", lambda e, k=k: e.tensor_tensor(acc[:], pb[k][:], sg[k][:], ALU.mult),
                         reads=[("pb", k), ("sg", k)], writes=["acc"])
                else:
                    P.op("dve", lambda e, k=k: e.tensor_tensor(pr[k][:], pb[k][:], sg[k][:], ALU.mult),
                         reads=[("pb", k), ("sg", k)], writes=[("pr", k)])
                    if n < 3:
                        P.op("dve", lambda e, k=k: e.tensor_tensor(acc[:], acc[:], pr[k][:], ALU.add),
                             reads=["acc", ("pr", k)], writes=["acc"])
                    else:
                        P.op("dve", lambda e, k=k, j=j: e.tensor_tensor(mg[:, j, :], acc[:], pr[k][:], ALU.add),
                             reads=["acc", ("pr", k)], writes=["mg"])
        for j in range(8):
            k = cnt % 2
            cnt += 1
            wt, wk = WL.load(wo[j], 1024)
            for c in range(8):
                P.mm(pg[k][:], wt[:, c * 128:(c + 1) * 128], mg[:, c, :], start=(c == 0), stop=(c == 7),
                     reads=[wk, "mg"], writes=[("pg", k)])
            P.op("dve", lambda e, k=k, j=j, xs=xs: e.tensor_tensor(xs[:, j, :], pg[k][:], xs[:, j, :], ALU.add),
                 reads=[("pg", k), xk], writes=[xk])
        rmsnorm_fm(P, xs, xk, g2s, "g2s", xn, "xn", ones, sq, pss, rstd, TG)
        for f in range(32):
            k = cnt % 2
            cnt += 1
            wt, wk = WL.load(w1[f], 1024)
            for c in range(8):
                P.mm(pg[k][:], wt[:, c * 128:(c + 1) * 128], xn[:, c, :], start=(c == 0), stop=(c == 7),
                     reads=[wk, "xn"], writes=[("pg", k)])
            P.act(rl[k][:], pg[k][:], AF.Relu, reads=[("pg", k)], writes=[("rl", k)])
            P.op("dve", lambda e, k=k, f=f: e.tensor_tensor(hT[:, f, :], rl[k][:], rl[k][:], ALU.mult),
                 reads=[("rl", k)], writes=["hT"])
            if f == 7 and tg + 1 < NT // TG:
                prep(tg + 1)
        for j in range(8):
            k = cnt % 2
            cnt += 1
            for fb in range(4):
                wt, wk = WL.load(w2[j][:, fb * 1024:(fb + 1) * 1024], 1024)
                for f8 in range(8):
                    f = fb * 8 + f8
                    P.mm(pb[k][:], wt[:, f8 * 128:(f8 + 1) * 128], hT[:, f, :], start=(f == 0), stop=(f == 31),
                         reads=[wk, "hT"], writes=[("pb", k)])
            P.op("dve", lambda e, k=k, j=j, xs=xs: e.tensor_tensor(xs[:, j, :], pb[k][:], xs[:, j, :], ALU.add),
                 reads=[("pb", k), xk], writes=[xk])
        if final:
            rmsnorm_fm(P, xs, xk, g3s, "g3s", xo, "xo", ones, sq, pss, rstd, TG)
            P.dma("sp", out[:, :, tsl], xo[:], "out", reads=["xo"], is_out=True)
        else:
            P.dma("sp", out[:, :, tsl], xs[:], ("out", tg % 2), reads=[xk], is_out=True)
    print("dense stats", P.stats())
    return (P.finish() if own else None)


def dense_layout(w_gate, w_branch, w_out, w_mlp1, w_mlp2):
    wg = w_gate.reshape(8, 128, 4, 8, 128).transpose(2, 3, 1, 0, 4).reshape(4, 8, 128, 1024)
    wb = w_branch.reshape(4, 4, 128, 8, 128).transpose(0, 3, 2, 1, 4).reshape(4, 8, 128, 512)
    wo = w_out.reshape(8, 128, 8, 128).transpose(2, 1, 0, 3).reshape(8, 128, 1024)
    w1 = w_mlp1.reshape(8, 128, 32, 128).transpose(2, 1, 0, 3).reshape(32, 128, 1024)
    w2 = w_mlp2.reshape(32, 128, 8, 128).transpose(2, 1, 0, 3).reshape(8, 128, 4096)
    return dict(wg=np.ascontiguousarray(wg), wb=np.ascontiguousarray(wb), wo=np.ascontiguousarray(wo),
                w1=np.ascontiguousarray(w1), w2=np.ascontiguousarray(w2))


def fm(a):
    T, D = a.shape
    return np.ascontiguousarray(a.T.reshape(D // 128, 128, T).transpose(1, 0, 2))


def unfm(a):
    p, C, T = a.shape
    return np.ascontiguousarray(a.transpose(2, 1, 0).reshape(T, C * 128))


S = 4096
NTILE = S // 128
TG = 512
NG = S // TG


def prologue(P, xT, g1, wsrc, ncols, S=S):
    xnT = P.sb("xnT", [128, 8, S], BF16)
    wbf = P.sb("wbf", [128, 8, max(ncols, 1)], BF16)
    xs = [P.sb("xs0", [128, 8, TG], F32)] * 2
    sq = P.sb("sq", [128, 8, TG], BF16)
    rstd = P.sb("rstd", [128, TG], F32)
    ones = P.sb("ones", [128, 128], BF16)
    g1s = P.sb("g1s", [128, 8], F32)
    pss = P.ps("pss", [128, TG], F32)
    P.op("pool", lambda e: e.memset(ones[:], 1.0), writes=["ones"])
    P.dma("sp", g1s[:], g1, "g1s", writes=["g1s"])
    wst = [P.sb("wst%d" % i, [128, max(ncols, 1)], F32) for i in range(2)]
    for c in range(8 if wsrc is not None else 0):
        i = c % 2
        P.dma("sp", wst[i][:], wsrc[:, c * ncols:(c + 1) * ncols], ("wst", i), writes=[("wst", i)])
        P.op("pool", lambda e, i=i, c=c: e.tensor_copy(wbf[:, c, :], wst[i][:]), reads=[("wst", i)],
             writes=["wbf"])
    xn_src = (getattr(P, "override", None) or {}).get("xnT_src")
    if xn_src is not None:
        for g in range(S // TG):
            tsl = slice(g * TG, (g + 1) * TG)
            P.dma("sp", xnT[:, :, tsl], xn_src[:, :, tsl], ("xnld", g), writes=[("xnT", g)])
        P.xs0 = xs[0]
        return xnT, wbf, ones
    for g in range(S // TG):
        i = 0
        tsl = slice(g * TG, (g + 1) * TG)
        P.dma("sp", xs[i][:], xT[:, :, tsl], ("xs", i), writes=[("xs", i)])
        P.act(sq[:], xs[i][:], AF.Square, reads=[("xs", i)], writes=["sq"])
        for c in range(8):
            P.mm(pss[:], ones[:], sq[:, c, :], start=(c == 0), stop=(c == 7), reads=["ones", "sq"], writes=["pss"])
        P.act(rstd[:], pss[:], AF.Sqrt, reads=["pss"], writes=["rstd"], scale=1.0 / 1024, bias=1e-6)
        P.op("dve", lambda e: e.reciprocal(rstd[:], rstd[:]), reads=["rstd"], writes=["rstd"])
        for c in range(8):
            P.op("dve", lambda e, c=c, i=i, tsl=tsl: e.scalar_tensor_tensor(
                xnT[:, c, tsl], xs[i][:, c, :], g1s[:, c:c + 1], rstd[:], ALU.mult, ALU.mult),
                reads=[("xs", i), "g1s", "rstd"], writes=[("xnT", g)])
    P.xs0 = xs[0]
    return xnT, wbf, ones


def proj_fm(P, out_ps, okey, wbf, col0, ncol, xnT, g):
    for c in range(8):
        P.mm(out_ps[0:ncol, :], wbf[:, c, col0:col0 + ncol], xnT[:, c, g * TG:(g + 1) * TG], start=(c == 0),
             stop=(c == 7), reads=["wbf", ("xnT", g)], writes=[okey])


def proj_tm(P, out_ps, okey, wbf, col0, ncol, xnT, t):
    g = (t * 128) // TG
    for c in range(8):
        P.mm(out_ps, xnT[:, c, t * 128:(t + 1) * 128], wbf[:, c, col0:col0 + ncol], start=(c == 0), stop=(c == 7),
             reads=["wbf", ("xnT", g)], writes=[okey])


def rope_proj(P, dstT, dkey, wbf, col_a, col_sw, xnT, cosT, sinT, pp, tmp, S=S):
    for g in range(S // TG):
        tsl = slice(g * TG, (g + 1) * TG)
        proj_fm(P, pp[0], ("pp", 0), wbf, col_a, 128, xnT, g)
        proj_fm(P, pp[1], ("pp", 1), wbf, col_sw, 128, xnT, g)
        P.op("dve", lambda e, tsl=tsl: e.tensor_tensor(tmp[0][:], pp[0][:], cosT[:, tsl], ALU.mult),
             reads=[("pp", 0), "cos"], writes=[("rtmp", 0)])
        P.op("dve", lambda e, tsl=tsl: e.tensor_tensor(tmp[1][:], pp[1][:], sinT[:, tsl], ALU.mult),
             reads=[("pp", 1), "sin"], writes=[("rtmp", 1)])
        P.op("pool", lambda e, tsl=tsl: e.tensor_tensor(dstT[:, tsl], tmp[0][:], tmp[1][:], ALU.add),
             reads=[("rtmp", 0), ("rtmp", 1)], writes=[(dkey, g)])


def rope_tables_np(S=S):
    inv = 1.0 / (10000.0 ** (np.arange(0, 64, 2, dtype=np.float32) / 64))
    ang = np.arange(S, dtype=np.float32)[:, None] * inv[None, :]
    ang = np.concatenate([ang, ang], axis=-1)
    cos, sin = np.cos(ang).astype(np.float32), np.sin(ang).astype(np.float32)
    sgn = np.concatenate([-np.ones(32, np.float32), np.ones(32, np.float32)])
    cosT = np.ascontiguousarray(np.tile(cos.T, (2, 1)))
    sinT = np.ascontiguousarray(np.tile((sin * sgn[None, :]).T, (2, 1)))
    return cosT, sinT


def swap_halves(cols):
    cols = np.asarray(cols).reshape(-1, 64)
    return np.concatenate([cols[:, 32:], cols[:, :32]], axis=1).reshape(-1)


def wlayout(w, cols):
    sub = w[:, cols]
    n = sub.shape[1]
    return np.ascontiguousarray(sub.reshape(8, 128, n).transpose(1, 0, 2).reshape(128, 8 * n))


C_NCOLS = 256 + 256 + 128 + 128 + 64


def build_swa(S=S, P=None):
    own = P is None
    if own:
        P = Prog()
    xT = P.dram("xT", [128, 8, S], F32, "ExternalInput")
    g1 = P.dram("g1", [128, 8], F32, "ExternalInput")
    w = P.dram("w", [128, 8 * C_NCOLS], F32, "ExternalInput")
    cosd = P.dram("cos", [128, S], F32, "ExternalInput")
    sind = P.dram("sin", [128, S], F32, "ExternalInput")
    maskd = P.dram("mask", [128, 2, 512], BF16, "ExternalInput")
    sinkd = P.dram("sink", [128, 4], F32, "ExternalInput")
    y = P.dram("y", [S, 256], BF16, "ExternalOutput")

    xnT, wbf, ones = prologue(P, xT, g1, w, C_NCOLS, S)
    cosT = P.sb("cosT", [128, S], F32)
    sinT = P.sb("sinT", [128, S], F32)
    P.dma("sp", cosT[:], cosd, "cos", writes=["cos"])
    P.dma("sp", sinT[:], sind, "sin", writes=["sin"])
    mask = P.sb("mask", [128, 2, 512], BF16)
    P.dma("sp", mask[:], maskd, "mask", writes=["mask"])
    esink = P.sb("esink", [128, 4], F32)
    P.dma("sp", esink[:], sinkd, "esink", writes=["esink"])
    P.act(esink[:], esink[:], AF.Exp, reads=["esink"], writes=["esink"])

    NT = S // 128
    qT = [P.sb("qT%d" % i, [128, S], BF16) for i in range(2)]
    kT = P.sb("kT", [128, S], BF16)
    vaug = P.sb("vaug", [128, NT, 65], BF16)
    pp = [P.ps("pp%d" % i, [128, TG], F32) for i in range(2)]
    tmp = [P.sb("rtmp%d" % i, [128, TG], F32) for i in range(2)]
    rope_proj(P, qT[0], "qT0", wbf, 0, 256, xnT, cosT, sinT, pp, tmp, S)
    rope_proj(P, qT[1], "qT1", wbf, 128, 384, xnT, cosT, sinT, pp, tmp, S)
    rope_proj(P, kT, "kT", wbf, 512, 640, xnT, cosT, sinT, pp, tmp, S)
    kz = [P.sb("kz%d" % i, [128, S], BF16) for i in range(2)]
    P.op("pool", lambda e: e.memset(kz[0][64:128, :], 0.0), writes=["kz0"])
    P.op("pool", lambda e: e.memset(kz[1][0:64, :], 0.0), writes=["kz1"])
    allk = [("kT", g) for g in range(S // TG)]
    P.op("pool", lambda e: e.tensor_copy(kz[0][0:64, :], kT[0:64, :]), reads=allk, writes=["kz0"])
    P.op("act", lambda e: e.copy(kz[1][64:128, :], kT[64:128, :]), reads=allk, writes=["kz1"])
    P.op("pool", lambda e: e.memset(vaug[:], 1.0), writes=["vaug"])
    pv = P.ps("pv", [128, 64], F32)
    for t in range(NT):
        proj_tm(P, pv[:], "pv", wbf, 768, 64, xnT, t)
        P.act(vaug[:, t, 0:64], pv[:], AF.Copy, reads=["pv"], writes=["vaug"])

    st = [P.ps("st%d" % i, [128, 512], F32) for i in range(3)]
    pt = [P.sb("pt%d" % i, [128, 512], BF16) for i in range(6)]
    po = P.ps("po", [128, 4, 65], F32)
    den = P.sb("den", [128, 4], F32)
    yt = [P.sb("yt%d" % i, [128, 4, 64], BF16) for i in range(2)]
    qkeys = lambda n: [("qT0", (n * 128) // TG), ("qT1", (n * 128) // TG)]
    dsl = lambda n: [dd for dd in (-1, 0, 1) if 0 <= n + dd < NT]

    def front(n):
        for di, dd in enumerate(dsl(n)):
            m = n + dd
            pi = (n % 2) * 3 + di
            for h in range(4):
                P.mm(st[di][:, h * 128:(h + 1) * 128], kz[h % 2][:, m * 128:(m + 1) * 128],
                     qT[h // 2][:, n * 128:(n + 1) * 128], reads=["kz0", "kz1"] + qkeys(n),
                     writes=[("st", di)])
            P.act(pt[pi][:], st[di][:], AF.Exp, reads=[("st", di)], writes=[("pt", pi)], scale=0.125)
            if dd != 0:
                mi = 0 if dd == -1 else 1
                P.op("pool", lambda e, pi=pi, mi=mi: e.tensor_tensor(pt[pi][:], pt[pi][:], mask[:, mi, :], ALU.mult),
                     reads=[("pt", pi), "mask"], writes=[("pt", pi)])

    def back(n):
        ds = dsl(n)
        for h in range(4):
            for di, dd in enumerate(ds):
                m = n + dd
                pi = (n % 2) * 3 + di
                P.mm(po[:, h, :], pt[pi][:, h * 128:(h + 1) * 128], vaug[:, m, :], start=(di == 0),
                     stop=(di == len(ds) - 1), reads=[("pt", pi), "vaug"], writes=["po"])
        P.op("dve", lambda e: e.tensor_tensor(den[:], po[:, :, 64], esink[:], ALU.add), reads=["po", "esink"],
             writes=["den"])
        P.op("dve", lambda e: e.reciprocal(den[:], den[:]), reads=["den"], writes=["den"])
        yb = yt[n % 2]
        for h in range(4):
            P.op("dve", lambda e, h=h, yb=yb: e.tensor_scalar(yb[:, h, :], po[:, h, 0:64], den[:, h:h + 1], None,
                                                           ALU.mult), reads=["po", "den"], writes=[("yt", n % 2)])
        P.dma("sp", y[n * 128:(n + 1) * 128, :], yb[:].rearrange("p h d -> p (h d)"), ("yt", n % 2),
              reads=[("yt", n % 2)], is_out=True)

    front(0)
    for n in range(NT):
        if n + 1 < NT:
            front(n + 1)
        back(n)
    print("swa stats", P.stats())
    return (P.finish() if own else None)


def swa_inputs(xTb, g1l, w_in_l, sink_l, hh, cosT, sinT):
    offs = np.cumsum((0,) + (1536, 512, 16, 512, 512, 512, 512, 128, 128, 256, 256, 512, 16, 512))
    cq, ck, cv = offs[6], offs[7], offs[8]
    qcols = cq + np.arange(hh * 256, hh * 256 + 256)
    kcols = ck + np.arange(hh * 64, hh * 64 + 64)
    vcols = cv + np.arange(hh * 64, hh * 64 + 64)
    k2 = np.concatenate([kcols, kcols])
    cols = np.concatenate([qcols, swap_halves(qcols), k2, swap_halves(k2), vcols])
    assert len(cols) == C_NCOLS
    j = np.arange(128)[:, None]
    i = np.arange(128)[None, :]
    prev = (j >= i).astype(np.float32)
    nxt = (j <= i).astype(np.float32)
    mask = np.stack([np.tile(prev, (1, 4)), np.tile(nxt, (1, 4))], axis=1).astype(ml_dtypes.bfloat16)
    sink = np.tile(sink_l[hh * 4:hh * 4 + 4][None, :], (128, 1)).astype(np.float32)
    return dict(xT=xTb, g1=g1l, w=wlayout(w_in_l, cols), cos=cosT, sin=sinT, mask=np.ascontiguousarray(mask),
                sink=np.ascontiguousarray(sink))


B_NCOLS = 5 * 256


def build_diff(lam_init, S=S, P=None):
    own = P is None
    if own:
        P = Prog()
    xT = P.dram("xT", [128, 8, S], F32, "ExternalInput")
    g1 = P.dram("g1", [128, 8], F32, "ExternalInput")
    w = P.dram("w", [128, 8 * B_NCOLS], F32, "ExternalInput")
    cosd = P.dram("cos", [128, S], F32, "ExternalInput")
    sind = P.dram("sin", [128, S], F32, "ExternalInput")
    lpd = P.dram("lp", [128, 4, 64], F32, "ExternalInput")
    gbd = P.dram("gb", [128, 128], F32, "ExternalInput")
    y = P.dram("y", [S, 256], BF16, "ExternalOutput")
    NT = S // 128
    NQ = S // 512

    xnT, wbf, ones = prologue(P, xT, g1, w, B_NCOLS, S)
    cosT = P.sb("cosT", [128, S], F32)
    sinT = P.sb("sinT", [128, S], F32)
    P.dma("sp", cosT[:], cosd, "cos", writes=["cos"])
    P.dma("sp", sinT[:], sind, "sin", writes=["sin"])
    lp = P.sb("lp", [128, 4, 64], F32)
    gb = P.sb("gb", [128, 128], F32)
    P.dma("sp", lp[:], lpd, "lp", writes=["lp"])
    P.dma("sp", gb[:], gbd, "gb", writes=["gb"])
    P.op("dve", lambda e: e.tensor_scalar(gb[:], gb[:], 1.0 - lam_init, None, ALU.mult), reads=["gb"], writes=["gb"])
    junk = P.sb("junk", [128, 128], F32)
    s12 = P.sb("s12", [128, 2], F32)
    nlam = P.sb("nlam", [128, 1], F32)
    for i in range(2):
        P.op("dve", lambda e, i=i: e.scalar_tensor_tensor(junk[:, 0:64], lp[:, 2 * i, :], 1.0, lp[:, 2 * i + 1, :],
                                                     ALU.mult, ALU.mult, accum_out=s12[:, i:i + 1]),
             reads=["lp"], writes=["junk", "s12"])
    P.act(s12[:], s12[:], AF.Exp, reads=["s12"], writes=["s12"])
    P.op("dve", lambda e: e.tensor_tensor(nlam[:], s12[:, 0:1], s12[:, 1:2], ALU.subtract), reads=["s12"],
         writes=["nlam"])
    P.op("dve", lambda e: e.tensor_scalar(nlam[:], nlam[:], -1.0, -lam_init, ALU.mult, ALU.add), reads=["nlam"],
         writes=["nlam"])

    zt = P.sb("zt", [128, 512], BF16)
    P.op("pool", lambda e: e.memset(zt[:], 0.0), writes=["zt"])
    qT = P.sb("qT", [128, S], BF16)
    kT = P.sb("kT", [128, S], BF16)
    kz = [P.sb("kz%d" % i, [128, S], BF16) for i in range(2)]
    vt = P.sb("vt", [128, NT, 129], BF16)
    P.op("pool", lambda e: e.memset(vt[:], 1.0), writes=["vt"])
    pp = [P.ps("pp%d" % i, [128, TG], F32) for i in range(2)]
    tmp = [P.sb("rtmp%d" % i, [128, TG], F32) for i in range(2)]
    st = pp
    pt = [P.sb("pt%d" % i, [128, 512], BF16) for i in range(3)]
    po = [[P.ps("po%d_%d" % (i, hf), [128, 2, 129], F32) for hf in range(2)] for i in range(2)]
    rd = P.sb("rd", [128, 2, 4], F32)
    t0 = [P.sb("t0_%d" % i, [128, 128], F32) for i in range(2)]
    ot = [P.sb("ot%d" % i, [128, 128], F32) for i in range(2)]
    ssq = P.sb("ssq", [128, 2], F32)
    yt = [P.sb("yt%d" % i, [128, 4, 128], BF16) for i in range(2)]
    P.op("pool", lambda e: e.memset(kz[0][64:128, :], 0.0), writes=["kz0"])
    P.op("pool", lambda e: e.memset(kz[1][0:64, :], 0.0), writes=["kz1"])
    allg = lambda k: [(k, g) for g in range(S // TG)]
    cnt = 0
    ycnt = 0
    for h in range(2):
        rope_proj(P, qT, "qT", wbf, h * 128, 256 + h * 128, xnT, cosT, sinT, pp, tmp, S)
        rope_proj(P, kT, "kT", wbf, 512 + h * 128, 768 + h * 128, xnT, cosT, sinT, pp, tmp, S)
        P.op("pool", lambda e: e.tensor_copy(kz[0][0:64, :], kT[0:64, :]), reads=allg("kT"), writes=["kz0"])
        P.op("act", lambda e: e.copy(kz[1][64:128, :], kT[64:128, :]), reads=allg("kT"), writes=["kz1"])
        for t in range(NT):
            proj_tm(P, pp[0][:, 0:128], ("pp", 0), wbf, 1024 + h * 128, 128, xnT, t)
            P.act(vt[:, t, 0:128], pp[0][:, 0:128], AF.Copy, reads=[("pp", 0)], writes=["vt"])
        for g in range(NQ):
            qsl = slice(g * 512, (g + 1) * 512)
            for m in range(2):
                for hf in range(2):
                    P.mm(po[m][hf][:].rearrange("p a b -> p (a b)"), zt[:, 0:128], zt[:, 0:258], start=True, stop=False,
                         reads=["zt"], writes=[("po", m)])
            blocks = [(t, m) for t in range(NT) for m in range(2)]

            def front(i, base):
                t, m = blocks[i]
                b, b3 = (base + i) % 2, (base + i) % 3
                P.mm(st[b][:], kz[m][:, t * 128:(t + 1) * 128], qT[:, qsl], reads=["kz%d" % m, ("qT", g)],
                     writes=[("pp", b)])
                P.act(pt[b3][:], st[b][:], AF.Exp, reads=[("pp", b)], writes=[("pt", b3)], scale=0.125)

            def back(i, base):
                t, m = blocks[i]
                b3 = (base + i) % 3
                last = (t == NT - 1)
                for qs in range(4):
                    P.mm(po[m][qs // 2][:, qs % 2, :], pt[b3][:, qs * 128:(qs + 1) * 128], vt[:, t, :], start=False,
                         stop=(last and qs % 2 == 1), reads=[("pt", b3), "vt"], writes=[("po", m)])

            front(0, cnt)
            for i in range(len(blocks)):
                if i + 1 < len(blocks):
                    front(i + 1, cnt)
                back(i, cnt)
            cnt += len(blocks)
            for m in range(2):
                for hf in range(2):
                    P.op("dve", lambda e, m=m, hf=hf: e.reciprocal(rd[:, m, 2 * hf:2 * hf + 2], po[m][hf][:, :, 128]),
                         reads=[("po", m)], writes=["rd"])
            P.op("dve", lambda e: e.tensor_scalar(rd[:, 1, :], rd[:, 1, :], nlam[:, 0:1], None, ALU.mult),
                 reads=["rd", "nlam"], writes=["rd"])
            yb = yt[ycnt % 2]
            ykey = ("yt", ycnt % 2)
            ycnt += 1
            for qs in range(4):
                k2 = qs % 2
                P.act(t0[k2][:], po[0][qs // 2][:, qs % 2, 0:128], AF.Copy, reads=[("po", 0), "rd"], writes=[("t0", k2)],
                      scale=rd[:, 0, qs:qs + 1])
                P.op("dve", lambda e, qs=qs, k2=k2: e.scalar_tensor_tensor(ot[k2][:], po[1][qs // 2][:, qs % 2, 0:128],
                                                                       rd[:, 1, qs:qs + 1], t0[k2][:], ALU.mult,
                                                                       ALU.add),
                     reads=[("po", 1), "rd", ("t0", k2)], writes=[("ot", k2)])
                P.act(junk[:], ot[k2][:], AF.Square, reads=[("ot", k2)], writes=["junk", ("ssq", k2)],
                      accum_out=ssq[:, k2:k2 + 1])
                P.act(ssq[:, k2:k2 + 1], ssq[:, k2:k2 + 1], AF.Sqrt, reads=[("ssq", k2)], writes=[("ssq", k2)],
                      scale=1.0 / 128, bias=1e-6)
                P.op("dve", lambda e, k2=k2: e.reciprocal(ssq[:, k2:k2 + 1], ssq[:, k2:k2 + 1]), reads=[("ssq", k2)],
                     writes=[("ssq", k2)])
                P.op("dve", lambda e, qs=qs, k2=k2, yb=yb: e.scalar_tensor_tensor(yb[:, qs, :], ot[k2][:],
                                                                              ssq[:, k2:k2 + 1], gb[:], ALU.mult,
                                                                              ALU.mult),
                     reads=[("ot", k2), ("ssq", k2), "gb"], writes=[ykey])
            P.dma("sp", y[g * 512:(g + 1) * 512, h * 128:(h + 1) * 128].rearrange("(q p) e -> p q e", p=128), yb[:],
                  ykey, reads=[ykey], is_out=True)
    print("diff stats", P.stats())
    return (P.finish() if own else None)


def diff_inputs(xTb, g1l, w_in_l, lam_l, ng_l, hh, cosT, sinT):
    offs = np.cumsum((0,) + (1536, 512, 16, 512, 512, 512, 512, 128, 128, 256, 256, 512, 16, 512))
    cq, ck, cv = offs[3], offs[4], offs[5]
    r = np.arange(hh * 256, hh * 256 + 256)
    cols = np.concatenate([cq + r, swap_halves(cq + r), ck + r, swap_halves(ck + r), cv + r])
    assert len(cols) == B_NCOLS
    lp = np.ascontiguousarray(np.tile(lam_l[None], (128, 1, 1)).astype(np.float32))
    gb = np.ascontiguousarray(np.tile(ng_l[None, :], (128, 1)).astype(np.float32))
    return dict(xT=xTb, g1=g1l, w=wlayout(w_in_l, cols), cos=cosT, sin=sinT, lp=lp, gb=gb)


D_NCOLS = 128 + 128 + 256 + 8 + 256


def dve(P, fn, reads, writes, eng="dve"):
    return P.op(eng, fn, reads, writes)


def build_mlstm(S=S, P=None):
    own = P is None
    if own:
        P = Prog()
    NT = S // 128
    NQ = S // 512
    xT = P.dram("xT", [128, 8, S], F32, "ExternalInput")
    g1 = P.dram("g1", [128, 8], F32, "ExternalInput")
    w = P.dram("w", [128, 8 * D_NCOLS], F32, "ExternalInput")
    identd = P.dram("ident", [128, 128], F32, "ExternalInput")
    antid = P.dram("anti", [128, 128], F32, "ExternalInput")
    seld = P.dram("sel", [NT, NT * 128], F32, "ExternalInput")
    maskd = P.dram("mask", [128, 2 * 4 * 512], BF16, "ExternalInput")
    biasd = P.dram("gbias", [128, NT * 8], F32, "ExternalInput")
    gbd = P.dram("gb", [128, 128], F32, "ExternalInput")
    y = P.dram("y", [S, 256], BF16, "ExternalOutput")

    xnT, wbf, ones = prologue(P, xT, g1, w, D_NCOLS, S)
    ident = P.sb("ident", [128, 128], F32)
    anti = P.sb("anti", [128, 128], F32)
    sel2 = P.xs0[:].rearrange("p a b -> p (a b)")
    mask = P.sb("mask", [128, 2, 4, 512], BF16)
    gbias = P.sb("gbias", [128, NT * 8], F32)
    gb = P.sb("gb", [128, 128], F32)
    P.dma("sp", ident[:], identd, "ident", writes=["ident"])
    P.dma("sp", anti[:], antid, "anti", writes=["anti"])
    P.dma("sp", sel2[0:NT, 0:NT * 128], seld, "sel", writes=["sel", ("xs", 0)])
    P.dma("sp", mask[:].rearrange("p a b c -> p (a b c)"), maskd, "mask", writes=["mask"])
    P.dma("sp", gbias[:], biasd, "gbias", writes=["gbias"])
    P.dma("sp", gb[:], gbd, "gb", writes=["gb"])
    zt = P.sb("zt", [128, 512], BF16)
    P.op("pool", lambda e: e.memset(zt[:], 0.0), writes=["zt"])
    onesf = P.sb("onesf", [NT, 128], F32)
    P.op("pool", lambda e: e.memset(onesf[:], 1.0), writes=["onesf"])

    pp = [P.ps("pp%d" % i, [128, 512], F32) for i in range(2)]
    st = [P.ps("st%d" % i, [128, 512], F32) for i in range(2)]
    po = [P.ps("po%d" % i, [128, 4, 128], F32) for i in range(2)]
    pd = P.ps("pd", [128, 8], F32)

    qT = P.sb("qT", [128, S], BF16)
    kz = [P.sb("kz%d" % i, [128, S], BF16) for i in range(2)]
    vt = P.sb("vt", [128, NT, 256], BF16)
    P.op("pool", lambda e: e.memset(kz[0][64:128, :], 0.0), writes=["kz0"])
    P.op("pool", lambda e: e.memset(kz[1][0:64, :], 0.0), writes=["kz1"])
    for g in range(S // TG):
        tsl = slice(g * TG, (g + 1) * TG)
        proj_fm(P, pp[0], ("pp", 0), wbf, 0, 128, xnT, g)
        P.act(qT[:, tsl], pp[0][:], AF.Copy, reads=[("pp", 0)], writes=[("qT", g)])
        proj_fm(P, pp[1], ("pp", 1), wbf, 128, 128, xnT, g)
        P.act(kz[0][0:64, tsl], pp[1][0:64, :], AF.Copy, reads=[("pp", 1)], writes=["kz0"], scale=0.125)
        P.act(kz[1][64:128, tsl], pp[1][64:128, :], AF.Copy, reads=[("pp", 1)], writes=["kz1"], scale=0.125)
    for t in range(NT):
        k = t % 2
        proj_tm(P, pp[k][:, 0:256], ("pp", k), wbf, 256, 256, xnT, t)
        P.act(vt[:, t, :], pp[k][:, 0:256], AF.Copy, reads=[("pp", k)], writes=["vt"])
    for t in range(NT):
        proj_tm(P, pp[0][:, t * 8:(t + 1) * 8], ("pp", 0), wbf, 512, 8, xnT, t)
    gtok = P.sb("gtok", [128, NT, 8], F32)
    gtokR = P.sb("gtokR", [128, NT, 8], F32)
    dve(P, lambda e: e.tensor_tensor(gtok[:].rearrange("p a b -> p (a b)"), pp[0][:, 0:NT * 8], gbias[:], ALU.add),
        [("pp", 0), "gbias"], ["gtok"])
    P.mm(pp[1][:, 0:NT * 8], anti[:], gtok[:].rearrange("p a b -> p (a b)"), reads=["anti", "gtok"],
         writes=[("pp", 1)])
    P.act(gtokR[:].rearrange("p a b -> p (a b)"), pp[1][:, 0:NT * 8], AF.Copy, reads=[("pp", 1)], writes=["gtokR"])

    tok = [P.sb("tok%d" % d, [128, 4, NT], F32) for d in range(2)]
    A = [P.sb("A%d" % d, [NT, 2, 128], F32) for d in range(2)]
    def gate_dir(d):
        src = gtok if d == 0 else gtokR
        sk = "gtok" if d == 0 else "gtokR"
        LP = P.sb("LP%d" % d, [NT, 4, 128], F32)
        k = d
        for j in range(4):
            col = (0, 1, 4, 5)[j] + 2 * d
            P.mm(pp[k][0:NT, j * 128:(j + 1) * 128], src[:, :, col], ident[:], reads=[sk, "ident"], writes=[("pp", k)])
        P.act(LP[:].rearrange("p a b -> p (a b)"), pp[k][0:NT, :], AF.Copy, reads=[("pp", k)], writes=["LP%d" % d])
        L = "LP%d" % d
        T1 = P.sb("T1_%d" % d, [NT, 2, 128], F32)
        T2 = P.sb("T2_%d" % d, [NT, 2, 128], F32)
        Wt = P.sb("W_%d" % d, [NT, 2, 128], F32)
        Ct = P.sb("C_%d" % d, [NT, 2, 128], F32)
        Mt = P.sb("M_%d" % d, [NT, 2, 128], F32)
        Et = P.sb("E_%d" % d, [NT, 2, 128], F32)
        n1, n2, nW, nC, nM, nE = ["%s_%d" % (s, d) for s in ("T1", "T2", "W", "C", "M", "E")]
        pf = LP[:, 2:4, :]
        li = LP[:, 0:2, :]
        P.act(T1[:], pf, AF.Abs, reads=[L], writes=[n1])
        P.act(T1[:], T1[:], AF.Exp, reads=[n1], writes=[n1], scale=-1.0)
        P.act(T1[:], T1[:], AF.Ln, reads=[n1], writes=[n1], bias=1.0)
        dve(P, lambda e: e.tensor_single_scalar(T2[:], pf, 0.0, ALU.min), [L], [n2])
        dve(P, lambda e: e.tensor_tensor(T2[:], T2[:], T1[:], ALU.subtract), [n1, n2], [n2])
        for hl in range(2):
            dve(P, lambda e, hl=hl: e.tensor_tensor_scan(Wt[:, hl, :], onesf[:], T2[:, hl, :], 0.0, ALU.mult,
                                                         ALU.add), [n2, "onesf"], [nW])
        r = P.sb("r_%d" % d, [2, NT], F32)
        rs = P.sb("rs_%d" % d, [2, NT], F32)
        tot = P.sb("tot_%d" % d, [2, 1], F32)
        car = P.sb("car_%d" % d, [NT, 2], F32)
        P.mm(pp[k][0:2, 0:NT], Wt[:, :, 127], ident[0:NT, 0:NT], reads=[nW, "ident"], writes=[("pp", k)])
        P.act(r[:], pp[k][0:2, 0:NT], AF.Copy, reads=[("pp", k)], writes=["r%d" % d])
        onesr = onesf[0:2, 0:NT]
        dve(P, lambda e: e.tensor_tensor_scan(rs[:], onesr, r[:], 0.0, ALU.mult, ALU.add), ["r%d" % d, "onesf"],
            ["rs%d" % d])
        if d == 0:
            dve(P, lambda e: e.tensor_tensor(rs[:], rs[:], r[:], ALU.subtract), ["rs%d" % d, "r%d" % d], ["rs%d" % d])
        else:
            dve(P, lambda e: e.tensor_copy(tot[:], rs[:, NT - 1:NT]), ["rs%d" % d], ["tot%d" % d])
            dve(P, lambda e: e.tensor_scalar(rs[:], rs[:], -1.0, tot[:, 0:1], ALU.mult, ALU.add),
                ["rs%d" % d, "tot%d" % d], ["rs%d" % d])
        P.mm(pp[k][0:NT, 0:2], rs[:], ident[0:2, 0:2], reads=["rs%d" % d, "ident"], writes=[("pp", k)])
        P.act(car[:], pp[k][0:NT, 0:2], AF.Copy, reads=[("pp", k)], writes=["car%d" % d])
        for hl in range(2):
            dve(P, lambda e, hl=hl: e.tensor_scalar(Wt[:, hl, :], Wt[:, hl, :], car[:, hl:hl + 1], None, ALU.add),
                [nW, "car%d" % d], [nW])
        dve(P, lambda e: e.tensor_tensor(Ct[:], li, Wt[:], ALU.subtract), [L, nW], [nC])
        for hl in range(2):
            dve(P, lambda e, hl=hl: e.tensor_tensor_scan(Mt[:, hl, :], Ct[:, hl, :], Ct[:, hl, :], -1e30, ALU.max,
                                                         ALU.max), [nC], [nM])
        mr = P.sb("mr_%d" % d, [2, NT], F32)
        mr2 = P.sb("mr2_%d" % d, [2, NT], F32)
        mx = P.sb("mx_%d" % d, [2, NT], F32)
        cmx = P.sb("cmx_%d" % d, [NT, 2], F32)
        P.mm(pp[k][0:2, 0:NT], Mt[:, :, 127], ident[0:NT, 0:NT], reads=[nM, "ident"], writes=[("pp", k)])
        P.act(mr[:], pp[k][0:2, 0:NT], AF.Copy, reads=[("pp", k)], writes=["mr%d" % d])
        dve(P, lambda e: e.memset(mx[:], -1e30), [], ["mx%d" % d])
        if NT > 1:
            if d == 0:
                dve(P, lambda e: e.tensor_tensor_scan(mr2[:], mr[:], mr[:], -1e30, ALU.max, ALU.max), ["mr%d" % d],
                    ["mr2%d" % d])
                dve(P, lambda e: e.tensor_copy(mx[:, 1:NT], mr2[:, 0:NT - 1]), ["mr2%d" % d, "mx%d" % d], ["mx%d" % d])
            else:
                cur, ck_, oth, ok_ = mr, "mr%d" % d, mr2, "mr2%d" % d
                s = 1
                while s < NT:
                    dve(P, lambda e, cur=cur, oth=oth, s=s: e.tensor_tensor(oth[:, 0:NT - s], cur[:, 0:NT - s],
                                                                        cur[:, s:NT], ALU.max), [ck_], [ok_])
                    dve(P, lambda e, cur=cur, oth=oth, s=s: e.tensor_copy(oth[:, NT - s:NT], cur[:, NT - s:NT]),
                        [ck_, ok_], [ok_])
                    cur, ck_, oth, ok_ = oth, ok_, cur, ck_
                    s *= 2
                dve(P, lambda e, cur=cur: e.tensor_copy(mx[:, 0:NT - 1], cur[:, 1:NT]), [ck_, "mx%d" % d], ["mx%d" % d])
        P.mm(pp[k][0:NT, 0:2], mx[:], ident[0:2, 0:2], reads=["mx%d" % d, "ident"], writes=[("pp", k)])
        P.act(cmx[:], pp[k][0:NT, 0:2], AF.Copy, reads=[("pp", k)], writes=["cmx%d" % d])
        for hl in range(2):
            dve(P, lambda e, hl=hl: e.tensor_scalar(Mt[:, hl, :], Mt[:, hl, :], cmx[:, hl:hl + 1], None, ALU.max),
                [nM, "cmx%d" % d], [nM])
        dve(P, lambda e: e.tensor_scalar(Mt[:], Mt[:], -1.0, 0.0, ALU.mult, ALU.min), [nM], [nM])
        dve(P, lambda e: e.tensor_tensor(Et[:], Mt[:], Wt[:], ALU.subtract), [nM, nW], [nE])
        P.act(Et[:], Et[:], AF.Exp, reads=[nE], writes=[nE])
        if d == 0:
            for j, (src_t, sn) in enumerate(((Ct, nC), (Ct, nC), (Et, nE), (Et, nE))):
                P.mm(pp[k][:, j * NT:(j + 1) * NT], src_t[:, j % 2, :], ident[0:NT, 0:NT], reads=[sn, "ident"],
                     writes=[("pp", k)])
            P.act(tok[0][:].rearrange("p a b -> p (a b)"), pp[k][:, 0:4 * NT], AF.Copy, reads=[("pp", k)],
                  writes=["tok0"])
            dve(P, lambda e: e.tensor_copy(A[0][:], Mt[:]), [nM], ["A0"])
        else:
            Yb = P.sb("Yb", [128, 6, NT], F32)
            for j, (src_t, sn) in enumerate(((Ct, nC), (Ct, nC), (Et, nE), (Et, nE), (Mt, nM), (Mt, nM))):
                P.mm(pp[k][:, j * NT:(j + 1) * NT], src_t[:, j % 2, :], ident[0:NT, 0:NT], reads=[sn, "ident"],
                     writes=[("pp", k)])
            P.act(Yb[:].rearrange("p a b -> p (a b)"), pp[k][:, 0:6 * NT], AF.Copy, reads=[("pp", k)], writes=["Yb"])
            P.mm(pp[k][:, 0:4 * NT], anti[:], Yb[:, 0:4, :].rearrange("p a b -> p (a b)"), reads=["anti", "Yb"],
                 writes=[("pp", k)])
            P.act(tok[1][:].rearrange("p a b -> p (a b)"), pp[k][:, 0:4 * NT], AF.Copy, reads=[("pp", k)],
                  writes=["tok1"])
            for hl in range(2):
                P.mm(pp[k][0:NT, hl * 128:(hl + 1) * 128], Yb[:, 4 + hl, :], anti[:], reads=["Yb", "anti"],
                     writes=[("pp", k)])
            P.act(A[1][:].rearrange("p a b -> p (a b)"), pp[k][0:NT, 0:256], AF.Copy, reads=[("pp", k)], writes=["A1"])


    for d in range(2):
        gate_dir(d)

    pa = pp[1]
    wg = [P.sb("wg%d" % i, [128, 512], F32) for i in range(2)]
    pt = [P.sb("pt%d" % i, [128, 512], BF16) for i in range(3)]
    ad = P.sb("ad", [128, 4], F32)
    hs = P.sb("hs", [128, 4, 128], F32)
    junk = P.sb("junk", [128, 128], F32)
    ssq = P.sb("ssq", [128, 2], F32)
    sgo = [P.sb("sgo%d" % i, [128, 128], F32) for i in range(2)]
    yt = [P.sb("yt%d" % i, [128, 4, 128], BF16) for i in range(2)]
    cnt = 0
    ycnt = 0
    pcnt = 0
    for hl in range(2):
        for g in range(NQ):
            qsl = slice(g * 512, (g + 1) * 512)
            for d in range(2):
                pob = po[pcnt % 2]
                pok = ("po", pcnt % 2)
                pcnt += 1
                for tl in range(4):
                    P.mm(pa[:, tl * 128:(tl + 1) * 128], sel2[0:NT, (4 * g + tl) * 128:(4 * g + tl + 1) * 128], A[d][:, hl, :], reads=["sel", "A%d" % d],
                         writes=[("pp", 1)])
                P.mm(pob[:].rearrange("p a b -> p (a b)"), zt[:, 0:128], zt[:, 0:512], start=True, stop=False,
                     reads=["zt"], writes=[pok])
                P.mm(pd[:, 0:4], zt[:, 0:128], zt[:, 0:4], start=True, stop=False, reads=["zt"], writes=["pd"])
                tiles = list(range(0, 4 * g + 4)) if d == 0 else list(range(4 * g, NT))
                def mfront(ti, base, tiles=tiles, d=d, hl=hl, g=g, qsl=qsl):
                    t = tiles[ti]
                    b, b3 = (base + ti) % 2, (base + ti) % 3
                    tp = t - 4 * g
                    diag = 0 <= tp <= 3
                    P.mm(st[b][:], kz[hl][:, t * 128:(t + 1) * 128], qT[:, qsl], reads=["kz%d" % hl, ("qT", g)],
                         writes=[("st", b)])
                    if diag:
                        dve(P, lambda e: e.tensor_scalar(wg[b][:], pa[:], tok[d][:, hl, t:t + 1], 0.0, ALU.add, ALU.min),
                            [("pp", 1), "tok%d" % d], [("wg", b)])
                        P.act(wg[b][:], wg[b][:], AF.Exp, reads=[("wg", b)], writes=[("wg", b)])
                        dve(P, lambda e: e.tensor_tensor(wg[b][:], wg[b][:], mask[:, d, tp, :], ALU.mult),
                            [("wg", b), "mask"], [("wg", b)])
                    else:
                        P.act(wg[b][:], pa[:], AF.Exp, reads=[("pp", 1), "tok%d" % d], writes=[("wg", b)],
                              bias=tok[d][:, hl, t:t + 1])
                    dve(P, lambda e: e.tensor_tensor(pt[b3][:], st[b][:], wg[b][:], ALU.mult),
                        [("st", b), ("wg", b)], [("pt", b3)])

                def mback(ti, base, tiles=tiles, d=d, hl=hl, g=g, pob=pob, pok=pok):
                    t = tiles[ti]
                    b3 = (base + ti) % 3
                    last = (ti == len(tiles) - 1)
                    tp = t - 4 * g
                    diag = 0 <= tp <= 3
                    for qs in range(4):
                        if diag and ((d == 0 and qs < tp) or (d == 1 and qs > tp)):
                            continue
                        P.mm(pob[:, qs, :], pt[b3][:, qs * 128:(qs + 1) * 128], vt[:, t, hl * 128:(hl + 1) * 128],
                             start=False, stop=(last and qs == 3), reads=[("pt", b3), "vt"], writes=[pok])
                        P.mm(pd[:, qs:qs + 1], pt[b3][:, qs * 128:(qs + 1) * 128], ones[:, 0:1], start=False,
                             stop=(last and qs == 3), reads=[("pt", b3), "ones"], writes=["pd"])

                mfront(0, cnt)
                for ti in range(len(tiles)):
                    if ti + 1 < len(tiles):
                        mfront(ti + 1, cnt)
                    mback(ti, cnt)
                cnt += len(tiles)
                P.act(ad[:], pd[:, 0:4], AF.Abs, reads=["pd"], writes=["ad"])
                dve(P, lambda e, d=d, hl=hl, g=g: e.tensor_tensor(ad[:], ad[:], tok[d][:, 2 + hl, 4 * g:4 * g + 4],
                                                              ALU.max), ["ad", "tok%d" % d], ["ad"])
                dve(P, lambda e: e.reciprocal(ad[:], ad[:]), ["ad"], ["ad"])
                for qs in range(4):
                    if d == 0:
                        P.act(hs[:, qs, :], pob[:, qs, :], AF.Copy, reads=[pok, "ad"], writes=["hs"],
                              scale=ad[:, qs:qs + 1])
                    else:
                        dve(P, lambda e, qs=qs, pob=pob: e.scalar_tensor_tensor(hs[:, qs, :], pob[:, qs, :],
                                                                            ad[:, qs:qs + 1], hs[:, qs, :], ALU.mult,
                                                                            ALU.add), [pok, "ad", "hs"], ["hs"])
            yb = yt[ycnt % 2]
            ykey = ("yt", ycnt % 2)
            ycnt += 1
            for qs in range(4):
                k2 = qs % 2
                t = 4 * g + qs
                proj_tm(P, pp[0][:, 0:128], ("pp", 0), wbf, 520 + hl * 128, 128, xnT, t)
                P.act(sgo[k2][:], pp[0][:, 0:128], AF.Sigmoid, reads=[("pp", 0)], writes=[("sgo", k2)])
                P.op("pool", lambda e, k2=k2: e.tensor_tensor(sgo[k2][:], sgo[k2][:], gb[:], ALU.mult),
                     reads=[("sgo", k2), "gb"], writes=[("sgo", k2)])
                P.act(junk[:], hs[:, qs, :], AF.Square, reads=["hs"], writes=["junk", ("ssq", k2)],
                      accum_out=ssq[:, k2:k2 + 1])
                P.act(ssq[:, k2:k2 + 1], ssq[:, k2:k2 + 1], AF.Sqrt, reads=[("ssq", k2)], writes=[("ssq", k2)],
                      scale=1.0 / 128, bias=1e-6)
                dve(P, lambda e, k2=k2: e.reciprocal(ssq[:, k2:k2 + 1], ssq[:, k2:k2 + 1]), [("ssq", k2)],
                    [("ssq", k2)])
                dve(P, lambda e, qs=qs, k2=k2, yb=yb: e.scalar_tensor_tensor(yb[:, qs, :], hs[:, qs, :],
                                                                         ssq[:, k2:k2 + 1], sgo[k2][:], ALU.mult,
                                                                         ALU.mult),
                    ["hs", ("ssq", k2), ("sgo", k2)], [ykey])
            P.dma("sp", y[g * 512:(g + 1) * 512, hl * 128:(hl + 1) * 128].rearrange("(q p) e -> p q e", p=128), yb[:],
                  ykey, reads=[ykey], is_out=True)
    print("mlstm stats", P.stats())
    return (P.finish() if own else None)


def mlstm_inputs(xTb, g1l, w_in_l, gate_b_l, ng_l, hh, S=S):
    NT = S // 128
    offs = np.cumsum((0,) + (1536, 512, 16, 512, 512, 512, 512, 128, 128, 256, 256, 512, 16, 512))
    cq, ck, cv, cif, co = offs[9], offs[10], offs[11], offs[12], offs[13]
    hs_ = [2 * hh, 2 * hh + 1]
    qc = np.concatenate([cq + h * 64 + np.arange(64) for h in hs_])
    kc = np.concatenate([ck + h * 64 + np.arange(64) for h in hs_])
    vc = np.concatenate([cv + h * 128 + np.arange(128) for h in hs_])
    oc = np.concatenate([co + h * 128 + np.arange(128) for h in hs_])
    gsel = [(i_f, dr, h) for i_f in range(2) for dr in range(2) for h in hs_]
    gc = np.array([cif + i_f * 8 + dr * 4 + h for (i_f, dr, h) in gsel])
    gbv = np.array([gate_b_l[i_f, dr, h] for (i_f, dr, h) in gsel], np.float32)
    cols = np.concatenate([qc, kc, vc, gc, oc])
    assert len(cols) == D_NCOLS
    ident = np.eye(128, dtype=np.float32)
    anti = np.ascontiguousarray(ident[::-1])
    sel = np.zeros((NT, NT, 128), np.float32)
    for k in range(NT):
        sel[k, k, :] = 1.0
    j = np.arange(128)[:, None]
    i = np.arange(512)[None, :]
    mask = np.zeros((128, 2, 4, 512), np.float32)
    for tp in range(4):
        mask[:, 0, tp, :] = (128 * tp + j <= i)
        mask[:, 1, tp, :] = (128 * tp + j >= i)
    gbias = np.ascontiguousarray(np.tile(gbv[None, None, :], (128, NT, 1)).reshape(128, NT * 8))
    gb = np.ascontiguousarray(np.tile(ng_l[None, :], (128, 1)).astype(np.float32))
    return dict(xT=xTb, g1=g1l, w=wlayout(w_in_l, cols), ident=ident, anti=anti,
                sel=np.ascontiguousarray(sel.reshape(NT, NT * 128)),
                mask=np.ascontiguousarray(mask.reshape(128, -1)).astype(ml_dtypes.bfloat16), gbias=gbias, gb=gb)


A_NCOLS = 256 * 4 + 8
PTG = 256


def prologue_small(P, xT, g1, wsrc, ncols, S, wst=None):
    xnT = P.sb("xnT", [128, 8, S], BF16)
    wbf = P.sb("wbf", [128, 8, ncols], BF16)
    xs = P.sb("xs0", [128, 8, PTG], F32)
    sq = P.sb("sq", [128, 8, PTG], BF16)
    rstd = P.sb("rstd", [128, 512], F32)
    ones = P.sb("ones", [128, 128], BF16)
    g1s = P.sb("g1s", [128, 8], F32)
    pss = P.ps("pss", [128, 512], F32)
    P.op("pool", lambda e: e.memset(ones[:], 1.0), writes=["ones"])
    P.dma("sp", g1s[:], g1, "g1s", writes=["g1s"])
    if wst is None:
        wst = [P.sb("wst%d" % i, [128, ncols], F32)[:] for i in range(2)]
    for c in range(8):
        i = c % 2
        P.dma("sp", wst[i], wsrc[:, c * ncols:(c + 1) * ncols], ("wst", i), writes=[("wst", i)])
        P.op("pool", lambda e, i=i, c=c: e.tensor_copy(wbf[:, c, :], wst[i]), reads=[("wst", i)],
             writes=["wbf"])
    xn_src = (getattr(P, "override", None) or {}).get("xnT_src")
    if xn_src is not None:
        for g in range(S // TG):
            tsl = slice(g * TG, (g + 1) * TG)
            P.dma("sp", xnT[:, :, tsl], xn_src[:, :, tsl], ("xnld", g), writes=[("xnT", g)])
        P.xs0 = xs
        return xnT, wbf, ones, pss, rstd
    for g in range(S // PTG):
        tsl = slice(g * PTG, (g + 1) * PTG)
        P.dma("sp", xs[:], xT[:, :, tsl], ("xs", 0), writes=[("xs", 0)])
        P.act(sq[:], xs[:], AF.Square, reads=[("xs", 0)], writes=["sq"])
        for c in range(8):
            P.mm(pss[:, 0:PTG], ones[:], sq[:, c, :], start=(c == 0), stop=(c == 7), reads=["ones", "sq"],
                 writes=["pss"])
        P.act(rstd[:, 0:PTG], pss[:, 0:PTG], AF.Sqrt, reads=["pss"], writes=["rstd"], scale=1.0 / 1024, bias=1e-6)
        P.op("dve", lambda e: e.reciprocal(rstd[:, 0:PTG], rstd[:, 0:PTG]), reads=["rstd"], writes=["rstd"])
        for c in range(8):
            P.op("dve", lambda e, c=c, tsl=tsl: e.scalar_tensor_tensor(
                xnT[:, c, tsl], xs[:, c, :], g1s[:, c:c + 1], rstd[:, 0:PTG], ALU.mult, ALU.mult),
                reads=[("xs", 0), "g1s", "rstd"], writes=[("xnT", (g * PTG) // TG)])
    P.xs0 = xs
    return xnT, wbf, ones, pss, rstd


def build_gdn(S=S, P=None):
    import os
    STOP = int(os.environ.get("GDN_STOP", "99"))
    PST = int(os.environ.get("GDN_PST", "99"))
    LST = int(os.environ.get("GDN_LST", "99"))
    VAR = int(os.environ.get("GDN_VAR", "0"))
    own = P is None
    if own:
        P = Prog()
    NT = S // 128
    NG = S // TG
    NC = S // 64
    xT = P.dram("xT", [128, 8, S], F32, "ExternalInput")
    g1 = P.dram("g1", [128, 8], F32, "ExternalInput")
    w = P.dram("w", [128, 8 * A_NCOLS], F32, "ExternalInput")
    convd = P.dram("convw", [128, 6, 5], F32, "ExternalInput")
    identd = P.dram("ident", [128, 128], F32, "ExternalInput")
    antid = P.dram("anti", [128, 128], F32, "ExternalInput")
    maskd = P.dram("mask", [128, 4 * 128], BF16, "ExternalInput")
    biasd = P.dram("gbias", [128, NT * 8], F32, "ExternalInput")
    alogd = P.dram("alog", [128, 4], F32, "ExternalInput")
    rmd = P.dram("rmask", [NT, 128], F32, "ExternalInput")
    gbd = P.dram("gb", [128, 128], F32, "ExternalInput")
    y = P.dram("y", [S, 256], BF16, "ExternalOutput")

    raw = P.sb("raw", [128, S + 4], F32)
    half = (S + 4) // 2
    xnT, wbf, ones, pss, rstd = prologue_small(P, xT, g1, w, A_NCOLS, S,
                                               wst=[raw[:, 0:A_NCOLS], raw[:, half:half + A_NCOLS]] if half >= A_NCOLS else None)
    ident = P.sb("ident", [128, 128], F32)
    anti = P.sb("anti", [128, 128], F32)
    identb = P.sb("identb", [128, 128], BF16)
    mask = P.sb("mask", [128, 4, 128], BF16)
    gbias = P.sb("gbias", [128, NT * 8], F32)
    nea = P.sb("nea", [128, 4], F32)
    rm = P.sb("rm", [NT, 128], F32)
    gb = P.sb("gb", [128, 128], F32)
    convw = P.sb("convw", [128, 6, 5], F32)
    P.dma("sp", ident[:], identd, "ident", writes=["ident"])
    P.dma("sp", anti[:], antid, "anti", writes=["anti"])
    P.dma("sp", mask[:].rearrange("p a b -> p (a b)"), maskd, "mask", writes=["mask"])
    P.dma("sp", gbias[:], biasd, "gbias", writes=["gbias"])
    P.dma("sp", nea[:], alogd, "nea", writes=["nea"])
    P.dma("sp", rm[:], rmd, "rm", writes=["rm"])
    P.dma("sp", gb[:], gbd, "gb", writes=["gb"])
    P.dma("sp", convw[:], convd, "convw", writes=["convw"])
    P.op("pool", lambda e: e.tensor_copy(identb[:], ident[:]), reads=["ident"], writes=["identb"])
    P.act(nea[:], nea[:], AF.Exp, reads=["nea"], writes=["nea"])
    P.op("dve", lambda e: e.tensor_scalar(nea[:], nea[:], -1.0, None, ALU.mult), reads=["nea"], writes=["nea"])
    onesf = P.sb("onesf", [NT, 128], F32)
    P.op("pool", lambda e: e.memset(onesf[:], 1.0), writes=["onesf"])

    pp = [P.ps("pp%d" % i, [128, 512], F32) for i in range(2)]
    ptr = P.ps("ptr", [128, 4, 128], BF16)

    qT = [P.sb("qT%d" % h, [128, S], BF16) for h in range(2)]
    kT = [P.sb("kT%d" % h, [128, S], BF16) for h in range(2)]
    vtok = [P.sb("vtok%d" % h, [128, NT, 128], BF16) for h in range(2)]
    sz = P.sb("sz", [128, NT, 256], BF16)
    xsf = P.xs0[:].rearrange("p a b -> p (a b)")
    acc = [xsf[:, i * TG:(i + 1) * TG] for i in range(2)]
    sil = [xsf[:, (2 + i) * TG:(3 + i) * TG] for i in range(2)]
    P.op("pool", lambda e: e.memset(xsf[:, 0:4 * TG], 0.0),
         writes=[("xs", 0), ("acc", 0), ("acc", 1), ("sil", 0), ("sil", 1)])
    sqg = P.sb("sqg", [128, TG], BF16)
    vTg = P.sb("vTg", [128, TG], BF16)
    P.op("pool", lambda e: e.memset(raw[:, 0:2], 0.0), reads=["wbf"], writes=["rawpad"])
    P.op("pool", lambda e: e.memset(raw[:, S + 2:S + 4], 0.0), reads=["wbf"], writes=["rawpad"])
    allraw = [("raw", g) for g in range(NG)]
    for ci in range(6):
        kind, hl = ci // 2, ci % 2
        for g in range(NG):
            k = g % 2
            proj_fm(P, pp[k], ("pp", k), wbf, ci * 128, 128, xnT, g)
            P.act(raw[:, 2 + g * TG:2 + (g + 1) * TG], pp[k][:], AF.Copy, reads=[("pp", k)], writes=[("raw", g)])
        for g in range(NG):
            k = g % 2
            a_, s_ = acc[k], sil[k]
            nb = [("raw", gg) for gg in (g - 1, g, g + 1) if 0 <= gg < NG] + ["rawpad", "convw"]
            base = g * TG
            P.op("dve", lambda e, a_=a_, base=base, ci=ci: e.tensor_scalar(a_, raw[:, base:base + TG],
                                                                        convw[:, ci, 0:1], None, ALU.mult),
                 reads=nb, writes=[("acc", k)])
            for tap in range(1, 5):
                P.op("dve", lambda e, a_=a_, base=base, ci=ci, tap=tap: e.scalar_tensor_tensor(
                    a_, raw[:, base + tap:base + tap + TG], convw[:, ci, tap:tap + 1], a_, ALU.mult, ALU.add),
                    reads=nb + [("acc", k)], writes=[("acc", k)])
            if kind < 2:
                P.act(s_, a_, AF.Silu, reads=[("acc", k)], writes=[("sil", k)])
                P.op("pool", lambda e, s_=s_: e.tensor_tensor(sqg[:], s_, s_, ALU.mult), reads=[("sil", k)],
                     writes=["sqg"])
                P.mm(pss[:], ones[:], sqg[:], reads=["ones", "sqg"], writes=["pss"])
                P.act(rstd[:], pss[:], AF.Sqrt, reads=["pss"], writes=["rstd"], bias=1e-6)
                P.op("dve", lambda e: e.reciprocal(rstd[:], rstd[:]), reads=["rstd"], writes=["rstd"])
                dst = (qT if kind == 0 else kT)[hl]
                dkey = ("qT%d" % hl if kind == 0 else "kT%d" % hl, g)
                sc = (128 ** -0.5) if kind == 0 else 1.0
                P.op("dve", lambda e, s_=s_, dst=dst, g=g, sc=sc: e.scalar_tensor_tensor(
                    dst[:, g * TG:(g + 1) * TG], s_, sc, rstd[:], ALU.mult, ALU.mult),
                    reads=[("sil", k), "rstd"], writes=[dkey])
            else:
                P.act(vTg[:], a_, AF.Silu, reads=[("acc", k)], writes=["vTg"])
                for j in range(4):
                    P.tr(ptr[:, j, :], vTg[:, j * 128:(j + 1) * 128], identb[:], reads=["vTg", "identb"],
                         writes=["ptr"])
                P.op("pool" if False else "act", lambda e, hl=hl, g=g: e.copy(
                    vtok[hl][:, 4 * g:4 * g + 4, :], ptr[:]), reads=["ptr"], writes=["vtok%d" % hl])
    if STOP <= 1:
        return (P.finish() if own else None)
    for t in range(NT):
        k = t % 2
        proj_tm(P, pp[k][:, 0:256], ("pp", k), wbf, 768, 256, xnT, t)
        P.act(sz[:, t, :], pp[k][:, 0:256], AF.Silu, reads=[("pp", k)], writes=["sz"])
    for t in range(NT):
        proj_tm(P, pp[0][:, t * 8:(t + 1) * 8], ("pp", 0), wbf, 1024, 8, xnT, t)
    gtok = P.sb("gtok", [128, NT, 8], F32)
    gtokR = P.sb("gtokR", [128, NT, 8], F32)
    dve(P, lambda e: e.tensor_tensor(gtok[:].rearrange("p a b -> p (a b)"), pp[0][:, 0:NT * 8], gbias[:], ALU.add),
        [("pp", 0), "gbias"], ["gtok"])
    P.mm(pp[1][:, 0:NT * 8], anti[:], gtok[:].rearrange("p a b -> p (a b)"), reads=["anti", "gtok"],
         writes=[("pp", 1)])
    P.act(gtokR[:].rearrange("p a b -> p (a b)"), pp[1][:, 0:NT * 8], AF.Copy, reads=[("pp", 1)], writes=["gtokR"])

    if STOP <= 2:
        return (P.finish() if own else None)
    tokq = [P.sb("tokq%d" % d, [128, 2, 6, NT], F32) for d in range(2)]
    Gc = [P.sb("Gc%d" % d, [NT, 2, 128], F32) for d in range(2)]
    egt = [P.sb("egt%d" % d, [128, 2, 2, NT], F32) for d in range(2)]

    LPs = P.sb("LPs", [NT, 4, 128], F32)
    gtmp = [P.sb("gtmp%d" % i, [NT, 2, 128], F32) for i in range(5)]
    TBs = P.sb("TBs", [NT, 128], F32)
    Yb = P.sb("Yb", [128, 8, NT], F32)

    def gate_dir(d):
        src = gtok if d == 0 else gtokR
        sk = "gtok" if d == 0 else "gtokR"
        k = d
        LP = LPs
        L = "LP"
        for j in range(4):
            col = (0, 1, 4, 5)[j] + 2 * d
            P.mm(pp[k][0:NT, j * 128:(j + 1) * 128], src[:, :, col], ident[:], reads=[sk, "ident"], writes=[("pp", k)])
        P.act(LP[:].rearrange("p a b -> p (a b)"), pp[k][0:NT, :], AF.Copy, reads=[("pp", k)], writes=[L])
        T1, Gt, Bt, Et, Rt = gtmp
        n1, nG, nB, nE, nR = ["%s_s" % s for s in ("T1", "G", "B", "E", "R")]
        al = LP[:, 0:2, :]
        be = LP[:, 2:4, :]
        P.act(T1[:], al, AF.Abs, reads=[L], writes=[n1])
        P.act(T1[:], T1[:], AF.Exp, reads=[n1], writes=[n1], scale=-1.0)
        P.act(T1[:], T1[:], AF.Ln, reads=[n1], writes=[n1], bias=1.0)
        dve(P, lambda e: e.tensor_single_scalar(Gt[:], al, 0.0, ALU.max), [L], [nG])
        dve(P, lambda e: e.tensor_tensor(Gt[:], Gt[:], T1[:], ALU.add), [nG, n1], [nG])
        for hl in range(2):
            dve(P, lambda e, hl=hl: e.tensor_scalar(Gt[:, hl, :], Gt[:, hl, :], nea[0:NT, 2 * d + hl:2 * d + hl + 1],
                                                    None, ALU.mult), [nG, "nea"], [nG])
        P.act(Bt[:], be, AF.Sigmoid, reads=[L], writes=[nB])
        for hl in range(2):
            dve(P, lambda e, hl=hl: e.tensor_tensor_scan(T1[:, hl, :], rm[:], Gt[:, hl, :], 0.0, ALU.mult, ALU.add),
                [nG, "rm", n1], [n1])
        for hl in range(2):
            for hf in range(2):
                dve(P, lambda e, hl=hl, hf=hf: e.tensor_scalar(
                    Rt[:, hl, hf * 64:(hf + 1) * 64], T1[:, hl, hf * 64:(hf + 1) * 64], -1.0,
                    T1[:, hl, hf * 64 + 63:hf * 64 + 64], ALU.mult, ALU.add), [n1], [nR])
        P.act(Et[:], T1[:], AF.Exp, reads=[n1], writes=[nE])
        P.act(Rt[:], Rt[:], AF.Exp, reads=[nR], writes=[nR])
        TB = TBs
        for hl in range(2):
            for hf in range(2):
                ah = hf if d == 0 else 1 - hf
                dve(P, lambda e, hl=hl, hf=hf: e.tensor_scalar(TB[:], onesf[:], T1[:, hl, hf * 64 + 63:hf * 64 + 64],
                                                           None, ALU.mult), [n1, "onesf", ("pp", k)], ["TBs"])
                P.mm(pp[k][:, (hl * 2 + ah) * NT:(hl * 2 + ah + 1) * NT], TB[:], ident[0:NT, 0:NT],
                     reads=["TBs", "ident"], writes=[("pp", k)])
        P.act(egt[d][:].rearrange("p a b c -> p (a b c)"), pp[k][:, 0:4 * NT], AF.Exp, reads=[("pp", k)],
              writes=["egt%d" % d])
        srcs = ((T1, n1), (Bt, nB), (Et, nE), (Rt, nR))
        if d == 0:
            for hl in range(2):
                for qi, (tt, nn) in enumerate(srcs):
                    P.mm(pp[k][:, (hl * 4 + qi) * NT:(hl * 4 + qi + 1) * NT], tt[:, hl, :], ident[0:NT, 0:NT],
                         reads=[nn, "ident"], writes=[("pp", k)])
            for hl in range(2):
                P.act(tokq[0][:, hl, 0:4, :].rearrange("p a b -> p (a b)"), pp[k][:, hl * 4 * NT:(hl + 1) * 4 * NT],
                      AF.Copy, reads=[("pp", k)], writes=["tokq0"])
            dve(P, lambda e: e.tensor_copy(Gc[0][:], T1[:]), [n1], ["Gc0"])
        else:
            for hl in range(2):
                for qi, (tt, nn) in enumerate(srcs):
                    P.mm(pp[k][:, (hl * 4 + qi) * NT:(hl * 4 + qi + 1) * NT], tt[:, hl, :], ident[0:NT, 0:NT],
                         reads=[nn, "ident"], writes=[("pp", k)])
            P.act(Yb[:].rearrange("p a b -> p (a b)"), pp[k][:, 0:8 * NT], AF.Copy, reads=[("pp", k)], writes=["Yb"])
            P.mm(pp[k][:, 0:8 * NT], anti[:], Yb[:].rearrange("p a b -> p (a b)"), reads=["anti", "Yb"],
                 writes=[("pp", k)])
            for hl in range(2):
                P.act(tokq[1][:, hl, 0:4, :].rearrange("p a b -> p (a b)"), pp[k][:, hl * 4 * NT:(hl + 1) * 4 * NT],
                      AF.Copy, reads=[("pp", k)], writes=["tokq1"])
            for hl in range(2):
                P.mm(pp[k][0:NT, hl * 128:(hl + 1) * 128], Yb[:, hl * 4 + 0, :], anti[:], reads=["Yb", "anti"],
                     writes=[("pp", k)])
            P.act(Gc[1][:].rearrange("p a b -> p (a b)"), pp[k][0:NT, 0:256], AF.Copy, reads=[("pp", k)],
                  writes=["Gc1"])
        for hl in range(2):
            dve(P, lambda e, hl=hl: e.tensor_scalar(tokq[d][:, hl, 4, :], tokq[d][:, hl, 0, :], -1.0, None, ALU.mult),
                ["tokq%d" % d], ["tokq%d" % d])
            dve(P, lambda e, hl=hl: e.tensor_scalar(tokq[d][:, hl, 5, :], tokq[d][:, hl, 2, :], -1.0, None, ALU.mult),
                ["tokq%d" % d], ["tokq%d" % d])

    for d in range(2):
        gate_dir(d)
        if STOP <= 3 + d:
            return (P.finish() if own else None)

    XK = [("xnT", g) for g in range(NG)]
    slot = lambda i: xnT[:, i, :].rearrange("p (t c) -> p t c", c=128)
    TI = [slot(0), slot(1)]
    QK = [slot(2), slot(3)]
    KD = [slot(4), slot(5)]
    pw = pp[0]
    pn = pp[1]
    cA = [P.ps("cA%d" % i, [128, 4, 128], F32) for i in range(2)]
    cSb = [P.ps("cS%d" % i, [128, 128], F32) for i in range(2)]
    Dm = [P.sb("Dm%d" % i, [128, 128], F32) for i in range(2)]
    DT = [P.sb("DT%d" % i, [128, 128], F32) for i in range(2)]
    A0 = [P.sb("A0_%d" % i, [128, 128], BF16) for i in range(2)]
    IA = [P.sb("IA_%d" % i, [128, 128], BF16) for i in range(2)]
    AK = [P.sb("AK_%d" % i, [128, 128], BF16) for i in range(2)]
    NK = [P.sb("NK_%d" % i, [128, 128], BF16) for i in range(2)]
    PT = [P.sb("PT_%d" % i, [128, 128], BF16) for i in range(2)]
    oacc = raw[:, 0:S].rearrange("p (t c) -> p t c", c=128)
    gsel = [P.sb("gsel%d" % i, [NT, 128], F32) for i in range(2)]
    St = [P.sb("St%d" % i, [128, 128], BF16) for i in range(2)]
    Xp = [[P.sb("Xp%d_%d" % (i, hf), [128, 128], BF16) for hf in range(2)] for i in range(2)]
    Vn = [[P.sb("Vn%d_%d" % (i, hf), [128, 128], BF16) for hf in range(2)] for i in range(2)]
    otmp = [P.sb("otmp%d" % i, [128, 128], F32) for i in range(2)]
    junk_ = rstd[:, 256:384]
    ssq = P.sb("ssq", [128, 2], F32)
    gm = [rstd[:, i * 128:(i + 1) * 128] for i in range(2)]
    P.op("pool", lambda e: e.memset(rstd[:, 0:384], 0.0), writes=["rstd", ("gm", 0), ("gm", 1), "junk"])
    yt = [P.sb("yt%d" % i, [128, 128], BF16) for i in range(2)]
    fence = P.sb("fence", [128, 2], F32)
    for i in range(2):
        for hf in range(2):
            P.op("pool", lambda e, i=i, hf=hf: e.memset(Xp[i][hf][:], 0.0), writes=[("Xp", i, hf)])
            P.op("pool", lambda e, i=i, hf=hf: e.memset(Vn[i][hf][:], 0.0), writes=[("Vn", i, hf)])

    cAf = [cA[i][:].rearrange("p a b -> p (a b)") for i in range(2)]
    PW = [pp[0][:], cAf[0]]
    PN = [pp[1][:], cAf[1]]

    def precompute(hl, d, t, b):
        tsl = slice(t * 128, (t + 1) * 128)
        pw, pn = PW[b], PN[b]
        pwk, pnk = (("pp", 0), ("pp", 1)) if b == 0 else (("cA", 0), ("cA", 1))
        pwr, pnr, ptrr = ("pw_rd", b), ("pn_rd", b), "ptr_rd"
        ptk = "ptr"
        tq = tokq[d]
        gc_p, be_p, eg_p, er_p, ngc_p, neg_p = [tq[:, hl, qi, t:t + 1] for qi in range(6)]
        dve(P, lambda e: e.tensor_scalar(gsel[b][:], Gc[d][:, hl, :], ident[0:NT, t:t + 1], None, ALU.mult),
            ["Gc%d" % d, "ident"], [("gsel", b)])
        P.mm(pw[:, 0:128], kT[hl][:, tsl], kT[hl][:, tsl], reads=[("kT%d" % hl, t // 4)], writes=[pwk])
        P.mm(pw[:, 128:256], kT[hl][:, tsl], qT[hl][:, tsl], reads=[("kT%d" % hl, t // 4), ("qT%d" % hl, t // 4)],
             writes=[pwk])
        P.mm(pw[:, 256:384], onesf[:], gsel[b][:], reads=["onesf", ("gsel", b)], writes=[pwk])
        yield
        P.act(Dm[b][:], pw[:, 256:384], AF.Abs, reads=[pwk, "tokq%d" % d], writes=[("Dm", b), pwr], scale=-1.0,
              bias=gc_p)
        P.act(DT[b][:], pw[:, 256:384], AF.Abs, reads=[pwk, "tokq%d" % d], writes=[("DT", b), pwr], bias=ngc_p)
        P.act(Dm[b][:], Dm[b][:], AF.Exp, reads=[("Dm", b)], writes=[("Dm", b)], scale=-1.0)
        P.act(DT[b][:], DT[b][:], AF.Exp, reads=[("DT", b)], writes=[("DT", b)], scale=-1.0)
        yield
        dve(P, lambda e: e.tensor_tensor(Dm[b][:], Dm[b][:], mask[:, d, :], ALU.mult), [("Dm", b), "mask"], [("Dm", b)])
        dve(P, lambda e: e.tensor_tensor(DT[b][:], DT[b][:], mask[:, 2 + d, :], ALU.mult), [("DT", b), "mask"],
            [("DT", b)])
        dve(P, lambda e: e.scalar_tensor_tensor(A0[b][:], pw[:, 0:128], be_p, Dm[b][:], ALU.mult, ALU.mult),
            [pwk, ("Dm", b), "tokq%d" % d], [("A0", b), pwr])
        dve(P, lambda e: e.tensor_tensor(QK[d][:, t, :], pw[:, 128:256], DT[b][:], ALU.mult),
            [pwk, ("DT", b)], XK + [("QK", d), pwr])
        yield
        P.tr(ptr[:, 2 * b, :], A0[b][:], identb[:], reads=[("A0", b), "identb"], writes=[ptk])
        yield
        P.act(NK[b][:], ptr[:, 2 * b, :], AF.Copy, reads=[ptk], writes=[("NK", b), ptrr])
        dve(P, lambda e: e.tensor_tensor(PT[b][:], identb[:], ptr[:, 2 * b, :], ALU.subtract), [ptk, "identb"],
            [("PT", b), ptrr])
        yield
        cur_a = A0[b]
        ck = ("A0", b)
        for lev in range(5):
            P.mm(pn[:, 0:128], NK[b][:], cur_a[:], reads=[("NK", b), ck], writes=[pnk])
            if lev < 4:
                P.mm(pn[:, 128:256], cur_a[:], NK[b][:], reads=[("NK", b), ck], writes=[pnk])
            yield
            dve(P, lambda e: e.tensor_tensor(IA[b][:], pn[:, 0:128], ident[:], ALU.add), [pnk, "ident"],
                [("IA", b), pnr])
            if lev < 4:
                P.act(AK[b][:], pn[:, 0:128], AF.Copy, reads=[pnk], writes=[("AK", b), pnr])
                P.act(NK[b][:], pn[:, 128:256], AF.Copy, reads=[pnk], writes=[("NK", b), pnr])
                cur_a = AK[b]
                ck = ("AK", b)
            yield
            P.mm(pn[:, 256:384], IA[b][:], PT[b][:], reads=[("IA", b), ("PT", b)], writes=[pnk])
            yield
            if lev < 4:
                dve(P, lambda e: e.tensor_copy(PT[b][:], pn[:, 256:384]), [pnk], [("PT", b), pnr])
            else:
                P.act(TI[d][:, t, :], pn[:, 256:384], AF.Copy, reads=[pnk, "tokq%d" % d],
                      writes=XK + [("TI", d), pnr], scale=be_p)
            yield
        P.tr(ptr[:, 2 * b + 1, :], kT[hl][:, tsl], identb[:], reads=[("kT%d" % hl, t // 4), "identb"], writes=[ptk])
        yield
        P.act(KD[d][:, t, :], ptr[:, 2 * b + 1, :], AF.Copy, reads=[ptk, "tokq%d" % d], writes=XK + [("KD", d), ptrr],
              scale=er_p)

    def run_interleaved(gens):
        active = list(gens)
        while active:
            for g_ in list(active):
                try:
                    next(g_)
                except StopIteration:
                    active.remove(g_)

    def chunk_step(hl, d, c):
        t, hf = c // 2, c % 2
        R = slice(64 * hf, 64 * hf + 64)
        tsl = slice(t * 128, (t + 1) * 128)
        S_ = St[d]
        sk = ("St", d)
        ca = cA[d]
        cak = ("cA", d)
        tq = tokq[d]
        P.mm(ca[:, 0, :], kT[hl][:, tsl], S_[:], reads=[("kT%d" % hl, t // 4), sk], writes=[cak])
        P.mm(ca[:, 1, :], qT[hl][:, tsl], S_[:], reads=[("qT%d" % hl, t // 4), sk], writes=[cak])
        yield
        X = Xp[d][hf]
        dve(P, lambda e: e.scalar_tensor_tensor(X[R, :], ca[R, 0, :], tq[R, hl, 5, t:t + 1], vtok[hl][R, t, :],
                                                ALU.mult, ALU.add),
            [cak, "tokq%d" % d, "vtok%d" % hl], [("Xp", d, hf), ("ca_rd", d)])
        ot = otmp[d]
        P.act(ot[R, :], ca[R, 1, :], AF.Copy, reads=[cak, "tokq%d" % d], writes=[("otmp", d), ("ca_rd", d)],
              scale=tq[R, hl, 2, t:t + 1])
        yield
        P.mm(ca[:, 2, :], TI[d][:, t, :], X[:], reads=[("TI", d), ("Xp", d, hf)], writes=[cak])
        yield
        V = Vn[d][hf]
        P.act(V[R, :], ca[R, 2, :], AF.Copy, reads=[cak], writes=[("Vn", d, hf), ("ca_rd", d)])
        yield
        P.mm(cSb[d][:], KD[d][:, t, :], V[:], reads=[("KD", d), ("Vn", d, hf)], writes=[("cS", d)])
        P.mm(ca[:, 3, :], QK[d][:, t, :], V[:], reads=[("QK", d), ("Vn", d, hf)], writes=[cak])
        yield
        dve(P, lambda e: e.scalar_tensor_tensor(S_[:], S_[:], egt[d][:, hl, hf, t:t + 1], cSb[d][:], ALU.mult,
                                                ALU.add), [sk, ("cS", d), "egt%d" % d], [sk])
        P.op("dve", lambda e: e.tensor_tensor(ot[R, :], ca[R, 3, :], ot[R, :], ALU.add),
             reads=[cak, ("otmp", d)], writes=[("otmp", d), ("ca_rd", d)])
        P.op("pool", lambda e: e.tensor_tensor(oacc[R, t, :], oacc[R, t, :], ot[R, :], ALU.add),
             reads=[("otmp", d), ("oacc", t)], writes=[("oacc", t)])

    ycnt = 0
    for hl in range(2):
        for d in range(2):
            for t in range(0, NT, 2):
                run_interleaved([precompute(hl, d, t + i, i) for i in range(2) if t + i < NT])
            P.op("pool", lambda e, d=d: e.memset(St[d][:], 0.0), writes=[("St", d)])
        P.op("pool", lambda e: e.memset(raw[:, 0:S], 0.0),
             writes=[("oacc", t) for t in range(NT)] + allraw + ["rawpad"])
        for step in range(NC):
            run_interleaved([chunk_step(hl, 0, step), chunk_step(hl, 1, NC - 1 - step)])
        if STOP <= 6:
            return (P.finish() if own else None)
        for t in range(NT):
            k2 = t % 2
            P.act(junk_, oacc[:, t, :], AF.Square, reads=[("oacc", t)], writes=["junk", ("ssq", k2)],
                  accum_out=ssq[:, k2:k2 + 1])
            P.act(ssq[:, k2:k2 + 1], ssq[:, k2:k2 + 1], AF.Sqrt, reads=[("ssq", k2)], writes=[("ssq", k2)],
                  scale=1.0 / 128, bias=1e-6)
            dve(P, lambda e, k2=k2: e.reciprocal(ssq[:, k2:k2 + 1], ssq[:, k2:k2 + 1]), [("ssq", k2)], [("ssq", k2)])
            P.op("pool", lambda e, k2=k2, t=t, hl=hl: e.tensor_tensor(gm[k2], sz[:, t, hl * 128:(hl + 1) * 128], gb[:],
                                                                  ALU.mult), reads=["sz", "gb"], writes=[("gm", k2)])
            dve(P, lambda e, k2=k2, t=t: e.scalar_tensor_tensor(yt[k2][:], oacc[:, t, :], ssq[:, k2:k2 + 1], gm[k2],
                                                                ALU.mult, ALU.mult),
                [("oacc", t), ("ssq", k2), ("gm", k2)], [("yt", k2)])
            P.dma("sp", y[t * 128:(t + 1) * 128, hl * 128:(hl + 1) * 128], yt[k2][:], ("yt", k2), reads=[("yt", k2)],
                  is_out=True)
    print("gdn stats", P.stats())
    return (P.finish() if own else None)


def gdn_inputs(xTb, g1l, w_in_l, conv_l, alog_l, dtb_l, ng_l, hh, S=S):
    NT = S // 128
    offs = np.cumsum((0,) + (1536, 512, 16, 512, 512, 512, 512, 128, 128, 256, 256, 512, 16, 512))
    cqkv, cz, cab = offs[0], offs[1], offs[2]
    hs_ = [2 * hh, 2 * hh + 1]
    chunks = []
    for kind in range(3):
        for h in hs_:
            chunks.append(cqkv + kind * 512 + h * 128 + np.arange(128))
    qkvc = np.concatenate(chunks)
    zc = np.concatenate([cz + h * 128 + np.arange(128) for h in hs_])
    gsel = [(ab, dr, h) for ab in range(2) for dr in range(2) for h in hs_]
    gc = np.array([cab + ab * 8 + dr * 4 + h for (ab, dr, h) in gsel])
    gbv = np.array([dtb_l[dr, h] if ab == 0 else 0.0 for (ab, dr, h) in gsel], np.float32)
    cols = np.concatenate([qkvc, zc, gc])
    assert len(cols) == A_NCOLS
    convw = np.ascontiguousarray(np.stack([conv_l[:, c - cqkv].T for c in chunks], axis=1).astype(np.float32))
    ident = np.eye(128, dtype=np.float32)
    anti = np.ascontiguousarray(ident[::-1])
    sel = np.zeros((NT, NT, 128), np.float32)
    for k in range(NT):
        sel[k, k, :] = 1.0
    p = np.arange(128)[:, None]
    f = np.arange(128)[None, :]
    same = (p // 64) == (f // 64)
    MA0 = same & (f < p)
    MA1 = same & (f > p)
    MQ0 = same & (p <= f)
    MQ1 = same & (p >= f)
    mask = np.stack([MA0, MA1, MQ0, MQ1], axis=1).astype(np.float32).reshape(128, 4 * 128)
    gbias = np.ascontiguousarray(np.tile(gbv[None, None, :], (128, NT, 1)).reshape(128, NT * 8))
    alog = np.ascontiguousarray(np.tile(np.array([alog_l[dr, h] for dr in range(2) for h in hs_], np.float32)[None],
                                        (128, 1)))
    rmask = np.ones((NT, 128), np.float32)
    rmask[:, 0] = 0.0
    rmask[:, 64] = 0.0
    gb = np.ascontiguousarray(np.tile(ng_l[None, :], (128, 1)).astype(np.float32))
    return dict(xT=xTb, g1=g1l, w=wlayout(w_in_l, cols), convw=convw, ident=ident, anti=anti,
                mask=mask.astype(ml_dtypes.bfloat16),
                gbias=gbias, alog=alog, rmask=rmask, gb=gb)


class XSrc:
    def __init__(self, fn):
        self.fn = fn

    def __getitem__(self, idx):
        tsl = idx[2]
        return self.fn(tsl.start, tsl.stop)


class RowChunks:
    def __init__(self, chunks, rows_per, col0=0, ncols=None, rowmap=None):
        self.chunks, self.rows_per, self.col0, self.ncols, self.rowmap = chunks, rows_per, col0, ncols, rowmap

    def __getitem__(self, idx):
        rs, cs = idx
        a, b_ = rs.start, rs.stop
        c0 = self.col0 + (cs.start or 0)
        c1 = self.col0 + (cs.stop if cs.stop is not None else self.ncols)
        ci, off = self.rowmap(a) if self.rowmap else (a // self.rows_per, a % self.rows_per)
        return self.chunks[ci][off:off + (b_ - a), c0:c1]


def build_fused(S_=S, depth=2):
    import math
    P = Prog()
    H = S_ // 2
    x_full = P.dram("x_full", [128, 8, S_], F32, "ExternalInput")
    x_half = P.dram("x_half", [128, 8, H], F32, "ExternalInput")
    out = P.dram("out", [128, 8, H], F32, "ExternalOutput")
    YR = 1024
    NYC = S_ // YR
    ymine = [P.scratch("ymine%d" % c, [YR, 1024], BF16) for c in range(NYC)]
    ypair = [P.scratch("ypair%d" % c, [2 * YR, 1024], BF16) for c in range(NYC)]
    NXC = H // 512
    xh = [P.scratch("xh%d" % c, [1024, 512], F32) for c in range(NXC)]
    xpair = [P.scratch("xpair%d" % c, [2048, 512], F32) for c in range(NXC)]
    RG = [[0, 1], [2, 3], [4, 5], [6, 7]]
    xn_scr = P.scratch("xn_scr", [128, 8, S_], BF16)

    def xpair_view(a, b):
        r, tg, off = a // H, (a % H) // 512, a % 512
        assert off + (b - a) <= 512
        return xpair[tg][r * 1024:(r + 1) * 1024, off:off + (b - a)].rearrange("(p c) t -> p c t", c=8)

    def xh_view(a, b):
        tg, off = a // 512, a % 512
        assert off + (b - a) <= 512
        return xh[tg][:, off:off + (b - a)].rearrange("(p c) t -> p c t", c=8)

    def ypair_rowmap(row):
        r, T = row // S_, row % S_
        return T // YR, r * YR + (T % YR)

    stats = []
    for l in range(depth):
        lam_init = 0.8 - 0.6 * math.exp(-0.3 * l)
        xsrc = x_full if l == 0 else XSrc(xpair_view)
        P.pre = "l%d_xn_" % l
        P.override = {}
        g1d = P.dram("g1", [128, 8], F32, "ExternalInput")
        xnT_, _, _ = prologue(P, xsrc, g1d, None, 0, S_)
        for c in range(8):
            P.dma("sp", xn_scr[:, c, :], xnT_[:, c, :], ("xnst", c), reads=[("xnT", g) for g in range(S_ // TG)])
        stats.append((P.pre, P.end_phase()))
        for n, (nm, bld) in enumerate((("gdn", lambda: build_gdn(S_, P=P)), ("diff", lambda: build_diff(lam_init, S_, P=P)),
                                       ("swa", lambda: build_swa(S_, P=P)), ("mlstm", lambda: build_mlstm(S_, P=P)))):
            P.pre = "l%d_%s_" % (l, nm)
            P.override = {"xT": xsrc, "y": RowChunks(ymine, YR, col0=n * 256, ncols=256), "xnT_src": xn_scr}
            bld()
            stats.append((P.pre, P.end_phase()))
        for c in range(NYC):
            P.op("pool", lambda e, c=c: e.collective_compute("AllGather", ALU.bypass, replica_groups=RG,
                                                             ins=[ymine[c].opt()], outs=[ypair[c].opt()]),
                 dma=("cc_y", c), dma_inc=1)
        P.end_phase()
        final = (l == depth - 1)
        P.pre = "l%d_dense_" % l
        P.override = {"xT": (x_half if l == 0 else XSrc(xh_view)), "ypair": RowChunks(ypair, YR, ncols=1024, rowmap=ypair_rowmap),
                      "out": (out if final else XSrc(xh_view))}
        build_dense(H, final=final, P=P)
        stats.append((P.pre, P.end_phase()))
        if not final:
            for c in range(NXC):
                P.op("pool", lambda e, c=c: e.collective_compute("AllGather", ALU.bypass, replica_groups=RG,
                                                                 ins=[xh[c].opt()], outs=[xpair[c].opt()]),
                     dma=("cc_x", c), dma_inc=1)
            P.end_phase()
    P.pre = ""
    P.override = {}
    for s_ in stats:
        print(s_)
    return P.finish()


_NC = {}


def kernel(x, norm1_g, w_in, gdn_conv_w, gdn_a_log, gdn_dt_bias, gdn_norm_g, diff_lambda, diff_norm_g, swa_sink,
           mlstm_gate_b, mlstm_norm_g, w_branch, w_gate, w_out, norm2_g, w_mlp1, w_mlp2, final_norm_g):
    f32 = lambda a: np.asarray(a, dtype=np.float32)
    x = f32(x)
    norm1_g, w_in, gdn_conv_w, gdn_a_log, gdn_dt_bias, gdn_norm_g = map(f32, (norm1_g, w_in, gdn_conv_w, gdn_a_log,
                                                                            gdn_dt_bias, gdn_norm_g))
    diff_lambda, diff_norm_g, swa_sink, mlstm_gate_b, mlstm_norm_g = map(f32, (diff_lambda, diff_norm_g, swa_sink,
                                                                             mlstm_gate_b, mlstm_norm_g))
    w_branch, w_gate, w_out, norm2_g, w_mlp1, w_mlp2, final_norm_g = map(f32, (w_branch, w_gate, w_out, norm2_g,
                                                                             w_mlp1, w_mlp2, final_norm_g))
    B, S_, D = x.shape
    depth = norm1_g.shape[0]
    H = S_ // 2
    gl = lambda g: np.ascontiguousarray(g.reshape(8, 128).T)
    if "nc" not in _NC:
        _NC["nc"] = build_fused(S_, depth)
    nc = _NC["nc"]
    xT = [fm(x[b]) for b in range(B)]
    cosT, sinT = rope_tables_np(S_)
    cores = [(b, hh) for b in range(B) for hh in range(2)]
    dense_w = [dense_layout(w_gate[l], w_branch[l], w_out[l], w_mlp1[l], w_mlp2[l]) for l in range(depth)]
    identb = np.eye(128, dtype=np.float32).astype(ml_dtypes.bfloat16)
    in_maps = []
    for (b, hh) in cores:
        im = {"x_full": xT[b], "x_half": np.ascontiguousarray(xT[b][:, :, hh * H:(hh + 1) * H])}
        for l in range(depth):
            g1 = gl(norm1_g[l])
            parts = {
                "gdn": gdn_inputs(None, g1, w_in[l], gdn_conv_w[l], gdn_a_log[l], gdn_dt_bias[l], gdn_norm_g[l], hh, S_),
                "diff": diff_inputs(None, g1, w_in[l], diff_lambda[l], diff_norm_g[l], hh, cosT, sinT),
                "swa": swa_inputs(None, g1, w_in[l], swa_sink[l], hh, cosT, sinT),
                "mlstm": mlstm_inputs(None, g1, w_in[l], mlstm_gate_b[l], mlstm_norm_g[l], hh, S_),
                "dense": dict(g1=g1, g2=gl(norm2_g[l]), g3=gl(final_norm_g), identb=identb,
                              msel=np.ascontiguousarray(np.tile(np.array([[1.0 - hh, float(hh)]], np.float32), (128, 1))),
                              **dense_w[l]),
            }
            im["l%d_xn_g1" % l] = g1
            for nm, d in parts.items():
                for k, v in d.items():
                    if k == "xT":
                        continue
                    im["l%d_%s_%s" % (l, nm, k)] = v
        in_maps.append(im)
    r = run_bass_kernel_spmd(nc, in_maps, core_ids=list(range(len(cores)))).results
    for (b, hh), o in zip(cores, r):
        xT[b][:, :, hh * H:(hh + 1) * H] = np.asarray(o["out"])
    return np.stack([unfm(xT[b]) for b in range(B)]).astype(np.float32)
```
